# Optimizing a Trainium2 kernel written in Bass

```python
import math
import jax, jax.numpy as jnp
from jax import lax
import numpy as np

D_MODEL = 2048
BATCH = 2
SEQ = 8192
DEPTH = 2

GRID_W = 64
CTX_LEN = 256

RET_HEADS = 4
RET_DK = 256
RET_DV = 256
RET_W = RET_HEADS * RET_DV
RET_CHUNK = 128
ATT_HEADS = 8
ATT_KV_HEADS = 2
ATT_HD = 128
ATT_W = ATT_HEADS * ATT_HD
ATT_KV_W = ATT_KV_HEADS * ATT_HD
Q_BLOCK = 128
ROPE_THETA = 10000.0
S5_GROUP = 16
S5_W = 768
S5_GROUPS = S5_W // S5_GROUP
S5_STATE = 64

N_BRANCH = 3
DEEPNORM_ALPHA = (2.0 * DEPTH) ** 0.25
DEEPNORM_BETA = (8.0 * DEPTH) ** -0.25
LN_EPS = 1e-6
RMS_EPS = 1e-6

IN_LAYOUT = (
    ("ret_q", RET_HEADS * RET_DK), ("ret_k", RET_HEADS * RET_DK), ("ret_v", RET_W), ("ret_g", RET_W),
    ("att_q", ATT_W), ("att_k", ATT_KV_W), ("att_v", ATT_KV_W), ("att_g", ATT_W),
    ("s5_u", S5_W), ("s5_g", S5_W),
    ("merge", N_BRANCH * D_MODEL),
)
IN_WIDTH = 4 * RET_W + 2 * ATT_W + 2 * ATT_KV_W + 2 * S5_W + N_BRANCH * D_MODEL

kernel_name = "hybrid_retention_gqa_s5_prefix_dit"


def _in_proj(u, w_in, name):
    start = 0
    for seg_name, width in IN_LAYOUT:
        if seg_name == name:
            return u @ w_in[:, start:start + width]
        start += width
    raise KeyError(name)


def _layernorm(x, eps):
    xf = x.astype(jnp.float32)
    mu = jnp.mean(xf, axis=-1, keepdims=True)
    var = jnp.mean(jnp.square(xf - mu), axis=-1, keepdims=True)
    return ((xf - mu) * lax.rsqrt(var + eps)).astype(x.dtype)


def _rmsnorm(x, w):
    xf = x.astype(jnp.float32)
    y = xf * lax.rsqrt(jnp.mean(jnp.square(xf), axis=-1, keepdims=True) + RMS_EPS)
    return y.astype(x.dtype) * w


def _axial_rope(rows, head_dim):
    row = jnp.repeat(jnp.arange(rows, dtype=jnp.float32), GRID_W)
    col = jnp.tile(jnp.arange(GRID_W, dtype=jnp.float32), rows)
    per_axis = head_dim // 4
    inv = ROPE_THETA ** (-jnp.arange(per_axis, dtype=jnp.float32) / per_axis)
    ang = jnp.concatenate([row[:, None] * inv, col[:, None] * inv], axis=-1)
    return jnp.cos(ang), jnp.sin(ang)


def _rope(x, cos, sin):
    half = x.shape[-1] // 2
    x1, x2 = x[..., :half], x[..., half:]
    cs, sn = cos[:, None, :], sin[:, None, :]
    return jnp.concatenate([x1 * cs - x2 * sn, x2 * cs + x1 * sn], axis=-1)


def _to_chunks(t):
    bsz, nh, n, d = t.shape
    return t.reshape(bsz, nh, n // RET_CHUNK, RET_CHUNK, d).transpose(2, 0, 1, 3, 4)


def _retention_dir(q, k, v, log_g, s0, include_diag):
    bsz, nh, n, _ = q.shape
    dv = v.shape[-1]
    log_g = log_g.astype(jnp.float32)
    idx = jnp.arange(RET_CHUNK, dtype=jnp.float32)
    rel = idx[:, None] - idx[None, :]
    mask = (rel >= 0) if include_diag else (rel > 0)
    intra = jnp.where(mask, jnp.exp(log_g[:, None, None] * jnp.where(mask, rel, 0.0)), 0.0)
    q_dec = jnp.exp(log_g[:, None] * (idx + 1.0))[..., None]
    k_dec = jnp.exp(log_g[:, None] * (RET_CHUNK - 1.0 - idx))[..., None]
    chunk_dec = jnp.exp(log_g * RET_CHUNK)[:, None, None]

    def step(state, blk):
        qi, ki, vi = blk
        scores = jnp.einsum('bhqd,bhkd->bhqk', qi, ki) * intra
        out = (jnp.einsum('bhqk,bhkv->bhqv', scores, vi)
               + jnp.einsum('bhqd,bhdv->bhqv', qi * q_dec, state))
        state = state * chunk_dec + jnp.einsum('bhkd,bhkv->bhdv', ki * k_dec, vi)
        return state, out

    s_final, out = lax.scan(step, s0, (_to_chunks(q), _to_chunks(k), _to_chunks(v)))
    return out.transpose(1, 2, 0, 3, 4).reshape(bsz, nh, n, dv), s_final


def _bidir_retention(q_l, k_l, v_l, q_c, k_c, v_c, log_decay):
    bsz = q_l.shape[0]
    zero = jnp.zeros((bsz, RET_HEADS, RET_DK, RET_DV), jnp.float32)
    flip = lambda t: t[:, :, ::-1]
    o_cf, s_f = _retention_dir(q_c, k_c, v_c, log_decay[0], zero, True)
    o_lf, _ = _retention_dir(q_l, k_l, v_l, log_decay[0], s_f, True)
    o_cb, s_b = _retention_dir(flip(q_c), flip(k_c), flip(v_c), log_decay[1], zero, False)
    o_lb, _ = _retention_dir(flip(q_l), flip(k_l), flip(v_l), log_decay[1], s_b, False)
    return o_lf + flip(o_lb), o_cf + flip(o_cb)


def _attn_blocks(q, k, v):
    bsz, n = q.shape[:2]
    nb = n // Q_BLOCK
    grp = ATT_HEADS // ATT_KV_HEADS
    qb = q.reshape(bsz, nb, Q_BLOCK, ATT_KV_HEADS, grp, ATT_HD).swapaxes(0, 1)
    scale = ATT_HD ** -0.5

    def block(qi):
        s = jnp.einsum('bqhgd,bkhd->bhgqk', qi, k).astype(jnp.float32) * scale
        p = jax.nn.softmax(s, axis=-1)
        return jnp.einsum('bhgqk,bkhd->bqhgd', p.astype(v.dtype), v)

    o = lax.map(block, qb)
    return o.swapaxes(0, 1).reshape(bsz, n, ATT_W)


def _s5_dir(u, a_re, a_im, log_dt, b_re, b_im, c_re, c_im, s0_re, s0_im):
    f32 = jnp.float32
    a_re, a_im = a_re.astype(f32), a_im.astype(f32)
    dt = jnp.exp(log_dt.astype(f32))[:, None]
    mag = jnp.exp(a_re * dt)
    abr, abi = mag * jnp.cos(a_im * dt), mag * jnp.sin(a_im * dt)
    den = a_re * a_re + a_im * a_im
    fr = ((abr - 1.0) * a_re + abi * a_im) / den
    fi = (abi * a_re - (abr - 1.0) * a_im) / den
    b_re, b_im = b_re.astype(f32), b_im.astype(f32)
    bbr = fr[..., None] * b_re - fi[..., None] * b_im
    bbi = fr[..., None] * b_im + fi[..., None] * b_re
    bur = jnp.einsum('bngc,gpc->bngp', u, bbr)
    bui = jnp.einsum('bngc,gpc->bngp', u, bbi)
    bur = bur.at[:, 0].add(abr * s0_re - abi * s0_im)
    bui = bui.at[:, 0].add(abr * s0_im + abi * s0_re)
    n = u.shape[1]
    ar_seq = jnp.broadcast_to(abr, (1, n) + abr.shape)
    ai_seq = jnp.broadcast_to(abi, (1, n) + abi.shape)

    def combine(e1, e2):
        a1r, a1i, b1r, b1i = e1
        a2r, a2i, b2r, b2i = e2
        return (a2r * a1r - a2i * a1i, a2r * a1i + a2i * a1r,
                a2r * b1r - a2i * b1i + b2r, a2r * b1i + a2i * b1r + b2i)

    _, _, xr, xi = lax.associative_scan(combine, (ar_seq, ai_seq, bur, bui), axis=1)
    y = (jnp.einsum('bngp,gcp->bngc', xr, c_re.astype(f32))
         - jnp.einsum('bngp,gcp->bngc', xi, c_im.astype(f32)))
    return y, xr[:, -1], xi[:, -1]


def _bidir_s5(u_l, u_c, a_re, a_im, log_dt, b_re, b_im, c_re, c_im, d_skip):
    bsz, n, _ = u_l.shape
    n_ctx = u_c.shape[1]
    ul = u_l.reshape(bsz, n, S5_GROUPS, S5_GROUP)
    uc = u_c.reshape(bsz, n_ctx, S5_GROUPS, S5_GROUP)
    zero = jnp.zeros((bsz, S5_GROUPS, S5_STATE), jnp.float32)
    d = d_skip.reshape(S5_GROUPS, S5_GROUP)
    y_l = d * ul
    y_c = d * uc
    for direction in range(2):
        rev = (lambda t: t[:, ::-1]) if direction == 1 else (lambda t: t)
        pars = (a_re[direction], a_im[direction], log_dt[direction], b_re[direction], b_im[direction],
                c_re[direction], c_im[direction])
        yc, sr, si = _s5_dir(rev(uc), *pars, zero, zero)
        yl, _, _ = _s5_dir(rev(ul), *pars, sr, si)
        y_l = y_l + rev(yl)
        y_c = y_c + rev(yc)
    return y_l.reshape(bsz, n, S5_W), y_c.reshape(bsz, n_ctx, S5_W)


def _finish(u, r, a, s_pre, w_in, ret_gn_w, s5_glu_w, s5_glu_b, w_br_ret, w_br_att, w_br_s5, w_out):
    bsz, n, _ = u.shape
    r = _layernorm(r.transpose(0, 2, 1, 3), LN_EPS).reshape(bsz, n, RET_W) * ret_gn_w
    r = r * jax.nn.silu(_in_proj(u, w_in, 'ret_g'))
    a = a * jax.nn.silu(_in_proj(u, w_in, 'att_g'))
    s = jax.nn.gelu(s_pre)
    s = s * jax.nn.sigmoid(s @ s5_glu_w + s5_glu_b)
    s = s * jax.nn.silu(_in_proj(u, w_in, 's5_g'))
    gates = jax.nn.sigmoid(_in_proj(u, w_in, 'merge')).reshape(bsz, n, N_BRANCH, D_MODEL)
    m = (gates[..., 0, :] * (r @ w_br_ret) + gates[..., 1, :] * (a @ w_br_att)
         + gates[..., 2, :] * (s @ w_br_s5))
    return m @ w_out


def _layer(x, h, c, c_ctx, ada_w, ada_b, w_in, ret_log_decay, ret_gn_w, att_q_norm, att_k_norm,
           s5_a_re, s5_a_im, s5_log_dt, s5_b_re, s5_b_im, s5_c_re, s5_c_im, s5_d, s5_glu_w, s5_glu_b,
           w_br_ret, w_br_att, w_br_s5, w_out, ln_w, ln_b, cos_r, sin_r, cos_a, sin_a, need_ctx):
    bsz, n, _ = x.shape
    n_ctx = h.shape[1]
    shift_l, scale_l, gate_l = jnp.split(jax.nn.silu(c) @ ada_w + ada_b, 3, axis=-1)
    shift_c, scale_c, gate_c = jnp.split(jax.nn.silu(c_ctx) @ ada_w + ada_b, 3, axis=-1)
    u_l = x * (1.0 + scale_l[:, None]) + shift_l[:, None]
    u_c = h * (1.0 + scale_c) + shift_c

    def ret_qkv(u, m, rotary):
        q = _in_proj(u, w_in, 'ret_q').reshape(bsz, m, RET_HEADS, RET_DK)
        k = _in_proj(u, w_in, 'ret_k').reshape(bsz, m, RET_HEADS, RET_DK) * (RET_DK ** -0.5)
        v = _in_proj(u, w_in, 'ret_v').reshape(bsz, m, RET_HEADS, RET_DV)
        if rotary:
            q, k = _rope(q, cos_r, sin_r), _rope(k, cos_r, sin_r)
        return q.transpose(0, 2, 1, 3), k.transpose(0, 2, 1, 3), v.transpose(0, 2, 1, 3)

    r_l, r_c = _bidir_retention(*ret_qkv(u_l, n, True), *ret_qkv(u_c, n_ctx, False), ret_log_decay)

    def att_kv(u, m, rotary):
        k = _rmsnorm(_in_proj(u, w_in, 'att_k').reshape(bsz, m, ATT_KV_HEADS, ATT_HD), att_k_norm)
        v = _in_proj(u, w_in, 'att_v').reshape(bsz, m, ATT_KV_HEADS, ATT_HD)
        return (_rope(k, cos_a, sin_a) if rotary else k), v

    def att_q(u, m, rotary):
        q = _rmsnorm(_in_proj(u, w_in, 'att_q').reshape(bsz, m, ATT_HEADS, ATT_HD), att_q_norm)
        return _rope(q, cos_a, sin_a) if rotary else q

    k_l, v_l = att_kv(u_l, n, True)
    k_c, v_c = att_kv(u_c, n_ctx, False)
    a_l = _attn_blocks(att_q(u_l, n, True), jnp.concatenate([k_c, k_l], axis=1),
                       jnp.concatenate([v_c, v_l], axis=1))

    s_l, s_c = _bidir_s5(_in_proj(u_l, w_in, 's5_u'), _in_proj(u_c, w_in, 's5_u'), s5_a_re, s5_a_im,
                         s5_log_dt, s5_b_re, s5_b_im, s5_c_re, s5_c_im, s5_d)

    y_l = _finish(u_l, r_l, a_l, s_l, w_in, ret_gn_w, s5_glu_w, s5_glu_b, w_br_ret, w_br_att, w_br_s5, w_out)
    x_new = _layernorm(DEEPNORM_ALPHA * x + gate_l[:, None] * y_l, LN_EPS) * ln_w + ln_b
    if not need_ctx:
        return x_new, h
    a_c = _attn_blocks(att_q(u_c, n_ctx, False), k_c, v_c)
    y_c = _finish(u_c, r_c, a_c, s_c, w_in, ret_gn_w, s5_glu_w, s5_glu_b, w_br_ret, w_br_att, w_br_s5, w_out)
    h_new = _layernorm(DEEPNORM_ALPHA * h + gate_c * y_c, LN_EPS) * ln_w + ln_b
    return x_new, h_new


def setup_inputs(seed: int = 0) -> dict:
    key = jax.random.key(seed)
    ks = jax.random.split(key, 32)
    f32 = jnp.float32
    nrm = lambda k, shape, scale: jax.random.normal(k, shape, f32) * scale
    L, D, G, P = DEPTH, D_MODEL, S5_GROUPS, S5_STATE
    base_decay = jnp.log1p(-jnp.exp2(-5.0 - jnp.arange(RET_HEADS, dtype=f32)))
    n_idx = jnp.arange(P, dtype=f32)
    return {
        "x": nrm(ks[0], (BATCH, SEQ, D), 1.0),
        "c": nrm(ks[1], (BATCH, D), 1.0),
        "ctx": nrm(ks[2], (BATCH, CTX_LEN, D), 1.0),
        "c_ctx": nrm(ks[3], (D,), 1.0),
        "ada_w": nrm(ks[4], (L, D, 3 * D), 0.5 * D ** -0.5),
        "ada_b": nrm(ks[5], (L, 3 * D), 0.01),
        "w_in": nrm(ks[6], (L, D, IN_WIDTH), D ** -0.5),
        "ret_log_decay": base_decay * (1.0 + nrm(ks[7], (L, 2, RET_HEADS), 0.05)),
        "ret_gn_w": 1.0 + nrm(ks[8], (L, RET_W), 0.02),
        "att_q_norm": 1.0 + nrm(ks[9], (L, ATT_HD), 0.02),
        "att_k_norm": 1.0 + nrm(ks[10], (L, ATT_HD), 0.02),
        "s5_a_re": -0.5 + nrm(ks[11], (L, 2, G, P), 0.01),
        "s5_a_im": jnp.pi * n_idx + nrm(ks[12], (L, 2, G, P), 0.01),
        "s5_log_dt": jax.random.uniform(ks[13], (L, 2, G), f32, math.log(1e-3), math.log(1e-1)),
        "s5_b_re": nrm(ks[14], (L, 2, G, P, S5_GROUP), (2.0 * S5_GROUP) ** -0.5),
        "s5_b_im": nrm(ks[15], (L, 2, G, P, S5_GROUP), (2.0 * S5_GROUP) ** -0.5),
        "s5_c_re": nrm(ks[16], (L, 2, G, S5_GROUP, P), (2.0 * P) ** -0.5),
        "s5_c_im": nrm(ks[17], (L, 2, G, S5_GROUP, P), (2.0 * P) ** -0.5),
        "s5_d": nrm(ks[18], (L, S5_W), 1.0),
        "s5_glu_w": nrm(ks[19], (L, S5_W, S5_W), S5_W ** -0.5),
        "s5_glu_b": nrm(ks[20], (L, S5_W), 0.01),
        "w_br_ret": nrm(ks[21], (L, RET_W, D), DEEPNORM_BETA * RET_W ** -0.5),
        "w_br_att": nrm(ks[22], (L, ATT_W, D), DEEPNORM_BETA * ATT_W ** -0.5),
        "w_br_s5": nrm(ks[23], (L, S5_W, D), DEEPNORM_BETA * S5_W ** -0.5),
        "w_out": nrm(ks[24], (L, D, D), DEEPNORM_BETA * D ** -0.5),
        "ln_w": 1.0 + nrm(ks[25], (L, D), 0.02),
        "ln_b": nrm(ks[26], (L, D), 0.01),
    }


def reference(x, c, ctx, c_ctx, ada_w, ada_b, w_in, ret_log_decay, ret_gn_w, att_q_norm, att_k_norm,
              s5_a_re, s5_a_im, s5_log_dt, s5_b_re, s5_b_im, s5_c_re, s5_c_im, s5_d, s5_glu_w, s5_glu_b,
              w_br_ret, w_br_att, w_br_s5, w_out, ln_w, ln_b):
    rows = x.shape[1] // GRID_W
    cos_r, sin_r = _axial_rope(rows, RET_DK)
    cos_a, sin_a = _axial_rope(rows, ATT_HD)
    h = ctx
    for l in range(DEPTH):
        x, h = _layer(x, h, c, c_ctx, ada_w[l], ada_b[l], w_in[l], ret_log_decay[l], ret_gn_w[l],
                      att_q_norm[l], att_k_norm[l], s5_a_re[l], s5_a_im[l], s5_log_dt[l], s5_b_re[l],
                      s5_b_im[l], s5_c_re[l], s5_c_im[l], s5_d[l], s5_glu_w[l], s5_glu_b[l], w_br_ret[l],
                      w_br_att[l], w_br_s5[l], w_out[l], ln_w[l], ln_b[l], cos_r, sin_r, cos_a, sin_a,
                      l < DEPTH - 1)
    return x
```

```python
import numpy as np
import concourse.bass as bass
import concourse.mybir as mybir

F32 = mybir.dt.float32
BF16 = mybir.dt.bfloat16
ALU = mybir.AluOpType
AF = mybir.ActivationFunctionType
AX = mybir.AxisListType

ENGS = ("pe", "act", "dve", "pool", "sp")
N_DMA_SEMS = 40


class Res:
    __slots__ = ("name", "w", "r")

    def __init__(self, name):
        self.name = name
        self.w = None
        self.r = []


class Op:
    __slots__ = ("id", "eng", "fn", "deps", "dma", "sig", "ev")

    def __init__(self, id, eng, fn, deps, dma):
        self.id = id
        self.eng = eng
        self.fn = fn
        self.deps = deps
        self.dma = dma
        self.sig = False
        self.ev = None


class Prog:
    def __init__(self, nc):
        self.nc = nc
        self.ops = []
        self.sems = {}
        self.final_dmas = []

    def res(self, name="r"):
        return Res(name)

    def add(self, eng, fn, reads=(), writes=(), dma=False):
        deps = set()
        for r in reads:
            if r.w is not None:
                deps.add(r.w)
        for w in writes:
            if w.w is not None:
                deps.add(w.w)
            deps.update(w.r)
        op = Op(len(self.ops), eng, fn, deps, dma)
        self.ops.append(op)
        for r in reads:
            r.r.append(op.id)
        for w in writes:
            w.w = op.id
            w.r = []
        return op.id

    def emit(self, sem_ctx):
        nc = self.nc
        ops = self.ops
        for op in ops:
            for d in op.deps:
                dop = ops[d]
                if dop.dma:
                    continue
                if dop.eng == op.eng and not op.dma and op.eng == "pe":
                    continue
                dop.sig = True
        cnt = {e: 0 for e in ENGS}
        dcnt = [0] * N_DMA_SEMS
        dlast = [None] * N_DMA_SEMS
        dnext = 0
        for op in ops:
            if op.dma:
                s = dnext
                dnext = (dnext + 1) % N_DMA_SEMS
                if dlast[s] is not None:
                    op.deps.add(dlast[s])
                dcnt[s] += 16
                op.ev = ("d%d" % s, dcnt[s])
                dlast[s] = op.id
            elif op.sig:
                cnt[op.eng] += 1
                op.ev = (op.eng, cnt[op.eng])
        streams = {e: [] for e in ENGS}
        seen = {e: {} for e in ENGS}
        for op in ops:
            waits = {}
            for d in op.deps:
                dop = ops[d]
                if dop.ev is None:
                    continue
                s, v = dop.ev
                if (not dop.dma) and dop.eng == op.eng and op.eng == "pe" and not op.dma:
                    continue
                if seen[op.eng].get(s, 0) >= v:
                    continue
                if waits.get(s, 0) < v:
                    waits[s] = v
            for s, v in waits.items():
                seen[op.eng][s] = v
            streams[op.eng].append((waits, op))
        return streams

    def run(self, streams, sems, block):
        nc = self.nc

        def mk(engname):
            def body(eng):
                for waits, op in streams[engname]:
                    for s, v in waits.items():
                        eng.wait_ge(sems[s], v)
                    if op.fn is None:
                        continue
                    ins = op.fn(eng)
                    if op.ev is not None:
                        s, v = op.ev
                        ins.then_inc(sems[s], 16 if op.dma else 1)
            return body

        block.tensor(mk("pe"))
        block.scalar(mk("act"))
        block.vector(mk("dve"))
        block.gpsimd(mk("pool"))
        block.sync(mk("sp"))


from contextlib import ExitStack
from concourse.bass_utils import run_bass_kernel_spmd


class KB:
    def __init__(self):
        self.nc = bass.Bass("TRN2", target_bir_lowering=False)
        self.p = Prog(self.nc)
        self.es = ExitStack()
        self.n = 0

    def sb(self, shape, dt, name=None):
        self.n += 1
        t = self.es.enter_context(self.nc.sbuf_tensor(name or ("s%d" % self.n), shape, dt))
        return t, self.p.res(name or "s")

    def ps(self, shape=(128, 512), dt=F32):
        self.n += 1
        t = self.es.enter_context(self.nc.psum_tensor("p%d" % self.n, list(shape), dt))
        return t, self.p.res("p")

    def din(self, name, shape, dt=F32):
        return self.nc.dram_tensor(name, list(shape), dt, kind="ExternalInput").ap(), self.p.res(name)

    def dout(self, name, shape, dt=F32):
        return self.nc.dram_tensor(name, list(shape), dt, kind="ExternalOutput").ap(), self.p.res(name)

    def finish(self, out_res):
        p, nc, es = self.p, self.nc, self.es
        p.add("sp", None, reads=out_res)
        p.add("act", None, reads=out_res)
        sems = {e: es.enter_context(nc.semaphore(e)) for e in ENGS}
        for i in range(N_DMA_SEMS):
            sems["d%d" % i] = es.enter_context(nc.semaphore("d%d" % i))
        streams = p.emit(None)
        with nc.Block() as block:
            p.run(streams, sems, block)
        es.close()
        return nc


D = 2048
NLAT = 8192
NCTX = 256
INW = 14336
EPS = 1e-6
SEG = dict(ret_q=0, ret_k=1024, ret_v=2048, ret_g=3072, att_q=4096, att_k=5120, att_v=5376, att_g=5632,
           s5_u=6656, s5_g=7424, merge=8192)


def rope_ops(kb, dst, dres, src, sres, cos, sin, tres, h, tmp, tmpres, extra_reads=()):
    p = kb.p
    x1, x2 = src[:, 0:h], src[:, h:2 * h]
    rd = [sres, tres] + list(extra_reads)
    p.add("dve", lambda e: e.tensor_tensor(out=tmp[:, 0:h], in0=x1, in1=cos, op=ALU.mult), reads=rd, writes=[tmpres])
    p.add("dve", lambda e: e.tensor_tensor(out=tmp[:, h:2 * h], in0=x2, in1=sin, op=ALU.mult), reads=rd, writes=[tmpres])
    p.add("dve", lambda e: e.tensor_tensor(out=tmp[:, 2 * h:3 * h], in0=x2, in1=cos, op=ALU.mult), reads=rd, writes=[tmpres])
    p.add("dve", lambda e: e.tensor_tensor(out=tmp[:, 3 * h:4 * h], in0=x1, in1=sin, op=ALU.mult), reads=rd, writes=[tmpres])
    p.add("dve", lambda e: e.tensor_tensor(out=dst[:, 0:h], in0=tmp[:, 0:h], in1=tmp[:, h:2 * h], op=ALU.subtract), reads=[tmpres], writes=[dres])
    p.add("dve", lambda e: e.tensor_tensor(out=dst[:, h:2 * h], in0=tmp[:, 2 * h:3 * h], in1=tmp[:, 3 * h:4 * h], op=ALU.add), reads=[tmpres], writes=[dres])


def build_LA(NT=17, CBS=None):
    CBS = list(range(28)) if CBS is None else CBS
    kb = KB()
    nc, p = kb.nc, kb.p
    NTOK = NT * 128
    xT, r_xT = kb.din("xT", [D, NTOK])
    modv, r_modv = kb.din("modv", [D, 4])
    w_in, r_w = kb.din("w_in", [D, INW])
    ropeR, r_ropeR = kb.din("ropeR", [128, NT, 2, 128])
    ropeA, r_ropeA = kb.din("ropeA", [128, NT, 2, 64])
    qkw, r_qkw = kb.din("qkw", [128, 2, 128])
    O, r_O = kb.dout("O", [NTOK, INW], BF16)

    mv, r_mv = kb.sb([128, 16, 4], F32)
    sc1, r_sc1 = kb.sb([128, 16, 2], F32)
    uT, r_uT = kb.sb([128, 16, NTOK], BF16)
    tR, r_tR = kb.sb([128, NT, 2, 128], F32)
    tA, r_tA = kb.sb([128, NT, 2, 64], F32)
    tW, r_tW = kb.sb([128, 2, 128], F32)
    xs = [kb.sb([128, NTOK], F32) for _ in range(2)]
    wb = [kb.sb([128, 16, 512], BF16) for _ in range(2)]
    ob = [kb.sb([128, 512], BF16) for _ in range(3)]
    pm = [kb.ps() for _ in range(4)]
    tmp, r_tmp = kb.sb([128, 512], F32)
    xn, r_xn = kb.sb([128, 512], F32)
    junk, r_junk = kb.sb([128, 128], F32)
    ss, r_ss = kb.sb([128, 4], F32)

    p.add("sp", lambda e: e.dma_start(out=mv[:], in_=modv.rearrange("(a p) c -> p a c", p=128)), writes=[r_mv], dma=True)
    p.add("sp", lambda e: e.dma_start(out=tR[:], in_=ropeR), writes=[r_tR], dma=True)
    p.add("sp", lambda e: e.dma_start(out=tA[:], in_=ropeA), writes=[r_tA], dma=True)
    p.add("sp", lambda e: e.dma_start(out=tW[:], in_=qkw), writes=[r_tW], dma=True)
    p.add("dve", lambda e: e.tensor_scalar(out=sc1[:, :, 0], in0=mv[:, :, 1], scalar1=1.0, scalar2=None, op0=ALU.add), reads=[r_mv], writes=[r_sc1])
    p.add("dve", lambda e: e.tensor_scalar(out=sc1[:, :, 1], in0=mv[:, :, 3], scalar1=1.0, scalar2=None, op0=ALU.add), reads=[r_mv], writes=[r_sc1])
    NL = (NT - 1) * 128
    for kt in range(16):
        b = kt % 2
        xsb, r_xsb = xs[b]
        p.add("sp" if kt % 2 == 0 else "act", lambda e, kt=kt, xsb=xsb: e.dma_start(out=xsb[:], in_=xT[kt * 128:(kt + 1) * 128, :]), writes=[r_xsb], dma=True)
        if NL > 0:
            p.add("act", lambda e, kt=kt, xsb=xsb: e.activation(out=uT[:, kt, 0:NL], in_=xsb[:, 0:NL], func=AF.Identity, scale=sc1[:, kt, 0:1], bias=mv[:, kt, 0:1]),
                  reads=[r_xsb, r_sc1, r_mv], writes=[r_uT])
        p.add("act", lambda e, kt=kt, xsb=xsb: e.activation(out=uT[:, kt, NL:NTOK], in_=xsb[:, NL:NTOK], func=AF.Identity, scale=sc1[:, kt, 1:2], bias=mv[:, kt, 2:3]),
              reads=[r_xsb, r_sc1, r_mv], writes=[r_uT])

    cnt = 0
    for ci, cb in enumerate(CBS):
        wbb, r_wbb = wb[ci % 2]
        p.add("pool", lambda e, cb=cb, wbb=wbb: e.dma_start(out=wbb[:], in_=w_in[:, cb * 512:(cb + 1) * 512].rearrange("(a p) n -> p a n", p=128)), writes=[r_wbb], dma=True)
        c0 = cb * 512
        for tt in range(NT):
            pmm, r_pm = pm[cnt % 4]
            obb, r_ob = ob[cnt % 3]
            cnt += 1
            for kt in range(16):
                p.add("pe", lambda e, kt=kt, tt=tt, pmm=pmm, wbb=wbb: e.matmul(pmm[:, :], uT[:, kt, tt * 128:(tt + 1) * 128], wbb[:, kt, :], start=(kt == 0), stop=(kt == 15)),
                      reads=[r_uT, r_wbb], writes=[r_pm])
            for hf in range(2):
                cc = c0 + hf * 256
                sl = slice(hf * 256, hf * 256 + 256)
                if cc < SEG["ret_v"]:
                    cos, sin = tR[:, tt, 0, :], tR[:, tt, 1, :]
                    if cc >= SEG["ret_k"]:
                        p.add("act", lambda e, pmm=pmm, sl=sl: e.activation(out=xn[:, sl], in_=pmm[:, sl], func=AF.Copy, scale=1.0 / 16.0), reads=[r_pm], writes=[r_xn])
                        rope_ops(kb, obb[:, sl], r_ob, xn[:, sl], r_xn, cos, sin, r_tR, 128, tmp, r_tmp)
                    else:
                        rope_ops(kb, obb[:, sl], r_ob, pmm[:, sl], r_pm, cos, sin, r_tR, 128, tmp, r_tmp)
                elif SEG["att_q"] <= cc < SEG["att_v"]:
                    wi = 0 if cc < SEG["att_k"] else 1
                    p.add("act", lambda e, pmm=pmm, sl=sl: e.activation(out=xn[:, sl], in_=pmm[:, sl], func=AF.Copy), reads=[r_pm], writes=[r_xn])
                    for hh in range(2):
                        s2 = slice(hf * 256 + hh * 128, hf * 256 + hh * 128 + 128)
                        p.add("act", lambda e, s2=s2, hh=hh: e.activation(out=junk[:, :], in_=xn[:, s2], func=AF.Square, accum_out=ss[:, hh:hh + 1]), reads=[r_xn], writes=[r_junk, r_ss])
                    p.add("act", lambda e: e.activation(out=ss[:, 2:4], in_=ss[:, 0:2], func=AF.Sqrt, scale=1.0 / 128.0, bias=EPS), reads=[r_ss], writes=[r_ss])
                    p.add("dve", lambda e: e.reciprocal(out=ss[:, 2:4], in_=ss[:, 2:4]), reads=[r_ss], writes=[r_ss])
                    for hh in range(2):
                        s2 = slice(hf * 256 + hh * 128, hf * 256 + hh * 128 + 128)
                        p.add("dve", lambda e, s2=s2, hh=hh, wi=wi: e.scalar_tensor_tensor(out=xn[:, s2], in0=xn[:, s2], scalar=ss[:, 2 + hh:3 + hh], in1=tW[:, wi, :], op0=ALU.mult, op1=ALU.mult),
                              reads=[r_xn, r_ss, r_tW], writes=[r_xn])
                        rope_ops(kb, obb[:, s2], r_ob, xn[:, s2], r_xn, tA[:, tt, 0, :], tA[:, tt, 1, :], r_tA, 64, tmp, r_tmp)
                else:
                    if cc >= SEG["merge"]:
                        fn = AF.Sigmoid
                    elif (SEG["ret_g"] <= cc < SEG["att_q"]) or (SEG["att_g"] <= cc < SEG["s5_u"]) or (SEG["s5_g"] <= cc < SEG["merge"]):
                        fn = AF.Silu
                    else:
                        fn = AF.Copy
                    p.add("act", lambda e, pmm=pmm, sl=sl, obb=obb, fn=fn: e.activation(out=obb[:, sl], in_=pmm[:, sl], func=fn), reads=[r_pm], writes=[r_ob])
            p.add("sp", lambda e, tt=tt, c0=c0, obb=obb: e.dma_start(out=O[tt * 128:(tt + 1) * 128, c0:c0 + 512], in_=obb[:, :]), reads=[r_ob], writes=[r_O], dma=True)
    return kb.finish([r_O])


def build_ATT(NQ, NK, QB=512):
    kb = KB()
    nc, p = kb.nc, kb.p
    qT, r_q = kb.din("qT", [2, 128, NQ], BF16)
    kT, r_k = kb.din("kT", [128, NK], BF16)
    v, r_v = kb.din("v", [NK, 128], BF16)
    aT, r_a = kb.dout("aT", [2, 128, NQ])
    NKT = NK // 128
    qs, r_qs = kb.sb([128, 2, NQ], BF16)
    ks, r_ks = kb.sb([128, NK], BF16)
    vs, r_vs = kb.sb([128, NKT, 128], BF16)
    ones, r_ones = kb.sb([128, 128], BF16)
    pT = [kb.sb([128, QB], BF16) for _ in range(3)]
    pss = [kb.ps() for _ in range(3)]
    pso = [kb.ps() for _ in range(2)]
    psr = [kb.ps() for _ in range(2)]
    rinv, r_rinv = kb.sb([128, QB], F32)
    ob = [kb.sb([128, QB], F32) for _ in range(2)]
    p.add("sp", lambda e: e.dma_start(out=qs[:], in_=qT.rearrange("h p n -> p h n")), writes=[r_qs], dma=True)
    p.add("act", lambda e: e.dma_start(out=ks[:], in_=kT), writes=[r_ks], dma=True)
    p.add("sp", lambda e: e.dma_start(out=vs[:], in_=v.rearrange("(a p) d -> p a d", p=128)), writes=[r_vs], dma=True)
    p.add("pool", lambda e: e.memset(ones[:], 1.0), writes=[r_ones])
    sc = 128.0 ** -0.5
    it = 0
    blk = 0
    for h in range(2):
        for qb in range(NQ // QB):
            po, r_po = pso[blk % 2]
            pr, r_pr = psr[blk % 2]
            obb, r_ob = ob[blk % 2]
            blk += 1
            qsl = slice(qb * QB, (qb + 1) * QB)
            for kt in range(NKT):
                pscore, r_ps = pss[it % 3]
                pt_, r_pt = pT[it % 3]
                it += 1
                p.add("pe", lambda e, kt=kt, h=h, qsl=qsl, pscore=pscore: e.matmul(pscore[:, 0:QB], ks[:, kt * 128:(kt + 1) * 128], qs[:, h, qsl], start=True, stop=True),
                      reads=[r_ks, r_qs], writes=[r_ps])
                p.add("act", lambda e, pscore=pscore, pt_=pt_: e.activation(out=pt_[:, :], in_=pscore[:, 0:QB], func=AF.Exp, scale=sc), reads=[r_ps], writes=[r_pt])
                p.add("pe", lambda e, kt=kt, po=po, pt_=pt_: e.matmul(po[:, 0:QB], vs[:, kt, :], pt_[:, :], start=(kt == 0), stop=(kt == NKT - 1)), reads=[r_vs, r_pt], writes=[r_po])
                p.add("pe", lambda e, kt=kt, pr=pr, pt_=pt_: e.matmul(pr[:, 0:QB], ones[:, :], pt_[:, :], start=(kt == 0), stop=(kt == NKT - 1)), reads=[r_ones, r_pt], writes=[r_pr])
            p.add("dve", lambda e, pr=pr: e.reciprocal(out=rinv[:, :], in_=pr[:, 0:QB]), reads=[r_pr], writes=[r_rinv])
            p.add("dve", lambda e, po=po, obb=obb: e.tensor_tensor(out=obb[:, :], in0=po[:, 0:QB], in1=rinv[:, :], op=ALU.mult), reads=[r_po, r_rinv], writes=[r_ob])
            p.add("sp", lambda e, h=h, qsl=qsl, obb=obb: e.dma_start(out=aT[h, :, qsl], in_=obb[:, :]), reads=[r_ob], writes=[r_a], dma=True)
    return kb.finish([r_a])


def build_RET(N):
    kb = KB()
    nc, p = kb.nc, kb.p
    NCH = N // 128
    qT, r_q = kb.din("qT", [256, N], BF16)
    kT, r_k = kb.din("kT", [256, N], BF16)
    kk, r_kk = kb.din("k", [N, 256], BF16)
    vv, r_vv = kb.din("v", [N, 256], BF16)
    cst, r_cst = kb.din("cst", [128, 260])
    ro, r_ro = kb.dout("ro", [N, 256])
    qs, r_qs = kb.sb([128, 2, N], BF16)
    ks, r_ks = kb.sb([128, 2, N], BF16)
    kt_, r_kt = kb.sb([128, NCH, 256], BF16)
    vs, r_vs = kb.sb([128, NCH, 256], BF16)
    cs, r_cs = kb.sb([128, 260], F32)
    intra, r_intra = kb.sb([128, 128], F32)
    dec, r_dec = kb.sb([128, 4], F32)
    S, r_S = kb.sb([128, 2, 256], F32)
    Sb, r_Sb = kb.sb([128, 2, 256], BF16)
    pT = [kb.sb([128, 128], BF16) for _ in range(2)]
    kd = [kb.sb([128, 256], BF16) for _ in range(2)]
    isb = [kb.sb([128, 256], F32) for _ in range(2)]
    ob = [kb.sb([128, 256], F32) for _ in range(2)]
    ps_s = [kb.ps() for _ in range(2)]
    ps_i = [kb.ps() for _ in range(2)]
    ps_x = [kb.ps() for _ in range(2)]
    ps_u = [kb.ps() for _ in range(2)]
    p.add("sp", lambda e: e.dma_start(out=cs[:], in_=cst), writes=[r_cs], dma=True)
    p.add("sp", lambda e: e.dma_start(out=qs[:], in_=qT.rearrange("(a p) n -> p a n", p=128)), writes=[r_qs], dma=True)
    p.add("act", lambda e: e.dma_start(out=ks[:], in_=kT.rearrange("(a p) n -> p a n", p=128)), writes=[r_ks], dma=True)
    p.add("sp", lambda e: e.dma_start(out=kt_[:], in_=kk.rearrange("(a p) d -> p a d", p=128)), writes=[r_kt], dma=True)
    p.add("act", lambda e: e.dma_start(out=vs[:], in_=vv.rearrange("(a p) d -> p a d", p=128)), writes=[r_vs], dma=True)
    lg = cs[:, 256:257]
    p.add("act", lambda e: e.activation(out=intra[:, :], in_=cs[:, 0:128], func=AF.Exp, scale=lg), reads=[r_cs], writes=[r_intra])
    p.add("dve", lambda e: e.tensor_tensor(out=intra[:, :], in0=intra[:, :], in1=cs[:, 128:256], op=ALU.mult), reads=[r_intra, r_cs], writes=[r_intra])
    p.add("act", lambda e: e.activation(out=dec[:, 0:3], in_=cs[:, 257:260], func=AF.Exp, scale=lg), reads=[r_cs], writes=[r_dec])
    p.add("pool", lambda e: e.memset(S[:], 0.0), writes=[r_S])
    p.add("pool", lambda e: e.memset(Sb[:], 0.0), writes=[r_Sb])
    for c in range(NCH):
        b = c % 2
        csl = slice(c * 128, (c + 1) * 128)
        pss, r_pss = ps_s[b]
        psi, r_psi = ps_i[b]
        psx, r_psx = ps_x[b]
        psu, r_psu = ps_u[b]
        ptb, r_ptb = pT[b]
        kdb, r_kdb = kd[b]
        isbb, r_isb = isb[b]
        obb, r_ob = ob[b]
        for dt in range(2):
            p.add("pe", lambda e, dt=dt, csl=csl, pss=pss: e.matmul(pss[:, 0:128], ks[:, dt, csl], qs[:, dt, csl], start=(dt == 0), stop=(dt == 1)), reads=[r_ks, r_qs], writes=[r_pss])
        p.add("dve", lambda e, pss=pss, ptb=ptb: e.tensor_tensor(out=ptb[:, :], in0=pss[:, 0:128], in1=intra[:, :], op=ALU.mult), reads=[r_pss, r_intra], writes=[r_ptb])
        p.add("pe", lambda e, c=c, psi=psi, ptb=ptb: e.matmul(psi[:, 0:256], ptb[:, :], vs[:, c, :], start=True, stop=True), reads=[r_ptb, r_vs], writes=[r_psi])
        for dt in range(2):
            p.add("pe", lambda e, dt=dt, csl=csl, psx=psx: e.matmul(psx[:, 0:256], qs[:, dt, csl], Sb[:, dt, :], start=(dt == 0), stop=(dt == 1)), reads=[r_qs, r_Sb], writes=[r_psx])
        p.add("act", lambda e, psi=psi, isbb=isbb: e.activation(out=isbb[:, :], in_=psi[:, 0:256], func=AF.Copy), reads=[r_psi], writes=[r_isb])
        p.add("dve", lambda e, psx=psx, isbb=isbb, obb=obb: e.scalar_tensor_tensor(out=obb[:, :], in0=psx[:, 0:256], scalar=dec[:, 0:1], in1=isbb[:, :], op0=ALU.mult, op1=ALU.add),
              reads=[r_psx, r_dec, r_isb], writes=[r_ob])
        p.add("sp", lambda e, csl=csl, obb=obb: e.dma_start(out=ro[csl, :], in_=obb[:, :]), reads=[r_ob], writes=[r_ro], dma=True)
        p.add("act", lambda e, c=c, kdb=kdb: e.activation(out=kdb[:, :], in_=kt_[:, c, :], func=AF.Copy, scale=dec[:, 1:2]), reads=[r_kt, r_dec], writes=[r_kdb])
        for dt in range(2):
            p.add("pe", lambda e, dt=dt, c=c, psu=psu, kdb=kdb: e.matmul(psu[:, dt * 256:(dt + 1) * 256], kdb[:, dt * 128:(dt + 1) * 128], vs[:, c, :], start=True, stop=True), reads=[r_kdb, r_vs], writes=[r_psu])
        for dt in range(2):
            p.add("dve", lambda e, dt=dt, psu=psu: e.scalar_tensor_tensor(out=S[:, dt, :], in0=S[:, dt, :], scalar=dec[:, 2:3], in1=psu[:, dt * 256:(dt + 1) * 256], op0=ALU.mult, op1=ALU.add),
                  reads=[r_S, r_dec, r_psu], writes=[r_S])
        p.add("act", lambda e: e.activation(out=Sb[:, :, :], in_=S[:, :, :], func=AF.Copy), reads=[r_S], writes=[r_Sb])
    return kb.finish([r_ro])


PI = float(np.pi)


def range_reduce(kb, dst, r_dst, src, r_src, off, ki, kf, tq, r_sc):
    p = kb.p
    I2P = 1.0 / (2 * PI)
    dve = lambda fn, rd, wr: p.add("dve", fn, reads=rd, writes=wr)
    dve(lambda e: e.tensor_scalar(out=ki, in0=src, scalar1=I2P, scalar2=off * I2P, op0=ALU.mult, op1=ALU.add), [r_src], [r_sc])
    dve(lambda e: e.tensor_copy(out=kf, in_=ki), [r_sc], [r_sc])
    dve(lambda e: e.tensor_scalar(out=tq, in0=src, scalar1=off, scalar2=None, op0=ALU.add), [r_src], [r_sc])
    dve(lambda e: e.scalar_tensor_tensor(out=dst, in0=kf, scalar=-2 * PI, in1=tq, op0=ALU.mult, op1=ALU.add), [r_sc], [r_dst])
    dve(lambda e: e.tensor_scalar(out=tq, in0=dst, scalar1=PI, scalar2=2 * PI, op0=ALU.is_gt, op1=ALU.mult), [r_dst], [r_sc])
    dve(lambda e: e.tensor_tensor(out=dst, in0=dst, in1=tq, op=ALU.subtract), [r_dst, r_sc], [r_dst])
    dve(lambda e: e.tensor_scalar(out=tq, in0=dst, scalar1=-PI, scalar2=2 * PI, op0=ALU.is_lt, op1=ALU.mult), [r_dst], [r_sc])
    dve(lambda e: e.tensor_tensor(out=dst, in0=dst, in1=tq, op=ALU.add), [r_dst, r_sc], [r_dst])


def build_S5(N, NB=512):
    kb = KB()
    nc, p = kb.nc, kb.p
    NBLK = N // NB
    uT, r_u = kb.din("uT", [6, 32, N], BF16)
    prm, r_prm = kb.din("prm", [128, 3, 6])
    Bm, r_Bm = kb.din("Bm", [128, 6, 2, 32])
    Cm, r_Cm = kb.din("Cm", [128, 6, 2, 32])
    dsk, r_dsk = kb.din("dsk", [32, 6])
    tix, r_tix = kb.din("tix", [128, NB])
    idn, r_idn = kb.din("idn", [128, 128])
    yT, r_y = kb.dout("yT", [6, 32, N])
    us, r_us = kb.sb([32, 6, N], BF16)
    pr, r_pr = kb.sb([128, 3, 6], F32)
    Bs, r_Bs = kb.sb([128, 6, 2, 32], F32)
    Cs, r_Cs = kb.sb([128, 6, 2, 32], F32)
    Cb, r_Cb = kb.sb([128, 6, 2, 32], BF16)
    ds, r_ds = kb.sb([32, 6], F32)
    tx, r_tx = kb.sb([128, NB], F32)
    ident, r_id = kb.sb([128, 128], F32)
    w_, r_w = kb.sb([128, 16, 6], F32)
    bb, r_bb = kb.sb([128, 6, 2, 32], F32)
    BT, r_BT = kb.sb([32, 6, 2, 128], BF16)
    cosT, r_cos = kb.sb([128, 6, NB], F32)
    sinT, r_sin = kb.sb([128, 6, NB], F32)
    rtab, r_rt = kb.sb([128, 6, NB], F32)
    wre, r_wre = kb.sb([128, 6, NB], F32)
    wim, r_wim = kb.sb([128, 6, NB], F32)
    ini, r_ini = kb.sb([128, 2, 6], F32)
    tmpc, r_tmpc = kb.sb([128, 4, 6], F32)
    tt_, r_tt = kb.sb([128, 4, NB], F32)
    bp, r_bp = kb.sb([128, 2, NB], F32)
    xb = [kb.sb([128, 2, NB], BF16) for _ in range(2)]
    yo = [kb.sb([32, 6, NB], F32) for _ in range(2)]
    pbu = [kb.ps() for _ in range(4)]
    py = [kb.ps() for _ in range(2)]
    ptr, r_ptr = kb.ps()
    ki, r_sc = kb.sb([128, NB], mybir.dt.int32)
    kf, _ = kb.sb([128, NB], F32)
    tq, _ = kb.sb([128, NB], F32)
    for dst, src, r, eng in ((us, uT.rearrange("m p n -> p m n"), r_us, "sp"), (pr, prm, r_pr, "act"), (Bs, Bm, r_Bs, "sp"), (Cs, Cm, r_Cs, "act"),
                             (ds, dsk, r_ds, "sp"), (tx, tix, r_tx, "act"), (ident, idn, r_id, "sp")):
        p.add(eng, lambda e, dst=dst, src=src: e.dma_start(out=dst[:], in_=src), writes=[r], dma=True)
    W = lambda i: w_[:, i, :]
    are, aim, ldt = pr[:, 0, :], pr[:, 1, :], pr[:, 2, :]
    dve = lambda fn, rd, wr: p.add("dve", fn, reads=rd, writes=wr)
    act = lambda fn, rd, wr: p.add("act", fn, reads=rd, writes=wr)
    act(lambda e: e.activation(out=W(0), in_=ldt, func=AF.Exp), [r_pr], [r_w])
    dve(lambda e: e.tensor_tensor(out=W(1), in0=are, in1=W(0), op=ALU.mult), [r_pr, r_w], [r_w])
    act(lambda e: e.activation(out=W(1), in_=W(1), func=AF.Exp), [r_w], [r_w])
    dve(lambda e: e.tensor_tensor(out=W(2), in0=aim, in1=W(0), op=ALU.mult), [r_pr, r_w], [r_w])
    range_reduce(kb, W(3), r_w, W(2), r_w, 0.0, ki[:, 0:6], kf[:, 0:6], tq[:, 0:6], r_sc)
    range_reduce(kb, W(4), r_w, W(2), r_w, 0.5 * PI, ki[:, 0:6], kf[:, 0:6], tq[:, 0:6], r_sc)
    act(lambda e: e.activation(out=W(5), in_=W(3), func=AF.Sin), [r_w], [r_w])
    act(lambda e: e.activation(out=W(6), in_=W(4), func=AF.Sin), [r_w], [r_w])
    dve(lambda e: e.tensor_tensor(out=W(7), in0=W(1), in1=W(6), op=ALU.mult), [r_w], [r_w])
    dve(lambda e: e.tensor_tensor(out=W(8), in0=W(1), in1=W(5), op=ALU.mult), [r_w], [r_w])
    dve(lambda e: e.tensor_tensor(out=W(13), in0=are, in1=are, op=ALU.mult), [r_pr], [r_w])
    dve(lambda e: e.tensor_tensor(out=W(14), in0=aim, in1=aim, op=ALU.mult), [r_pr], [r_w])
    dve(lambda e: e.tensor_tensor(out=W(9), in0=W(13), in1=W(14), op=ALU.add), [r_w], [r_w])
    dve(lambda e: e.reciprocal(out=W(9), in_=W(9)), [r_w], [r_w])
    dve(lambda e: e.tensor_scalar(out=W(12), in0=W(7), scalar1=-1.0, scalar2=None, op0=ALU.add), [r_w], [r_w])
    dve(lambda e: e.tensor_tensor(out=W(13), in0=W(12), in1=are, op=ALU.mult), [r_w, r_pr], [r_w])
    dve(lambda e: e.tensor_tensor(out=W(14), in0=W(8), in1=aim, op=ALU.mult), [r_w, r_pr], [r_w])
    dve(lambda e: e.tensor_tensor(out=W(10), in0=W(13), in1=W(14), op=ALU.add), [r_w], [r_w])
    dve(lambda e: e.tensor_tensor(out=W(10), in0=W(10), in1=W(9), op=ALU.mult), [r_w], [r_w])
    dve(lambda e: e.tensor_tensor(out=W(13), in0=W(8), in1=are, op=ALU.mult), [r_w, r_pr], [r_w])
    dve(lambda e: e.tensor_tensor(out=W(14), in0=W(12), in1=aim, op=ALU.mult), [r_w, r_pr], [r_w])
    dve(lambda e: e.tensor_tensor(out=W(11), in0=W(13), in1=W(14), op=ALU.subtract), [r_w], [r_w])
    dve(lambda e: e.tensor_tensor(out=W(11), in0=W(11), in1=W(9), op=ALU.mult), [r_w], [r_w])
    dve(lambda e: e.tensor_scalar(out=W(15), in0=W(2), scalar1=float(NB), scalar2=None, op0=ALU.mult), [r_w], [r_w])
    range_reduce(kb, tmpc[:, 0, :], r_tmpc, W(15), r_w, 0.0, ki[:, 0:6], kf[:, 0:6], tq[:, 0:6], r_sc)
    range_reduce(kb, tmpc[:, 1, :], r_tmpc, W(15), r_w, 0.5 * PI, ki[:, 0:6], kf[:, 0:6], tq[:, 0:6], r_sc)
    act(lambda e: e.activation(out=tmpc[:, 2, :], in_=tmpc[:, 0, :], func=AF.Sin), [r_tmpc], [r_tmpc])
    act(lambda e: e.activation(out=tmpc[:, 3, :], in_=tmpc[:, 1, :], func=AF.Sin), [r_tmpc], [r_tmpc])
    act(lambda e: e.activation(out=Cb[:, :, 0, :], in_=Cs[:, :, 0, :], func=AF.Copy), [r_Cs], [r_Cb])
    act(lambda e: e.activation(out=Cb[:, :, 1, :], in_=Cs[:, :, 1, :], func=AF.Copy, scale=-1.0), [r_Cs], [r_Cb])
    for mt in range(6):
        fr, fi = w_[:, 10, mt:mt + 1], w_[:, 11, mt:mt + 1]
        dve(lambda e, mt=mt, fi=fi: e.tensor_scalar(out=bb[:, mt, 0, :], in0=Bs[:, mt, 1, :], scalar1=fi, scalar2=None, op0=ALU.mult), [r_Bs, r_w], [r_bb])
        dve(lambda e, mt=mt, fr=fr: e.scalar_tensor_tensor(out=bb[:, mt, 0, :], in0=Bs[:, mt, 0, :], scalar=fr, in1=bb[:, mt, 0, :], op0=ALU.mult, op1=ALU.subtract), [r_Bs, r_w, r_bb], [r_bb])
        dve(lambda e, mt=mt, fi=fi: e.tensor_scalar(out=bb[:, mt, 1, :], in0=Bs[:, mt, 0, :], scalar1=fi, scalar2=None, op0=ALU.mult), [r_Bs, r_w], [r_bb])
        dve(lambda e, mt=mt, fr=fr: e.scalar_tensor_tensor(out=bb[:, mt, 1, :], in0=Bs[:, mt, 1, :], scalar=fr, in1=bb[:, mt, 1, :], op0=ALU.mult, op1=ALU.add), [r_Bs, r_w, r_bb], [r_bb])
        for ri in range(2):
            p.add("pe", lambda e, mt=mt, ri=ri: e.transpose(ptr[0:32, 0:128], bb[:, mt, ri, :], ident[:, :]), reads=[r_bb, r_id], writes=[r_ptr])
            act(lambda e, mt=mt, ri=ri: e.activation(out=BT[:, mt, ri, :], in_=ptr[0:32, 0:128], func=AF.Copy), [r_ptr], [r_BT])
        ang = w_[:, 2, mt:mt + 1]
        dve(lambda e, ang=ang: e.tensor_scalar(out=tt_[:, 2, :], in0=tx[:, :], scalar1=ang, scalar2=None, op0=ALU.mult), [r_tx, r_w], [r_tt])
        range_reduce(kb, tt_[:, 0, :], r_tt, tt_[:, 2, :], r_tt, 0.0, ki[:, :], kf[:, :], tq[:, :], r_sc)
        range_reduce(kb, tt_[:, 1, :], r_tt, tt_[:, 2, :], r_tt, 0.5 * PI, ki[:, :], kf[:, :], tq[:, :], r_sc)
        act(lambda e, mt=mt: e.activation(out=sinT[:, mt, :], in_=tt_[:, 0, :], func=AF.Sin), [r_tt], [r_sin])
        act(lambda e, mt=mt: e.activation(out=cosT[:, mt, :], in_=tt_[:, 1, :], func=AF.Sin), [r_tt], [r_cos])
        act(lambda e, mt=mt: e.activation(out=rtab[:, mt, :], in_=tx[:, :], func=AF.Identity, scale=0.0, bias=w_[:, 1, mt:mt + 1]), [r_tx, r_w], [r_rt])
    p.add("pool", lambda e: e.memset(ini[:], 0.0), writes=[r_ini])
    k = 0
    for blk in range(NBLK):
        bsl = slice(blk * NB, (blk + 1) * NB)
        yob, r_yo = yo[blk % 2]
        pyb, r_py = py[blk % 2]
        for mt in range(6):
            pre, r_pre = pbu[(2 * k) % 4]
            pim, r_pim = pbu[(2 * k + 1) % 4]
            xbb, r_xb = xb[k % 2]
            k += 1
            p.add("pe", lambda e, mt=mt, bsl=bsl, pre=pre: e.matmul(pre[:, 0:NB], BT[:, mt, 0, :], us[:, mt, bsl], start=True, stop=True), reads=[r_BT, r_us], writes=[r_pre])
            p.add("pe", lambda e, mt=mt, bsl=bsl, pim=pim: e.matmul(pim[:, 0:NB], BT[:, mt, 1, :], us[:, mt, bsl], start=True, stop=True), reads=[r_BT, r_us], writes=[r_pim])
            c_, s_ = cosT[:, mt, :], sinT[:, mt, :]
            dve(lambda e, pre=pre, c_=c_: e.tensor_tensor(out=tt_[:, 0, :], in0=pre[:, 0:NB], in1=c_, op=ALU.mult), [r_pre, r_cos], [r_tt])
            dve(lambda e, pim=pim, s_=s_: e.tensor_tensor(out=tt_[:, 1, :], in0=pim[:, 0:NB], in1=s_, op=ALU.mult), [r_pim, r_sin], [r_tt])
            dve(lambda e, pim=pim, c_=c_: e.tensor_tensor(out=tt_[:, 2, :], in0=pim[:, 0:NB], in1=c_, op=ALU.mult), [r_pim, r_cos], [r_tt])
            dve(lambda e, pre=pre, s_=s_: e.tensor_tensor(out=tt_[:, 3, :], in0=pre[:, 0:NB], in1=s_, op=ALU.mult), [r_pre, r_sin], [r_tt])
            dve(lambda e: e.tensor_tensor(out=bp[:, 0, :], in0=tt_[:, 0, :], in1=tt_[:, 1, :], op=ALU.add), [r_tt], [r_bp])
            dve(lambda e: e.tensor_tensor(out=bp[:, 1, :], in0=tt_[:, 2, :], in1=tt_[:, 3, :], op=ALU.subtract), [r_tt], [r_bp])
            dve(lambda e, mt=mt: e.tensor_tensor_scan(out=wre[:, mt, :], data0=rtab[:, mt, :], data1=bp[:, 0, :], initial=ini[:, 0, mt:mt + 1], op0=ALU.mult, op1=ALU.add), [r_rt, r_bp, r_ini], [r_wre])
            dve(lambda e, mt=mt: e.tensor_tensor_scan(out=wim[:, mt, :], data0=rtab[:, mt, :], data1=bp[:, 1, :], initial=ini[:, 1, mt:mt + 1], op0=ALU.mult, op1=ALU.add), [r_rt, r_bp, r_ini], [r_wim])
            dve(lambda e, mt=mt, c_=c_: e.tensor_tensor(out=tt_[:, 0, :], in0=wre[:, mt, :], in1=c_, op=ALU.mult), [r_wre, r_cos], [r_tt])
            dve(lambda e, mt=mt, s_=s_: e.tensor_tensor(out=tt_[:, 1, :], in0=wim[:, mt, :], in1=s_, op=ALU.mult), [r_wim, r_sin], [r_tt])
            dve(lambda e, mt=mt, c_=c_: e.tensor_tensor(out=tt_[:, 2, :], in0=wim[:, mt, :], in1=c_, op=ALU.mult), [r_wim, r_cos], [r_tt])
            dve(lambda e, mt=mt, s_=s_: e.tensor_tensor(out=tt_[:, 3, :], in0=wre[:, mt, :], in1=s_, op=ALU.mult), [r_wre, r_sin], [r_tt])
            dve(lambda e, xbb=xbb: e.tensor_tensor(out=xbb[:, 0, :], in0=tt_[:, 0, :], in1=tt_[:, 1, :], op=ALU.subtract), [r_tt], [r_xb])
            dve(lambda e, xbb=xbb: e.tensor_tensor(out=xbb[:, 1, :], in0=tt_[:, 2, :], in1=tt_[:, 3, :], op=ALU.add), [r_tt], [r_xb])
            p.add("pe", lambda e, mt=mt, pyb=pyb, xbb=xbb: e.matmul(pyb[0:32, 0:NB], Cb[:, mt, 0, :], xbb[:, 0, :], start=True, stop=False), reads=[r_Cb, r_xb], writes=[r_py])
            p.add("pe", lambda e, mt=mt, pyb=pyb, xbb=xbb: e.matmul(pyb[0:32, 0:NB], Cb[:, mt, 1, :], xbb[:, 1, :], start=False, stop=True), reads=[r_Cb, r_xb], writes=[r_py])
            dve(lambda e, mt=mt, bsl=bsl, pyb=pyb, yob=yob: e.scalar_tensor_tensor(out=yob[:, mt, :], in0=us[:, mt, bsl], scalar=ds[:, mt:mt + 1], in1=pyb[0:32, 0:NB], op0=ALU.mult, op1=ALU.add),
                [r_us, r_ds, r_py], [r_yo])
        cN, sN = tmpc[:, 3, :], tmpc[:, 2, :]
        wlr, wli = wre[:, :, NB - 1], wim[:, :, NB - 1]
        dve(lambda e: e.tensor_tensor(out=tmpc[:, 0, :], in0=wlr, in1=cN, op=ALU.mult), [r_wre, r_tmpc], [r_tmpc])
        dve(lambda e: e.tensor_tensor(out=tmpc[:, 1, :], in0=wli, in1=sN, op=ALU.mult), [r_wim, r_tmpc], [r_tmpc])
        dve(lambda e: e.tensor_tensor(out=ini[:, 0, :], in0=tmpc[:, 0, :], in1=tmpc[:, 1, :], op=ALU.subtract), [r_tmpc], [r_ini])
        dve(lambda e: e.tensor_tensor(out=tmpc[:, 0, :], in0=wli, in1=cN, op=ALU.mult), [r_wim, r_tmpc], [r_tmpc])
        dve(lambda e: e.tensor_tensor(out=tmpc[:, 1, :], in0=wlr, in1=sN, op=ALU.mult), [r_wre, r_tmpc], [r_tmpc])
        dve(lambda e: e.tensor_tensor(out=ini[:, 1, :], in0=tmpc[:, 0, :], in1=tmpc[:, 1, :], op=ALU.add), [r_tmpc], [r_ini])
        p.add("sp", lambda e, bsl=bsl, yob=yob: e.dma_start(out=yT[:, :, bsl].rearrange("m p n -> p m n"), in_=yob[:, :, :]), reads=[r_yo], writes=[r_y], dma=True)
    return kb.finish([r_y])


def build_L0():
    kb = KB()
    nc, p = kb.nc, kb.p
    adaw, r_aw = kb.din("adaw", [2, D, 768])
    adab, r_ab = kb.din("adab", [128, 2, 6])
    c3T, r_c = kb.din("c3T", [D, 3])
    modT, r_m = kb.dout("modT", [2, 768, 3])
    aw = [kb.sb([128, 16, 768], F32) for _ in range(2)]
    cs, r_cs = kb.sb([128, 16, 3], F32)
    sc, r_sc = kb.sb([128, 16, 3], F32)
    bs, r_bs = kb.sb([128, 2, 6], F32)
    ob, r_ob = kb.sb([128, 2, 6, 3], F32)
    pp = [kb.ps() for _ in range(2)]
    p.add("sp", lambda e: e.dma_start(out=cs[:], in_=c3T.rearrange("(a p) c -> p a c", p=128)), writes=[r_cs], dma=True)
    p.add("sp", lambda e: e.dma_start(out=bs[:], in_=adab), writes=[r_bs], dma=True)
    p.add("act", lambda e: e.activation(out=sc[:], in_=cs[:], func=AF.Silu), reads=[r_cs], writes=[r_sc])
    k = 0
    for l in range(2):
        awl, r_awl = aw[l]
        p.add("sp" if l == 0 else "act", lambda e, l=l, awl=awl: e.dma_start(out=awl[:], in_=adaw[l].rearrange("(a p) n -> p a n", p=128)), writes=[r_awl], dma=True)
        for j in range(6):
            pj, r_pj = pp[k % 2]
            k += 1
            for kt in range(16):
                p.add("pe", lambda e, kt=kt, j=j, awl=awl, pj=pj: e.matmul(pj[:, 0:3], awl[:, kt, j * 128:(j + 1) * 128], sc[:, kt, :], start=(kt == 0), stop=(kt == 15)), reads=[r_awl, r_sc], writes=[r_pj])
            p.add("act", lambda e, l=l, j=j, pj=pj: e.activation(out=ob[:, l, j, :], in_=pj[:, 0:3], func=AF.Identity, bias=bs[:, l, j:j + 1]), reads=[r_pj, r_bs], writes=[r_ob])
    p.add("sp", lambda e: e.dma_start(out=modT.rearrange("l (j p) c -> p l j c", p=128), in_=ob[:]), reads=[r_ob], writes=[r_m], dma=True)
    return kb.finish([r_m])


def build_LC1(NT=17):
    kb = KB()
    nc, p = kb.nc, kb.p
    T = NT * 128
    rf, r_rf = kb.din("rf", [T, 1024])
    rb, r_rb = kb.din("rb", [T, 1024])
    g, r_g = kb.din("g", [T, 1024], BF16)
    gnw, r_gn = kb.din("gnw", [128, 1024])
    ro, r_ro = kb.dout("r", [T, 1024], BF16)
    gw, r_gw = kb.sb([128, 1024], F32)
    p.add("sp", lambda e: e.dma_start(out=gw[:], in_=gnw), writes=[r_gw], dma=True)
    A = [kb.sb([128, 1024], F32) for _ in range(2)]
    B = [kb.sb([128, 1024], F32) for _ in range(2)]
    Gt = [kb.sb([128, 1024], BF16) for _ in range(2)]
    Ot = [kb.sb([128, 1024], BF16) for _ in range(2)]
    st, r_st = kb.sb([128, 4, 6], F32)
    mv, r_mv = kb.sb([128, 4, 2], F32)
    rs, r_rs = kb.sb([128, 4], F32)
    for tt in range(NT):
        b = tt % 2
        a_, r_a = A[b]; b_, r_b = B[b]; g_, r_g_ = Gt[b]; o_, r_o = Ot[b]
        tsl = slice(tt * 128, (tt + 1) * 128)
        p.add("sp", lambda e, a_=a_, tsl=tsl: e.dma_start(out=a_[:], in_=rf[tsl, :]), writes=[r_a], dma=True)
        p.add("act", lambda e, b_=b_, tsl=tsl: e.dma_start(out=b_[:], in_=rb[tsl, :]), writes=[r_b], dma=True)
        p.add("sp", lambda e, g_=g_, tsl=tsl: e.dma_start(out=g_[:], in_=g[tsl, :]), writes=[r_g_], dma=True)
        p.add("pool", lambda e, a_=a_, b_=b_: e.tensor_tensor(out=a_[:], in0=a_[:], in1=b_[:], op=ALU.add), reads=[r_a, r_b], writes=[r_a])
        for h in range(4):
            p.add("dve", lambda e, h=h, a_=a_: e.bn_stats(out=st[:, h, :], in_=a_[:, h * 256:(h + 1) * 256]), reads=[r_a], writes=[r_st])
            p.add("dve", lambda e, h=h: e.bn_aggr(out=mv[:, h, :], in_=st[:, h, :]), reads=[r_st], writes=[r_mv])
        p.add("act", lambda e: e.activation(out=rs[:, :], in_=mv[:, :, 1], func=AF.Sqrt, bias=EPS), reads=[r_mv], writes=[r_rs])
        p.add("dve", lambda e: e.reciprocal(out=rs[:, :], in_=rs[:, :]), reads=[r_rs], writes=[r_rs])
        for h in range(4):
            hs = slice(h * 256, (h + 1) * 256)
            p.add("dve", lambda e, h=h, hs=hs, a_=a_: e.tensor_scalar(out=a_[:, hs], in0=a_[:, hs], scalar1=mv[:, h, 0:1], scalar2=rs[:, h:h + 1], op0=ALU.subtract, op1=ALU.mult), reads=[r_a, r_mv, r_rs], writes=[r_a])
        p.add("pool", lambda e, a_=a_: e.tensor_tensor(out=a_[:], in0=a_[:], in1=gw[:], op=ALU.mult), reads=[r_a, r_gw], writes=[r_a])
        p.add("dve", lambda e, a_=a_, g_=g_, o_=o_: e.tensor_tensor(out=o_[:], in0=a_[:], in1=g_[:], op=ALU.mult), reads=[r_a, r_g_], writes=[r_o])
        p.add("sp", lambda e, o_=o_, tsl=tsl: e.dma_start(out=ro[tsl, :], in_=o_[:]), reads=[r_o], writes=[r_ro], dma=True)
    return kb.finish([r_ro])


def build_LC2a(NT=17, NB=256):
    kb = KB()
    nc, p = kb.nc, kb.p
    T = NT * 128
    rT, r_rT = kb.din("rT", [1024, T], BF16)
    aT, r_aT = kb.din("aT", [1024, T])
    agT, r_agT = kb.din("agT", [1024, T], BF16)
    yf, r_yf = kb.din("yf", [768, T])
    yb, r_yb = kb.din("yb", [768, T])
    sgT, r_sgT = kb.din("sgT", [768, T], BF16)
    mgT, r_mgT = kb.din("mgT", [6144, T], BF16)
    wr_, r_wr_ = kb.din("w_br_ret", [1024, D])
    wa_, r_wa_ = kb.din("w_br_att", [1024, D])
    ws_, r_ws_ = kb.din("w_br_s5", [768, D])
    gl_, r_gl_ = kb.din("glu_w", [768, 768])
    gb_, r_gb_ = kb.din("glu_b", [128, 6])
    mT, r_mT = kb.dout("mT", [D, T], BF16)
    wr, r_wr = kb.sb([128, 8, D], BF16)
    wa, r_wa = kb.sb([128, 8, D], BF16)
    ws, r_ws = kb.sb([128, 6, D], BF16)
    gl, r_gl = kb.sb([128, 6, 768], BF16)
    gb, r_gb = kb.sb([128, 6], F32)
    p.add("pool", lambda e: e.dma_start(out=wr[:], in_=wr_.rearrange("(a p) n -> p a n", p=128)), writes=[r_wr], dma=True)
    p.add("pool", lambda e: e.dma_start(out=wa[:], in_=wa_.rearrange("(a p) n -> p a n", p=128)), writes=[r_wa], dma=True)
    p.add("pool", lambda e: e.dma_start(out=ws[:], in_=ws_.rearrange("(a p) n -> p a n", p=128)), writes=[r_ws], dma=True)
    p.add("pool", lambda e: e.dma_start(out=gl[:], in_=gl_.rearrange("(a p) n -> p a n", p=128)), writes=[r_gl], dma=True)
    p.add("sp", lambda e: e.dma_start(out=gb[:], in_=gb_), writes=[r_gb], dma=True)
    rt, r_rt = kb.sb([128, 8, NB], BF16)
    at, r_at = kb.sb([128, 8, NB], F32)
    agt, r_agt = kb.sb([128, 8, NB], BF16)
    ap_, r_ap = kb.sb([128, 8, NB], BF16)
    yft, r_yft = kb.sb([128, 6, NB], F32)
    ybt, r_ybt = kb.sb([128, 6, NB], F32)
    sgt, r_sgt = kb.sb([128, 6, NB], BF16)
    t1, r_t1 = kb.sb([128, 6, NB], F32)
    s1, r_s1 = kb.sb([128, 6, NB], BF16)
    s2, r_s2 = kb.sb([128, 6, NB], BF16)
    zs, r_zs = kb.sb([128, NB], F32)
    mg = [kb.sb([128, 3, NB], BF16) for _ in range(2)]
    m1, r_m1 = kb.sb([128, 3, NB], F32)
    mo = [kb.sb([128, NB], BF16) for _ in range(2)]
    pz = [kb.ps() for _ in range(2)]
    pb = [kb.ps() for _ in range(6)]
    blocks = [(s, min(NB, T - s)) for s in range(0, T, NB)]
    k = 0
    for (s0, n) in blocks:
        bs = slice(s0, s0 + n)
        for dst, src, r, eng, kk in ((rt, rT, r_rt, "sp", 8), (at, aT, r_at, "act", 8), (agt, agT, r_agt, "sp", 8), (yft, yf, r_yft, "act", 6), (ybt, yb, r_ybt, "sp", 6), (sgt, sgT, r_sgt, "act", 6)):
            p.add(eng, lambda e, dst=dst, src=src, bs=bs, n=n: e.dma_start(out=dst[:, :, 0:n], in_=src[:, bs].rearrange("(a p) n -> p a n", p=128)), writes=[r], dma=True)
        p.add("dve", lambda e, n=n: e.tensor_tensor(out=ap_[:, :, 0:n], in0=at[:, :, 0:n], in1=agt[:, :, 0:n], op=ALU.mult), reads=[r_at, r_agt], writes=[r_ap])
        p.add("pool", lambda e, n=n: e.tensor_tensor(out=yft[:, :, 0:n], in0=yft[:, :, 0:n], in1=ybt[:, :, 0:n], op=ALU.add), reads=[r_yft, r_ybt], writes=[r_yft])
        p.add("act", lambda e, n=n: e.activation(out=t1[:, :, 0:n], in_=yft[:, :, 0:n], func=AF.Square), reads=[r_yft], writes=[r_t1])
        p.add("dve", lambda e, n=n: e.tensor_scalar(out=t1[:, :, 0:n], in0=t1[:, :, 0:n], scalar1=0.044715, scalar2=1.0, op0=ALU.mult, op1=ALU.add), reads=[r_t1], writes=[r_t1])
        p.add("dve", lambda e, n=n: e.tensor_tensor(out=t1[:, :, 0:n], in0=t1[:, :, 0:n], in1=yft[:, :, 0:n], op=ALU.mult), reads=[r_t1, r_yft], writes=[r_t1])
        p.add("act", lambda e, n=n: e.activation(out=t1[:, :, 0:n], in_=t1[:, :, 0:n], func=AF.Sigmoid, scale=1.5957691216057308), reads=[r_t1], writes=[r_t1])
        p.add("dve", lambda e, n=n: e.tensor_tensor(out=s1[:, :, 0:n], in0=t1[:, :, 0:n], in1=yft[:, :, 0:n], op=ALU.mult), reads=[r_t1, r_yft], writes=[r_s1])
        for j in range(6):
            pzz, r_pz = pz[j % 2]
            for kt in range(6):
                p.add("pe", lambda e, j=j, kt=kt, n=n, pzz=pzz: e.matmul(pzz[:, 0:n], gl[:, kt, j * 128:(j + 1) * 128], s1[:, kt, 0:n], start=(kt == 0), stop=(kt == 5)), reads=[r_gl, r_s1], writes=[r_pz])
            p.add("act", lambda e, j=j, n=n, pzz=pzz: e.activation(out=zs[:, 0:n], in_=pzz[:, 0:n], func=AF.Sigmoid, bias=gb[:, j:j + 1]), reads=[r_pz, r_gb], writes=[r_zs])
            p.add("dve", lambda e, j=j, n=n: e.tensor_tensor(out=zs[:, 0:n], in0=zs[:, 0:n], in1=s1[:, j, 0:n], op=ALU.mult), reads=[r_zs, r_s1], writes=[r_zs])
            p.add("dve", lambda e, j=j, n=n: e.tensor_tensor(out=s2[:, j, 0:n], in0=zs[:, 0:n], in1=sgt[:, j, 0:n], op=ALU.mult), reads=[r_zs, r_sgt], writes=[r_s2])
        for f in range(16):
            mgb, r_mg = mg[k % 2]
            mob, r_mo = mo[k % 2]
            pp = [pb[(3 * k + i) % 6] for i in range(3)]
            k += 1
            fs = slice(f * 128, (f + 1) * 128)
            p.add("sp", lambda e, f=f, bs=bs, n=n, mgb=mgb: e.dma_start(out=mgb[:, :, 0:n], in_=mgT[:, bs].rearrange("(b f p) n -> f p b n", b=3, p=128)[f]), writes=[r_mg], dma=True)
            for (w, r_w, src, r_src, nk, (pt, r_pt)) in ((wr, r_wr, rt, r_rt, 8, pp[0]), (wa, r_wa, ap_, r_ap, 8, pp[1]), (ws, r_ws, s2, r_s2, 6, pp[2])):
                for kt in range(nk):
                    p.add("pe", lambda e, w=w, src=src, kt=kt, nk=nk, fs=fs, n=n, pt=pt: e.matmul(pt[:, 0:n], w[:, kt, fs], src[:, kt, 0:n], start=(kt == 0), stop=(kt == nk - 1)), reads=[r_w, r_src], writes=[r_pt])
            for i in range(3):
                pt, r_pt = pp[i]
                p.add("dve", lambda e, i=i, n=n, pt=pt, mgb=mgb: e.tensor_tensor(out=m1[:, i, 0:n], in0=pt[:, 0:n], in1=mgb[:, i, 0:n], op=ALU.mult), reads=[r_pt, r_mg], writes=[r_m1])
            p.add("pool", lambda e, n=n: e.tensor_tensor(out=m1[:, 0, 0:n], in0=m1[:, 0, 0:n], in1=m1[:, 1, 0:n], op=ALU.add), reads=[r_m1], writes=[r_m1])
            p.add("pool", lambda e, n=n, mob=mob: e.tensor_tensor(out=mob[:, 0:n], in0=m1[:, 0, 0:n], in1=m1[:, 2, 0:n], op=ALU.add), reads=[r_m1], writes=[r_mo])
            p.add("act", lambda e, fs=fs, bs=bs, n=n, mob=mob: e.dma_start(out=mT[fs, bs], in_=mob[:, 0:n]), reads=[r_mo], writes=[r_mT], dma=True)
    return kb.finish([r_mT])


ALPHA = (2.0 * 2) ** 0.25


def build_LC2b(NT=17):
    kb = KB()
    nc, p = kb.nc, kb.p
    T = NT * 128
    mT, r_mT = kb.din("mT", [D, T], BF16)
    wo_, r_wo_ = kb.din("w_out", [D, D])
    x, r_x = kb.din("x", [T, D])
    rows, r_rows = kb.din("rows", [128, 4, D])
    xo, r_xo = kb.dout("xo", [T, D])
    wo, r_wo = kb.sb([128, 16, D], BF16)
    rw, r_rw = kb.sb([128, 4, D], F32)
    p.add("pool", lambda e: e.dma_start(out=wo[:], in_=wo_.rearrange("(a p) n -> p a n", p=128)), writes=[r_wo], dma=True)
    p.add("sp", lambda e: e.dma_start(out=rw[:], in_=rows), writes=[r_rw], dma=True)
    mt = [kb.sb([128, 16, 128], BF16) for _ in range(2)]
    xt = [kb.sb([128, D], F32) for _ in range(2)]
    z = [kb.sb([128, D], F32) for _ in range(2)]
    pb = [kb.ps() for _ in range(8)]
    st, r_st = kb.sb([128, 4, 6], F32)
    mv, r_mv = kb.sb([128, 2], F32)
    rs, r_rs = kb.sb([128, 1], F32)
    for tt in range(NT):
        b = tt % 2
        mtb, r_mt = mt[b]; xtb, r_xt = xt[b]; zb, r_z = z[b]
        tsl = slice(tt * 128, (tt + 1) * 128)
        gi = 0 if tt < NT - 1 else 1
        p.add("sp", lambda e, mtb=mtb, tsl=tsl: e.dma_start(out=mtb[:], in_=mT[:, tsl].rearrange("(a p) n -> p a n", p=128)), writes=[r_mt], dma=True)
        p.add("act", lambda e, xtb=xtb, tsl=tsl: e.dma_start(out=xtb[:], in_=x[tsl, :]), writes=[r_xt], dma=True)
        for cb in range(4):
            pt, r_pt = pb[(tt * 4 + cb) % 8]
            cs = slice(cb * 512, (cb + 1) * 512)
            for kt in range(16):
                p.add("pe", lambda e, kt=kt, cs=cs, pt=pt, mtb=mtb: e.matmul(pt[:, :], mtb[:, kt, :], wo[:, kt, cs], start=(kt == 0), stop=(kt == 15)), reads=[r_mt, r_wo], writes=[r_pt])
            p.add("dve", lambda e, cs=cs, pt=pt, zb=zb, gi=gi: e.tensor_tensor(out=zb[:, cs], in0=pt[:, :], in1=rw[:, gi, cs], op=ALU.mult), reads=[r_pt, r_rw], writes=[r_z])
            p.add("dve", lambda e, cs=cs, zb=zb, xtb=xtb: e.scalar_tensor_tensor(out=zb[:, cs], in0=xtb[:, cs], scalar=ALPHA, in1=zb[:, cs], op0=ALU.mult, op1=ALU.add), reads=[r_xt, r_z], writes=[r_z])
            p.add("dve", lambda e, cb=cb, cs=cs, zb=zb: e.bn_stats(out=st[:, cb, :], in_=zb[:, cs]), reads=[r_z], writes=[r_st])
        p.add("dve", lambda e: e.bn_aggr(out=mv[:, :], in_=st[:, :, :].rearrange("p a b -> p (a b)")), reads=[r_st], writes=[r_mv])
        p.add("act", lambda e: e.activation(out=rs[:, :], in_=mv[:, 1:2], func=AF.Sqrt, bias=EPS), reads=[r_mv], writes=[r_rs])
        p.add("dve", lambda e: e.reciprocal(out=rs[:, :], in_=rs[:, :]), reads=[r_rs], writes=[r_rs])
        p.add("dve", lambda e, zb=zb: e.tensor_scalar(out=zb[:, :], in0=zb[:, :], scalar1=mv[:, 0:1], scalar2=rs[:, 0:1], op0=ALU.subtract, op1=ALU.mult), reads=[r_z, r_mv, r_rs], writes=[r_z])
        p.add("pool", lambda e, zb=zb: e.tensor_tensor(out=zb[:, :], in0=zb[:, :], in1=rw[:, 2, :], op=ALU.mult), reads=[r_z, r_rw], writes=[r_z])
        p.add("pool", lambda e, zb=zb: e.tensor_tensor(out=zb[:, :], in0=zb[:, :], in1=rw[:, 3, :], op=ALU.add), reads=[r_z, r_rw], writes=[r_z])
        p.add("sp", lambda e, zb=zb, tsl=tsl: e.dma_start(out=xo[tsl, :], in_=zb[:, :]), reads=[r_z], writes=[r_xo], dma=True)
    return kb.finish([r_xo])


import ml_dtypes
_BF = ml_dtypes.bfloat16
_PROGS = {}


def _prog(key, fn, *a):
    if key not in _PROGS:
        _PROGS[key] = fn(*a)
    return _PROGS[key]


def _run(nc, maps):
    res = run_bass_kernel_spmd(nc, maps, core_ids=list(range(8)))
    return res.results


def _rope_tables(hd):
    per = hd // 4
    n = np.arange(NLAT)
    row = (n // 64).astype(np.float32)
    col = (n % 64).astype(np.float32)
    inv = (np.float32(10000.0) ** (-np.arange(per, dtype=np.float32) / np.float32(per))).astype(np.float32)
    ang = np.concatenate([row[:, None] * inv, col[:, None] * inv], axis=-1).astype(np.float32)
    return np.cos(ang).astype(np.float32), np.sin(ang).astype(np.float32)


def _ret_cst(lg, diag):
    idx = np.arange(128, dtype=np.float32)
    rel = idx[None, :] - idx[:, None]
    mask = (rel >= 0) if diag else (rel > 0)
    cst = np.zeros((128, 260), np.float32)
    cst[:, 0:128] = np.where(mask, rel, 0)
    cst[:, 128:256] = mask
    cst[:, 256] = lg
    cst[:, 257] = idx + 1
    cst[:, 258] = 127 - idx
    cst[:, 259] = 128
    return cst


def _s5_inputs(u, a_re, a_im, log_dt, b_re, b_im, c_re, c_im, dsk, NB):
    N = u.shape[0]
    prm = np.zeros((128, 3, 6), np.float32)
    Bm = np.zeros((128, 6, 2, 32), np.float32)
    Cm = np.zeros((128, 6, 2, 32), np.float32)
    dk = np.zeros((32, 6), np.float32)
    for mt in range(6):
        for gp in range(2):
            g = 2 * mt + gp
            ps = slice(gp * 64, gp * 64 + 64)
            cs = slice(gp * 16, gp * 16 + 16)
            prm[ps, 0, mt] = a_re[g]
            prm[ps, 1, mt] = a_im[g]
            prm[ps, 2, mt] = log_dt[g]
            Bm[ps, mt, 0, cs] = b_re[g]
            Bm[ps, mt, 1, cs] = b_im[g]
            Cm[ps, mt, 0, cs] = c_re[g].T
            Cm[ps, mt, 1, cs] = c_im[g].T
            dk[cs, mt] = dsk[g]
    uT = np.ascontiguousarray(u.reshape(N, 6, 32).transpose(1, 2, 0))
    tix = np.ascontiguousarray(np.broadcast_to(np.arange(NB, dtype=np.float32)[None], (128, NB)))
    return dict(uT=uT, prm=prm, Bm=Bm, Cm=Cm, dsk=dk, tix=tix, idn=np.eye(128, dtype=np.float32))


def kernel(x, c, ctx, c_ctx, ada_w, ada_b, w_in, ret_log_decay, ret_gn_w, att_q_norm, att_k_norm,
           s5_a_re, s5_a_im, s5_log_dt, s5_b_re, s5_b_im, s5_c_re, s5_c_im, s5_d, s5_glu_w, s5_glu_b,
           w_br_ret, w_br_att, w_br_s5, w_out, ln_w, ln_b):
    A = lambda v: np.ascontiguousarray(np.asarray(v))
    x = A(x); ctx = A(ctx)
    NS = NCTX + NLAT
    C = np.ascontiguousarray
    c3T = C(np.stack([A(c)[0], A(c)[1], A(c_ctx)], axis=1).astype(np.float32))
    maps = []
    for k in range(8):
        cs = slice(k * 768, (k + 1) * 768)
        maps.append(dict(adaw=C(A(ada_w)[:, :, cs]), adab=C(A(ada_b)[:, cs].reshape(2, 6, 128).transpose(2, 0, 1)), c3T=c3T))
    r0 = _run(_prog("L0", build_L0), maps)
    mod = np.concatenate([r["modT"] for r in r0], axis=1)
    cosR, sinR = _rope_tables(256)
    cosA, sinA = _rope_tables(128)
    xl = x.reshape(2 * NLAT, D)
    h = ctx.reshape(2 * NCTX, D)
    for l in range(2):
        shift, scale, gate = mod[l, 0:D], mod[l, D:2 * D], mod[l, 2 * D:3 * D]
        maps = []
        for k in range(8):
            b = k // 4
            ct = h[k * 128:(k + 1) * 128] if k < 4 else np.zeros((128, D), np.float32)
            xT = C(np.concatenate([xl[k * 2048:(k + 1) * 2048], ct], axis=0).T)
            modv = C(np.stack([shift[:, b], scale[:, b], shift[:, 2], scale[:, 2]], axis=1))
            ropeR = np.zeros((128, 17, 2, 128), np.float32)
            ropeA = np.zeros((128, 17, 2, 64), np.float32)
            n0 = (k % 4) * 2048
            ropeR[:, :16, 0] = cosR[n0:n0 + 2048].reshape(16, 128, 128).transpose(1, 0, 2)
            ropeR[:, :16, 1] = sinR[n0:n0 + 2048].reshape(16, 128, 128).transpose(1, 0, 2)
            ropeA[:, :16, 0] = cosA[n0:n0 + 2048].reshape(16, 128, 64).transpose(1, 0, 2)
            ropeA[:, :16, 1] = sinA[n0:n0 + 2048].reshape(16, 128, 64).transpose(1, 0, 2)
            ropeR[:, 16, 0] = 1.0
            ropeA[:, 16, 0] = 1.0
            qkw = C(np.broadcast_to(np.stack([A(att_q_norm)[l], A(att_k_norm)[l]])[None], (128, 2, 128)).astype(np.float32))
            maps.append(dict(xT=xT, modv=modv, w_in=A(w_in)[l], ropeR=ropeR, ropeA=ropeA, qkw=qkw))
        rA = _run(_prog("LA", build_LA), maps)
        O = [r["O"] for r in rA]
        OL = np.concatenate([o[:2048] for o in O], axis=0).reshape(2, NLAT, INW)
        OC = np.concatenate([o[2048:] for o in O[:4]], axis=0).reshape(2, NCTX, INW)
        seq = np.concatenate([OC, OL], axis=1)
        seqb = np.concatenate([OC[:, ::-1], OL[:, ::-1]], axis=1)
        sg = lambda S_, name, w: S_[:, :, SEG[name]:SEG[name] + w]
        ret = []
        for dr, S_ in ((0, seq), (1, seqb)):
            maps = []
            for k in range(8):
                b, j = k // 4, k % 4
                hs = slice(j * 256, (j + 1) * 256)
                q = sg(S_, "ret_q", 1024)[b][:, hs]; kk = sg(S_, "ret_k", 1024)[b][:, hs]; vv = sg(S_, "ret_v", 1024)[b][:, hs]
                maps.append(dict(qT=C(q.T), kT=C(kk.T), k=C(kk), v=C(vv), cst=_ret_cst(np.float32(A(ret_log_decay)[l, dr, j]), dr == 0)))
            rr = _run(_prog("RET", build_RET, NS), maps)
            o = np.stack([r["ro"] for r in rr]).reshape(2, 4, NS, 256).transpose(0, 2, 1, 3).reshape(2, NS, 1024)
            if dr == 1:
                o = np.concatenate([o[:, :NCTX][:, ::-1], o[:, NCTX:][:, ::-1]], axis=1)
            ret.append(o)
        maps = []
        for k in range(8):
            b, j = k // 4, k % 4
            q = sg(OL, "att_q", 1024)[b][:, j * 256:(j + 1) * 256].reshape(NLAT, 2, 128)
            kv = j // 2
            kk = sg(seq, "att_k", 256)[b][:, kv * 128:(kv + 1) * 128]
            vv = sg(seq, "att_v", 256)[b][:, kv * 128:(kv + 1) * 128]
            maps.append(dict(qT=C(q.transpose(1, 2, 0)), kT=C(kk.T), v=C(vv)))
        ra = _run(_prog("ATT", build_ATT, NLAT, NS), maps)
        aTl = np.stack([r["aT"] for r in ra]).reshape(2, 1024, NLAT)
        aTc = np.zeros((2, 1024, NCTX), np.float32)
        if l == 0:
            maps = []
            for k in range(8):
                b, j = k // 4, k % 4
                q = sg(OC, "att_q", 1024)[b][:, j * 256:(j + 1) * 256].reshape(NCTX, 2, 128)
                kv = j // 2
                kk = sg(OC, "att_k", 256)[b][:, kv * 128:(kv + 1) * 128]
                vv = sg(OC, "att_v", 256)[b][:, kv * 128:(kv + 1) * 128]
                maps.append(dict(qT=C(q.transpose(1, 2, 0)), kT=C(kk.T), v=C(vv)))
            rc = _run(_prog("ATTC", build_ATT, NCTX, NCTX, 256), maps)
            aTc = np.stack([r["aT"] for r in rc]).reshape(2, 1024, NCTX)
        ys = []
        for dr, S_ in ((0, seq), (1, seqb)):
            maps = []
            for k in range(8):
                b, j = k // 4, k % 4
                gs = slice(12 * j, 12 * j + 12)
                u = sg(S_, "s5_u", 768)[b][:, j * 192:(j + 1) * 192]
                dk = A(s5_d)[l].reshape(48, 16)[gs] if dr == 0 else np.zeros((12, 16), np.float32)
                maps.append(_s5_inputs(u, A(s5_a_re)[l, dr, gs], A(s5_a_im)[l, dr, gs], A(s5_log_dt)[l, dr, gs], A(s5_b_re)[l, dr, gs], A(s5_b_im)[l, dr, gs],
                                       A(s5_c_re)[l, dr, gs], A(s5_c_im)[l, dr, gs], dk, 256))
            rs_ = _run(_prog("S5", build_S5, NS, 256), maps)
            y = np.stack([r["yT"].reshape(192, NS) for r in rs_]).reshape(2, 768, NS)
            if dr == 1:
                y = np.concatenate([y[:, :, :NCTX][:, :, ::-1], y[:, :, NCTX:][:, :, ::-1]], axis=2)
            ys.append(y)
        def tok(arr_lat, arr_ctx, k):
            W = arr_lat.shape[-1]
            ct = arr_ctx.reshape(2 * NCTX, W)[k * 128:(k + 1) * 128] if k < 4 else np.zeros((128, W), arr_lat.dtype)
            return np.concatenate([arr_lat.reshape(2 * NLAT, W)[k * 2048:(k + 1) * 2048], ct], axis=0)

        def tokT(arr_lat, arr_ctx, k):
            b = k // 4
            n0 = (k % 4) * 2048
            W = arr_lat.shape[1]
            if k < 4:
                bc, c0 = k // 2, (k % 2) * 128
                ct = arr_ctx[bc][:, c0:c0 + 128]
            else:
                ct = np.zeros((W, 128), arr_lat.dtype)
            return np.concatenate([arr_lat[b][:, n0:n0 + 2048], ct], axis=1)
        gnw = C(np.broadcast_to(A(ret_gn_w)[l][None], (128, 1024)).astype(np.float32))
        maps = [dict(rf=C(tok(ret[0][:, NCTX:], ret[0][:, :NCTX], k)), rb=C(tok(ret[1][:, NCTX:], ret[1][:, :NCTX], k)), g=C(O[k][:, SEG["ret_g"]:SEG["ret_g"] + 1024]), gnw=gnw) for k in range(8)]
        r1 = _run(_prog("LC1", build_LC1), maps)
        maps = []
        for k in range(8):
            Ok = O[k]
            maps.append(dict(rT=C(r1[k]["r"].T), aT=C(tokT(aTl, aTc, k)), agT=C(Ok[:, SEG["att_g"]:SEG["att_g"] + 1024].T),
                             yf=C(tokT(ys[0][:, :, NCTX:], ys[0][:, :, :NCTX], k)), yb=C(tokT(ys[1][:, :, NCTX:], ys[1][:, :, :NCTX], k)),
                             sgT=C(Ok[:, SEG["s5_g"]:SEG["s5_g"] + 768].T), mgT=C(Ok[:, SEG["merge"]:].T),
                             w_br_ret=A(w_br_ret)[l], w_br_att=A(w_br_att)[l], w_br_s5=A(w_br_s5)[l], glu_w=A(s5_glu_w)[l],
                             glu_b=C(A(s5_glu_b)[l].reshape(6, 128).T)))
        r2 = _run(_prog("LC2a", build_LC2a), maps)
        maps = []
        for k in range(8):
            b = k // 4
            ct = h[k * 128:(k + 1) * 128] if k < 4 else np.zeros((128, D), np.float32)
            xk = C(np.concatenate([xl[k * 2048:(k + 1) * 2048], ct], axis=0))
            rows = C(np.broadcast_to(np.stack([gate[:, b], gate[:, 2], A(ln_w)[l], A(ln_b)[l]])[None], (128, 4, D)).astype(np.float32))
            maps.append(dict(mT=r2[k]["mT"], w_out=A(w_out)[l], x=xk, rows=rows))
        r3 = _run(_prog("LC2b", build_LC2b), maps)
        xl = np.concatenate([r["xo"][:2048] for r in r3], axis=0)
        h = np.concatenate([r["xo"][2048:] for r in r3[:4]], axis=0)
    return xl.reshape(2, NLAT, D).astype(np.float32)
```

```python
import numpy as np
import concourse.bass as bass
import concourse.mybir as mybir

F32 = mybir.dt.float32
BF16 = mybir.dt.bfloat16
ALU = mybir.AluOpType
AF = mybir.ActivationFunctionType
AX = mybir.AxisListType

ENGS = ("pe", "act", "dve", "pool", "sp")
N_DMA_SEMS = 40


class Res:
    __slots__ = ("name", "w", "r")

    def __init__(self, name):
        self.name = name
        self.w = None
        self.r = []


class Op:
    __slots__ = ("id", "eng", "fn", "deps", "dma", "sig", "ev")

    def __init__(self, id, eng, fn, deps, dma):
        self.id = id
        self.eng = eng
        self.fn = fn
        self.deps = deps
        self.dma = dma
        self.sig = False
        self.ev = None


class Prog:
    def __init__(self, nc):
        self.nc = nc
        self.ops = []
        self.sems = {}
        self.final_dmas = []

    def res(self, name="r"):
        return Res(name)

    def add(self, eng, fn, reads=(), writes=(), dma=False):
        deps = set()
        for r in reads:
            if r.w is not None:
                deps.add(r.w)
        for w in writes:
            if w.w is not None:
                deps.add(w.w)
            deps.update(w.r)
        op = Op(len(self.ops), eng, fn, deps, dma)
        self.ops.append(op)
        for r in reads:
            r.r.append(op.id)
        for w in writes:
            w.w = op.id
            w.r = []
        return op.id

    def emit(self, sem_ctx):
        nc = self.nc
        ops = self.ops
        for op in ops:
            for d in op.deps:
                dop = ops[d]
                if dop.dma:
                    continue
                if dop.eng == op.eng and not op.dma and op.eng == "pe":
                    continue
                dop.sig = True
        cnt = {e: 0 for e in ENGS}
        dcnt = [0] * N_DMA_SEMS
        dlast = [None] * N_DMA_SEMS
        dnext = 0
        for op in ops:
            if op.dma:
                s = dnext
                dnext = (dnext + 1) % N_DMA_SEMS
                if dlast[s] is not None:
                    op.deps.add(dlast[s])
                dcnt[s] += 16
                op.ev = ("d%d" % s, dcnt[s])
                dlast[s] = op.id
            elif op.sig:
                cnt[op.eng] += 1
                op.ev = (op.eng, cnt[op.eng])
        streams = {e: [] for e in ENGS}
        seen = {e: {} for e in ENGS}
        for op in ops:
            waits = {}
            for d in op.deps:
                dop = ops[d]
                if dop.ev is None:
                    continue
                s, v = dop.ev
                if (not dop.dma) and dop.eng == op.eng and op.eng == "pe" and not op.dma:
                    continue
                if seen[op.eng].get(s, 0) >= v:
                    continue
                if waits.get(s, 0) < v:
                    waits[s] = v
            for s, v in waits.items():
                seen[op.eng][s] = v
            streams[op.eng].append((waits, op))
        return streams

    def run(self, streams, sems, block):
        nc = self.nc

        def mk(engname):
            def body(eng):
                for waits, op in streams[engname]:
                    for s, v in waits.items():
                        eng.wait_ge(sems[s], v)
                    if op.fn is None:
                        continue
                    ins = op.fn(eng)
                    if op.ev is not None:
                        s, v = op.ev
                        ins.then_inc(sems[s], 16 if op.dma else 1)
            return body

        block.tensor(mk("pe"))
        block.scalar(mk("act"))
        block.vector(mk("dve"))
        block.gpsimd(mk("pool"))
        block.sync(mk("sp"))


from contextlib import ExitStack
from concourse.bass_utils import run_bass_kernel_spmd


class KB:
    def __init__(self):
        self.nc = bass.Bass("TRN2", target_bir_lowering=False)
        self.p = Prog(self.nc)
        self.es = ExitStack()
        self.n = 0

    def sb(self, shape, dt, name=None):
        self.n += 1
        t = self.es.enter_context(self.nc.sbuf_tensor(name or ("s%d" % self.n), shape, dt))
        return t, self.p.res(name or "s")

    def ps(self, shape=(128, 512), dt=F32):
        self.n += 1
        t = self.es.enter_context(self.nc.psum_tensor("p%d" % self.n, list(shape), dt))
        return t, self.p.res("p")

    def din(self, name, shape, dt=F32):
        return self.nc.dram_tensor(name, list(shape), dt, kind="ExternalInput").ap(), self.p.res(name)

    def dout(self, name, shape, dt=F32):
        return self.nc.dram_tensor(name, list(shape), dt, kind="ExternalOutput").ap(), self.p.res(name)

    def finish(self, out_res):
        p, nc, es = self.p, self.nc, self.es
        p.add("sp", None, reads=out_res)
        p.add("act", None, reads=out_res)
        sems = {e: es.enter_context(nc.semaphore(e)) for e in ENGS}
        for i in range(N_DMA_SEMS):
            sems["d%d" % i] = es.enter_context(nc.semaphore("d%d" % i))
        streams = p.emit(None)
        with nc.Block() as block:
            p.run(streams, sems, block)
        es.close()
        return nc


D = 2048
NLAT = 8192
NCTX = 256
INW = 14336
EPS = 1e-6
SEG = dict(ret_q=0, ret_k=1024, ret_v=2048, ret_g=3072, att_q=4096, att_k=5120, att_v=5376, att_g=5632,
           s5_u=6656, s5_g=7424, merge=8192)


def rope_ops(kb, dst, dres, src, sres, cos, sin, tres, h, tmp, tmpres, extra_reads=()):
    p = kb.p
    x1, x2 = src[:, 0:h], src[:, h:2 * h]
    rd = [sres, tres] + list(extra_reads)
    p.add("dve", lambda e: e.tensor_tensor(out=tmp[:, 0:h], in0=x1, in1=cos, op=ALU.mult), reads=rd, writes=[tmpres])
    p.add("dve", lambda e: e.tensor_tensor(out=tmp[:, h:2 * h], in0=x2, in1=sin, op=ALU.mult), reads=rd, writes=[tmpres])
    p.add("dve", lambda e: e.tensor_tensor(out=tmp[:, 2 * h:3 * h], in0=x2, in1=cos, op=ALU.mult), reads=rd, writes=[tmpres])
    p.add("dve", lambda e: e.tensor_tensor(out=tmp[:, 3 * h:4 * h], in0=x1, in1=sin, op=ALU.mult), reads=rd, writes=[tmpres])
    p.add("dve", lambda e: e.tensor_tensor(out=dst[:, 0:h], in0=tmp[:, 0:h], in1=tmp[:, h:2 * h], op=ALU.subtract), reads=[tmpres], writes=[dres])
    p.add("dve", lambda e: e.tensor_tensor(out=dst[:, h:2 * h], in0=tmp[:, 2 * h:3 * h], in1=tmp[:, 3 * h:4 * h], op=ALU.add), reads=[tmpres], writes=[dres])


def build_LA(NT=17, CBS=None):
    CBS = list(range(28)) if CBS is None else CBS
    kb = KB()
    nc, p = kb.nc, kb.p
    NTOK = NT * 128
    xT, r_xT = kb.din("xT", [D, NTOK])
    modv, r_modv = kb.din("modv", [D, 4])
    w_in, r_w = kb.din("w_in", [D, INW])
    ropeR, r_ropeR = kb.din("ropeR", [128, NT, 2, 128])
    ropeA, r_ropeA = kb.din("ropeA", [128, NT, 2, 64])
    qkw, r_qkw = kb.din("qkw", [128, 2, 128])
    O, r_O = kb.dout("O", [NTOK, INW], BF16)

    mv, r_mv = kb.sb([128, 16, 4], F32)
    sc1, r_sc1 = kb.sb([128, 16, 2], F32)
    uT, r_uT = kb.sb([128, 16, NTOK], BF16)
    tR, r_tR = kb.sb([128, NT, 2, 128], F32)
    tA, r_tA = kb.sb([128, NT, 2, 64], F32)
    tW, r_tW = kb.sb([128, 2, 128], F32)
    xs = [kb.sb([128, NTOK], F32) for _ in range(2)]
    wb = [kb.sb([128, 16, 512], BF16) for _ in range(2)]
    ob = [kb.sb([128, 512], BF16) for _ in range(3)]
    pm = [kb.ps() for _ in range(4)]
    tmp, r_tmp = kb.sb([128, 512], F32)
    xn, r_xn = kb.sb([128, 512], F32)
    junk, r_junk = kb.sb([128, 128], F32)
    ss, r_ss = kb.sb([128, 4], F32)

    p.add("sp", lambda e: e.dma_start(out=mv[:], in_=modv.rearrange("(a p) c -> p a c", p=128)), writes=[r_mv], dma=True)
    p.add("sp", lambda e: e.dma_start(out=tR[:], in_=ropeR), writes=[r_tR], dma=True)
    p.add("sp", lambda e: e.dma_start(out=tA[:], in_=ropeA), writes=[r_tA], dma=True)
    p.add("sp", lambda e: e.dma_start(out=tW[:], in_=qkw), writes=[r_tW], dma=True)
    p.add("dve", lambda e: e.tensor_scalar(out=sc1[:, :, 0], in0=mv[:, :, 1], scalar1=1.0, scalar2=None, op0=ALU.add), reads=[r_mv], writes=[r_sc1])
    p.add("dve", lambda e: e.tensor_scalar(out=sc1[:, :, 1], in0=mv[:, :, 3], scalar1=1.0, scalar2=None, op0=ALU.add), reads=[r_mv], writes=[r_sc1])
    NL = (NT - 1) * 128
    for kt in range(16):
        b = kt % 2
        xsb, r_xsb = xs[b]
        p.add("sp" if kt % 2 == 0 else "act", lambda e, kt=kt, xsb=xsb: e.dma_start(out=xsb[:], in_=xT[kt * 128:(kt + 1) * 128, :]), writes=[r_xsb], dma=True)
        if NL > 0:
            p.add("act", lambda e, kt=kt, xsb=xsb: e.activation(out=uT[:, kt, 0:NL], in_=xsb[:, 0:NL], func=AF.Identity, scale=sc1[:, kt, 0:1], bias=mv[:, kt, 0:1]),
                  reads=[r_xsb, r_sc1, r_mv], writes=[r_uT])
        p.add("act", lambda e, kt=kt, xsb=xsb: e.activation(out=uT[:, kt, NL:NTOK], in_=xsb[:, NL:NTOK], func=AF.Identity, scale=sc1[:, kt, 1:2], bias=mv[:, kt, 2:3]),
              reads=[r_xsb, r_sc1, r_mv], writes=[r_uT])

    cnt = 0
    for ci, cb in enumerate(CBS):
        wbb, r_wbb = wb[ci % 2]
        p.add("pool", lambda e, cb=cb, wbb=wbb: e.dma_start(out=wbb[:], in_=w_in[:, cb * 512:(cb + 1) * 512].rearrange("(a p) n -> p a n", p=128)), writes=[r_wbb], dma=True)
        c0 = cb * 512
        for tt in range(NT):
            pmm, r_pm = pm[cnt % 4]
            obb, r_ob = ob[cnt % 3]
            cnt += 1
            for kt in range(16):
                p.add("pe", lambda e, kt=kt, tt=tt, pmm=pmm, wbb=wbb: e.matmul(pmm[:, :], uT[:, kt, tt * 128:(tt + 1) * 128], wbb[:, kt, :], start=(kt == 0), stop=(kt == 15)),
                      reads=[r_uT, r_wbb], writes=[r_pm])
            for hf in range(2):
                cc = c0 + hf * 256
                sl = slice(hf * 256, hf * 256 + 256)
                if cc < SEG["ret_v"]:
                    cos, sin = tR[:, tt, 0, :], tR[:, tt, 1, :]
                    if cc >= SEG["ret_k"]:
                        p.add("act", lambda e, pmm=pmm, sl=sl: e.activation(out=xn[:, sl], in_=pmm[:, sl], func=AF.Copy, scale=1.0 / 16.0), reads=[r_pm], writes=[r_xn])
                        rope_ops(kb, obb[:, sl], r_ob, xn[:, sl], r_xn, cos, sin, r_tR, 128, tmp, r_tmp)
                    else:
                        rope_ops(kb, obb[:, sl], r_ob, pmm[:, sl], r_pm, cos, sin, r_tR, 128, tmp, r_tmp)
                elif SEG["att_q"] <= cc < SEG["att_v"]:
                    wi = 0 if cc < SEG["att_k"] else 1
                    p.add("act", lambda e, pmm=pmm, sl=sl: e.activation(out=xn[:, sl], in_=pmm[:, sl], func=AF.Copy), reads=[r_pm], writes=[r_xn])
                    for hh in range(2):
                        s2 = slice(hf * 256 + hh * 128, hf * 256 + hh * 128 + 128)
                        p.add("act", lambda e, s2=s2, hh=hh: e.activation(out=junk[:, :], in_=xn[:, s2], func=AF.Square, accum_out=ss[:, hh:hh + 1]), reads=[r_xn], writes=[r_junk, r_ss])
                    p.add("act", lambda e: e.activation(out=ss[:, 2:4], in_=ss[:, 0:2], func=AF.Sqrt, scale=1.0 / 128.0, bias=EPS), reads=[r_ss], writes=[r_ss])
                    p.add("dve", lambda e: e.reciprocal(out=ss[:, 2:4], in_=ss[:, 2:4]), reads=[r_ss], writes=[r_ss])
                    for hh in range(2):
                        s2 = slice(hf * 256 + hh * 128, hf * 256 + hh * 128 + 128)
                        p.add("dve", lambda e, s2=s2, hh=hh, wi=wi: e.scalar_tensor_tensor(out=xn[:, s2], in0=xn[:, s2], scalar=ss[:, 2 + hh:3 + hh], in1=tW[:, wi, :], op0=ALU.mult, op1=ALU.mult),
                              reads=[r_xn, r_ss, r_tW], writes=[r_xn])
                        rope_ops(kb, obb[:, s2], r_ob, xn[:, s2], r_xn, tA[:, tt, 0, :], tA[:, tt, 1, :], r_tA, 64, tmp, r_tmp)
                else:
                    if cc >= SEG["merge"]:
                        fn = AF.Sigmoid
                    elif (SEG["ret_g"] <= cc < SEG["att_q"]) or (SEG["att_g"] <= cc < SEG["s5_u"]) or (SEG["s5_g"] <= cc < SEG["merge"]):
                        fn = AF.Silu
                    else:
                        fn = AF.Copy
                    p.add("act", lambda e, pmm=pmm, sl=sl, obb=obb, fn=fn: e.activation(out=obb[:, sl], in_=pmm[:, sl], func=fn), reads=[r_pm], writes=[r_ob])
            p.add("sp", lambda e, tt=tt, c0=c0, obb=obb: e.dma_start(out=O[tt * 128:(tt + 1) * 128, c0:c0 + 512], in_=obb[:, :]), reads=[r_ob], writes=[r_O], dma=True)
    return kb.finish([r_O])


def build_ATT(NQ, NK, QB=512):
    kb = KB()
    nc, p = kb.nc, kb.p
    qT, r_q = kb.din("qT", [2, 128, NQ], BF16)
    kT, r_k = kb.din("kT", [128, NK], BF16)
    v, r_v = kb.din("v", [NK, 128], BF16)
    aT, r_a = kb.dout("aT", [2, 128, NQ])
    NKT = NK // 128
    qs, r_qs = kb.sb([128, 2, NQ], BF16)
    ks, r_ks = kb.sb([128, NK], BF16)
    vs, r_vs = kb.sb([128, NKT, 128], BF16)
    ones, r_ones = kb.sb([128, 128], BF16)
    pT = [kb.sb([128, QB], BF16) for _ in range(4)]
    pss = [kb.ps() for _ in range(4)]
    pso = [kb.ps() for _ in range(2)]
    psr = [kb.ps() for _ in range(2)]
    rinv, r_rinv = kb.sb([128, QB], F32)
    ob = [kb.sb([128, QB], F32) for _ in range(2)]
    p.add("sp", lambda e: e.dma_start(out=qs[:], in_=qT.rearrange("h p n -> p h n")), writes=[r_qs], dma=True)
    p.add("act", lambda e: e.dma_start(out=ks[:], in_=kT), writes=[r_ks], dma=True)
    p.add("sp", lambda e: e.dma_start(out=vs[:], in_=v.rearrange("(a p) d -> p a d", p=128)), writes=[r_vs], dma=True)
    p.add("pool", lambda e: e.memset(ones[:], 1.0), writes=[r_ones])
    sc = 128.0 ** -0.5
    LOOK = 3
    its = [(h, qb, kt) for h in range(2) for qb in range(NQ // QB) for kt in range(NKT)]

    def front(i):
        h, qb, kt = its[i]
        pscore, r_ps = pss[i % 4]
        pt_, r_pt = pT[i % 4]
        qsl = slice(qb * QB, (qb + 1) * QB)
        p.add("pe", lambda e: e.matmul(pscore[:, 0:QB], ks[:, kt * 128:(kt + 1) * 128], qs[:, h, qsl], start=True, stop=True), reads=[r_ks, r_qs], writes=[r_ps])
        p.add("act", lambda e: e.activation(out=pt_[:, :], in_=pscore[:, 0:QB], func=AF.Exp, scale=sc), reads=[r_ps], writes=[r_pt])

    for i in range(min(LOOK, len(its))):
        front(i)
    for i, (h, qb, kt) in enumerate(its):
        blk = h * (NQ // QB) + qb
        po, r_po = pso[blk % 2]
        pr, r_pr = psr[blk % 2]
        obb, r_ob = ob[blk % 2]
        pt_, r_pt = pT[i % 4]
        qsl = slice(qb * QB, (qb + 1) * QB)
        p.add("pe", lambda e, kt=kt, po=po, pt_=pt_: e.matmul(po[:, 0:QB], vs[:, kt, :], pt_[:, :], start=(kt == 0), stop=(kt == NKT - 1)), reads=[r_vs, r_pt], writes=[r_po])
        p.add("pe", lambda e, kt=kt, pr=pr, pt_=pt_: e.matmul(pr[:, 0:QB], ones[:, :], pt_[:, :], start=(kt == 0), stop=(kt == NKT - 1)), reads=[r_ones, r_pt], writes=[r_pr])
        if i + LOOK < len(its):
            front(i + LOOK)
        if kt == NKT - 1:
            p.add("dve", lambda e, pr=pr: e.reciprocal(out=rinv[:, :], in_=pr[:, 0:QB]), reads=[r_pr], writes=[r_rinv])
            p.add("dve", lambda e, po=po, obb=obb: e.tensor_tensor(out=obb[:, :], in0=po[:, 0:QB], in1=rinv[:, :], op=ALU.mult), reads=[r_po, r_rinv], writes=[r_ob])
            p.add("sp", lambda e, h=h, qsl=qsl, obb=obb: e.dma_start(out=aT[h, :, qsl], in_=obb[:, :]), reads=[r_ob], writes=[r_a], dma=True)
    return kb.finish([r_a])


def build_RET(N):
    kb = KB()
    nc, p = kb.nc, kb.p
    NCH = N // 128
    qT, r_q = kb.din("qT", [256, N], BF16)
    kT, r_k = kb.din("kT", [256, N], BF16)
    kk, r_kk = kb.din("k", [N, 256], BF16)
    vv, r_vv = kb.din("v", [N, 256], BF16)
    cst, r_cst = kb.din("cst", [128, 260])
    ro, r_ro = kb.dout("ro", [N, 256])
    qs, r_qs = kb.sb([128, 2, N], BF16)
    ks, r_ks = kb.sb([128, 2, N], BF16)
    kt_, r_kt = kb.sb([128, NCH, 256], BF16)
    vs, r_vs = kb.sb([128, NCH, 256], BF16)
    cs, r_cs = kb.sb([128, 260], F32)
    intra, r_intra = kb.sb([128, 128], F32)
    dec, r_dec = kb.sb([128, 4], F32)
    S, r_S = kb.sb([128, 2, 256], F32)
    Sb, r_Sb = kb.sb([128, 2, 256], BF16)
    pT = [kb.sb([128, 128], BF16) for _ in range(2)]
    kd = [kb.sb([128, 256], BF16) for _ in range(2)]
    isb = [kb.sb([128, 256], F32) for _ in range(2)]
    ob = [kb.sb([128, 256], F32) for _ in range(2)]
    ps_s = [kb.ps() for _ in range(2)]
    ps_i = [kb.ps() for _ in range(2)]
    ps_x = [kb.ps() for _ in range(2)]
    ps_u = [kb.ps() for _ in range(2)]
    p.add("sp", lambda e: e.dma_start(out=cs[:], in_=cst), writes=[r_cs], dma=True)
    p.add("sp", lambda e: e.dma_start(out=qs[:], in_=qT.rearrange("(a p) n -> p a n", p=128)), writes=[r_qs], dma=True)
    p.add("act", lambda e: e.dma_start(out=ks[:], in_=kT.rearrange("(a p) n -> p a n", p=128)), writes=[r_ks], dma=True)
    p.add("sp", lambda e: e.dma_start(out=kt_[:], in_=kk.rearrange("(a p) d -> p a d", p=128)), writes=[r_kt], dma=True)
    p.add("act", lambda e: e.dma_start(out=vs[:], in_=vv.rearrange("(a p) d -> p a d", p=128)), writes=[r_vs], dma=True)
    lg = cs[:, 256:257]
    p.add("act", lambda e: e.activation(out=intra[:, :], in_=cs[:, 0:128], func=AF.Exp, scale=lg), reads=[r_cs], writes=[r_intra])
    p.add("dve", lambda e: e.tensor_tensor(out=intra[:, :], in0=intra[:, :], in1=cs[:, 128:256], op=ALU.mult), reads=[r_intra, r_cs], writes=[r_intra])
    p.add("act", lambda e: e.activation(out=dec[:, 0:3], in_=cs[:, 257:260], func=AF.Exp, scale=lg), reads=[r_cs], writes=[r_dec])
    p.add("pool", lambda e: e.memset(S[:], 0.0), writes=[r_S])
    p.add("pool", lambda e: e.memset(Sb[:], 0.0), writes=[r_Sb])
    for c in range(NCH):
        b = c % 2
        csl = slice(c * 128, (c + 1) * 128)
        pss, r_pss = ps_s[b]
        psi, r_psi = ps_i[b]
        psx, r_psx = ps_x[b]
        psu, r_psu = ps_u[b]
        ptb, r_ptb = pT[b]
        kdb, r_kdb = kd[b]
        isbb, r_isb = isb[b]
        obb, r_ob = ob[b]
        for dt in range(2):
            p.add("pe", lambda e, dt=dt, csl=csl, pss=pss: e.matmul(pss[:, 0:128], ks[:, dt, csl], qs[:, dt, csl], start=(dt == 0), stop=(dt == 1)), reads=[r_ks, r_qs], writes=[r_pss])
        p.add("dve", lambda e, pss=pss, ptb=ptb: e.tensor_tensor(out=ptb[:, :], in0=pss[:, 0:128], in1=intra[:, :], op=ALU.mult), reads=[r_pss, r_intra], writes=[r_ptb])
        p.add("pe", lambda e, c=c, psi=psi, ptb=ptb: e.matmul(psi[:, 0:256], ptb[:, :], vs[:, c, :], start=True, stop=True), reads=[r_ptb, r_vs], writes=[r_psi])
        for dt in range(2):
            p.add("pe", lambda e, dt=dt, csl=csl, psx=psx: e.matmul(psx[:, 0:256], qs[:, dt, csl], Sb[:, dt, :], start=(dt == 0), stop=(dt == 1)), reads=[r_qs, r_Sb], writes=[r_psx])
        p.add("act", lambda e, psi=psi, isbb=isbb: e.activation(out=isbb[:, :], in_=psi[:, 0:256], func=AF.Copy), reads=[r_psi], writes=[r_isb])
        p.add("dve", lambda e, psx=psx, isbb=isbb, obb=obb: e.scalar_tensor_tensor(out=obb[:, :], in0=psx[:, 0:256], scalar=dec[:, 0:1], in1=isbb[:, :], op0=ALU.mult, op1=ALU.add),
              reads=[r_psx, r_dec, r_isb], writes=[r_ob])
        p.add("sp", lambda e, csl=csl, obb=obb: e.dma_start(out=ro[csl, :], in_=obb[:, :]), reads=[r_ob], writes=[r_ro], dma=True)
        p.add("act", lambda e, c=c, kdb=kdb: e.activation(out=kdb[:, :], in_=kt_[:, c, :], func=AF.Copy, scale=dec[:, 1:2]), reads=[r_kt, r_dec], writes=[r_kdb])
        for dt in range(2):
            p.add("pe", lambda e, dt=dt, c=c, psu=psu, kdb=kdb: e.matmul(psu[:, dt * 256:(dt + 1) * 256], kdb[:, dt * 128:(dt + 1) * 128], vs[:, c, :], start=True, stop=True), reads=[r_kdb, r_vs], writes=[r_psu])
        for dt in range(2):
            p.add("dve", lambda e, dt=dt, psu=psu: e.scalar_tensor_tensor(out=S[:, dt, :], in0=S[:, dt, :], scalar=dec[:, 2:3], in1=psu[:, dt * 256:(dt + 1) * 256], op0=ALU.mult, op1=ALU.add),
                  reads=[r_S, r_dec, r_psu], writes=[r_S])
        p.add("act", lambda e: e.activation(out=Sb[:, :, :], in_=S[:, :, :], func=AF.Copy), reads=[r_S], writes=[r_Sb])
    return kb.finish([r_ro])


PI = float(np.pi)
S5_POOL_ENG = "pool"
S5_FOURMM = True


def range_reduce(kb, dst, r_dst, src, r_src, off, ki, kf, tq, r_sc):
    p = kb.p
    I2P = 1.0 / (2 * PI)
    dve = lambda fn, rd, wr: p.add("dve", fn, reads=rd, writes=wr)
    dve(lambda e: e.tensor_scalar(out=ki, in0=src, scalar1=I2P, scalar2=off * I2P, op0=ALU.mult, op1=ALU.add), [r_src], [r_sc])
    dve(lambda e: e.tensor_copy(out=kf, in_=ki), [r_sc], [r_sc])
    dve(lambda e: e.tensor_scalar(out=tq, in0=src, scalar1=off, scalar2=None, op0=ALU.add), [r_src], [r_sc])
    dve(lambda e: e.scalar_tensor_tensor(out=dst, in0=kf, scalar=-2 * PI, in1=tq, op0=ALU.mult, op1=ALU.add), [r_sc], [r_dst])
    dve(lambda e: e.tensor_scalar(out=tq, in0=dst, scalar1=PI, scalar2=2 * PI, op0=ALU.is_gt, op1=ALU.mult), [r_dst], [r_sc])
    dve(lambda e: e.tensor_tensor(out=dst, in0=dst, in1=tq, op=ALU.subtract), [r_dst, r_sc], [r_dst])
    dve(lambda e: e.tensor_scalar(out=tq, in0=dst, scalar1=-PI, scalar2=2 * PI, op0=ALU.is_lt, op1=ALU.mult), [r_dst], [r_sc])
    dve(lambda e: e.tensor_tensor(out=dst, in0=dst, in1=tq, op=ALU.add), [r_dst, r_sc], [r_dst])


def s5_prep(kb, pr, r_pr, w_, r_w, ki, kf, tq, r_sc):
    p = kb.p
    W = lambda i: w_[:, i, :]
    are, aim, ldt = pr[:, 0, :], pr[:, 1, :], pr[:, 2, :]
    dve = lambda fn, rd, wr: p.add("dve", fn, reads=rd, writes=wr)
    act = lambda fn, rd, wr: p.add("act", fn, reads=rd, writes=wr)
    act(lambda e: e.activation(out=W(0), in_=ldt, func=AF.Exp), [r_pr], [r_w])
    dve(lambda e: e.tensor_tensor(out=W(1), in0=are, in1=W(0), op=ALU.mult), [r_pr, r_w], [r_w])
    act(lambda e: e.activation(out=W(1), in_=W(1), func=AF.Exp), [r_w], [r_w])
    dve(lambda e: e.tensor_tensor(out=W(2), in0=aim, in1=W(0), op=ALU.mult), [r_pr, r_w], [r_w])
    range_reduce(kb, W(3), r_w, W(2), r_w, 0.0, ki[:, 0:6], kf[:, 0:6], tq[:, 0:6], r_sc)
    range_reduce(kb, W(4), r_w, W(2), r_w, 0.5 * PI, ki[:, 0:6], kf[:, 0:6], tq[:, 0:6], r_sc)
    act(lambda e: e.activation(out=W(5), in_=W(3), func=AF.Sin), [r_w], [r_w])
    act(lambda e: e.activation(out=W(6), in_=W(4), func=AF.Sin), [r_w], [r_w])
    dve(lambda e: e.tensor_tensor(out=W(7), in0=W(1), in1=W(6), op=ALU.mult), [r_w], [r_w])
    dve(lambda e: e.tensor_tensor(out=W(8), in0=W(1), in1=W(5), op=ALU.mult), [r_w], [r_w])
    dve(lambda e: e.tensor_tensor(out=W(13), in0=are, in1=are, op=ALU.mult), [r_pr], [r_w])
    dve(lambda e: e.tensor_tensor(out=W(14), in0=aim, in1=aim, op=ALU.mult), [r_pr], [r_w])
    dve(lambda e: e.tensor_tensor(out=W(9), in0=W(13), in1=W(14), op=ALU.add), [r_w], [r_w])
    dve(lambda e: e.reciprocal(out=W(9), in_=W(9)), [r_w], [r_w])
    dve(lambda e: e.tensor_scalar(out=W(12), in0=W(7), scalar1=-1.0, scalar2=None, op0=ALU.add), [r_w], [r_w])
    dve(lambda e: e.tensor_tensor(out=W(13), in0=W(12), in1=are, op=ALU.mult), [r_w, r_pr], [r_w])
    dve(lambda e: e.tensor_tensor(out=W(14), in0=W(8), in1=aim, op=ALU.mult), [r_w, r_pr], [r_w])
    dve(lambda e: e.tensor_tensor(out=W(10), in0=W(13), in1=W(14), op=ALU.add), [r_w], [r_w])
    dve(lambda e: e.tensor_tensor(out=W(10), in0=W(10), in1=W(9), op=ALU.mult), [r_w], [r_w])
    dve(lambda e: e.tensor_tensor(out=W(13), in0=W(8), in1=are, op=ALU.mult), [r_w, r_pr], [r_w])
    dve(lambda e: e.tensor_tensor(out=W(14), in0=W(12), in1=aim, op=ALU.mult), [r_w, r_pr], [r_w])
    dve(lambda e: e.tensor_tensor(out=W(11), in0=W(13), in1=W(14), op=ALU.subtract), [r_w], [r_w])
    dve(lambda e: e.tensor_tensor(out=W(11), in0=W(11), in1=W(9), op=ALU.mult), [r_w], [r_w])


def build_S5v2(NC, L=16):
    kb = KB()
    nc, p = kb.nc, kb.p
    uR, r_u = kb.din("uR", [6, 32, L, NC], BF16)
    prm, r_prm = kb.din("prm", [128, 3, 6])
    Bm, r_Bm = kb.din("Bm", [128, 6, 2, 32])
    Cm, r_Cm = kb.din("Cm", [128, 6, 2, 32])
    dsk, r_dsk = kb.din("dsk", [32, 6])
    idn, r_idn = kb.din("idn", [128, 128])
    yR, r_y = kb.dout("yR", [6, 32, L, NC])
    us, r_us = kb.sb([32, 6, L, NC], BF16)
    pr, r_pr = kb.sb([128, 3, 6], F32)
    Bs, r_Bs = kb.sb([128, 6, 2, 32], F32)
    Cs, r_Cs = kb.sb([128, 6, 3, 32], F32)
    ds, r_ds = kb.sb([32, 6], F32)
    ident, r_id = kb.sb([128, 128], F32)
    w_, r_w = kb.sb([128, 16, 6], F32)
    ki, r_sc = kb.sb([128, 8], mybir.dt.int32)
    kf, _ = kb.sb([128, 8], F32)
    tq, _ = kb.sb([128, 8], F32)
    bb, r_bb = kb.sb([128, 6, 2, 32], F32)
    pw, r_pw = kb.sb([128, L + 2, 2, 6], F32)
    NST = max(1, (NC - 1).bit_length())
    dq, r_dq = kb.sb([128, NST, 3, 6], F32)
    t6, r_t6 = kb.sb([128, 4, 6], F32)
    for dst, src, r, eng in ((us, uR.rearrange("m p r n -> p m r n"), r_us, "sp"), (pr, prm, r_pr, "act"), (Bs, Bm, r_Bs, "sp"),
                             (ds, dsk, r_ds, "sp"), (ident, idn, r_id, "sp")):
        p.add(eng, lambda e, dst=dst, src=src: e.dma_start(out=dst[:], in_=src), writes=[r], dma=True)
    p.add("act", lambda e: e.dma_start(out=Cs[:, :, 0:2, :], in_=Cm), writes=[r_Cs], dma=True)
    dve = lambda fn, rd, wr: p.add("dve", fn, reads=rd, writes=wr)
    act = lambda fn, rd, wr: p.add("act", fn, reads=rd, writes=wr)
    s5_prep(kb, pr, r_pr, w_, r_w, ki[:, 0:6], kf[:, 0:6], tq[:, 0:6], r_sc)
    act(lambda e: e.activation(out=Cs[:, :, 2, :], in_=Cs[:, :, 1, :], func=AF.Copy, scale=-1.0), [r_Cs], [r_Cs])
    abr, abi = w_[:, 7, :], w_[:, 8, :]

    def cmul(o_re, o_im, a_re, a_im, b_re, b_im, rd, wr):
        dve(lambda e: e.tensor_tensor(out=t6[:, 0, :], in0=a_re, in1=b_re, op=ALU.mult), rd, [r_t6])
        dve(lambda e: e.tensor_tensor(out=t6[:, 1, :], in0=a_im, in1=b_im, op=ALU.mult), rd, [r_t6])
        dve(lambda e: e.tensor_tensor(out=t6[:, 2, :], in0=a_re, in1=b_im, op=ALU.mult), rd, [r_t6])
        dve(lambda e: e.tensor_tensor(out=t6[:, 3, :], in0=a_im, in1=b_re, op=ALU.mult), rd, [r_t6])
        dve(lambda e: e.tensor_tensor(out=o_re, in0=t6[:, 0, :], in1=t6[:, 1, :], op=ALU.subtract), [r_t6], wr)
        dve(lambda e: e.tensor_tensor(out=o_im, in0=t6[:, 2, :], in1=t6[:, 3, :], op=ALU.add), [r_t6], wr)

    p.add("pool", lambda e: e.memset(pw[:, 0, 0, :], 1.0), writes=[r_pw])
    p.add("pool", lambda e: e.memset(pw[:, 0, 1, :], 0.0), writes=[r_pw])
    for j in range(1, L + 2):
        cmul(pw[:, j, 0, :], pw[:, j, 1, :], pw[:, j - 1, 0, :], pw[:, j - 1, 1, :], abr, abi, [r_pw, r_w], [r_pw])
    dve(lambda e: e.tensor_copy(out=dq[:, 0, 0:2, :], in_=pw[:, L, :, :]), [r_pw], [r_dq])
    for i in range(1, NST):
        cmul(dq[:, i, 0, :], dq[:, i, 1, :], dq[:, i - 1, 0, :], dq[:, i - 1, 1, :], dq[:, i - 1, 0, :], dq[:, i - 1, 1, :], [r_dq], [r_dq])
    dve(lambda e: e.tensor_scalar(out=dq[:, :, 2, :], in0=dq[:, :, 1, :], scalar1=-1.0, scalar2=None, op0=ALU.mult), [r_dq], [r_dq])
    for mt in range(6):
        fr, fi = w_[:, 10, mt:mt + 1], w_[:, 11, mt:mt + 1]
        dve(lambda e, mt=mt, fi=fi: e.tensor_scalar(out=bb[:, mt, 0, :], in0=Bs[:, mt, 1, :], scalar1=fi, scalar2=None, op0=ALU.mult), [r_Bs, r_w], [r_bb])
        dve(lambda e, mt=mt, fr=fr: e.scalar_tensor_tensor(out=bb[:, mt, 0, :], in0=Bs[:, mt, 0, :], scalar=fr, in1=bb[:, mt, 0, :], op0=ALU.mult, op1=ALU.subtract), [r_Bs, r_w, r_bb], [r_bb])
        dve(lambda e, mt=mt, fi=fi: e.tensor_scalar(out=bb[:, mt, 1, :], in0=Bs[:, mt, 0, :], scalar1=fi, scalar2=None, op0=ALU.mult), [r_Bs, r_w], [r_bb])
        dve(lambda e, mt=mt, fr=fr: e.scalar_tensor_tensor(out=bb[:, mt, 1, :], in0=Bs[:, mt, 1, :], scalar=fr, in1=bb[:, mt, 1, :], op0=ALU.mult, op1=ALU.add), [r_Bs, r_w, r_bb], [r_bb])
    BbS = [kb.sb([128, L, 2, 32], F32) for _ in range(2)]
    Win = [kb.sb([32, L, 2, 128], BF16) for _ in range(2)]
    CRI = [kb.sb([128, L, 2, 32], BF16) for _ in range(3)]
    Kt = [kb.sb([32, L, 32], BF16) for _ in range(3)]
    tsm, r_tsm = kb.sb([128, 2, 32], F32)
    XA = [kb.sb([128, 2, NC], F32) for _ in range(4)]
    Xps = [kb.sb([128, 2, NC], BF16) for _ in range(2)]
    ysb = [kb.sb([32, 512], F32) for _ in range(2)]
    yo = [kb.sb([32, NC], F32) for _ in range(3)]
    ptr = [kb.ps() for _ in range(2)]
    pz = [kb.ps() for _ in range(2)]
    pya = [kb.ps() for _ in range(2)]
    pyb = [kb.ps() for _ in range(2)]
    npc = -(-NC // 512)
    psz = -(-NC // npc)
    pieces = [(s0, min(psz, NC - s0)) for s0 in range(0, NC, psz)]
    cnt_box = [0]

    def setup(mt):
        bbs, r_bbs = BbS[mt % 2]
        win, r_win = Win[mt % 2]
        cri, r_cri = CRI[mt % 3]
        kt, r_kt = Kt[mt % 3]
        for j in range(L):
            pre_, pim_ = pw[:, j, 0, mt:mt + 1], pw[:, j, 1, mt:mt + 1]
            dve(lambda e, pim_=pim_, mt=mt: e.tensor_scalar(out=tsm[:, 0, :], in0=bb[:, mt, 1, :], scalar1=pim_, scalar2=None, op0=ALU.mult), [r_bb, r_pw], [r_tsm])
            dve(lambda e, pre_=pre_, mt=mt, j=j, bbs=bbs: e.scalar_tensor_tensor(out=bbs[:, j, 0, :], in0=bb[:, mt, 0, :], scalar=pre_, in1=tsm[:, 0, :], op0=ALU.mult, op1=ALU.subtract), [r_bb, r_pw, r_tsm], [r_bbs])
            dve(lambda e, pim_=pim_, mt=mt: e.tensor_scalar(out=tsm[:, 1, :], in0=bb[:, mt, 0, :], scalar1=pim_, scalar2=None, op0=ALU.mult), [r_bb, r_pw], [r_tsm])
            dve(lambda e, pre_=pre_, mt=mt, j=j, bbs=bbs: e.scalar_tensor_tensor(out=bbs[:, j, 1, :], in0=bb[:, mt, 1, :], scalar=pre_, in1=tsm[:, 1, :], op0=ALU.mult, op1=ALU.add), [r_bb, r_pw, r_tsm], [r_bbs])
        for g in range(L * 2 // 4):
            pt, r_pt = ptr[g % 2]
            for q in range(4):
                idx = g * 4 + q
                s_, ri = idx // 2, idx % 2
                p.add("pe", lambda e, s_=s_, ri=ri, q=q, pt=pt, bbs=bbs: e.transpose(pt[0:32, q * 128:(q + 1) * 128], bbs[:, L - 1 - s_, ri, :], ident[:, :]), reads=[r_bbs, r_id], writes=[r_pt])
            s0_ = (g * 4) // 2
            act(lambda e, s0_=s0_, pt=pt, win=win: e.activation(out=win[:, s0_:s0_ + 2, :, :], in_=pt[0:32, 0:512].rearrange("p (a b c) -> p a b c", a=2, b=2), func=AF.Copy), [r_pt], [r_win])
        pk, r_pk = ptr[0]
        for j in range(L):
            p.add("pe", lambda e, j=j, mt=mt, pk=pk, bbs=bbs: e.matmul(pk[0:32, j * 32:(j + 1) * 32], bbs[:, j, 0, :], Cs[:, mt, 0, :], start=True, stop=False), reads=[r_bbs, r_Cs], writes=[r_pk])
            p.add("pe", lambda e, j=j, mt=mt, pk=pk, bbs=bbs: e.matmul(pk[0:32, j * 32:(j + 1) * 32], bbs[:, j, 1, :], Cs[:, mt, 2, :], start=False, stop=True), reads=[r_bbs, r_Cs], writes=[r_pk])
        act(lambda e, pk=pk, kt=kt: e.activation(out=kt[:, :, :], in_=pk[0:32, 0:L * 32].rearrange("p (a b) -> p a b", a=L), func=AF.Copy), [r_pk], [r_kt])
        for r in range(L):
            pre_, pim_ = pw[:, r + 1, 0, mt:mt + 1], pw[:, r + 1, 1, mt:mt + 1]
            dve(lambda e, pim_=pim_, mt=mt: e.tensor_scalar(out=tsm[:, 0, :], in0=Cs[:, mt, 2, :], scalar1=pim_, scalar2=None, op0=ALU.mult), [r_Cs, r_pw], [r_tsm])
            dve(lambda e, pre_=pre_, mt=mt, r=r, cri=cri: e.scalar_tensor_tensor(out=cri[:, r, 0, :], in0=Cs[:, mt, 0, :], scalar=pre_, in1=tsm[:, 0, :], op0=ALU.mult, op1=ALU.add), [r_Cs, r_pw, r_tsm], [r_cri])
            dve(lambda e, pim_=pim_, mt=mt: e.tensor_scalar(out=tsm[:, 1, :], in0=Cs[:, mt, 0, :], scalar1=pim_, scalar2=-1.0, op0=ALU.mult, op1=ALU.mult), [r_Cs, r_pw], [r_tsm])
            dve(lambda e, pre_=pre_, mt=mt, r=r, cri=cri: e.scalar_tensor_tensor(out=cri[:, r, 1, :], in0=Cs[:, mt, 2, :], scalar=pre_, in1=tsm[:, 1, :], op0=ALU.mult, op1=ALU.add), [r_Cs, r_pw, r_tsm], [r_cri])

    def mainA(mt):
        bbs, r_bbs = BbS[mt % 2]
        win, r_win = Win[mt % 2]
        cri, r_cri = CRI[mt % 3]
        kt, r_kt = Kt[mt % 3]
        xa, r_xa = XA[2 * (mt % 2)]
        xb_, r_xb = XA[2 * (mt % 2) + 1]
        Xp, r_Xp = Xps[mt % 2]
        for (s0, n) in pieces:
            for ri in range(2):
                pzz, r_pz = pz[ri]
                for s_ in range(L):
                    p.add("pe", lambda e, s_=s_, ri=ri, s0=s0, n=n, pzz=pzz, win=win, mt=mt: e.matmul(pzz[:, 0:n], win[:, s_, ri, :], us[:, mt, s_, s0:s0 + n], start=(s_ == 0), stop=(s_ == L - 1)), reads=[r_win, r_us], writes=[r_pz])
                act(lambda e, ri=ri, s0=s0, n=n, pzz=pzz, xa=xa: e.activation(out=xa[:, ri, s0:s0 + n], in_=pzz[:, 0:n], func=AF.Copy), [r_pz], [r_xa])
        cur, r_cur, nxt, r_nxt = xa, r_xa, xb_, r_xb
        for i in range(NST):
            d = 1 << i
            if d >= NC:
                break
            dre, dim_, ndim = dq[:, i, 0, mt:mt + 1], dq[:, i, 1, mt:mt + 1], dq[:, i, 2, mt:mt + 1]
            act(lambda e, d=d, cur=cur, nxt=nxt: e.activation(out=nxt[:, :, 0:d], in_=cur[:, :, 0:d], func=AF.Copy), [r_cur], [r_nxt])
            dve(lambda e, d=d, cur=cur, nxt=nxt, dre=dre: e.scalar_tensor_tensor(out=nxt[:, 0, d:NC], in0=cur[:, 0, 0:NC - d], scalar=dre, in1=cur[:, 0, d:NC], op0=ALU.mult, op1=ALU.add), [r_cur, r_dq], [r_nxt])
            dve(lambda e, d=d, cur=cur, nxt=nxt, ndim=ndim: e.scalar_tensor_tensor(out=nxt[:, 0, d:NC], in0=cur[:, 1, 0:NC - d], scalar=ndim, in1=nxt[:, 0, d:NC], op0=ALU.mult, op1=ALU.add), [r_cur, r_dq, r_nxt], [r_nxt])
            dve(lambda e, d=d, cur=cur, nxt=nxt, dre=dre: e.scalar_tensor_tensor(out=nxt[:, 1, d:NC], in0=cur[:, 1, 0:NC - d], scalar=dre, in1=cur[:, 1, d:NC], op0=ALU.mult, op1=ALU.add), [r_cur, r_dq], [r_nxt])
            dve(lambda e, d=d, cur=cur, nxt=nxt, dim_=dim_: e.scalar_tensor_tensor(out=nxt[:, 1, d:NC], in0=cur[:, 0, 0:NC - d], scalar=dim_, in1=nxt[:, 1, d:NC], op0=ALU.mult, op1=ALU.add), [r_cur, r_dq, r_nxt], [r_nxt])
            cur, r_cur, nxt, r_nxt = nxt, r_nxt, cur, r_cur
        p.add("pool", lambda e: e.memset(Xp[:, :, 0:1], 0.0), writes=[r_Xp])
        act(lambda e, cur=cur: e.activation(out=Xp[:, :, 1:NC], in_=cur[:, :, 0:NC - 1], func=AF.Copy), [r_cur], [r_Xp])

    def mainB(mt):
        cri, r_cri = CRI[mt % 3]
        kt, r_kt = Kt[mt % 3]
        Xp, r_Xp = Xps[mt % 2]
        for r in range(L):
            yob, r_yo = yo[cnt_box[0] % 3]
            cnt_box[0] += 1
            for pi_, (s0, n) in enumerate(pieces):
                pa, r_pa = pya[(r + pi_) % 2]
                pb_, r_pb = pyb[(r + pi_) % 2]
                ysbb, r_ysb = ysb[(r + pi_) % 2]
                for ri in range(2):
                    p.add("pe", lambda e, r=r, ri=ri, s0=s0, n=n, pa=pa, cri=cri: e.matmul(pa[0:32, 0:n], cri[:, r, ri, :], Xp[:, ri, s0:s0 + n], start=(ri == 0), stop=(ri == 1)), reads=[r_cri, r_Xp], writes=[r_pa])
                for s_ in range(r + 1):
                    p.add("pe", lambda e, r=r, s_=s_, s0=s0, n=n, pb_=pb_, kt=kt, mt=mt: e.matmul(pb_[0:32, 0:n], kt[:, r - s_, :], us[:, mt, s_, s0:s0 + n], start=(s_ == 0), stop=(s_ == r)), reads=[r_kt, r_us], writes=[r_pb])
                act(lambda e, n=n, pa=pa, ysbb=ysbb: e.activation(out=ysbb[:, 0:n], in_=pa[0:32, 0:n], func=AF.Copy), [r_pa], [r_ysb])
                dve(lambda e, s0=s0, n=n, pb_=pb_, ysbb=ysbb, yob=yob: e.tensor_tensor(out=yob[:, s0:s0 + n], in0=pb_[0:32, 0:n], in1=ysbb[:, 0:n], op=ALU.add), [r_pb, r_ysb], [r_yo])
                dve(lambda e, s0=s0, n=n, r=r, mt=mt, yob=yob: e.scalar_tensor_tensor(out=yob[:, s0:s0 + n], in0=us[:, mt, r, s0:s0 + n], scalar=ds[:, mt:mt + 1], in1=yob[:, s0:s0 + n], op0=ALU.mult, op1=ALU.add), [r_us, r_ds, r_yo], [r_yo])
            p.add("sp", lambda e, mt=mt, r=r, yob=yob: e.dma_start(out=yR[mt, :, r, :], in_=yob[:, :]), reads=[r_yo], writes=[r_y], dma=True)

    def record(fn, *a):
        lst = []
        orig = p.add
        p.add = lambda *aa, **kk: lst.append((aa, kk))
        try:
            fn(*a)
        finally:
            p.add = orig
        return lst

    def merged(lists):
        lists = [l for l in lists if l]
        tot = max(len(l) for l in lists) if lists else 0
        pos = [0] * len(lists)
        for step in range(1, tot + 1):
            for li, l in enumerate(lists):
                tgt = (step * len(l) + tot - 1) // tot
                while pos[li] < tgt:
                    aa, kk = l[pos[li]]
                    p.add(*aa, **kk)
                    pos[li] += 1

    setup(0)
    setup(1)
    mainA(0)
    for mt in range(6):
        ls = []
        if mt + 1 < 6:
            ls.append(record(mainA, mt + 1))
        ls.append(record(mainB, mt))
        if mt + 2 < 6:
            ls.append(record(setup, mt + 2))
        merged(ls)
    return kb.finish([r_y])


def build_S5(N, NB=512):
    kb = KB()
    nc, p = kb.nc, kb.p
    NBLK = N // NB
    uT, r_u = kb.din("uT", [6, 32, N], BF16)
    prm, r_prm = kb.din("prm", [128, 3, 6])
    Bm, r_Bm = kb.din("Bm", [128, 6, 2, 32])
    Cm, r_Cm = kb.din("Cm", [128, 6, 2, 32])
    dsk, r_dsk = kb.din("dsk", [32, 6])
    tix, r_tix = kb.din("tix", [128, NB])
    idn, r_idn = kb.din("idn", [128, 128])
    yT, r_y = kb.dout("yT", [6, 32, N])
    us, r_us = kb.sb([32, 6, N], BF16)
    pr, r_pr = kb.sb([128, 3, 6], F32)
    Bs, r_Bs = kb.sb([128, 6, 2, 32], F32)
    Cs, r_Cs = kb.sb([128, 6, 2, 32], F32)
    Cb, r_Cb = kb.sb([128, 6, 3, 32], BF16)
    ds, r_ds = kb.sb([32, 6], F32)
    tx, r_tx = kb.sb([128, NB], F32)
    ident, r_id = kb.sb([128, 128], F32)
    w_, r_w = kb.sb([128, 16, 6], F32)
    bb, r_bb = kb.sb([128, 6, 2, 32], F32)
    BT, r_BT = kb.sb([32, 6, 2, 128], BF16)
    cosT, r_cos = kb.sb([128, 6, NB], F32)
    sinT, r_sin = kb.sb([128, 6, NB], F32)
    rtab, r_rt = kb.sb([128, 6, NB], F32)
    wre, r_wre = kb.sb([128, 6, NB], F32)
    wim, r_wim = kb.sb([128, 6, NB], F32)
    ini, r_ini = kb.sb([128, 2, 6], F32)
    tmpc, r_tmpc = kb.sb([128, 4, 6], F32)
    tt_, r_tt = kb.sb([128, 4, NB], F32)
    yo = [kb.sb([32, 6, NB], F32) for _ in range(2)]
    pbu = [kb.ps() for _ in range(4)]
    py = [kb.ps() for _ in range(2)]
    ptr, r_ptr = kb.ps()
    ki, r_sc = kb.sb([128, NB], mybir.dt.int32)
    kf, _ = kb.sb([128, NB], F32)
    tq, _ = kb.sb([128, NB], F32)
    for dst, src, r, eng in ((us, uT.rearrange("m p n -> p m n"), r_us, "sp"), (pr, prm, r_pr, "act"), (Bs, Bm, r_Bs, "sp"), (Cs, Cm, r_Cs, "act"),
                             (ds, dsk, r_ds, "sp"), (tx, tix, r_tx, "act"), (ident, idn, r_id, "sp")):
        p.add(eng, lambda e, dst=dst, src=src: e.dma_start(out=dst[:], in_=src), writes=[r], dma=True)
    W = lambda i: w_[:, i, :]
    are, aim, ldt = pr[:, 0, :], pr[:, 1, :], pr[:, 2, :]
    dve = lambda fn, rd, wr: p.add("dve", fn, reads=rd, writes=wr)
    act = lambda fn, rd, wr: p.add("act", fn, reads=rd, writes=wr)
    act(lambda e: e.activation(out=W(0), in_=ldt, func=AF.Exp), [r_pr], [r_w])
    dve(lambda e: e.tensor_tensor(out=W(1), in0=are, in1=W(0), op=ALU.mult), [r_pr, r_w], [r_w])
    act(lambda e: e.activation(out=W(1), in_=W(1), func=AF.Exp), [r_w], [r_w])
    dve(lambda e: e.tensor_tensor(out=W(2), in0=aim, in1=W(0), op=ALU.mult), [r_pr, r_w], [r_w])
    range_reduce(kb, W(3), r_w, W(2), r_w, 0.0, ki[:, 0:6], kf[:, 0:6], tq[:, 0:6], r_sc)
    range_reduce(kb, W(4), r_w, W(2), r_w, 0.5 * PI, ki[:, 0:6], kf[:, 0:6], tq[:, 0:6], r_sc)
    act(lambda e: e.activation(out=W(5), in_=W(3), func=AF.Sin), [r_w], [r_w])
    act(lambda e: e.activation(out=W(6), in_=W(4), func=AF.Sin), [r_w], [r_w])
    dve(lambda e: e.tensor_tensor(out=W(7), in0=W(1), in1=W(6), op=ALU.mult), [r_w], [r_w])
    dve(lambda e: e.tensor_tensor(out=W(8), in0=W(1), in1=W(5), op=ALU.mult), [r_w], [r_w])
    dve(lambda e: e.tensor_tensor(out=W(13), in0=are, in1=are, op=ALU.mult), [r_pr], [r_w])
    dve(lambda e: e.tensor_tensor(out=W(14), in0=aim, in1=aim, op=ALU.mult), [r_pr], [r_w])
    dve(lambda e: e.tensor_tensor(out=W(9), in0=W(13), in1=W(14), op=ALU.add), [r_w], [r_w])
    dve(lambda e: e.reciprocal(out=W(9), in_=W(9)), [r_w], [r_w])
    dve(lambda e: e.tensor_scalar(out=W(12), in0=W(7), scalar1=-1.0, scalar2=None, op0=ALU.add), [r_w], [r_w])
    dve(lambda e: e.tensor_tensor(out=W(13), in0=W(12), in1=are, op=ALU.mult), [r_w, r_pr], [r_w])
    dve(lambda e: e.tensor_tensor(out=W(14), in0=W(8), in1=aim, op=ALU.mult), [r_w, r_pr], [r_w])
    dve(lambda e: e.tensor_tensor(out=W(10), in0=W(13), in1=W(14), op=ALU.add), [r_w], [r_w])
    dve(lambda e: e.tensor_tensor(out=W(10), in0=W(10), in1=W(9), op=ALU.mult), [r_w], [r_w])
    dve(lambda e: e.tensor_tensor(out=W(13), in0=W(8), in1=are, op=ALU.mult), [r_w, r_pr], [r_w])
    dve(lambda e: e.tensor_tensor(out=W(14), in0=W(12), in1=aim, op=ALU.mult), [r_w, r_pr], [r_w])
    dve(lambda e: e.tensor_tensor(out=W(11), in0=W(13), in1=W(14), op=ALU.subtract), [r_w], [r_w])
    dve(lambda e: e.tensor_tensor(out=W(11), in0=W(11), in1=W(9), op=ALU.mult), [r_w], [r_w])
    dve(lambda e: e.tensor_scalar(out=W(15), in0=W(2), scalar1=float(NB), scalar2=None, op0=ALU.mult), [r_w], [r_w])
    range_reduce(kb, tmpc[:, 0, :], r_tmpc, W(15), r_w, 0.0, ki[:, 0:6], kf[:, 0:6], tq[:, 0:6], r_sc)
    range_reduce(kb, tmpc[:, 1, :], r_tmpc, W(15), r_w, 0.5 * PI, ki[:, 0:6], kf[:, 0:6], tq[:, 0:6], r_sc)
    act(lambda e: e.activation(out=tmpc[:, 2, :], in_=tmpc[:, 0, :], func=AF.Sin), [r_tmpc], [r_tmpc])
    act(lambda e: e.activation(out=tmpc[:, 3, :], in_=tmpc[:, 1, :], func=AF.Sin), [r_tmpc], [r_tmpc])
    act(lambda e: e.activation(out=Cb[:, :, 0, :], in_=Cs[:, :, 0, :], func=AF.Copy), [r_Cs], [r_Cb])
    act(lambda e: e.activation(out=Cb[:, :, 1, :], in_=Cs[:, :, 1, :], func=AF.Copy, scale=-1.0), [r_Cs], [r_Cb])
    act(lambda e: e.activation(out=Cb[:, :, 2, :], in_=Cs[:, :, 0, :], func=AF.Copy, scale=-1.0), [r_Cs], [r_Cb])
    for mt in range(6):
        fr, fi = w_[:, 10, mt:mt + 1], w_[:, 11, mt:mt + 1]
        dve(lambda e, mt=mt, fi=fi: e.tensor_scalar(out=bb[:, mt, 0, :], in0=Bs[:, mt, 1, :], scalar1=fi, scalar2=None, op0=ALU.mult), [r_Bs, r_w], [r_bb])
        dve(lambda e, mt=mt, fr=fr: e.scalar_tensor_tensor(out=bb[:, mt, 0, :], in0=Bs[:, mt, 0, :], scalar=fr, in1=bb[:, mt, 0, :], op0=ALU.mult, op1=ALU.subtract), [r_Bs, r_w, r_bb], [r_bb])
        dve(lambda e, mt=mt, fi=fi: e.tensor_scalar(out=bb[:, mt, 1, :], in0=Bs[:, mt, 0, :], scalar1=fi, scalar2=None, op0=ALU.mult), [r_Bs, r_w], [r_bb])
        dve(lambda e, mt=mt, fr=fr: e.scalar_tensor_tensor(out=bb[:, mt, 1, :], in0=Bs[:, mt, 1, :], scalar=fr, in1=bb[:, mt, 1, :], op0=ALU.mult, op1=ALU.add), [r_Bs, r_w, r_bb], [r_bb])
        for ri in range(2):
            p.add("pe", lambda e, mt=mt, ri=ri: e.transpose(ptr[0:32, 0:128], bb[:, mt, ri, :], ident[:, :]), reads=[r_bb, r_id], writes=[r_ptr])
            act(lambda e, mt=mt, ri=ri: e.activation(out=BT[:, mt, ri, :], in_=ptr[0:32, 0:128], func=AF.Copy), [r_ptr], [r_BT])
        ang = w_[:, 2, mt:mt + 1]
        dve(lambda e, ang=ang: e.tensor_scalar(out=tt_[:, 2, :], in0=tx[:, :], scalar1=ang, scalar2=None, op0=ALU.mult), [r_tx, r_w], [r_tt])
        range_reduce(kb, tt_[:, 0, :], r_tt, tt_[:, 2, :], r_tt, 0.0, ki[:, :], kf[:, :], tq[:, :], r_sc)
        range_reduce(kb, tt_[:, 1, :], r_tt, tt_[:, 2, :], r_tt, 0.5 * PI, ki[:, :], kf[:, :], tq[:, :], r_sc)
        act(lambda e, mt=mt: e.activation(out=sinT[:, mt, :], in_=tt_[:, 0, :], func=AF.Sin), [r_tt], [r_sin])
        act(lambda e, mt=mt: e.activation(out=cosT[:, mt, :], in_=tt_[:, 1, :], func=AF.Sin), [r_tt], [r_cos])
        act(lambda e, mt=mt: e.activation(out=rtab[:, mt, :], in_=tx[:, :], func=AF.Identity, scale=0.0, bias=w_[:, 1, mt:mt + 1]), [r_tx, r_w], [r_rt])
    p.add("pool", lambda e: e.memset(ini[:], 0.0), writes=[r_ini])
    pool = lambda fn, rd, wr: p.add(S5_POOL_ENG, fn, reads=rd, writes=wr)
    busb = [kb.sb([128, 2, NB], F32) for _ in range(2)]
    tD = [kb.sb([128, 2, NB], F32) for _ in range(2)]
    tP = [kb.sb([128, 2, NB], F32) for _ in range(2)]
    bpr = [kb.sb([128, NB], F32) for _ in range(2)]
    bpi = [kb.sb([128, NB], F32) for _ in range(2)]
    xq = [kb.sb([128, 4, NB], BF16) for _ in range(2)]
    r_wre_m = [p.res("wre%d" % i) for i in range(6)]
    r_wim_m = [p.res("wim%d" % i) for i in range(6)]
    r_yo_m = [[p.res("yo") for _ in range(6)] for _ in range(2)]

    def iter_ops(blk, mt, k):
        ops = []
        rec = lambda eng, fn, rd, wr: ops.append((eng, fn, rd, wr))
        bsl = slice(blk * NB, (blk + 1) * NB)
        yob, _ = yo[blk % 2]
        r_yo = r_yo_m[blk % 2][mt]
        pyb, r_py = py[k % 2]
        pre, r_pre = pbu[(2 * k) % 4]
        pim, r_pim = pbu[(2 * k + 1) % 4]
        bs_, r_bs_ = busb[k % 2]
        td, r_td = tD[k % 2]
        tp, r_tp = tP[k % 2]
        br_, r_br = bpr[k % 2]
        bi_, r_bi = bpi[k % 2]
        xx, r_xx = xq[k % 2]
        r_wre, r_wim = r_wre_m[mt], r_wim_m[mt]
        c_, s_ = cosT[:, mt, :], sinT[:, mt, :]
        rec("pe", lambda e: e.matmul(pre[:, 0:NB], BT[:, mt, 0, :], us[:, mt, bsl], start=True, stop=True), [r_BT, r_us], [r_pre])
        rec("pe", lambda e: e.matmul(pim[:, 0:NB], BT[:, mt, 1, :], us[:, mt, bsl], start=True, stop=True), [r_BT, r_us], [r_pim])
        rec("act", lambda e: e.activation(out=bs_[:, 0, :], in_=pre[:, 0:NB], func=AF.Copy), [r_pre], [r_bs_])
        rec("act", lambda e: e.activation(out=bs_[:, 1, :], in_=pim[:, 0:NB], func=AF.Copy), [r_pim], [r_bs_])
        rec("dve", lambda e: e.tensor_tensor(out=td[:, 0, :], in0=bs_[:, 0, :], in1=c_, op=ALU.mult), [r_bs_, r_cos], [r_td])
        rec("dve", lambda e: e.tensor_tensor(out=td[:, 1, :], in0=bs_[:, 1, :], in1=s_, op=ALU.mult), [r_bs_, r_sin], [r_td])
        rec("dve", lambda e: e.tensor_tensor(out=br_[:, :], in0=td[:, 0, :], in1=td[:, 1, :], op=ALU.add), [r_td], [r_br])
        rec(S5_POOL_ENG, lambda e: e.tensor_tensor(out=tp[:, 0, :], in0=bs_[:, 1, :], in1=c_, op=ALU.mult), [r_bs_, r_cos], [r_tp])
        rec(S5_POOL_ENG, lambda e: e.tensor_tensor(out=tp[:, 1, :], in0=bs_[:, 0, :], in1=s_, op=ALU.mult), [r_bs_, r_sin], [r_tp])
        rec(S5_POOL_ENG, lambda e: e.tensor_tensor(out=bi_[:, :], in0=tp[:, 0, :], in1=tp[:, 1, :], op=ALU.subtract), [r_tp], [r_bi])
        rec("dve", lambda e: e.tensor_tensor_scan(out=wre[:, mt, :], data0=rtab[:, mt, :], data1=br_[:, :], initial=ini[:, 0, mt:mt + 1], op0=ALU.mult, op1=ALU.add), [r_rt, r_br, r_ini], [r_wre])
        rec("dve", lambda e: e.tensor_tensor_scan(out=wim[:, mt, :], data0=rtab[:, mt, :], data1=bi_[:, :], initial=ini[:, 1, mt:mt + 1], op0=ALU.mult, op1=ALU.add), [r_rt, r_bi, r_ini], [r_wim])
        rec("dve", lambda e: e.tensor_tensor(out=xx[:, 0, :], in0=wre[:, mt, :], in1=c_, op=ALU.mult), [r_wre, r_cos], [r_xx])
        rec("dve", lambda e: e.tensor_tensor(out=xx[:, 1, :], in0=wim[:, mt, :], in1=s_, op=ALU.mult), [r_wim, r_sin], [r_xx])
        rec(S5_POOL_ENG, lambda e: e.tensor_tensor(out=xx[:, 2, :], in0=wim[:, mt, :], in1=c_, op=ALU.mult), [r_wim, r_cos], [r_xx])
        rec(S5_POOL_ENG, lambda e: e.tensor_tensor(out=xx[:, 3, :], in0=wre[:, mt, :], in1=s_, op=ALU.mult), [r_wre, r_sin], [r_xx])
        for qi, ci in ((0, 0), (1, 2), (2, 1), (3, 1)):
            rec("pe", lambda e, qi=qi, ci=ci: e.matmul(pyb[0:32, 0:NB], Cb[:, mt, ci, :], xx[:, qi, :], start=(qi == 0), stop=(qi == 3)), [r_Cb, r_xx], [r_py])
        rec("dve", lambda e: e.scalar_tensor_tensor(out=yob[:, mt, :], in0=us[:, mt, bsl], scalar=ds[:, mt:mt + 1], in1=pyb[0:32, 0:NB], op0=ALU.mult, op1=ALU.add), [r_us, r_ds, r_py], [r_yo])
        return ops

    k = 0
    r_wre_all = r_wre_m
    r_wim_all = r_wim_m
    for blk in range(NBLK):
        bsl = slice(blk * NB, (blk + 1) * NB)
        yob, _ = yo[blk % 2]
        for m0 in range(0, 6, 2):
            oa = iter_ops(blk, m0, k)
            ob_ = iter_ops(blk, m0 + 1, k + 1)
            k += 2
            for i in range(max(len(oa), len(ob_))):
                for lst in (oa, ob_):
                    if i < len(lst):
                        eng, fn, rd, wr = lst[i]
                        p.add(eng, fn, reads=rd, writes=wr)
        cN, sN = tmpc[:, 3, :], tmpc[:, 2, :]
        wlr, wli = wre[:, :, NB - 1], wim[:, :, NB - 1]
        dve(lambda e: e.tensor_tensor(out=tmpc[:, 0, :], in0=wlr, in1=cN, op=ALU.mult), r_wre_all + [r_tmpc], [r_tmpc])
        dve(lambda e: e.tensor_tensor(out=tmpc[:, 1, :], in0=wli, in1=sN, op=ALU.mult), r_wim_all + [r_tmpc], [r_tmpc])
        dve(lambda e: e.tensor_tensor(out=ini[:, 0, :], in0=tmpc[:, 0, :], in1=tmpc[:, 1, :], op=ALU.subtract), [r_tmpc], [r_ini])
        dve(lambda e: e.tensor_tensor(out=tmpc[:, 0, :], in0=wli, in1=cN, op=ALU.mult), r_wim_all + [r_tmpc], [r_tmpc])
        dve(lambda e: e.tensor_tensor(out=tmpc[:, 1, :], in0=wlr, in1=sN, op=ALU.mult), r_wre_all + [r_tmpc], [r_tmpc])
        dve(lambda e: e.tensor_tensor(out=ini[:, 1, :], in0=tmpc[:, 0, :], in1=tmpc[:, 1, :], op=ALU.add), [r_tmpc], [r_ini])
        p.add("sp", lambda e, bsl=bsl, yob=yob: e.dma_start(out=yT[:, :, bsl].rearrange("m p n -> p m n"), in_=yob[:, :, :]), reads=r_yo_m[blk % 2], writes=[r_y], dma=True)
    return kb.finish([r_y])


def build_L0():
    kb = KB()
    nc, p = kb.nc, kb.p
    adaw, r_aw = kb.din("adaw", [2, D, 768])
    adab, r_ab = kb.din("adab", [128, 2, 6])
    c3T, r_c = kb.din("c3T", [D, 3])
    modT, r_m = kb.dout("modT", [2, 768, 3])
    aw = [kb.sb([128, 16, 768], F32) for _ in range(2)]
    cs, r_cs = kb.sb([128, 16, 3], F32)
    sc, r_sc = kb.sb([128, 16, 3], F32)
    bs, r_bs = kb.sb([128, 2, 6], F32)
    ob, r_ob = kb.sb([128, 2, 6, 3], F32)
    pp = [kb.ps() for _ in range(2)]
    p.add("sp", lambda e: e.dma_start(out=cs[:], in_=c3T.rearrange("(a p) c -> p a c", p=128)), writes=[r_cs], dma=True)
    p.add("sp", lambda e: e.dma_start(out=bs[:], in_=adab), writes=[r_bs], dma=True)
    p.add("act", lambda e: e.activation(out=sc[:], in_=cs[:], func=AF.Silu), reads=[r_cs], writes=[r_sc])
    k = 0
    for l in range(2):
        awl, r_awl = aw[l]
        p.add("sp" if l == 0 else "act", lambda e, l=l, awl=awl: e.dma_start(out=awl[:], in_=adaw[l].rearrange("(a p) n -> p a n", p=128)), writes=[r_awl], dma=True)
        for j in range(6):
            pj, r_pj = pp[k % 2]
            k += 1
            for kt in range(16):
                p.add("pe", lambda e, kt=kt, j=j, awl=awl, pj=pj: e.matmul(pj[:, 0:3], awl[:, kt, j * 128:(j + 1) * 128], sc[:, kt, :], start=(kt == 0), stop=(kt == 15)), reads=[r_awl, r_sc], writes=[r_pj])
            p.add("act", lambda e, l=l, j=j, pj=pj: e.activation(out=ob[:, l, j, :], in_=pj[:, 0:3], func=AF.Identity, bias=bs[:, l, j:j + 1]), reads=[r_pj, r_bs], writes=[r_ob])
    p.add("sp", lambda e: e.dma_start(out=modT.rearrange("l (j p) c -> p l j c", p=128), in_=ob[:]), reads=[r_ob], writes=[r_m], dma=True)
    return kb.finish([r_m])


def build_LC1(NT=17):
    kb = KB()
    nc, p = kb.nc, kb.p
    T = NT * 128
    rf, r_rf = kb.din("rf", [T, 1024])
    rb, r_rb = kb.din("rb", [T, 1024])
    g, r_g = kb.din("g", [T, 1024], BF16)
    gnw, r_gn = kb.din("gnw", [128, 1024])
    ro, r_ro = kb.dout("r", [T, 1024], BF16)
    gw, r_gw = kb.sb([128, 1024], F32)
    p.add("sp", lambda e: e.dma_start(out=gw[:], in_=gnw), writes=[r_gw], dma=True)
    A = [kb.sb([128, 1024], F32) for _ in range(2)]
    B = [kb.sb([128, 1024], F32) for _ in range(2)]
    Gt = [kb.sb([128, 1024], BF16) for _ in range(2)]
    Ot = [kb.sb([128, 1024], BF16) for _ in range(2)]
    st, r_st = kb.sb([128, 4, 6], F32)
    mv, r_mv = kb.sb([128, 4, 2], F32)
    rs, r_rs = kb.sb([128, 4], F32)
    for tt in range(NT):
        b = tt % 2
        a_, r_a = A[b]; b_, r_b = B[b]; g_, r_g_ = Gt[b]; o_, r_o = Ot[b]
        tsl = slice(tt * 128, (tt + 1) * 128)
        p.add("sp", lambda e, a_=a_, tsl=tsl: e.dma_start(out=a_[:], in_=rf[tsl, :]), writes=[r_a], dma=True)
        p.add("act", lambda e, b_=b_, tsl=tsl: e.dma_start(out=b_[:], in_=rb[tsl, :]), writes=[r_b], dma=True)
        p.add("sp", lambda e, g_=g_, tsl=tsl: e.dma_start(out=g_[:], in_=g[tsl, :]), writes=[r_g_], dma=True)
        p.add("pool", lambda e, a_=a_, b_=b_: e.tensor_tensor(out=a_[:], in0=a_[:], in1=b_[:], op=ALU.add), reads=[r_a, r_b], writes=[r_a])
        for h in range(4):
            p.add("dve", lambda e, h=h, a_=a_: e.bn_stats(out=st[:, h, :], in_=a_[:, h * 256:(h + 1) * 256]), reads=[r_a], writes=[r_st])
            p.add("dve", lambda e, h=h: e.bn_aggr(out=mv[:, h, :], in_=st[:, h, :]), reads=[r_st], writes=[r_mv])
        p.add("act", lambda e: e.activation(out=rs[:, :], in_=mv[:, :, 1], func=AF.Sqrt, bias=EPS), reads=[r_mv], writes=[r_rs])
        p.add("dve", lambda e: e.reciprocal(out=rs[:, :], in_=rs[:, :]), reads=[r_rs], writes=[r_rs])
        for h in range(4):
            hs = slice(h * 256, (h + 1) * 256)
            p.add("dve", lambda e, h=h, hs=hs, a_=a_: e.tensor_scalar(out=a_[:, hs], in0=a_[:, hs], scalar1=mv[:, h, 0:1], scalar2=rs[:, h:h + 1], op0=ALU.subtract, op1=ALU.mult), reads=[r_a, r_mv, r_rs], writes=[r_a])
        p.add("pool", lambda e, a_=a_: e.tensor_tensor(out=a_[:], in0=a_[:], in1=gw[:], op=ALU.mult), reads=[r_a, r_gw], writes=[r_a])
        p.add("dve", lambda e, a_=a_, g_=g_, o_=o_: e.tensor_tensor(out=o_[:], in0=a_[:], in1=g_[:], op=ALU.mult), reads=[r_a, r_g_], writes=[r_o])
        p.add("sp", lambda e, o_=o_, tsl=tsl: e.dma_start(out=ro[tsl, :], in_=o_[:]), reads=[r_o], writes=[r_ro], dma=True)
    return kb.finish([r_ro])


def build_LC2a(NT=17, NB=256):
    kb = KB()
    nc, p = kb.nc, kb.p
    T = NT * 128
    rT, r_rT = kb.din("rT", [1024, T], BF16)
    aT, r_aT = kb.din("aT", [1024, T])
    agT, r_agT = kb.din("agT", [1024, T], BF16)
    yf, r_yf = kb.din("yf", [768, T])
    yb, r_yb = kb.din("yb", [768, T])
    sgT, r_sgT = kb.din("sgT", [768, T], BF16)
    mgT, r_mgT = kb.din("mgT", [6144, T], BF16)
    wr_, r_wr_ = kb.din("w_br_ret", [1024, D])
    wa_, r_wa_ = kb.din("w_br_att", [1024, D])
    ws_, r_ws_ = kb.din("w_br_s5", [768, D])
    gl_, r_gl_ = kb.din("glu_w", [768, 768])
    gb_, r_gb_ = kb.din("glu_b", [128, 6])
    mT, r_mT = kb.dout("mT", [D, T], BF16)
    wr, r_wr = kb.sb([128, 8, D], BF16)
    wa, r_wa = kb.sb([128, 8, D], BF16)
    ws, r_ws = kb.sb([128, 6, D], BF16)
    gl, r_gl = kb.sb([128, 6, 768], BF16)
    gb, r_gb = kb.sb([128, 6], F32)
    p.add("pool", lambda e: e.dma_start(out=wr[:], in_=wr_.rearrange("(a p) n -> p a n", p=128)), writes=[r_wr], dma=True)
    p.add("pool", lambda e: e.dma_start(out=wa[:], in_=wa_.rearrange("(a p) n -> p a n", p=128)), writes=[r_wa], dma=True)
    p.add("pool", lambda e: e.dma_start(out=ws[:], in_=ws_.rearrange("(a p) n -> p a n", p=128)), writes=[r_ws], dma=True)
    p.add("pool", lambda e: e.dma_start(out=gl[:], in_=gl_.rearrange("(a p) n -> p a n", p=128)), writes=[r_gl], dma=True)
    p.add("sp", lambda e: e.dma_start(out=gb[:], in_=gb_), writes=[r_gb], dma=True)
    rt, r_rt = kb.sb([128, 8, NB], BF16)
    at, r_at = kb.sb([128, 8, NB], F32)
    agt, r_agt = kb.sb([128, 8, NB], BF16)
    ap_, r_ap = kb.sb([128, 8, NB], BF16)
    yft, r_yft = kb.sb([128, 6, NB], F32)
    ybt, r_ybt = kb.sb([128, 6, NB], F32)
    sgt, r_sgt = kb.sb([128, 6, NB], BF16)
    t1, r_t1 = kb.sb([128, 6, NB], F32)
    s1, r_s1 = kb.sb([128, 6, NB], BF16)
    s2, r_s2 = kb.sb([128, 6, NB], BF16)
    zs, r_zs = kb.sb([128, NB], F32)
    mg = [kb.sb([128, 3, NB], BF16) for _ in range(2)]
    m1, r_m1 = kb.sb([128, 3, NB], F32)
    mo = [kb.sb([128, NB], BF16) for _ in range(2)]
    pz = [kb.ps() for _ in range(2)]
    pb = [kb.ps() for _ in range(6)]
    blocks = [(s, min(NB, T - s)) for s in range(0, T, NB)]
    k = 0
    for (s0, n) in blocks:
        bs = slice(s0, s0 + n)
        for dst, src, r, eng, kk in ((rt, rT, r_rt, "sp", 8), (at, aT, r_at, "act", 8), (agt, agT, r_agt, "sp", 8), (yft, yf, r_yft, "act", 6), (ybt, yb, r_ybt, "sp", 6), (sgt, sgT, r_sgt, "act", 6)):
            p.add(eng, lambda e, dst=dst, src=src, bs=bs, n=n: e.dma_start(out=dst[:, :, 0:n], in_=src[:, bs].rearrange("(a p) n -> p a n", p=128)), writes=[r], dma=True)
        p.add("dve", lambda e, n=n: e.tensor_tensor(out=ap_[:, :, 0:n], in0=at[:, :, 0:n], in1=agt[:, :, 0:n], op=ALU.mult), reads=[r_at, r_agt], writes=[r_ap])
        p.add("pool", lambda e, n=n: e.tensor_tensor(out=yft[:, :, 0:n], in0=yft[:, :, 0:n], in1=ybt[:, :, 0:n], op=ALU.add), reads=[r_yft, r_ybt], writes=[r_yft])
        p.add("act", lambda e, n=n: e.activation(out=t1[:, :, 0:n], in_=yft[:, :, 0:n], func=AF.Square), reads=[r_yft], writes=[r_t1])
        p.add("dve", lambda e, n=n: e.tensor_scalar(out=t1[:, :, 0:n], in0=t1[:, :, 0:n], scalar1=0.044715, scalar2=1.0, op0=ALU.mult, op1=ALU.add), reads=[r_t1], writes=[r_t1])
        p.add("dve", lambda e, n=n: e.tensor_tensor(out=t1[:, :, 0:n], in0=t1[:, :, 0:n], in1=yft[:, :, 0:n], op=ALU.mult), reads=[r_t1, r_yft], writes=[r_t1])
        p.add("act", lambda e, n=n: e.activation(out=t1[:, :, 0:n], in_=t1[:, :, 0:n], func=AF.Sigmoid, scale=1.5957691216057308), reads=[r_t1], writes=[r_t1])
        p.add("dve", lambda e, n=n: e.tensor_tensor(out=s1[:, :, 0:n], in0=t1[:, :, 0:n], in1=yft[:, :, 0:n], op=ALU.mult), reads=[r_t1, r_yft], writes=[r_s1])
        for j in range(6):
            pzz, r_pz = pz[j % 2]
            for kt in range(6):
                p.add("pe", lambda e, j=j, kt=kt, n=n, pzz=pzz: e.matmul(pzz[:, 0:n], gl[:, kt, j * 128:(j + 1) * 128], s1[:, kt, 0:n], start=(kt == 0), stop=(kt == 5)), reads=[r_gl, r_s1], writes=[r_pz])
            p.add("act", lambda e, j=j, n=n, pzz=pzz: e.activation(out=zs[:, 0:n], in_=pzz[:, 0:n], func=AF.Sigmoid, bias=gb[:, j:j + 1]), reads=[r_pz, r_gb], writes=[r_zs])
            p.add("dve", lambda e, j=j, n=n: e.tensor_tensor(out=zs[:, 0:n], in0=zs[:, 0:n], in1=s1[:, j, 0:n], op=ALU.mult), reads=[r_zs, r_s1], writes=[r_zs])
            p.add("dve", lambda e, j=j, n=n: e.tensor_tensor(out=s2[:, j, 0:n], in0=zs[:, 0:n], in1=sgt[:, j, 0:n], op=ALU.mult), reads=[r_zs, r_sgt], writes=[r_s2])
        for f in range(16):
            mgb, r_mg = mg[k % 2]
            mob, r_mo = mo[k % 2]
            pp = [pb[(3 * k + i) % 6] for i in range(3)]
            k += 1
            fs = slice(f * 128, (f + 1) * 128)
            p.add("sp", lambda e, f=f, bs=bs, n=n, mgb=mgb: e.dma_start(out=mgb[:, :, 0:n], in_=mgT[:, bs].rearrange("(b f p) n -> f p b n", b=3, p=128)[f]), writes=[r_mg], dma=True)
            for (w, r_w, src, r_src, nk, (pt, r_pt)) in ((wr, r_wr, rt, r_rt, 8, pp[0]), (wa, r_wa, ap_, r_ap, 8, pp[1]), (ws, r_ws, s2, r_s2, 6, pp[2])):
                for kt in range(nk):
                    p.add("pe", lambda e, w=w, src=src, kt=kt, nk=nk, fs=fs, n=n, pt=pt: e.matmul(pt[:, 0:n], w[:, kt, fs], src[:, kt, 0:n], start=(kt == 0), stop=(kt == nk - 1)), reads=[r_w, r_src], writes=[r_pt])
            for i in range(3):
                pt, r_pt = pp[i]
                p.add("dve", lambda e, i=i, n=n, pt=pt, mgb=mgb: e.tensor_tensor(out=m1[:, i, 0:n], in0=pt[:, 0:n], in1=mgb[:, i, 0:n], op=ALU.mult), reads=[r_pt, r_mg], writes=[r_m1])
            p.add("pool", lambda e, n=n: e.tensor_tensor(out=m1[:, 0, 0:n], in0=m1[:, 0, 0:n], in1=m1[:, 1, 0:n], op=ALU.add), reads=[r_m1], writes=[r_m1])
            p.add("pool", lambda e, n=n, mob=mob: e.tensor_tensor(out=mob[:, 0:n], in0=m1[:, 0, 0:n], in1=m1[:, 2, 0:n], op=ALU.add), reads=[r_m1], writes=[r_mo])
            p.add("act", lambda e, fs=fs, bs=bs, n=n, mob=mob: e.dma_start(out=mT[fs, bs], in_=mob[:, 0:n]), reads=[r_mo], writes=[r_mT], dma=True)
    return kb.finish([r_mT])


ALPHA = (2.0 * 2) ** 0.25


def build_LC2b(NT=17):
    kb = KB()
    nc, p = kb.nc, kb.p
    T = NT * 128
    mT, r_mT = kb.din("mT", [D, T], BF16)
    wo_, r_wo_ = kb.din("w_out", [D, D])
    x, r_x = kb.din("x", [T, D])
    rows, r_rows = kb.din("rows", [128, 4, D])
    xo, r_xo = kb.dout("xo", [T, D])
    wo, r_wo = kb.sb([128, 16, D], BF16)
    rw, r_rw = kb.sb([128, 4, D], F32)
    p.add("pool", lambda e: e.dma_start(out=wo[:], in_=wo_.rearrange("(a p) n -> p a n", p=128)), writes=[r_wo], dma=True)
    p.add("sp", lambda e: e.dma_start(out=rw[:], in_=rows), writes=[r_rw], dma=True)
    mt = [kb.sb([128, 16, 128], BF16) for _ in range(2)]
    xt = [kb.sb([128, D], F32) for _ in range(2)]
    z = [kb.sb([128, D], F32) for _ in range(2)]
    pb = [kb.ps() for _ in range(8)]
    st, r_st = kb.sb([128, 4, 6], F32)
    mv, r_mv = kb.sb([128, 2], F32)
    rs, r_rs = kb.sb([128, 1], F32)
    for tt in range(NT):
        b = tt % 2
        mtb, r_mt = mt[b]; xtb, r_xt = xt[b]; zb, r_z = z[b]
        tsl = slice(tt * 128, (tt + 1) * 128)
        gi = 0 if tt < NT - 1 else 1
        p.add("sp", lambda e, mtb=mtb, tsl=tsl: e.dma_start(out=mtb[:], in_=mT[:, tsl].rearrange("(a p) n -> p a n", p=128)), writes=[r_mt], dma=True)
        p.add("act", lambda e, xtb=xtb, tsl=tsl: e.dma_start(out=xtb[:], in_=x[tsl, :]), writes=[r_xt], dma=True)
        for cb in range(4):
            pt, r_pt = pb[(tt * 4 + cb) % 8]
            cs = slice(cb * 512, (cb + 1) * 512)
            for kt in range(16):
                p.add("pe", lambda e, kt=kt, cs=cs, pt=pt, mtb=mtb: e.matmul(pt[:, :], mtb[:, kt, :], wo[:, kt, cs], start=(kt == 0), stop=(kt == 15)), reads=[r_mt, r_wo], writes=[r_pt])
            p.add("dve", lambda e, cs=cs, pt=pt, zb=zb, gi=gi: e.tensor_tensor(out=zb[:, cs], in0=pt[:, :], in1=rw[:, gi, cs], op=ALU.mult), reads=[r_pt, r_rw], writes=[r_z])
            p.add("dve", lambda e, cs=cs, zb=zb, xtb=xtb: e.scalar_tensor_tensor(out=zb[:, cs], in0=xtb[:, cs], scalar=ALPHA, in1=zb[:, cs], op0=ALU.mult, op1=ALU.add), reads=[r_xt, r_z], writes=[r_z])
            p.add("dve", lambda e, cb=cb, cs=cs, zb=zb: e.bn_stats(out=st[:, cb, :], in_=zb[:, cs]), reads=[r_z], writes=[r_st])
        p.add("dve", lambda e: e.bn_aggr(out=mv[:, :], in_=st[:, :, :].rearrange("p a b -> p (a b)")), reads=[r_st], writes=[r_mv])
        p.add("act", lambda e: e.activation(out=rs[:, :], in_=mv[:, 1:2], func=AF.Sqrt, bias=EPS), reads=[r_mv], writes=[r_rs])
        p.add("dve", lambda e: e.reciprocal(out=rs[:, :], in_=rs[:, :]), reads=[r_rs], writes=[r_rs])
        p.add("dve", lambda e, zb=zb: e.tensor_scalar(out=zb[:, :], in0=zb[:, :], scalar1=mv[:, 0:1], scalar2=rs[:, 0:1], op0=ALU.subtract, op1=ALU.mult), reads=[r_z, r_mv, r_rs], writes=[r_z])
        p.add("pool", lambda e, zb=zb: e.tensor_tensor(out=zb[:, :], in0=zb[:, :], in1=rw[:, 2, :], op=ALU.mult), reads=[r_z, r_rw], writes=[r_z])
        p.add("pool", lambda e, zb=zb: e.tensor_tensor(out=zb[:, :], in0=zb[:, :], in1=rw[:, 3, :], op=ALU.add), reads=[r_z, r_rw], writes=[r_z])
        p.add("sp", lambda e, zb=zb, tsl=tsl: e.dma_start(out=xo[tsl, :], in_=zb[:, :]), reads=[r_z], writes=[r_xo], dma=True)
    return kb.finish([r_xo])


import ml_dtypes
_BF = ml_dtypes.bfloat16
_PROGS = {}


def _prog(key, fn, *a):
    if key not in _PROGS:
        _PROGS[key] = fn(*a)
    return _PROGS[key]


def _run(nc, maps):
    res = run_bass_kernel_spmd(nc, maps, core_ids=list(range(8)))
    return res.results


def _rope_tables(hd):
    per = hd // 4
    n = np.arange(NLAT)
    row = (n // 64).astype(np.float32)
    col = (n % 64).astype(np.float32)
    inv = (np.float32(10000.0) ** (-np.arange(per, dtype=np.float32) / np.float32(per))).astype(np.float32)
    ang = np.concatenate([row[:, None] * inv, col[:, None] * inv], axis=-1).astype(np.float32)
    return np.cos(ang).astype(np.float32), np.sin(ang).astype(np.float32)


def _ret_cst(lg, diag):
    idx = np.arange(128, dtype=np.float32)
    rel = idx[None, :] - idx[:, None]
    mask = (rel >= 0) if diag else (rel > 0)
    cst = np.zeros((128, 260), np.float32)
    cst[:, 0:128] = np.where(mask, rel, 0)
    cst[:, 128:256] = mask
    cst[:, 256] = lg
    cst[:, 257] = idx + 1
    cst[:, 258] = 127 - idx
    cst[:, 259] = 128
    return cst


def _s5_inputs(u, a_re, a_im, log_dt, b_re, b_im, c_re, c_im, dsk, NB):
    N = u.shape[0]
    prm = np.zeros((128, 3, 6), np.float32)
    Bm = np.zeros((128, 6, 2, 32), np.float32)
    Cm = np.zeros((128, 6, 2, 32), np.float32)
    dk = np.zeros((32, 6), np.float32)
    for mt in range(6):
        for gp in range(2):
            g = 2 * mt + gp
            ps = slice(gp * 64, gp * 64 + 64)
            cs = slice(gp * 16, gp * 16 + 16)
            prm[ps, 0, mt] = a_re[g]
            prm[ps, 1, mt] = a_im[g]
            prm[ps, 2, mt] = log_dt[g]
            Bm[ps, mt, 0, cs] = b_re[g]
            Bm[ps, mt, 1, cs] = b_im[g]
            Cm[ps, mt, 0, cs] = c_re[g].T
            Cm[ps, mt, 1, cs] = c_im[g].T
            dk[cs, mt] = dsk[g]
    uR = np.ascontiguousarray(u.reshape(N // 16, 16, 6, 32).transpose(2, 3, 1, 0))
    return dict(uR=uR, prm=prm, Bm=Bm, Cm=Cm, dsk=dk, idn=np.eye(128, dtype=np.float32))


def kernel(x, c, ctx, c_ctx, ada_w, ada_b, w_in, ret_log_decay, ret_gn_w, att_q_norm, att_k_norm,
           s5_a_re, s5_a_im, s5_log_dt, s5_b_re, s5_b_im, s5_c_re, s5_c_im, s5_d, s5_glu_w, s5_glu_b,
           w_br_ret, w_br_att, w_br_s5, w_out, ln_w, ln_b):
    A = lambda v: np.ascontiguousarray(np.asarray(v))
    x = A(x); ctx = A(ctx)
    NS = NCTX + NLAT
    C = np.ascontiguousarray
    c3T = C(np.stack([A(c)[0], A(c)[1], A(c_ctx)], axis=1).astype(np.float32))
    maps = []
    for k in range(8):
        cs = slice(k * 768, (k + 1) * 768)
        maps.append(dict(adaw=C(A(ada_w)[:, :, cs]), adab=C(A(ada_b)[:, cs].reshape(2, 6, 128).transpose(2, 0, 1)), c3T=c3T))
    r0 = _run(_prog("L0", build_L0), maps)
    mod = np.concatenate([r["modT"] for r in r0], axis=1)
    cosR, sinR = _rope_tables(256)
    cosA, sinA = _rope_tables(128)
    xl = x.reshape(2 * NLAT, D)
    h = ctx.reshape(2 * NCTX, D)
    for l in range(2):
        shift, scale, gate = mod[l, 0:D], mod[l, D:2 * D], mod[l, 2 * D:3 * D]
        maps = []
        for k in range(8):
            b = k // 4
            ct = h[k * 128:(k + 1) * 128] if k < 4 else np.zeros((128, D), np.float32)
            xT = C(np.concatenate([xl[k * 2048:(k + 1) * 2048], ct], axis=0).T)
            modv = C(np.stack([shift[:, b], scale[:, b], shift[:, 2], scale[:, 2]], axis=1))
            ropeR = np.zeros((128, 17, 2, 128), np.float32)
            ropeA = np.zeros((128, 17, 2, 64), np.float32)
            n0 = (k % 4) * 2048
            ropeR[:, :16, 0] = cosR[n0:n0 + 2048].reshape(16, 128, 128).transpose(1, 0, 2)
            ropeR[:, :16, 1] = sinR[n0:n0 + 2048].reshape(16, 128, 128).transpose(1, 0, 2)
            ropeA[:, :16, 0] = cosA[n0:n0 + 2048].reshape(16, 128, 64).transpose(1, 0, 2)
            ropeA[:, :16, 1] = sinA[n0:n0 + 2048].reshape(16, 128, 64).transpose(1, 0, 2)
            ropeR[:, 16, 0] = 1.0
            ropeA[:, 16, 0] = 1.0
            qkw = C(np.broadcast_to(np.stack([A(att_q_norm)[l], A(att_k_norm)[l]])[None], (128, 2, 128)).astype(np.float32))
            maps.append(dict(xT=xT, modv=modv, w_in=A(w_in)[l], ropeR=ropeR, ropeA=ropeA, qkw=qkw))
        rA = _run(_prog("LA", build_LA), maps)
        O = [r["O"] for r in rA]
        OL = np.concatenate([o[:2048] for o in O], axis=0).reshape(2, NLAT, INW)
        OC = np.concatenate([o[2048:] for o in O[:4]], axis=0).reshape(2, NCTX, INW)
        seq = np.concatenate([OC, OL], axis=1)
        seqb = np.concatenate([OC[:, ::-1], OL[:, ::-1]], axis=1)
        sg = lambda S_, name, w: S_[:, :, SEG[name]:SEG[name] + w]
        ret = []
        for dr, S_ in ((0, seq), (1, seqb)):
            maps = []
            for k in range(8):
                b, j = k // 4, k % 4
                hs = slice(j * 256, (j + 1) * 256)
                q = sg(S_, "ret_q", 1024)[b][:, hs]; kk = sg(S_, "ret_k", 1024)[b][:, hs]; vv = sg(S_, "ret_v", 1024)[b][:, hs]
                maps.append(dict(qT=C(q.T), kT=C(kk.T), k=C(kk), v=C(vv), cst=_ret_cst(np.float32(A(ret_log_decay)[l, dr, j]), dr == 0)))
            rr = _run(_prog("RET", build_RET, NS), maps)
            o = np.stack([r["ro"] for r in rr]).reshape(2, 4, NS, 256).transpose(0, 2, 1, 3).reshape(2, NS, 1024)
            if dr == 1:
                o = np.concatenate([o[:, :NCTX][:, ::-1], o[:, NCTX:][:, ::-1]], axis=1)
            ret.append(o)
        maps = []
        for k in range(8):
            b, j = k // 4, k % 4
            q = sg(OL, "att_q", 1024)[b][:, j * 256:(j + 1) * 256].reshape(NLAT, 2, 128)
            kv = j // 2
            kk = sg(seq, "att_k", 256)[b][:, kv * 128:(kv + 1) * 128]
            vv = sg(seq, "att_v", 256)[b][:, kv * 128:(kv + 1) * 128]
            maps.append(dict(qT=C(q.transpose(1, 2, 0)), kT=C(kk.T), v=C(vv)))
        ra = _run(_prog("ATT", build_ATT, NLAT, NS), maps)
        aTl = np.stack([r["aT"] for r in ra]).reshape(2, 1024, NLAT)
        aTc = np.zeros((2, 1024, NCTX), np.float32)
        if l == 0:
            maps = []
            for k in range(8):
                b, j = k // 4, k % 4
                q = sg(OC, "att_q", 1024)[b][:, j * 256:(j + 1) * 256].reshape(NCTX, 2, 128)
                kv = j // 2
                kk = sg(OC, "att_k", 256)[b][:, kv * 128:(kv + 1) * 128]
                vv = sg(OC, "att_v", 256)[b][:, kv * 128:(kv + 1) * 128]
                maps.append(dict(qT=C(q.transpose(1, 2, 0)), kT=C(kk.T), v=C(vv)))
            rc = _run(_prog("ATTC", build_ATT, NCTX, NCTX, 256), maps)
            aTc = np.stack([r["aT"] for r in rc]).reshape(2, 1024, NCTX)
        ys = []
        for dr, S_ in ((0, seq), (1, seqb)):
            maps = []
            for k in range(8):
                b, j = k // 4, k % 4
                gs = slice(12 * j, 12 * j + 12)
                u = sg(S_, "s5_u", 768)[b][:, j * 192:(j + 1) * 192]
                dk = A(s5_d)[l].reshape(48, 16)[gs] if dr == 0 else np.zeros((12, 16), np.float32)
                maps.append(_s5_inputs(u, A(s5_a_re)[l, dr, gs], A(s5_a_im)[l, dr, gs], A(s5_log_dt)[l, dr, gs], A(s5_b_re)[l, dr, gs], A(s5_b_im)[l, dr, gs],
                                       A(s5_c_re)[l, dr, gs], A(s5_c_im)[l, dr, gs], dk, 256))
            rs_ = _run(_prog("S5v2", build_S5v2, NS // 16), maps)
            y = np.stack([r["yR"].transpose(0, 1, 3, 2).reshape(192, NS) for r in rs_]).reshape(2, 768, NS)
            if dr == 1:
                y = np.concatenate([y[:, :, :NCTX][:, :, ::-1], y[:, :, NCTX:][:, :, ::-1]], axis=2)
            ys.append(y)
        def tok(arr_lat, arr_ctx, k):
            W = arr_lat.shape[-1]
            ct = arr_ctx.reshape(2 * NCTX, W)[k * 128:(k + 1) * 128] if k < 4 else np.zeros((128, W), arr_lat.dtype)
            return np.concatenate([arr_lat.reshape(2 * NLAT, W)[k * 2048:(k + 1) * 2048], ct], axis=0)

        def tokT(arr_lat, arr_ctx, k):
            b = k // 4
            n0 = (k % 4) * 2048
            W = arr_lat.shape[1]
            if k < 4:
                bc, c0 = k // 2, (k % 2) * 128
                ct = arr_ctx[bc][:, c0:c0 + 128]
            else:
                ct = np.zeros((W, 128), arr_lat.dtype)
            return np.concatenate([arr_lat[b][:, n0:n0 + 2048], ct], axis=1)
        gnw = C(np.broadcast_to(A(ret_gn_w)[l][None], (128, 1024)).astype(np.float32))
        maps = [dict(rf=C(tok(ret[0][:, NCTX:], ret[0][:, :NCTX], k)), rb=C(tok(ret[1][:, NCTX:], ret[1][:, :NCTX], k)), g=C(O[k][:, SEG["ret_g"]:SEG["ret_g"] + 1024]), gnw=gnw) for k in range(8)]
        r1 = _run(_prog("LC1", build_LC1), maps)
        maps = []
        for k in range(8):
            Ok = O[k]
            maps.append(dict(rT=C(r1[k]["r"].T), aT=C(tokT(aTl, aTc, k)), agT=C(Ok[:, SEG["att_g"]:SEG["att_g"] + 1024].T),
                             yf=C(tokT(ys[0][:, :, NCTX:], ys[0][:, :, :NCTX], k)), yb=C(tokT(ys[1][:, :, NCTX:], ys[1][:, :, :NCTX], k)),
                             sgT=C(Ok[:, SEG["s5_g"]:SEG["s5_g"] + 768].T), mgT=C(Ok[:, SEG["merge"]:].T),
                             w_br_ret=A(w_br_ret)[l], w_br_att=A(w_br_att)[l], w_br_s5=A(w_br_s5)[l], glu_w=A(s5_glu_w)[l],
                             glu_b=C(A(s5_glu_b)[l].reshape(6, 128).T)))
        r2 = _run(_prog("LC2a", build_LC2a), maps)
        maps = []
        for k in range(8):
            b = k // 4
            ct = h[k * 128:(k + 1) * 128] if k < 4 else np.zeros((128, D), np.float32)
            xk = C(np.concatenate([xl[k * 2048:(k + 1) * 2048], ct], axis=0))
            rows = C(np.broadcast_to(np.stack([gate[:, b], gate[:, 2], A(ln_w)[l], A(ln_b)[l]])[None], (128, 4, D)).astype(np.float32))
            maps.append(dict(mT=r2[k]["mT"], w_out=A(w_out)[l], x=xk, rows=rows))
        r3 = _run(_prog("LC2b", build_LC2b), maps)
        xl = np.concatenate([r["xo"][:2048] for r in r3], axis=0)
        h = np.concatenate([r["xo"][2048:] for r in r3[:4]], axis=0)
    return xl.reshape(2, NLAT, D).astype(np.float32)
```

```python
import numpy as np
import concourse.bass as bass
import concourse.mybir as mybir

F32 = mybir.dt.float32
BF16 = mybir.dt.bfloat16
ALU = mybir.AluOpType
AF = mybir.ActivationFunctionType
AX = mybir.AxisListType

ENGS = ("pe", "act", "dve", "pool", "sp")
N_DMA_SEMS = 40


class Res:
    __slots__ = ("name", "w", "r")

    def __init__(self, name):
        self.name = name
        self.w = None
        self.r = []


class Op:
    __slots__ = ("id", "eng", "fn", "deps", "dma", "sig", "ev")

    def __init__(self, id, eng, fn, deps, dma):
        self.id = id
        self.eng = eng
        self.fn = fn
        self.deps = deps
        self.dma = dma
        self.sig = False
        self.ev = None


class Prog:
    def __init__(self, nc):
        self.nc = nc
        self.ops = []
        self.sems = {}
        self.final_dmas = []

    def res(self, name="r"):
        return Res(name)

    def add(self, eng, fn, reads=(), writes=(), dma=False):
        deps = set()
        for r in reads:
            if r.w is not None:
                deps.add(r.w)
        for w in writes:
            if w.w is not None:
                deps.add(w.w)
            deps.update(w.r)
        op = Op(len(self.ops), eng, fn, deps, dma)
        self.ops.append(op)
        for r in reads:
            r.r.append(op.id)
        for w in writes:
            w.w = op.id
            w.r = []
        return op.id

    def emit(self, sem_ctx):
        nc = self.nc
        ops = self.ops
        for op in ops:
            for d in op.deps:
                dop = ops[d]
                if dop.dma:
                    continue
                if dop.eng == op.eng and not op.dma and op.eng == "pe":
                    continue
                dop.sig = True
        cnt = {e: 0 for e in ENGS}
        dcnt = [0] * N_DMA_SEMS
        dlast = [None] * N_DMA_SEMS
        dnext = 0
        for op in ops:
            if op.dma:
                s = dnext
                dnext = (dnext + 1) % N_DMA_SEMS
                if dlast[s] is not None:
                    op.deps.add(dlast[s])
                dcnt[s] += 16
                op.ev = ("d%d" % s, dcnt[s])
                dlast[s] = op.id
            elif op.sig:
                cnt[op.eng] += 1
                op.ev = (op.eng, cnt[op.eng])
        streams = {e: [] for e in ENGS}
        seen = {e: {} for e in ENGS}
        for op in ops:
            waits = {}
            for d in op.deps:
                dop = ops[d]
                if dop.ev is None:
                    continue
                s, v = dop.ev
                if (not dop.dma) and dop.eng == op.eng and op.eng == "pe" and not op.dma:
                    continue
                if seen[op.eng].get(s, 0) >= v:
                    continue
                if waits.get(s, 0) < v:
                    waits[s] = v
            for s, v in waits.items():
                seen[op.eng][s] = v
            streams[op.eng].append((waits, op))
        return streams

    def run(self, streams, sems, block):
        nc = self.nc

        def mk(engname):
            def body(eng):
                for waits, op in streams[engname]:
                    for s, v in waits.items():
                        eng.wait_ge(sems[s], v)
                    if op.fn is None:
                        continue
                    ins = op.fn(eng)
                    if op.ev is not None:
                        s, v = op.ev
                        ins.then_inc(sems[s], 16 if op.dma else 1)
            return body

        block.tensor(mk("pe"))
        block.scalar(mk("act"))
        block.vector(mk("dve"))
        block.gpsimd(mk("pool"))
        block.sync(mk("sp"))


from contextlib import ExitStack
from concourse.bass_utils import run_bass_kernel_spmd


class KB:
    def __init__(self):
        self.nc = bass.Bass("TRN2", target_bir_lowering=False)
        self.p = Prog(self.nc)
        self.es = ExitStack()
        self.n = 0

    def sb(self, shape, dt, name=None):
        self.n += 1
        t = self.es.enter_context(self.nc.sbuf_tensor(name or ("s%d" % self.n), shape, dt))
        return t, self.p.res(name or "s")

    def ps(self, shape=(128, 512), dt=F32):
        self.n += 1
        t = self.es.enter_context(self.nc.psum_tensor("p%d" % self.n, list(shape), dt))
        return t, self.p.res("p")

    def din(self, name, shape, dt=F32):
        return self.nc.dram_tensor(name, list(shape), dt, kind="ExternalInput").ap(), self.p.res(name)

    def dout(self, name, shape, dt=F32):
        return self.nc.dram_tensor(name, list(shape), dt, kind="ExternalOutput").ap(), self.p.res(name)

    def finish(self, out_res):
        p, nc, es = self.p, self.nc, self.es
        p.add("sp", None, reads=out_res)
        p.add("act", None, reads=out_res)
        sems = {e: es.enter_context(nc.semaphore(e)) for e in ENGS}
        for i in range(N_DMA_SEMS):
            sems["d%d" % i] = es.enter_context(nc.semaphore("d%d" % i))
        streams = p.emit(None)
        with nc.Block() as block:
            p.run(streams, sems, block)
        es.close()
        return nc


D = 2048
NLAT = 8192
NCTX = 256
INW = 14336
EPS = 1e-6
SEG = dict(ret_q=0, ret_k=1024, ret_v=2048, ret_g=3072, att_q=4096, att_k=5120, att_v=5376, att_g=5632,
           s5_u=6656, s5_g=7424, merge=8192)


def rope_ops(kb, dst, dres, src, sres, cos, sin, tres, h, tmp, tmpres, extra_reads=()):
    p = kb.p
    x1, x2 = src[:, 0:h], src[:, h:2 * h]
    rd = [sres, tres] + list(extra_reads)
    p.add("dve", lambda e: e.tensor_tensor(out=tmp[:, 0:h], in0=x1, in1=cos, op=ALU.mult), reads=rd, writes=[tmpres])
    p.add("dve", lambda e: e.tensor_tensor(out=tmp[:, h:2 * h], in0=x2, in1=sin, op=ALU.mult), reads=rd, writes=[tmpres])
    p.add("dve", lambda e: e.tensor_tensor(out=tmp[:, 2 * h:3 * h], in0=x2, in1=cos, op=ALU.mult), reads=rd, writes=[tmpres])
    p.add("dve", lambda e: e.tensor_tensor(out=tmp[:, 3 * h:4 * h], in0=x1, in1=sin, op=ALU.mult), reads=rd, writes=[tmpres])
    p.add("dve", lambda e: e.tensor_tensor(out=dst[:, 0:h], in0=tmp[:, 0:h], in1=tmp[:, h:2 * h], op=ALU.subtract), reads=[tmpres], writes=[dres])
    p.add("dve", lambda e: e.tensor_tensor(out=dst[:, h:2 * h], in0=tmp[:, 2 * h:3 * h], in1=tmp[:, 3 * h:4 * h], op=ALU.add), reads=[tmpres], writes=[dres])


def build_LA(NT=17, CBS=None):
    CBS = list(range(28)) if CBS is None else CBS
    kb = KB()
    nc, p = kb.nc, kb.p
    NTOK = NT * 128
    xT, r_xT = kb.din("xT", [D, NTOK])
    modv, r_modv = kb.din("modv", [D, 4])
    w_in, r_w = kb.din("w_in", [D, INW])
    ropeR, r_ropeR = kb.din("ropeR", [128, NT, 2, 128])
    ropeA, r_ropeA = kb.din("ropeA", [128, NT, 2, 64])
    qkw, r_qkw = kb.din("qkw", [128, 2, 128])
    O, r_O = kb.dout("O", [NTOK, INW], BF16)

    mv, r_mv = kb.sb([128, 16, 4], F32)
    sc1, r_sc1 = kb.sb([128, 16, 2], F32)
    uT, r_uT = kb.sb([128, 16, NTOK], BF16)
    tR, r_tR = kb.sb([128, NT, 2, 128], F32)
    tA, r_tA = kb.sb([128, NT, 2, 64], F32)
    tW, r_tW = kb.sb([128, 2, 128], F32)
    xs = [kb.sb([128, NTOK], F32) for _ in range(2)]
    wb = [kb.sb([128, 16, 512], BF16) for _ in range(2)]
    ob = [kb.sb([128, 512], BF16) for _ in range(3)]
    pm = [kb.ps() for _ in range(4)]
    tmp, r_tmp = kb.sb([128, 512], F32)
    xn, r_xn = kb.sb([128, 512], F32)
    junk, r_junk = kb.sb([128, 128], F32)
    ss, r_ss = kb.sb([128, 4], F32)

    p.add("sp", lambda e: e.dma_start(out=mv[:], in_=modv.rearrange("(a p) c -> p a c", p=128)), writes=[r_mv], dma=True)
    p.add("sp", lambda e: e.dma_start(out=tR[:], in_=ropeR), writes=[r_tR], dma=True)
    p.add("sp", lambda e: e.dma_start(out=tA[:], in_=ropeA), writes=[r_tA], dma=True)
    p.add("sp", lambda e: e.dma_start(out=tW[:], in_=qkw), writes=[r_tW], dma=True)
    p.add("dve", lambda e: e.tensor_scalar(out=sc1[:, :, 0], in0=mv[:, :, 1], scalar1=1.0, scalar2=None, op0=ALU.add), reads=[r_mv], writes=[r_sc1])
    p.add("dve", lambda e: e.tensor_scalar(out=sc1[:, :, 1], in0=mv[:, :, 3], scalar1=1.0, scalar2=None, op0=ALU.add), reads=[r_mv], writes=[r_sc1])
    NL = (NT - 1) * 128
    for kt in range(16):
        b = kt % 2
        xsb, r_xsb = xs[b]
        p.add("sp" if kt % 2 == 0 else "act", lambda e, kt=kt, xsb=xsb: e.dma_start(out=xsb[:], in_=xT[kt * 128:(kt + 1) * 128, :]), writes=[r_xsb], dma=True)
        if NL > 0:
            p.add("act", lambda e, kt=kt, xsb=xsb: e.activation(out=uT[:, kt, 0:NL], in_=xsb[:, 0:NL], func=AF.Identity, scale=sc1[:, kt, 0:1], bias=mv[:, kt, 0:1]),
                  reads=[r_xsb, r_sc1, r_mv], writes=[r_uT])
        p.add("act", lambda e, kt=kt, xsb=xsb: e.activation(out=uT[:, kt, NL:NTOK], in_=xsb[:, NL:NTOK], func=AF.Identity, scale=sc1[:, kt, 1:2], bias=mv[:, kt, 2:3]),
              reads=[r_xsb, r_sc1, r_mv], writes=[r_uT])

    cnt = 0
    for ci, cb in enumerate(CBS):
        wbb, r_wbb = wb[ci % 2]
        p.add("pool", lambda e, cb=cb, wbb=wbb: e.dma_start(out=wbb[:], in_=w_in[:, cb * 512:(cb + 1) * 512].rearrange("(a p) n -> p a n", p=128)), writes=[r_wbb], dma=True)
        c0 = cb * 512
        for tt in range(NT):
            pmm, r_pm = pm[cnt % 4]
            obb, r_ob = ob[cnt % 3]
            cnt += 1
            for kt in range(16):
                p.add("pe", lambda e, kt=kt, tt=tt, pmm=pmm, wbb=wbb: e.matmul(pmm[:, :], uT[:, kt, tt * 128:(tt + 1) * 128], wbb[:, kt, :], start=(kt == 0), stop=(kt == 15)),
                      reads=[r_uT, r_wbb], writes=[r_pm])
            for hf in range(2):
                cc = c0 + hf * 256
                sl = slice(hf * 256, hf * 256 + 256)
                if cc < SEG["ret_v"]:
                    cos, sin = tR[:, tt, 0, :], tR[:, tt, 1, :]
                    if cc >= SEG["ret_k"]:
                        p.add("act", lambda e, pmm=pmm, sl=sl: e.activation(out=xn[:, sl], in_=pmm[:, sl], func=AF.Copy, scale=1.0 / 16.0), reads=[r_pm], writes=[r_xn])
                        rope_ops(kb, obb[:, sl], r_ob, xn[:, sl], r_xn, cos, sin, r_tR, 128, tmp, r_tmp)
                    else:
                        rope_ops(kb, obb[:, sl], r_ob, pmm[:, sl], r_pm, cos, sin, r_tR, 128, tmp, r_tmp)
                elif SEG["att_q"] <= cc < SEG["att_v"]:
                    wi = 0 if cc < SEG["att_k"] else 1
                    p.add("act", lambda e, pmm=pmm, sl=sl: e.activation(out=xn[:, sl], in_=pmm[:, sl], func=AF.Copy), reads=[r_pm], writes=[r_xn])
                    for hh in range(2):
                        s2 = slice(hf * 256 + hh * 128, hf * 256 + hh * 128 + 128)
                        p.add("act", lambda e, s2=s2, hh=hh: e.activation(out=junk[:, :], in_=xn[:, s2], func=AF.Square, accum_out=ss[:, hh:hh + 1]), reads=[r_xn], writes=[r_junk, r_ss])
                    p.add("act", lambda e: e.activation(out=ss[:, 2:4], in_=ss[:, 0:2], func=AF.Sqrt, scale=1.0 / 128.0, bias=EPS), reads=[r_ss], writes=[r_ss])
                    p.add("dve", lambda e: e.reciprocal(out=ss[:, 2:4], in_=ss[:, 2:4]), reads=[r_ss], writes=[r_ss])
                    for hh in range(2):
                        s2 = slice(hf * 256 + hh * 128, hf * 256 + hh * 128 + 128)
                        p.add("dve", lambda e, s2=s2, hh=hh, wi=wi: e.scalar_tensor_tensor(out=xn[:, s2], in0=xn[:, s2], scalar=ss[:, 2 + hh:3 + hh], in1=tW[:, wi, :], op0=ALU.mult, op1=ALU.mult),
                              reads=[r_xn, r_ss, r_tW], writes=[r_xn])
                        rope_ops(kb, obb[:, s2], r_ob, xn[:, s2], r_xn, tA[:, tt, 0, :], tA[:, tt, 1, :], r_tA, 64, tmp, r_tmp)
                else:
                    if cc >= SEG["merge"]:
                        fn = AF.Sigmoid
                    elif (SEG["ret_g"] <= cc < SEG["att_q"]) or (SEG["att_g"] <= cc < SEG["s5_u"]) or (SEG["s5_g"] <= cc < SEG["merge"]):
                        fn = AF.Silu
                    else:
                        fn = AF.Copy
                    p.add("act", lambda e, pmm=pmm, sl=sl, obb=obb, fn=fn: e.activation(out=obb[:, sl], in_=pmm[:, sl], func=fn), reads=[r_pm], writes=[r_ob])
            p.add("sp", lambda e, tt=tt, c0=c0, obb=obb: e.dma_start(out=O[tt * 128:(tt + 1) * 128, c0:c0 + 512], in_=obb[:, :]), reads=[r_ob], writes=[r_O], dma=True)
    return kb.finish([r_O])


def build_ATT(NQ, NK, QB=512):
    kb = KB()
    nc, p = kb.nc, kb.p
    qT, r_q = kb.din("qT", [2, 128, NQ], BF16)
    kT, r_k = kb.din("kT", [128, NK], BF16)
    v, r_v = kb.din("v", [NK, 128], BF16)
    aT, r_a = kb.dout("aT", [2, 128, NQ])
    NKT = NK // 128
    qs, r_qs = kb.sb([128, 2, NQ], BF16)
    ks, r_ks = kb.sb([128, NK], BF16)
    vs, r_vs = kb.sb([128, NKT, 128], BF16)
    ones, r_ones = kb.sb([128, 128], BF16)
    pT = [kb.sb([128, QB], BF16) for _ in range(4)]
    pss = [kb.ps() for _ in range(4)]
    pso = [kb.ps() for _ in range(2)]
    psr = [kb.ps() for _ in range(2)]
    rinv, r_rinv = kb.sb([128, QB], F32)
    ob = [kb.sb([128, QB], F32) for _ in range(2)]
    p.add("sp", lambda e: e.dma_start(out=qs[:], in_=qT.rearrange("h p n -> p h n")), writes=[r_qs], dma=True)
    p.add("act", lambda e: e.dma_start(out=ks[:], in_=kT), writes=[r_ks], dma=True)
    p.add("sp", lambda e: e.dma_start(out=vs[:], in_=v.rearrange("(a p) d -> p a d", p=128)), writes=[r_vs], dma=True)
    p.add("pool", lambda e: e.memset(ones[:], 1.0), writes=[r_ones])
    sc = 128.0 ** -0.5
    LOOK = 3
    its = [(h, qb, kt) for h in range(2) for qb in range(NQ // QB) for kt in range(NKT)]

    def front(i):
        h, qb, kt = its[i]
        pscore, r_ps = pss[i % 4]
        pt_, r_pt = pT[i % 4]
        qsl = slice(qb * QB, (qb + 1) * QB)
        p.add("pe", lambda e: e.matmul(pscore[:, 0:QB], ks[:, kt * 128:(kt + 1) * 128], qs[:, h, qsl], start=True, stop=True), reads=[r_ks, r_qs], writes=[r_ps])
        p.add("act", lambda e: e.activation(out=pt_[:, :], in_=pscore[:, 0:QB], func=AF.Exp, scale=sc), reads=[r_ps], writes=[r_pt])

    for i in range(min(LOOK, len(its))):
        front(i)
    for i, (h, qb, kt) in enumerate(its):
        blk = h * (NQ // QB) + qb
        po, r_po = pso[blk % 2]
        pr, r_pr = psr[blk % 2]
        obb, r_ob = ob[blk % 2]
        pt_, r_pt = pT[i % 4]
        qsl = slice(qb * QB, (qb + 1) * QB)
        p.add("pe", lambda e, kt=kt, po=po, pt_=pt_: e.matmul(po[:, 0:QB], vs[:, kt, :], pt_[:, :], start=(kt == 0), stop=(kt == NKT - 1)), reads=[r_vs, r_pt], writes=[r_po])
        p.add("pe", lambda e, kt=kt, pr=pr, pt_=pt_: e.matmul(pr[:, 0:QB], ones[:, :], pt_[:, :], start=(kt == 0), stop=(kt == NKT - 1)), reads=[r_ones, r_pt], writes=[r_pr])
        if i + LOOK < len(its):
            front(i + LOOK)
        if kt == NKT - 1:
            p.add("dve", lambda e, pr=pr: e.reciprocal(out=rinv[:, :], in_=pr[:, 0:QB]), reads=[r_pr], writes=[r_rinv])
            p.add("dve", lambda e, po=po, obb=obb: e.tensor_tensor(out=obb[:, :], in0=po[:, 0:QB], in1=rinv[:, :], op=ALU.mult), reads=[r_po, r_rinv], writes=[r_ob])
            p.add("sp", lambda e, h=h, qsl=qsl, obb=obb: e.dma_start(out=aT[h, :, qsl], in_=obb[:, :]), reads=[r_ob], writes=[r_a], dma=True)
    return kb.finish([r_a])


def build_RET(N):
    kb = KB()
    nc, p = kb.nc, kb.p
    NCH = N // 128
    qT, r_q = kb.din("qT", [256, N], BF16)
    kT, r_k = kb.din("kT", [256, N], BF16)
    kk, r_kk = kb.din("k", [N, 256], BF16)
    vv, r_vv = kb.din("v", [N, 256], BF16)
    cst, r_cst = kb.din("cst", [128, 260])
    ro, r_ro = kb.dout("ro", [N, 256])
    qs, r_qs = kb.sb([128, 2, N], BF16)
    ks, r_ks = kb.sb([128, 2, N], BF16)
    kt_, r_kt = kb.sb([128, NCH, 256], BF16)
    vs, r_vs = kb.sb([128, NCH, 256], BF16)
    cs, r_cs = kb.sb([128, 260], F32)
    intra, r_intra = kb.sb([128, 128], F32)
    dec, r_dec = kb.sb([128, 4], F32)
    S, r_S = kb.sb([128, 2, 256], F32)
    Sb, r_Sb = kb.sb([128, 2, 256], BF16)
    pT = [kb.sb([128, 128], BF16) for _ in range(2)]
    kd = [kb.sb([128, 256], BF16) for _ in range(2)]
    isb = [kb.sb([128, 256], F32) for _ in range(2)]
    ob = [kb.sb([128, 256], F32) for _ in range(2)]
    ps_s = [kb.ps() for _ in range(2)]
    ps_i = [kb.ps() for _ in range(2)]
    ps_x = [kb.ps() for _ in range(2)]
    ps_u = [kb.ps() for _ in range(2)]
    p.add("sp", lambda e: e.dma_start(out=cs[:], in_=cst), writes=[r_cs], dma=True)
    p.add("sp", lambda e: e.dma_start(out=qs[:], in_=qT.rearrange("(a p) n -> p a n", p=128)), writes=[r_qs], dma=True)
    p.add("act", lambda e: e.dma_start(out=ks[:], in_=kT.rearrange("(a p) n -> p a n", p=128)), writes=[r_ks], dma=True)
    p.add("sp", lambda e: e.dma_start(out=kt_[:], in_=kk.rearrange("(a p) d -> p a d", p=128)), writes=[r_kt], dma=True)
    p.add("act", lambda e: e.dma_start(out=vs[:], in_=vv.rearrange("(a p) d -> p a d", p=128)), writes=[r_vs], dma=True)
    lg = cs[:, 256:257]
    p.add("act", lambda e: e.activation(out=intra[:, :], in_=cs[:, 0:128], func=AF.Exp, scale=lg), reads=[r_cs], writes=[r_intra])
    p.add("dve", lambda e: e.tensor_tensor(out=intra[:, :], in0=intra[:, :], in1=cs[:, 128:256], op=ALU.mult), reads=[r_intra, r_cs], writes=[r_intra])
    p.add("act", lambda e: e.activation(out=dec[:, 0:3], in_=cs[:, 257:260], func=AF.Exp, scale=lg), reads=[r_cs], writes=[r_dec])
    p.add("pool", lambda e: e.memset(S[:], 0.0), writes=[r_S])
    p.add("pool", lambda e: e.memset(Sb[:], 0.0), writes=[r_Sb])
    for c in range(NCH):
        b = c % 2
        csl = slice(c * 128, (c + 1) * 128)
        pss, r_pss = ps_s[b]
        psi, r_psi = ps_i[b]
        psx, r_psx = ps_x[b]
        psu, r_psu = ps_u[b]
        ptb, r_ptb = pT[b]
        kdb, r_kdb = kd[b]
        isbb, r_isb = isb[b]
        obb, r_ob = ob[b]
        for dt in range(2):
            p.add("pe", lambda e, dt=dt, csl=csl, pss=pss: e.matmul(pss[:, 0:128], ks[:, dt, csl], qs[:, dt, csl], start=(dt == 0), stop=(dt == 1)), reads=[r_ks, r_qs], writes=[r_pss])
        p.add("dve", lambda e, pss=pss, ptb=ptb: e.tensor_tensor(out=ptb[:, :], in0=pss[:, 0:128], in1=intra[:, :], op=ALU.mult), reads=[r_pss, r_intra], writes=[r_ptb])
        p.add("pe", lambda e, c=c, psi=psi, ptb=ptb: e.matmul(psi[:, 0:256], ptb[:, :], vs[:, c, :], start=True, stop=True), reads=[r_ptb, r_vs], writes=[r_psi])
        for dt in range(2):
            p.add("pe", lambda e, dt=dt, csl=csl, psx=psx: e.matmul(psx[:, 0:256], qs[:, dt, csl], Sb[:, dt, :], start=(dt == 0), stop=(dt == 1)), reads=[r_qs, r_Sb], writes=[r_psx])
        p.add("act", lambda e, psi=psi, isbb=isbb: e.activation(out=isbb[:, :], in_=psi[:, 0:256], func=AF.Copy), reads=[r_psi], writes=[r_isb])
        p.add("dve", lambda e, psx=psx, isbb=isbb, obb=obb: e.scalar_tensor_tensor(out=obb[:, :], in0=psx[:, 0:256], scalar=dec[:, 0:1], in1=isbb[:, :], op0=ALU.mult, op1=ALU.add),
              reads=[r_psx, r_dec, r_isb], writes=[r_ob])
        p.add("sp", lambda e, csl=csl, obb=obb: e.dma_start(out=ro[csl, :], in_=obb[:, :]), reads=[r_ob], writes=[r_ro], dma=True)
        p.add("act", lambda e, c=c, kdb=kdb: e.activation(out=kdb[:, :], in_=kt_[:, c, :], func=AF.Copy, scale=dec[:, 1:2]), reads=[r_kt, r_dec], writes=[r_kdb])
        for dt in range(2):
            p.add("pe", lambda e, dt=dt, c=c, psu=psu, kdb=kdb: e.matmul(psu[:, dt * 256:(dt + 1) * 256], kdb[:, dt * 128:(dt + 1) * 128], vs[:, c, :], start=True, stop=True), reads=[r_kdb, r_vs], writes=[r_psu])
        for dt in range(2):
            p.add("dve", lambda e, dt=dt, psu=psu: e.scalar_tensor_tensor(out=S[:, dt, :], in0=S[:, dt, :], scalar=dec[:, 2:3], in1=psu[:, dt * 256:(dt + 1) * 256], op0=ALU.mult, op1=ALU.add),
                  reads=[r_S, r_dec, r_psu], writes=[r_S])
        p.add("act", lambda e: e.activation(out=Sb[:, :, :], in_=S[:, :, :], func=AF.Copy), reads=[r_S], writes=[r_Sb])
    return kb.finish([r_ro])


PI = float(np.pi)
S5_POOL_ENG = "pool"
S5_FOURMM = True


def range_reduce(kb, dst, r_dst, src, r_src, off, ki, kf, tq, r_sc):
    p = kb.p
    I2P = 1.0 / (2 * PI)
    dve = lambda fn, rd, wr: p.add("dve", fn, reads=rd, writes=wr)
    dve(lambda e: e.tensor_scalar(out=ki, in0=src, scalar1=I2P, scalar2=off * I2P, op0=ALU.mult, op1=ALU.add), [r_src], [r_sc])
    dve(lambda e: e.tensor_copy(out=kf, in_=ki), [r_sc], [r_sc])
    dve(lambda e: e.tensor_scalar(out=tq, in0=src, scalar1=off, scalar2=None, op0=ALU.add), [r_src], [r_sc])
    dve(lambda e: e.scalar_tensor_tensor(out=dst, in0=kf, scalar=-2 * PI, in1=tq, op0=ALU.mult, op1=ALU.add), [r_sc], [r_dst])
    dve(lambda e: e.tensor_scalar(out=tq, in0=dst, scalar1=PI, scalar2=2 * PI, op0=ALU.is_gt, op1=ALU.mult), [r_dst], [r_sc])
    dve(lambda e: e.tensor_tensor(out=dst, in0=dst, in1=tq, op=ALU.subtract), [r_dst, r_sc], [r_dst])
    dve(lambda e: e.tensor_scalar(out=tq, in0=dst, scalar1=-PI, scalar2=2 * PI, op0=ALU.is_lt, op1=ALU.mult), [r_dst], [r_sc])
    dve(lambda e: e.tensor_tensor(out=dst, in0=dst, in1=tq, op=ALU.add), [r_dst, r_sc], [r_dst])


def s5_prep(kb, pr, r_pr, w_, r_w, ki, kf, tq, r_sc):
    p = kb.p
    W = lambda i: w_[:, i, :]
    are, aim, ldt = pr[:, 0, :], pr[:, 1, :], pr[:, 2, :]
    dve = lambda fn, rd, wr: p.add("dve", fn, reads=rd, writes=wr)
    act = lambda fn, rd, wr: p.add("act", fn, reads=rd, writes=wr)
    act(lambda e: e.activation(out=W(0), in_=ldt, func=AF.Exp), [r_pr], [r_w])
    dve(lambda e: e.tensor_tensor(out=W(1), in0=are, in1=W(0), op=ALU.mult), [r_pr, r_w], [r_w])
    act(lambda e: e.activation(out=W(1), in_=W(1), func=AF.Exp), [r_w], [r_w])
    dve(lambda e: e.tensor_tensor(out=W(2), in0=aim, in1=W(0), op=ALU.mult), [r_pr, r_w], [r_w])
    range_reduce(kb, W(3), r_w, W(2), r_w, 0.0, ki[:, 0:6], kf[:, 0:6], tq[:, 0:6], r_sc)
    range_reduce(kb, W(4), r_w, W(2), r_w, 0.5 * PI, ki[:, 0:6], kf[:, 0:6], tq[:, 0:6], r_sc)
    act(lambda e: e.activation(out=W(5), in_=W(3), func=AF.Sin), [r_w], [r_w])
    act(lambda e: e.activation(out=W(6), in_=W(4), func=AF.Sin), [r_w], [r_w])
    dve(lambda e: e.tensor_tensor(out=W(7), in0=W(1), in1=W(6), op=ALU.mult), [r_w], [r_w])
    dve(lambda e: e.tensor_tensor(out=W(8), in0=W(1), in1=W(5), op=ALU.mult), [r_w], [r_w])
    dve(lambda e: e.tensor_tensor(out=W(13), in0=are, in1=are, op=ALU.mult), [r_pr], [r_w])
    dve(lambda e: e.tensor_tensor(out=W(14), in0=aim, in1=aim, op=ALU.mult), [r_pr], [r_w])
    dve(lambda e: e.tensor_tensor(out=W(9), in0=W(13), in1=W(14), op=ALU.add), [r_w], [r_w])
    dve(lambda e: e.reciprocal(out=W(9), in_=W(9)), [r_w], [r_w])
    dve(lambda e: e.tensor_scalar(out=W(12), in0=W(7), scalar1=-1.0, scalar2=None, op0=ALU.add), [r_w], [r_w])
    dve(lambda e: e.tensor_tensor(out=W(13), in0=W(12), in1=are, op=ALU.mult), [r_w, r_pr], [r_w])
    dve(lambda e: e.tensor_tensor(out=W(14), in0=W(8), in1=aim, op=ALU.mult), [r_w, r_pr], [r_w])
    dve(lambda e: e.tensor_tensor(out=W(10), in0=W(13), in1=W(14), op=ALU.add), [r_w], [r_w])
    dve(lambda e: e.tensor_tensor(out=W(10), in0=W(10), in1=W(9), op=ALU.mult), [r_w], [r_w])
    dve(lambda e: e.tensor_tensor(out=W(13), in0=W(8), in1=are, op=ALU.mult), [r_w, r_pr], [r_w])
    dve(lambda e: e.tensor_tensor(out=W(14), in0=W(12), in1=aim, op=ALU.mult), [r_w, r_pr], [r_w])
    dve(lambda e: e.tensor_tensor(out=W(11), in0=W(13), in1=W(14), op=ALU.subtract), [r_w], [r_w])
    dve(lambda e: e.tensor_tensor(out=W(11), in0=W(11), in1=W(9), op=ALU.mult), [r_w], [r_w])


def build_S5v2(NC, L=16):
    kb = KB()
    nc, p = kb.nc, kb.p
    uR, r_u = kb.din("uR", [6, 32, L, NC], BF16)
    prm, r_prm = kb.din("prm", [128, 3, 6])
    Bm, r_Bm = kb.din("Bm", [128, 6, 2, 32])
    Cm, r_Cm = kb.din("Cm", [128, 6, 2, 32])
    dsk, r_dsk = kb.din("dsk", [32, 6])
    idn, r_idn = kb.din("idn", [128, 128])
    yR, r_y = kb.dout("yR", [6, 32, L, NC])
    us, r_us = kb.sb([32, 6, L, NC], BF16)
    pr, r_pr = kb.sb([128, 3, 6], F32)
    Bs, r_Bs = kb.sb([128, 6, 2, 32], F32)
    Cs, r_Cs = kb.sb([128, 6, 3, 32], F32)
    ds, r_ds = kb.sb([32, 6], F32)
    ident, r_id = kb.sb([128, 128], F32)
    w_, r_w = kb.sb([128, 16, 6], F32)
    ki, r_sc = kb.sb([128, 8], mybir.dt.int32)
    kf, _ = kb.sb([128, 8], F32)
    tq, _ = kb.sb([128, 8], F32)
    bb, r_bb = kb.sb([128, 6, 2, 32], F32)
    pw, r_pw = kb.sb([128, L + 2, 2, 6], F32)
    NST = max(1, (NC - 1).bit_length())
    dq, r_dq = kb.sb([128, NST, 3, 6], F32)
    t6, r_t6 = kb.sb([128, 4, 6], F32)
    for dst, src, r, eng in ((us, uR.rearrange("m p r n -> p m r n"), r_us, "sp"), (pr, prm, r_pr, "act"), (Bs, Bm, r_Bs, "sp"),
                             (ds, dsk, r_ds, "sp"), (ident, idn, r_id, "sp")):
        p.add(eng, lambda e, dst=dst, src=src: e.dma_start(out=dst[:], in_=src), writes=[r], dma=True)
    p.add("act", lambda e: e.dma_start(out=Cs[:, :, 0:2, :], in_=Cm), writes=[r_Cs], dma=True)
    dve = lambda fn, rd, wr: p.add("dve", fn, reads=rd, writes=wr)
    act = lambda fn, rd, wr: p.add("act", fn, reads=rd, writes=wr)
    s5_prep(kb, pr, r_pr, w_, r_w, ki[:, 0:6], kf[:, 0:6], tq[:, 0:6], r_sc)
    act(lambda e: e.activation(out=Cs[:, :, 2, :], in_=Cs[:, :, 1, :], func=AF.Copy, scale=-1.0), [r_Cs], [r_Cs])
    abr, abi = w_[:, 7, :], w_[:, 8, :]

    def cmul(o_re, o_im, a_re, a_im, b_re, b_im, rd, wr):
        dve(lambda e: e.tensor_tensor(out=t6[:, 0, :], in0=a_re, in1=b_re, op=ALU.mult), rd, [r_t6])
        dve(lambda e: e.tensor_tensor(out=t6[:, 1, :], in0=a_im, in1=b_im, op=ALU.mult), rd, [r_t6])
        dve(lambda e: e.tensor_tensor(out=t6[:, 2, :], in0=a_re, in1=b_im, op=ALU.mult), rd, [r_t6])
        dve(lambda e: e.tensor_tensor(out=t6[:, 3, :], in0=a_im, in1=b_re, op=ALU.mult), rd, [r_t6])
        dve(lambda e: e.tensor_tensor(out=o_re, in0=t6[:, 0, :], in1=t6[:, 1, :], op=ALU.subtract), [r_t6], wr)
        dve(lambda e: e.tensor_tensor(out=o_im, in0=t6[:, 2, :], in1=t6[:, 3, :], op=ALU.add), [r_t6], wr)

    p.add("pool", lambda e: e.memset(pw[:, 0, 0, :], 1.0), writes=[r_pw])
    p.add("pool", lambda e: e.memset(pw[:, 0, 1, :], 0.0), writes=[r_pw])
    for j in range(1, L + 2):
        cmul(pw[:, j, 0, :], pw[:, j, 1, :], pw[:, j - 1, 0, :], pw[:, j - 1, 1, :], abr, abi, [r_pw, r_w], [r_pw])
    dve(lambda e: e.tensor_copy(out=dq[:, 0, 0:2, :], in_=pw[:, L, :, :]), [r_pw], [r_dq])
    for i in range(1, NST):
        cmul(dq[:, i, 0, :], dq[:, i, 1, :], dq[:, i - 1, 0, :], dq[:, i - 1, 1, :], dq[:, i - 1, 0, :], dq[:, i - 1, 1, :], [r_dq], [r_dq])
    dve(lambda e: e.tensor_scalar(out=dq[:, :, 2, :], in0=dq[:, :, 1, :], scalar1=-1.0, scalar2=None, op0=ALU.mult), [r_dq], [r_dq])
    for mt in range(6):
        fr, fi = w_[:, 10, mt:mt + 1], w_[:, 11, mt:mt + 1]
        dve(lambda e, mt=mt, fi=fi: e.tensor_scalar(out=bb[:, mt, 0, :], in0=Bs[:, mt, 1, :], scalar1=fi, scalar2=None, op0=ALU.mult), [r_Bs, r_w], [r_bb])
        dve(lambda e, mt=mt, fr=fr: e.scalar_tensor_tensor(out=bb[:, mt, 0, :], in0=Bs[:, mt, 0, :], scalar=fr, in1=bb[:, mt, 0, :], op0=ALU.mult, op1=ALU.subtract), [r_Bs, r_w, r_bb], [r_bb])
        dve(lambda e, mt=mt, fi=fi: e.tensor_scalar(out=bb[:, mt, 1, :], in0=Bs[:, mt, 0, :], scalar1=fi, scalar2=None, op0=ALU.mult), [r_Bs, r_w], [r_bb])
        dve(lambda e, mt=mt, fr=fr: e.scalar_tensor_tensor(out=bb[:, mt, 1, :], in0=Bs[:, mt, 1, :], scalar=fr, in1=bb[:, mt, 1, :], op0=ALU.mult, op1=ALU.add), [r_Bs, r_w, r_bb], [r_bb])
    BbS = [kb.sb([128, L, 2, 32], F32) for _ in range(2)]
    Win = [kb.sb([32, L, 2, 128], BF16) for _ in range(2)]
    CRI = [kb.sb([128, L, 2, 32], BF16) for _ in range(3)]
    Kt = [kb.sb([32, L, 32], BF16) for _ in range(3)]
    tsm, r_tsm = kb.sb([128, 2, 32], F32)
    XA = [kb.sb([128, 2, NC], F32) for _ in range(4)]
    Xps = [kb.sb([128, 2, NC], BF16) for _ in range(2)]
    ysb = [kb.sb([32, 512], F32) for _ in range(2)]
    yo = [kb.sb([32, NC], F32) for _ in range(3)]
    ptr = [kb.ps() for _ in range(2)]
    pz = [kb.ps() for _ in range(2)]
    pya = [kb.ps() for _ in range(2)]
    pyb = [kb.ps() for _ in range(2)]
    npc = -(-NC // 512)
    psz = -(-NC // npc)
    pieces = [(s0, min(psz, NC - s0)) for s0 in range(0, NC, psz)]
    cnt_box = [0]

    def setup(mt):
        bbs, r_bbs = BbS[mt % 2]
        win, r_win = Win[mt % 2]
        cri, r_cri = CRI[mt % 3]
        kt, r_kt = Kt[mt % 3]
        for j in range(L):
            pre_, pim_ = pw[:, j, 0, mt:mt + 1], pw[:, j, 1, mt:mt + 1]
            dve(lambda e, pim_=pim_, mt=mt: e.tensor_scalar(out=tsm[:, 0, :], in0=bb[:, mt, 1, :], scalar1=pim_, scalar2=None, op0=ALU.mult), [r_bb, r_pw], [r_tsm])
            dve(lambda e, pre_=pre_, mt=mt, j=j, bbs=bbs: e.scalar_tensor_tensor(out=bbs[:, j, 0, :], in0=bb[:, mt, 0, :], scalar=pre_, in1=tsm[:, 0, :], op0=ALU.mult, op1=ALU.subtract), [r_bb, r_pw, r_tsm], [r_bbs])
            dve(lambda e, pim_=pim_, mt=mt: e.tensor_scalar(out=tsm[:, 1, :], in0=bb[:, mt, 0, :], scalar1=pim_, scalar2=None, op0=ALU.mult), [r_bb, r_pw], [r_tsm])
            dve(lambda e, pre_=pre_, mt=mt, j=j, bbs=bbs: e.scalar_tensor_tensor(out=bbs[:, j, 1, :], in0=bb[:, mt, 1, :], scalar=pre_, in1=tsm[:, 1, :], op0=ALU.mult, op1=ALU.add), [r_bb, r_pw, r_tsm], [r_bbs])
        for g in range(L * 2 // 4):
            pt, r_pt = ptr[g % 2]
            for q in range(4):
                idx = g * 4 + q
                s_, ri = idx // 2, idx % 2
                p.add("pe", lambda e, s_=s_, ri=ri, q=q, pt=pt, bbs=bbs: e.transpose(pt[0:32, q * 128:(q + 1) * 128], bbs[:, L - 1 - s_, ri, :], ident[:, :]), reads=[r_bbs, r_id], writes=[r_pt])
            s0_ = (g * 4) // 2
            act(lambda e, s0_=s0_, pt=pt, win=win: e.activation(out=win[:, s0_:s0_ + 2, :, :], in_=pt[0:32, 0:512].rearrange("p (a b c) -> p a b c", a=2, b=2), func=AF.Copy), [r_pt], [r_win])
        pk, r_pk = ptr[0]
        for j in range(L):
            p.add("pe", lambda e, j=j, mt=mt, pk=pk, bbs=bbs: e.matmul(pk[0:32, j * 32:(j + 1) * 32], bbs[:, j, 0, :], Cs[:, mt, 0, :], start=True, stop=False), reads=[r_bbs, r_Cs], writes=[r_pk])
            p.add("pe", lambda e, j=j, mt=mt, pk=pk, bbs=bbs: e.matmul(pk[0:32, j * 32:(j + 1) * 32], bbs[:, j, 1, :], Cs[:, mt, 2, :], start=False, stop=True), reads=[r_bbs, r_Cs], writes=[r_pk])
        act(lambda e, pk=pk, kt=kt: e.activation(out=kt[:, :, :], in_=pk[0:32, 0:L * 32].rearrange("p (a b) -> p a b", a=L), func=AF.Copy), [r_pk], [r_kt])
        for r in range(L):
            pre_, pim_ = pw[:, r + 1, 0, mt:mt + 1], pw[:, r + 1, 1, mt:mt + 1]
            dve(lambda e, pim_=pim_, mt=mt: e.tensor_scalar(out=tsm[:, 0, :], in0=Cs[:, mt, 2, :], scalar1=pim_, scalar2=None, op0=ALU.mult), [r_Cs, r_pw], [r_tsm])
            dve(lambda e, pre_=pre_, mt=mt, r=r, cri=cri: e.scalar_tensor_tensor(out=cri[:, r, 0, :], in0=Cs[:, mt, 0, :], scalar=pre_, in1=tsm[:, 0, :], op0=ALU.mult, op1=ALU.add), [r_Cs, r_pw, r_tsm], [r_cri])
            dve(lambda e, pim_=pim_, mt=mt: e.tensor_scalar(out=tsm[:, 1, :], in0=Cs[:, mt, 0, :], scalar1=pim_, scalar2=-1.0, op0=ALU.mult, op1=ALU.mult), [r_Cs, r_pw], [r_tsm])
            dve(lambda e, pre_=pre_, mt=mt, r=r, cri=cri: e.scalar_tensor_tensor(out=cri[:, r, 1, :], in0=Cs[:, mt, 2, :], scalar=pre_, in1=tsm[:, 1, :], op0=ALU.mult, op1=ALU.add), [r_Cs, r_pw, r_tsm], [r_cri])

    def mainA(mt):
        bbs, r_bbs = BbS[mt % 2]
        win, r_win = Win[mt % 2]
        cri, r_cri = CRI[mt % 3]
        kt, r_kt = Kt[mt % 3]
        xa, r_xa = XA[2 * (mt % 2)]
        xb_, r_xb = XA[2 * (mt % 2) + 1]
        Xp, r_Xp = Xps[mt % 2]
        for (s0, n) in pieces:
            for ri in range(2):
                pzz, r_pz = pz[ri]
                for s_ in range(L):
                    p.add("pe", lambda e, s_=s_, ri=ri, s0=s0, n=n, pzz=pzz, win=win, mt=mt: e.matmul(pzz[:, 0:n], win[:, s_, ri, :], us[:, mt, s_, s0:s0 + n], start=(s_ == 0), stop=(s_ == L - 1)), reads=[r_win, r_us], writes=[r_pz])
                act(lambda e, ri=ri, s0=s0, n=n, pzz=pzz, xa=xa: e.activation(out=xa[:, ri, s0:s0 + n], in_=pzz[:, 0:n], func=AF.Copy), [r_pz], [r_xa])
        cur, r_cur, nxt, r_nxt = xa, r_xa, xb_, r_xb
        for i in range(NST):
            d = 1 << i
            if d >= NC:
                break
            dre, dim_, ndim = dq[:, i, 0, mt:mt + 1], dq[:, i, 1, mt:mt + 1], dq[:, i, 2, mt:mt + 1]
            act(lambda e, d=d, cur=cur, nxt=nxt: e.activation(out=nxt[:, :, 0:d], in_=cur[:, :, 0:d], func=AF.Copy), [r_cur], [r_nxt])
            dve(lambda e, d=d, cur=cur, nxt=nxt, dre=dre: e.scalar_tensor_tensor(out=nxt[:, 0, d:NC], in0=cur[:, 0, 0:NC - d], scalar=dre, in1=cur[:, 0, d:NC], op0=ALU.mult, op1=ALU.add), [r_cur, r_dq], [r_nxt])
            dve(lambda e, d=d, cur=cur, nxt=nxt, ndim=ndim: e.scalar_tensor_tensor(out=nxt[:, 0, d:NC], in0=cur[:, 1, 0:NC - d], scalar=ndim, in1=nxt[:, 0, d:NC], op0=ALU.mult, op1=ALU.add), [r_cur, r_dq, r_nxt], [r_nxt])
            dve(lambda e, d=d, cur=cur, nxt=nxt, dre=dre: e.scalar_tensor_tensor(out=nxt[:, 1, d:NC], in0=cur[:, 1, 0:NC - d], scalar=dre, in1=cur[:, 1, d:NC], op0=ALU.mult, op1=ALU.add), [r_cur, r_dq], [r_nxt])
            dve(lambda e, d=d, cur=cur, nxt=nxt, dim_=dim_: e.scalar_tensor_tensor(out=nxt[:, 1, d:NC], in0=cur[:, 0, 0:NC - d], scalar=dim_, in1=nxt[:, 1, d:NC], op0=ALU.mult, op1=ALU.add), [r_cur, r_dq, r_nxt], [r_nxt])
            cur, r_cur, nxt, r_nxt = nxt, r_nxt, cur, r_cur
        p.add("pool", lambda e: e.memset(Xp[:, :, 0:1], 0.0), writes=[r_Xp])
        act(lambda e, cur=cur: e.activation(out=Xp[:, :, 1:NC], in_=cur[:, :, 0:NC - 1], func=AF.Copy), [r_cur], [r_Xp])

    def mainB(mt):
        cri, r_cri = CRI[mt % 3]
        kt, r_kt = Kt[mt % 3]
        Xp, r_Xp = Xps[mt % 2]
        for r in range(L):
            yob, r_yo = yo[cnt_box[0] % 3]
            cnt_box[0] += 1
            for pi_, (s0, n) in enumerate(pieces):
                pa, r_pa = pya[(r + pi_) % 2]
                pb_, r_pb = pyb[(r + pi_) % 2]
                ysbb, r_ysb = ysb[(r + pi_) % 2]
                for ri in range(2):
                    p.add("pe", lambda e, r=r, ri=ri, s0=s0, n=n, pa=pa, cri=cri: e.matmul(pa[0:32, 0:n], cri[:, r, ri, :], Xp[:, ri, s0:s0 + n], start=(ri == 0), stop=(ri == 1)), reads=[r_cri, r_Xp], writes=[r_pa])
                for s_ in range(r + 1):
                    p.add("pe", lambda e, r=r, s_=s_, s0=s0, n=n, pb_=pb_, kt=kt, mt=mt: e.matmul(pb_[0:32, 0:n], kt[:, r - s_, :], us[:, mt, s_, s0:s0 + n], start=(s_ == 0), stop=(s_ == r)), reads=[r_kt, r_us], writes=[r_pb])
                act(lambda e, n=n, pa=pa, ysbb=ysbb: e.activation(out=ysbb[:, 0:n], in_=pa[0:32, 0:n], func=AF.Copy), [r_pa], [r_ysb])
                dve(lambda e, s0=s0, n=n, pb_=pb_, ysbb=ysbb, yob=yob: e.tensor_tensor(out=yob[:, s0:s0 + n], in0=pb_[0:32, 0:n], in1=ysbb[:, 0:n], op=ALU.add), [r_pb, r_ysb], [r_yo])
                dve(lambda e, s0=s0, n=n, r=r, mt=mt, yob=yob: e.scalar_tensor_tensor(out=yob[:, s0:s0 + n], in0=us[:, mt, r, s0:s0 + n], scalar=ds[:, mt:mt + 1], in1=yob[:, s0:s0 + n], op0=ALU.mult, op1=ALU.add), [r_us, r_ds, r_yo], [r_yo])
            p.add("sp", lambda e, mt=mt, r=r, yob=yob: e.dma_start(out=yR[mt, :, r, :], in_=yob[:, :]), reads=[r_yo], writes=[r_y], dma=True)

    def record(fn, *a):
        lst = []
        orig = p.add
        p.add = lambda *aa, **kk: lst.append((aa, kk))
        try:
            fn(*a)
        finally:
            p.add = orig
        return lst

    def merged(lists):
        lists = [l for l in lists if l]
        tot = max(len(l) for l in lists) if lists else 0
        pos = [0] * len(lists)
        for step in range(1, tot + 1):
            for li, l in enumerate(lists):
                tgt = (step * len(l) + tot - 1) // tot
                while pos[li] < tgt:
                    aa, kk = l[pos[li]]
                    p.add(*aa, **kk)
                    pos[li] += 1

    setup(0)
    setup(1)
    mainA(0)
    for mt in range(6):
        ls = []
        if mt + 1 < 6:
            ls.append(record(mainA, mt + 1))
        ls.append(record(mainB, mt))
        if mt + 2 < 6:
            ls.append(record(setup, mt + 2))
        merged(ls)
    return kb.finish([r_y])


def build_S5v3(NC, L=16):
    kb = KB()
    nc, p = kb.nc, kb.p
    G4 = L // 4
    uR, r_u = kb.din("uR4", [6, 128, G4, NC], BF16)
    prm, r_prm = kb.din("prm", [128, 3, 6])
    Bm, r_Bm = kb.din("Bm", [128, 6, 2, 32])
    Cm, r_Cm = kb.din("Cm", [128, 6, 2, 32])
    dsk, r_dsk = kb.din("dsk4", [128, 6])
    idq, r_idq = kb.din("idn4q", [128, 4, 32])
    idn, r_idn = kb.din("idn", [128, 128])
    yR, r_y = kb.dout("yR", [6, 32, L, NC])
    us, r_us = kb.sb([128, 6, G4, NC], BF16)
    pr, r_pr = kb.sb([128, 3, 6], F32)
    Bs, r_Bs = kb.sb([128, 6, 2, 32], F32)
    Cs, r_Cs = kb.sb([128, 6, 3, 32], F32)
    ds, r_ds = kb.sb([128, 6], F32)
    iq, r_iq = kb.sb([128, 4, 32], F32)
    ident, r_id = kb.sb([128, 128], F32)
    w_, r_w = kb.sb([128, 16, 6], F32)
    ki, r_sc = kb.sb([128, 8], mybir.dt.int32)
    kf, _ = kb.sb([128, 8], F32)
    tq, _ = kb.sb([128, 8], F32)
    bb, r_bb = kb.sb([128, 6, 2, 32], F32)
    pw, r_pw = kb.sb([128, L + 2, 2, 6], F32)
    NST = max(1, (NC - 1).bit_length())
    dq, r_dq = kb.sb([128, NST, 3, 6], F32)
    t6, r_t6 = kb.sb([128, 4, 6], F32)
    for dst, src, r, eng in ((us, uR.rearrange("m p g n -> p m g n"), r_us, "sp"), (iq, idq, r_iq, "act"), (pr, prm, r_pr, "act"), (Bs, Bm, r_Bs, "sp"),
                             (ds, dsk, r_ds, "sp"), (ident, idn, r_id, "sp")):
        p.add(eng, lambda e, dst=dst, src=src: e.dma_start(out=dst[:], in_=src), writes=[r], dma=True)
    p.add("act", lambda e: e.dma_start(out=Cs[:, :, 0:2, :], in_=Cm), writes=[r_Cs], dma=True)
    dve = lambda fn, rd, wr: p.add("dve", fn, reads=rd, writes=wr)
    act = lambda fn, rd, wr: p.add("act", fn, reads=rd, writes=wr)
    s5_prep(kb, pr, r_pr, w_, r_w, ki[:, 0:6], kf[:, 0:6], tq[:, 0:6], r_sc)
    act(lambda e: e.activation(out=Cs[:, :, 2, :], in_=Cs[:, :, 1, :], func=AF.Copy, scale=-1.0), [r_Cs], [r_Cs])
    abr, abi = w_[:, 7, :], w_[:, 8, :]

    def cmul(o_re, o_im, a_re, a_im, b_re, b_im, rd, wr):
        dve(lambda e: e.tensor_tensor(out=t6[:, 0, :], in0=a_re, in1=b_re, op=ALU.mult), rd, [r_t6])
        dve(lambda e: e.tensor_tensor(out=t6[:, 1, :], in0=a_im, in1=b_im, op=ALU.mult), rd, [r_t6])
        dve(lambda e: e.tensor_tensor(out=t6[:, 2, :], in0=a_re, in1=b_im, op=ALU.mult), rd, [r_t6])
        dve(lambda e: e.tensor_tensor(out=t6[:, 3, :], in0=a_im, in1=b_re, op=ALU.mult), rd, [r_t6])
        dve(lambda e: e.tensor_tensor(out=o_re, in0=t6[:, 0, :], in1=t6[:, 1, :], op=ALU.subtract), [r_t6], wr)
        dve(lambda e: e.tensor_tensor(out=o_im, in0=t6[:, 2, :], in1=t6[:, 3, :], op=ALU.add), [r_t6], wr)

    p.add("pool", lambda e: e.memset(pw[:, 0, 0, :], 1.0), writes=[r_pw])
    p.add("pool", lambda e: e.memset(pw[:, 0, 1, :], 0.0), writes=[r_pw])
    for j in range(1, L + 2):
        cmul(pw[:, j, 0, :], pw[:, j, 1, :], pw[:, j - 1, 0, :], pw[:, j - 1, 1, :], abr, abi, [r_pw, r_w], [r_pw])
    dve(lambda e: e.tensor_copy(out=dq[:, 0, 0:2, :], in_=pw[:, L, :, :]), [r_pw], [r_dq])
    for i in range(1, NST):
        cmul(dq[:, i, 0, :], dq[:, i, 1, :], dq[:, i - 1, 0, :], dq[:, i - 1, 1, :], dq[:, i - 1, 0, :], dq[:, i - 1, 1, :], [r_dq], [r_dq])
    dve(lambda e: e.tensor_scalar(out=dq[:, :, 2, :], in0=dq[:, :, 1, :], scalar1=-1.0, scalar2=None, op0=ALU.mult), [r_dq], [r_dq])
    for mt in range(6):
        fr, fi = w_[:, 10, mt:mt + 1], w_[:, 11, mt:mt + 1]
        dve(lambda e, mt=mt, fi=fi: e.tensor_scalar(out=bb[:, mt, 0, :], in0=Bs[:, mt, 1, :], scalar1=fi, scalar2=None, op0=ALU.mult), [r_Bs, r_w], [r_bb])
        dve(lambda e, mt=mt, fr=fr: e.scalar_tensor_tensor(out=bb[:, mt, 0, :], in0=Bs[:, mt, 0, :], scalar=fr, in1=bb[:, mt, 0, :], op0=ALU.mult, op1=ALU.subtract), [r_Bs, r_w, r_bb], [r_bb])
        dve(lambda e, mt=mt, fi=fi: e.tensor_scalar(out=bb[:, mt, 1, :], in0=Bs[:, mt, 0, :], scalar1=fi, scalar2=None, op0=ALU.mult), [r_Bs, r_w], [r_bb])
        dve(lambda e, mt=mt, fr=fr: e.scalar_tensor_tensor(out=bb[:, mt, 1, :], in0=Bs[:, mt, 1, :], scalar=fr, in1=bb[:, mt, 1, :], op0=ALU.mult, op1=ALU.add), [r_Bs, r_w, r_bb], [r_bb])
    base = []
    acc_ = 0
    for r in range(L):
        base.append(acc_)
        acc_ += r // 4 + 1
    NK = acc_
    NKB = -(-NK // 16)
    BbS = [kb.sb([128, 2, L + 4, 32], F32) for _ in range(2)]
    Win = [kb.sb([128, 2, G4, 128], BF16) for _ in range(2)]
    CRI = [kb.sb([128, L, 2, 32], BF16) for _ in range(3)]
    Kt = [kb.sb([128, NKB * 16, 32], BF16) for _ in range(3)]
    KSf, r_KSf = kb.sb([128, NKB * 16, 32], F32)
    dqt, r_dqt = kb.sb([128, 4, 32], F32)
    dqh, r_dqh = kb.sb([128, 4, 32], F32)
    Dhl = [kb.sb([128, 2, 4, 32], BF16) for _ in range(3)]
    tsm, r_tsm = kb.sb([128, 2, 32], F32)
    XA = [kb.sb([128, 2, NC], F32) for _ in range(4)]
    Xps = [kb.sb([128, 2, NC], BF16) for _ in range(2)]
    yo = [kb.sb([32, NC], F32) for _ in range(3)]
    ptr = [kb.ps() for _ in range(3)]
    pz = [kb.ps() for _ in range(2)]
    pyb = [kb.ps() for _ in range(3)]
    for bbs_, r_b_ in BbS:
        p.add("pool", lambda e, bbs_=bbs_: e.memset(bbs_[:, :, L:L + 4, :], 0.0), writes=[r_b_])
    npc = -(-NC // 512)
    psz = -(-NC // npc)
    pieces = [(s0, min(psz, NC - s0)) for s0 in range(0, NC, psz)]
    cnt_box = [0]

    def setup(mt):
        bbs, r_bbs = BbS[mt % 2]
        win, r_win = Win[mt % 2]
        cri, r_cri = CRI[mt % 3]
        kt, r_kt = Kt[mt % 3]
        for j in range(L):
            pre_, pim_ = pw[:, j, 0, mt:mt + 1], pw[:, j, 1, mt:mt + 1]
            dve(lambda e, pim_=pim_, mt=mt: e.tensor_scalar(out=tsm[:, 0, :], in0=bb[:, mt, 1, :], scalar1=pim_, scalar2=None, op0=ALU.mult), [r_bb, r_pw], [r_tsm])
            dve(lambda e, pre_=pre_, mt=mt, j=j, bbs=bbs: e.scalar_tensor_tensor(out=bbs[:, 0, L - 1 - j, :], in0=bb[:, mt, 0, :], scalar=pre_, in1=tsm[:, 0, :], op0=ALU.mult, op1=ALU.subtract), [r_bb, r_pw, r_tsm], [r_bbs])
            dve(lambda e, pim_=pim_, mt=mt: e.tensor_scalar(out=tsm[:, 1, :], in0=bb[:, mt, 0, :], scalar1=pim_, scalar2=None, op0=ALU.mult), [r_bb, r_pw], [r_tsm])
            dve(lambda e, pre_=pre_, mt=mt, j=j, bbs=bbs: e.scalar_tensor_tensor(out=bbs[:, 1, L - 1 - j, :], in0=bb[:, mt, 1, :], scalar=pre_, in1=tsm[:, 1, :], op0=ALU.mult, op1=ALU.add), [r_bb, r_pw, r_tsm], [r_bbs])
        for ri in range(2):
            pt, r_pt = ptr[ri]
            for g in range(G4):
                p.add("pe", lambda e, ri=ri, g=g, pt=pt, bbs=bbs: e.transpose(pt[:, g * 128:(g + 1) * 128], bbs[:, ri, 4 * g:4 * g + 4, :].rearrange("p a b -> p (a b)"), ident[:, :]), reads=[r_bbs, r_id], writes=[r_pt])
            act(lambda e, ri=ri, pt=pt, win=win: e.activation(out=win[:, ri, :, :], in_=pt[:, 0:G4 * 128].rearrange("p (a b) -> p a b", a=G4), func=AF.Copy), [r_pt], [r_win])
        dve(lambda e, mt=mt: e.tensor_scalar(out=dqt[:, :, :], in0=iq[:, :, :], scalar1=ds[:, mt:mt + 1], scalar2=None, op0=ALU.mult), [r_iq, r_ds], [r_dqt])
        for r in range(L):
            for g in range(r // 4 + 1):
                idx = base[r] + g
                pk, r_pk = ptr[idx // 16]
                sl = slice((idx % 16) * 32, (idx % 16) * 32 + 32)
                jj0 = L - 1 - r + 4 * g
                p.add("pe", lambda e, jj0=jj0, sl=sl, pk=pk, bbs=bbs, mt=mt: e.matmul(pk[:, sl], bbs[:, 0, jj0:jj0 + 4, :].rearrange("p a b -> p (a b)"), Cs[:, mt, 0, :], start=True, stop=False), reads=[r_bbs, r_Cs], writes=[r_pk])
                p.add("pe", lambda e, jj0=jj0, sl=sl, pk=pk, bbs=bbs, mt=mt: e.matmul(pk[:, sl], bbs[:, 1, jj0:jj0 + 4, :].rearrange("p a b -> p (a b)"), Cs[:, mt, 2, :], start=False, stop=True), reads=[r_bbs, r_Cs], writes=[r_pk])
        for bk in range(NKB):
            pk, r_pk = ptr[bk]
            nv = min(16, NK - bk * 16)
            act(lambda e, bk=bk, pk=pk, nv=nv: e.activation(out=KSf[:, bk * 16:bk * 16 + nv, :], in_=pk[:, 0:nv * 32].rearrange("p (a b) -> p a b", a=nv), func=AF.Copy), [r_pk], [r_KSf])
        dhl, r_dhl = Dhl[mt % 3]
        dve(lambda e, dhl=dhl: e.tensor_copy(out=dhl[:, 0, :, :], in_=dqt[:, :, :]), [r_dqt], [r_dhl])
        dve(lambda e, dhl=dhl: e.tensor_copy(out=dqh[:, :, :], in_=dhl[:, 0, :, :]), [r_dhl], [r_dqh])
        dve(lambda e, dhl=dhl: e.tensor_tensor(out=dhl[:, 1, :, :], in0=dqt[:, :, :], in1=dqh[:, :, :], op=ALU.subtract), [r_dqt, r_dqh], [r_dhl])
        act(lambda e, kt=kt: e.activation(out=kt[:, 0:NK, :], in_=KSf[:, 0:NK, :], func=AF.Copy), [r_KSf], [r_kt])
        for r in range(L):
            pre_, pim_ = pw[:, r + 1, 0, mt:mt + 1], pw[:, r + 1, 1, mt:mt + 1]
            dve(lambda e, pim_=pim_, mt=mt: e.tensor_scalar(out=tsm[:, 0, :], in0=Cs[:, mt, 2, :], scalar1=pim_, scalar2=None, op0=ALU.mult), [r_Cs, r_pw], [r_tsm])
            dve(lambda e, pre_=pre_, mt=mt, r=r, cri=cri: e.scalar_tensor_tensor(out=cri[:, r, 0, :], in0=Cs[:, mt, 0, :], scalar=pre_, in1=tsm[:, 0, :], op0=ALU.mult, op1=ALU.add), [r_Cs, r_pw, r_tsm], [r_cri])
            dve(lambda e, pim_=pim_, mt=mt: e.tensor_scalar(out=tsm[:, 1, :], in0=Cs[:, mt, 0, :], scalar1=pim_, scalar2=-1.0, op0=ALU.mult, op1=ALU.mult), [r_Cs, r_pw], [r_tsm])
            dve(lambda e, pre_=pre_, mt=mt, r=r, cri=cri: e.scalar_tensor_tensor(out=cri[:, r, 1, :], in0=Cs[:, mt, 2, :], scalar=pre_, in1=tsm[:, 1, :], op0=ALU.mult, op1=ALU.add), [r_Cs, r_pw, r_tsm], [r_cri])

    def mainA(mt):
        bbs, r_bbs = BbS[mt % 2]
        win, r_win = Win[mt % 2]
        cri, r_cri = CRI[mt % 3]
        kt, r_kt = Kt[mt % 3]
        xa, r_xa = XA[2 * (mt % 2)]
        xb_, r_xb = XA[2 * (mt % 2) + 1]
        Xp, r_Xp = Xps[mt % 2]
        for (s0, n) in pieces:
            for ri in range(2):
                pzz, r_pz = pz[ri]
                for g in range(G4):
                    p.add("pe", lambda e, g=g, ri=ri, s0=s0, n=n, pzz=pzz, win=win, mt=mt: e.matmul(pzz[:, 0:n], win[:, ri, g, :], us[:, mt, g, s0:s0 + n], start=(g == 0), stop=(g == G4 - 1)), reads=[r_win, r_us], writes=[r_pz])
                act(lambda e, ri=ri, s0=s0, n=n, pzz=pzz, xa=xa: e.activation(out=xa[:, ri, s0:s0 + n], in_=pzz[:, 0:n], func=AF.Copy), [r_pz], [r_xa])
        cur, r_cur, nxt, r_nxt = xa, r_xa, xb_, r_xb
        for i in range(NST):
            d = 1 << i
            if d >= NC:
                break
            dre, dim_, ndim = dq[:, i, 0, mt:mt + 1], dq[:, i, 1, mt:mt + 1], dq[:, i, 2, mt:mt + 1]
            act(lambda e, d=d, cur=cur, nxt=nxt: e.activation(out=nxt[:, :, 0:d], in_=cur[:, :, 0:d], func=AF.Copy), [r_cur], [r_nxt])
            dve(lambda e, d=d, cur=cur, nxt=nxt, dre=dre: e.scalar_tensor_tensor(out=nxt[:, 0, d:NC], in0=cur[:, 0, 0:NC - d], scalar=dre, in1=cur[:, 0, d:NC], op0=ALU.mult, op1=ALU.add), [r_cur, r_dq], [r_nxt])
            dve(lambda e, d=d, cur=cur, nxt=nxt, dre=dre: e.scalar_tensor_tensor(out=nxt[:, 1, d:NC], in0=cur[:, 1, 0:NC - d], scalar=dre, in1=cur[:, 1, d:NC], op0=ALU.mult, op1=ALU.add), [r_cur, r_dq], [r_nxt])
            dve(lambda e, d=d, cur=cur, nxt=nxt, ndim=ndim: e.scalar_tensor_tensor(out=nxt[:, 0, d:NC], in0=cur[:, 1, 0:NC - d], scalar=ndim, in1=nxt[:, 0, d:NC], op0=ALU.mult, op1=ALU.add), [r_cur, r_dq, r_nxt], [r_nxt])
            dve(lambda e, d=d, cur=cur, nxt=nxt, dim_=dim_: e.scalar_tensor_tensor(out=nxt[:, 1, d:NC], in0=cur[:, 0, 0:NC - d], scalar=dim_, in1=nxt[:, 1, d:NC], op0=ALU.mult, op1=ALU.add), [r_cur, r_dq, r_nxt], [r_nxt])
            cur, r_cur, nxt, r_nxt = nxt, r_nxt, cur, r_cur
        p.add("pool", lambda e: e.memset(Xp[:, :, 0:1], 0.0), writes=[r_Xp])
        act(lambda e, cur=cur: e.activation(out=Xp[:, :, 1:NC], in_=cur[:, :, 0:NC - 1], func=AF.Copy), [r_cur], [r_Xp])

    def mainB(mt):
        cri, r_cri = CRI[mt % 3]
        kt, r_kt = Kt[mt % 3]
        dhl, r_dhl = Dhl[mt % 3]
        Xp, r_Xp = Xps[mt % 2]
        for r in range(L):
            yob, r_yo = yo[cnt_box[0] % 3]
            for pi_, (s0, n) in enumerate(pieces):
                pb_, r_pb = pyb[(cnt_box[0] * len(pieces) + pi_) % 3]
                for ri in range(2):
                    p.add("pe", lambda e, r=r, ri=ri, s0=s0, n=n, pb_=pb_, cri=cri: e.matmul(pb_[0:32, 0:n], cri[:, r, ri, :], Xp[:, ri, s0:s0 + n], start=(ri == 0), stop=False), reads=[r_cri, r_Xp], writes=[r_pb])
                ng = r // 4 + 1
                for g in range(ng):
                    p.add("pe", lambda e, r=r, g=g, ng=ng, s0=s0, n=n, pb_=pb_, kt=kt, mt=mt: e.matmul(pb_[0:32, 0:n], kt[:, base[r] + g, :], us[:, mt, g, s0:s0 + n], start=False, stop=False), reads=[r_kt, r_us], writes=[r_pb])
                for hl in range(2):
                    p.add("pe", lambda e, r=r, hl=hl, s0=s0, n=n, pb_=pb_, dhl=dhl, mt=mt: e.matmul(pb_[0:32, 0:n], dhl[:, hl, r % 4, :], us[:, mt, r // 4, s0:s0 + n], start=False, stop=(hl == 1)), reads=[r_dhl, r_us], writes=[r_pb])
                act(lambda e, s0=s0, n=n, pb_=pb_, yob=yob: e.activation(out=yob[:, s0:s0 + n], in_=pb_[0:32, 0:n], func=AF.Copy), [r_pb], [r_yo])
            cnt_box[0] += 1
            p.add("sp", lambda e, mt=mt, r=r, yob=yob: e.dma_start(out=yR[mt, :, r, :], in_=yob[:, :]), reads=[r_yo], writes=[r_y], dma=True)

    def record(fn, *a):
        lst = []
        orig = p.add
        p.add = lambda *aa, **kk: lst.append((aa, kk))
        try:
            fn(*a)
        finally:
            p.add = orig
        return lst

    def merged(lists):
        lists = [l for l in lists if l]
        tot = max(len(l) for l in lists) if lists else 0
        pos = [0] * len(lists)
        for step in range(1, tot + 1):
            for li, l in enumerate(lists):
                tgt = (step * len(l) + tot - 1) // tot
                while pos[li] < tgt:
                    aa, kk = l[pos[li]]
                    p.add(*aa, **kk)
                    pos[li] += 1

    setup(0)
    setup(1)
    mainA(0)
    for mt in range(6):
        ls = []
        if mt + 1 < 6:
            ls.append(record(mainA, mt + 1))
        ls.append(record(mainB, mt))
        if mt + 2 < 6:
            ls.append(record(setup, mt + 2))
        merged(ls)
    return kb.finish([r_y])


def build_S5(N, NB=512):
    kb = KB()
    nc, p = kb.nc, kb.p
    NBLK = N // NB
    uT, r_u = kb.din("uT", [6, 32, N], BF16)
    prm, r_prm = kb.din("prm", [128, 3, 6])
    Bm, r_Bm = kb.din("Bm", [128, 6, 2, 32])
    Cm, r_Cm = kb.din("Cm", [128, 6, 2, 32])
    dsk, r_dsk = kb.din("dsk", [32, 6])
    tix, r_tix = kb.din("tix", [128, NB])
    idn, r_idn = kb.din("idn", [128, 128])
    yT, r_y = kb.dout("yT", [6, 32, N])
    us, r_us = kb.sb([32, 6, N], BF16)
    pr, r_pr = kb.sb([128, 3, 6], F32)
    Bs, r_Bs = kb.sb([128, 6, 2, 32], F32)
    Cs, r_Cs = kb.sb([128, 6, 2, 32], F32)
    Cb, r_Cb = kb.sb([128, 6, 3, 32], BF16)
    ds, r_ds = kb.sb([32, 6], F32)
    tx, r_tx = kb.sb([128, NB], F32)
    ident, r_id = kb.sb([128, 128], F32)
    w_, r_w = kb.sb([128, 16, 6], F32)
    bb, r_bb = kb.sb([128, 6, 2, 32], F32)
    BT, r_BT = kb.sb([32, 6, 2, 128], BF16)
    cosT, r_cos = kb.sb([128, 6, NB], F32)
    sinT, r_sin = kb.sb([128, 6, NB], F32)
    rtab, r_rt = kb.sb([128, 6, NB], F32)
    wre, r_wre = kb.sb([128, 6, NB], F32)
    wim, r_wim = kb.sb([128, 6, NB], F32)
    ini, r_ini = kb.sb([128, 2, 6], F32)
    tmpc, r_tmpc = kb.sb([128, 4, 6], F32)
    tt_, r_tt = kb.sb([128, 4, NB], F32)
    yo = [kb.sb([32, 6, NB], F32) for _ in range(2)]
    pbu = [kb.ps() for _ in range(4)]
    py = [kb.ps() for _ in range(2)]
    ptr, r_ptr = kb.ps()
    ki, r_sc = kb.sb([128, NB], mybir.dt.int32)
    kf, _ = kb.sb([128, NB], F32)
    tq, _ = kb.sb([128, NB], F32)
    for dst, src, r, eng in ((us, uT.rearrange("m p n -> p m n"), r_us, "sp"), (pr, prm, r_pr, "act"), (Bs, Bm, r_Bs, "sp"), (Cs, Cm, r_Cs, "act"),
                             (ds, dsk, r_ds, "sp"), (tx, tix, r_tx, "act"), (ident, idn, r_id, "sp")):
        p.add(eng, lambda e, dst=dst, src=src: e.dma_start(out=dst[:], in_=src), writes=[r], dma=True)
    W = lambda i: w_[:, i, :]
    are, aim, ldt = pr[:, 0, :], pr[:, 1, :], pr[:, 2, :]
    dve = lambda fn, rd, wr: p.add("dve", fn, reads=rd, writes=wr)
    act = lambda fn, rd, wr: p.add("act", fn, reads=rd, writes=wr)
    act(lambda e: e.activation(out=W(0), in_=ldt, func=AF.Exp), [r_pr], [r_w])
    dve(lambda e: e.tensor_tensor(out=W(1), in0=are, in1=W(0), op=ALU.mult), [r_pr, r_w], [r_w])
    act(lambda e: e.activation(out=W(1), in_=W(1), func=AF.Exp), [r_w], [r_w])
    dve(lambda e: e.tensor_tensor(out=W(2), in0=aim, in1=W(0), op=ALU.mult), [r_pr, r_w], [r_w])
    range_reduce(kb, W(3), r_w, W(2), r_w, 0.0, ki[:, 0:6], kf[:, 0:6], tq[:, 0:6], r_sc)
    range_reduce(kb, W(4), r_w, W(2), r_w, 0.5 * PI, ki[:, 0:6], kf[:, 0:6], tq[:, 0:6], r_sc)
    act(lambda e: e.activation(out=W(5), in_=W(3), func=AF.Sin), [r_w], [r_w])
    act(lambda e: e.activation(out=W(6), in_=W(4), func=AF.Sin), [r_w], [r_w])
    dve(lambda e: e.tensor_tensor(out=W(7), in0=W(1), in1=W(6), op=ALU.mult), [r_w], [r_w])
    dve(lambda e: e.tensor_tensor(out=W(8), in0=W(1), in1=W(5), op=ALU.mult), [r_w], [r_w])
    dve(lambda e: e.tensor_tensor(out=W(13), in0=are, in1=are, op=ALU.mult), [r_pr], [r_w])
    dve(lambda e: e.tensor_tensor(out=W(14), in0=aim, in1=aim, op=ALU.mult), [r_pr], [r_w])
    dve(lambda e: e.tensor_tensor(out=W(9), in0=W(13), in1=W(14), op=ALU.add), [r_w], [r_w])
    dve(lambda e: e.reciprocal(out=W(9), in_=W(9)), [r_w], [r_w])
    dve(lambda e: e.tensor_scalar(out=W(12), in0=W(7), scalar1=-1.0, scalar2=None, op0=ALU.add), [r_w], [r_w])
    dve(lambda e: e.tensor_tensor(out=W(13), in0=W(12), in1=are, op=ALU.mult), [r_w, r_pr], [r_w])
    dve(lambda e: e.tensor_tensor(out=W(14), in0=W(8), in1=aim, op=ALU.mult), [r_w, r_pr], [r_w])
    dve(lambda e: e.tensor_tensor(out=W(10), in0=W(13), in1=W(14), op=ALU.add), [r_w], [r_w])
    dve(lambda e: e.tensor_tensor(out=W(10), in0=W(10), in1=W(9), op=ALU.mult), [r_w], [r_w])
    dve(lambda e: e.tensor_tensor(out=W(13), in0=W(8), in1=are, op=ALU.mult), [r_w, r_pr], [r_w])
    dve(lambda e: e.tensor_tensor(out=W(14), in0=W(12), in1=aim, op=ALU.mult), [r_w, r_pr], [r_w])
    dve(lambda e: e.tensor_tensor(out=W(11), in0=W(13), in1=W(14), op=ALU.subtract), [r_w], [r_w])
    dve(lambda e: e.tensor_tensor(out=W(11), in0=W(11), in1=W(9), op=ALU.mult), [r_w], [r_w])
    dve(lambda e: e.tensor_scalar(out=W(15), in0=W(2), scalar1=float(NB), scalar2=None, op0=ALU.mult), [r_w], [r_w])
    range_reduce(kb, tmpc[:, 0, :], r_tmpc, W(15), r_w, 0.0, ki[:, 0:6], kf[:, 0:6], tq[:, 0:6], r_sc)
    range_reduce(kb, tmpc[:, 1, :], r_tmpc, W(15), r_w, 0.5 * PI, ki[:, 0:6], kf[:, 0:6], tq[:, 0:6], r_sc)
    act(lambda e: e.activation(out=tmpc[:, 2, :], in_=tmpc[:, 0, :], func=AF.Sin), [r_tmpc], [r_tmpc])
    act(lambda e: e.activation(out=tmpc[:, 3, :], in_=tmpc[:, 1, :], func=AF.Sin), [r_tmpc], [r_tmpc])
    act(lambda e: e.activation(out=Cb[:, :, 0, :], in_=Cs[:, :, 0, :], func=AF.Copy), [r_Cs], [r_Cb])
    act(lambda e: e.activation(out=Cb[:, :, 1, :], in_=Cs[:, :, 1, :], func=AF.Copy, scale=-1.0), [r_Cs], [r_Cb])
    act(lambda e: e.activation(out=Cb[:, :, 2, :], in_=Cs[:, :, 0, :], func=AF.Copy, scale=-1.0), [r_Cs], [r_Cb])
    for mt in range(6):
        fr, fi = w_[:, 10, mt:mt + 1], w_[:, 11, mt:mt + 1]
        dve(lambda e, mt=mt, fi=fi: e.tensor_scalar(out=bb[:, mt, 0, :], in0=Bs[:, mt, 1, :], scalar1=fi, scalar2=None, op0=ALU.mult), [r_Bs, r_w], [r_bb])
        dve(lambda e, mt=mt, fr=fr: e.scalar_tensor_tensor(out=bb[:, mt, 0, :], in0=Bs[:, mt, 0, :], scalar=fr, in1=bb[:, mt, 0, :], op0=ALU.mult, op1=ALU.subtract), [r_Bs, r_w, r_bb], [r_bb])
        dve(lambda e, mt=mt, fi=fi: e.tensor_scalar(out=bb[:, mt, 1, :], in0=Bs[:, mt, 0, :], scalar1=fi, scalar2=None, op0=ALU.mult), [r_Bs, r_w], [r_bb])
        dve(lambda e, mt=mt, fr=fr: e.scalar_tensor_tensor(out=bb[:, mt, 1, :], in0=Bs[:, mt, 1, :], scalar=fr, in1=bb[:, mt, 1, :], op0=ALU.mult, op1=ALU.add), [r_Bs, r_w, r_bb], [r_bb])
        for ri in range(2):
            p.add("pe", lambda e, mt=mt, ri=ri: e.transpose(ptr[0:32, 0:128], bb[:, mt, ri, :], ident[:, :]), reads=[r_bb, r_id], writes=[r_ptr])
            act(lambda e, mt=mt, ri=ri: e.activation(out=BT[:, mt, ri, :], in_=ptr[0:32, 0:128], func=AF.Copy), [r_ptr], [r_BT])
        ang = w_[:, 2, mt:mt + 1]
        dve(lambda e, ang=ang: e.tensor_scalar(out=tt_[:, 2, :], in0=tx[:, :], scalar1=ang, scalar2=None, op0=ALU.mult), [r_tx, r_w], [r_tt])
        range_reduce(kb, tt_[:, 0, :], r_tt, tt_[:, 2, :], r_tt, 0.0, ki[:, :], kf[:, :], tq[:, :], r_sc)
        range_reduce(kb, tt_[:, 1, :], r_tt, tt_[:, 2, :], r_tt, 0.5 * PI, ki[:, :], kf[:, :], tq[:, :], r_sc)
        act(lambda e, mt=mt: e.activation(out=sinT[:, mt, :], in_=tt_[:, 0, :], func=AF.Sin), [r_tt], [r_sin])
        act(lambda e, mt=mt: e.activation(out=cosT[:, mt, :], in_=tt_[:, 1, :], func=AF.Sin), [r_tt], [r_cos])
        act(lambda e, mt=mt: e.activation(out=rtab[:, mt, :], in_=tx[:, :], func=AF.Identity, scale=0.0, bias=w_[:, 1, mt:mt + 1]), [r_tx, r_w], [r_rt])
    p.add("pool", lambda e: e.memset(ini[:], 0.0), writes=[r_ini])
    pool = lambda fn, rd, wr: p.add(S5_POOL_ENG, fn, reads=rd, writes=wr)
    busb = [kb.sb([128, 2, NB], F32) for _ in range(2)]
    tD = [kb.sb([128, 2, NB], F32) for _ in range(2)]
    tP = [kb.sb([128, 2, NB], F32) for _ in range(2)]
    bpr = [kb.sb([128, NB], F32) for _ in range(2)]
    bpi = [kb.sb([128, NB], F32) for _ in range(2)]
    xq = [kb.sb([128, 4, NB], BF16) for _ in range(2)]
    r_wre_m = [p.res("wre%d" % i) for i in range(6)]
    r_wim_m = [p.res("wim%d" % i) for i in range(6)]
    r_yo_m = [[p.res("yo") for _ in range(6)] for _ in range(2)]

    def iter_ops(blk, mt, k):
        ops = []
        rec = lambda eng, fn, rd, wr: ops.append((eng, fn, rd, wr))
        bsl = slice(blk * NB, (blk + 1) * NB)
        yob, _ = yo[blk % 2]
        r_yo = r_yo_m[blk % 2][mt]
        pyb, r_py = py[k % 2]
        pre, r_pre = pbu[(2 * k) % 4]
        pim, r_pim = pbu[(2 * k + 1) % 4]
        bs_, r_bs_ = busb[k % 2]
        td, r_td = tD[k % 2]
        tp, r_tp = tP[k % 2]
        br_, r_br = bpr[k % 2]
        bi_, r_bi = bpi[k % 2]
        xx, r_xx = xq[k % 2]
        r_wre, r_wim = r_wre_m[mt], r_wim_m[mt]
        c_, s_ = cosT[:, mt, :], sinT[:, mt, :]
        rec("pe", lambda e: e.matmul(pre[:, 0:NB], BT[:, mt, 0, :], us[:, mt, bsl], start=True, stop=True), [r_BT, r_us], [r_pre])
        rec("pe", lambda e: e.matmul(pim[:, 0:NB], BT[:, mt, 1, :], us[:, mt, bsl], start=True, stop=True), [r_BT, r_us], [r_pim])
        rec("act", lambda e: e.activation(out=bs_[:, 0, :], in_=pre[:, 0:NB], func=AF.Copy), [r_pre], [r_bs_])
        rec("act", lambda e: e.activation(out=bs_[:, 1, :], in_=pim[:, 0:NB], func=AF.Copy), [r_pim], [r_bs_])
        rec("dve", lambda e: e.tensor_tensor(out=td[:, 0, :], in0=bs_[:, 0, :], in1=c_, op=ALU.mult), [r_bs_, r_cos], [r_td])
        rec("dve", lambda e: e.tensor_tensor(out=td[:, 1, :], in0=bs_[:, 1, :], in1=s_, op=ALU.mult), [r_bs_, r_sin], [r_td])
        rec("dve", lambda e: e.tensor_tensor(out=br_[:, :], in0=td[:, 0, :], in1=td[:, 1, :], op=ALU.add), [r_td], [r_br])
        rec(S5_POOL_ENG, lambda e: e.tensor_tensor(out=tp[:, 0, :], in0=bs_[:, 1, :], in1=c_, op=ALU.mult), [r_bs_, r_cos], [r_tp])
        rec(S5_POOL_ENG, lambda e: e.tensor_tensor(out=tp[:, 1, :], in0=bs_[:, 0, :], in1=s_, op=ALU.mult), [r_bs_, r_sin], [r_tp])
        rec(S5_POOL_ENG, lambda e: e.tensor_tensor(out=bi_[:, :], in0=tp[:, 0, :], in1=tp[:, 1, :], op=ALU.subtract), [r_tp], [r_bi])
        rec("dve", lambda e: e.tensor_tensor_scan(out=wre[:, mt, :], data0=rtab[:, mt, :], data1=br_[:, :], initial=ini[:, 0, mt:mt + 1], op0=ALU.mult, op1=ALU.add), [r_rt, r_br, r_ini], [r_wre])
        rec("dve", lambda e: e.tensor_tensor_scan(out=wim[:, mt, :], data0=rtab[:, mt, :], data1=bi_[:, :], initial=ini[:, 1, mt:mt + 1], op0=ALU.mult, op1=ALU.add), [r_rt, r_bi, r_ini], [r_wim])
        rec("dve", lambda e: e.tensor_tensor(out=xx[:, 0, :], in0=wre[:, mt, :], in1=c_, op=ALU.mult), [r_wre, r_cos], [r_xx])
        rec("dve", lambda e: e.tensor_tensor(out=xx[:, 1, :], in0=wim[:, mt, :], in1=s_, op=ALU.mult), [r_wim, r_sin], [r_xx])
        rec(S5_POOL_ENG, lambda e: e.tensor_tensor(out=xx[:, 2, :], in0=wim[:, mt, :], in1=c_, op=ALU.mult), [r_wim, r_cos], [r_xx])
        rec(S5_POOL_ENG, lambda e: e.tensor_tensor(out=xx[:, 3, :], in0=wre[:, mt, :], in1=s_, op=ALU.mult), [r_wre, r_sin], [r_xx])
        for qi, ci in ((0, 0), (1, 2), (2, 1), (3, 1)):
            rec("pe", lambda e, qi=qi, ci=ci: e.matmul(pyb[0:32, 0:NB], Cb[:, mt, ci, :], xx[:, qi, :], start=(qi == 0), stop=(qi == 3)), [r_Cb, r_xx], [r_py])
        rec("dve", lambda e: e.scalar_tensor_tensor(out=yob[:, mt, :], in0=us[:, mt, bsl], scalar=ds[:, mt:mt + 1], in1=pyb[0:32, 0:NB], op0=ALU.mult, op1=ALU.add), [r_us, r_ds, r_py], [r_yo])
        return ops

    k = 0
    r_wre_all = r_wre_m
    r_wim_all = r_wim_m
    for blk in range(NBLK):
        bsl = slice(blk * NB, (blk + 1) * NB)
        yob, _ = yo[blk % 2]
        for m0 in range(0, 6, 2):
            oa = iter_ops(blk, m0, k)
            ob_ = iter_ops(blk, m0 + 1, k + 1)
            k += 2
            for i in range(max(len(oa), len(ob_))):
                for lst in (oa, ob_):
                    if i < len(lst):
                        eng, fn, rd, wr = lst[i]
                        p.add(eng, fn, reads=rd, writes=wr)
        cN, sN = tmpc[:, 3, :], tmpc[:, 2, :]
        wlr, wli = wre[:, :, NB - 1], wim[:, :, NB - 1]
        dve(lambda e: e.tensor_tensor(out=tmpc[:, 0, :], in0=wlr, in1=cN, op=ALU.mult), r_wre_all + [r_tmpc], [r_tmpc])
        dve(lambda e: e.tensor_tensor(out=tmpc[:, 1, :], in0=wli, in1=sN, op=ALU.mult), r_wim_all + [r_tmpc], [r_tmpc])
        dve(lambda e: e.tensor_tensor(out=ini[:, 0, :], in0=tmpc[:, 0, :], in1=tmpc[:, 1, :], op=ALU.subtract), [r_tmpc], [r_ini])
        dve(lambda e: e.tensor_tensor(out=tmpc[:, 0, :], in0=wli, in1=cN, op=ALU.mult), r_wim_all + [r_tmpc], [r_tmpc])
        dve(lambda e: e.tensor_tensor(out=tmpc[:, 1, :], in0=wlr, in1=sN, op=ALU.mult), r_wre_all + [r_tmpc], [r_tmpc])
        dve(lambda e: e.tensor_tensor(out=ini[:, 1, :], in0=tmpc[:, 0, :], in1=tmpc[:, 1, :], op=ALU.add), [r_tmpc], [r_ini])
        p.add("sp", lambda e, bsl=bsl, yob=yob: e.dma_start(out=yT[:, :, bsl].rearrange("m p n -> p m n"), in_=yob[:, :, :]), reads=r_yo_m[blk % 2], writes=[r_y], dma=True)
    return kb.finish([r_y])


def build_L0():
    kb = KB()
    nc, p = kb.nc, kb.p
    adaw, r_aw = kb.din("adaw", [2, D, 768])
    adab, r_ab = kb.din("adab", [128, 2, 6])
    c3T, r_c = kb.din("c3T", [D, 3])
    modT, r_m = kb.dout("modT", [2, 768, 3])
    aw = [kb.sb([128, 16, 768], F32) for _ in range(2)]
    cs, r_cs = kb.sb([128, 16, 3], F32)
    sc, r_sc = kb.sb([128, 16, 3], F32)
    bs, r_bs = kb.sb([128, 2, 6], F32)
    ob, r_ob = kb.sb([128, 2, 6, 3], F32)
    pp = [kb.ps() for _ in range(2)]
    p.add("sp", lambda e: e.dma_start(out=cs[:], in_=c3T.rearrange("(a p) c -> p a c", p=128)), writes=[r_cs], dma=True)
    p.add("sp", lambda e: e.dma_start(out=bs[:], in_=adab), writes=[r_bs], dma=True)
    p.add("act", lambda e: e.activation(out=sc[:], in_=cs[:], func=AF.Silu), reads=[r_cs], writes=[r_sc])
    k = 0
    for l in range(2):
        awl, r_awl = aw[l]
        p.add("sp" if l == 0 else "act", lambda e, l=l, awl=awl: e.dma_start(out=awl[:], in_=adaw[l].rearrange("(a p) n -> p a n", p=128)), writes=[r_awl], dma=True)
        for j in range(6):
            pj, r_pj = pp[k % 2]
            k += 1
            for kt in range(16):
                p.add("pe", lambda e, kt=kt, j=j, awl=awl, pj=pj: e.matmul(pj[:, 0:3], awl[:, kt, j * 128:(j + 1) * 128], sc[:, kt, :], start=(kt == 0), stop=(kt == 15)), reads=[r_awl, r_sc], writes=[r_pj])
            p.add("act", lambda e, l=l, j=j, pj=pj: e.activation(out=ob[:, l, j, :], in_=pj[:, 0:3], func=AF.Identity, bias=bs[:, l, j:j + 1]), reads=[r_pj, r_bs], writes=[r_ob])
    p.add("sp", lambda e: e.dma_start(out=modT.rearrange("l (j p) c -> p l j c", p=128), in_=ob[:]), reads=[r_ob], writes=[r_m], dma=True)
    return kb.finish([r_m])


def build_LC1(NT=17):
    kb = KB()
    nc, p = kb.nc, kb.p
    T = NT * 128
    rf, r_rf = kb.din("rf", [T, 1024])
    rb, r_rb = kb.din("rb", [T, 1024])
    g, r_g = kb.din("g", [T, 1024], BF16)
    gnw, r_gn = kb.din("gnw", [128, 1024])
    ro, r_ro = kb.dout("r", [T, 1024], BF16)
    gw, r_gw = kb.sb([128, 1024], F32)
    p.add("sp", lambda e: e.dma_start(out=gw[:], in_=gnw), writes=[r_gw], dma=True)
    A = [kb.sb([128, 1024], F32) for _ in range(2)]
    B = [kb.sb([128, 1024], F32) for _ in range(2)]
    Gt = [kb.sb([128, 1024], BF16) for _ in range(2)]
    Ot = [kb.sb([128, 1024], BF16) for _ in range(2)]
    st, r_st = kb.sb([128, 4, 6], F32)
    mv, r_mv = kb.sb([128, 4, 2], F32)
    rs, r_rs = kb.sb([128, 4], F32)
    for tt in range(NT):
        b = tt % 2
        a_, r_a = A[b]; b_, r_b = B[b]; g_, r_g_ = Gt[b]; o_, r_o = Ot[b]
        tsl = slice(tt * 128, (tt + 1) * 128)
        p.add("sp", lambda e, a_=a_, tsl=tsl: e.dma_start(out=a_[:], in_=rf[tsl, :]), writes=[r_a], dma=True)
        p.add("act", lambda e, b_=b_, tsl=tsl: e.dma_start(out=b_[:], in_=rb[tsl, :]), writes=[r_b], dma=True)
        p.add("sp", lambda e, g_=g_, tsl=tsl: e.dma_start(out=g_[:], in_=g[tsl, :]), writes=[r_g_], dma=True)
        p.add("pool", lambda e, a_=a_, b_=b_: e.tensor_tensor(out=a_[:], in0=a_[:], in1=b_[:], op=ALU.add), reads=[r_a, r_b], writes=[r_a])
        for h in range(4):
            p.add("dve", lambda e, h=h, a_=a_: e.bn_stats(out=st[:, h, :], in_=a_[:, h * 256:(h + 1) * 256]), reads=[r_a], writes=[r_st])
            p.add("dve", lambda e, h=h: e.bn_aggr(out=mv[:, h, :], in_=st[:, h, :]), reads=[r_st], writes=[r_mv])
        p.add("act", lambda e: e.activation(out=rs[:, :], in_=mv[:, :, 1], func=AF.Sqrt, bias=EPS), reads=[r_mv], writes=[r_rs])
        p.add("dve", lambda e: e.reciprocal(out=rs[:, :], in_=rs[:, :]), reads=[r_rs], writes=[r_rs])
        for h in range(4):
            hs = slice(h * 256, (h + 1) * 256)
            p.add("dve", lambda e, h=h, hs=hs, a_=a_: e.tensor_scalar(out=a_[:, hs], in0=a_[:, hs], scalar1=mv[:, h, 0:1], scalar2=rs[:, h:h + 1], op0=ALU.subtract, op1=ALU.mult), reads=[r_a, r_mv, r_rs], writes=[r_a])
        p.add("pool", lambda e, a_=a_: e.tensor_tensor(out=a_[:], in0=a_[:], in1=gw[:], op=ALU.mult), reads=[r_a, r_gw], writes=[r_a])
        p.add("dve", lambda e, a_=a_, g_=g_, o_=o_: e.tensor_tensor(out=o_[:], in0=a_[:], in1=g_[:], op=ALU.mult), reads=[r_a, r_g_], writes=[r_o])
        p.add("sp", lambda e, o_=o_, tsl=tsl: e.dma_start(out=ro[tsl, :], in_=o_[:]), reads=[r_o], writes=[r_ro], dma=True)
    return kb.finish([r_ro])


def build_LC2a(NT=17, NB=256):
    kb = KB()
    nc, p = kb.nc, kb.p
    T = NT * 128
    rT, r_rT = kb.din("rT", [1024, T], BF16)
    aT, r_aT = kb.din("aT", [1024, T])
    agT, r_agT = kb.din("agT", [1024, T], BF16)
    yf, r_yf = kb.din("yf", [768, T])
    yb, r_yb = kb.din("yb", [768, T])
    sgT, r_sgT = kb.din("sgT", [768, T], BF16)
    mgT, r_mgT = kb.din("mgT", [6144, T], BF16)
    wr_, r_wr_ = kb.din("w_br_ret", [1024, D])
    wa_, r_wa_ = kb.din("w_br_att", [1024, D])
    ws_, r_ws_ = kb.din("w_br_s5", [768, D])
    gl_, r_gl_ = kb.din("glu_w", [768, 768])
    gb_, r_gb_ = kb.din("glu_b", [128, 6])
    mT, r_mT = kb.dout("mT", [D, T], BF16)
    wr, r_wr = kb.sb([128, 8, D], BF16)
    wa, r_wa = kb.sb([128, 8, D], BF16)
    ws, r_ws = kb.sb([128, 6, D], BF16)
    gl, r_gl = kb.sb([128, 6, 768], BF16)
    gb, r_gb = kb.sb([128, 6], F32)
    p.add("pool", lambda e: e.dma_start(out=wr[:], in_=wr_.rearrange("(a p) n -> p a n", p=128)), writes=[r_wr], dma=True)
    p.add("pool", lambda e: e.dma_start(out=wa[:], in_=wa_.rearrange("(a p) n -> p a n", p=128)), writes=[r_wa], dma=True)
    p.add("pool", lambda e: e.dma_start(out=ws[:], in_=ws_.rearrange("(a p) n -> p a n", p=128)), writes=[r_ws], dma=True)
    p.add("pool", lambda e: e.dma_start(out=gl[:], in_=gl_.rearrange("(a p) n -> p a n", p=128)), writes=[r_gl], dma=True)
    p.add("sp", lambda e: e.dma_start(out=gb[:], in_=gb_), writes=[r_gb], dma=True)
    rt, r_rt = kb.sb([128, 8, NB], BF16)
    at, r_at = kb.sb([128, 8, NB], F32)
    agt, r_agt = kb.sb([128, 8, NB], BF16)
    ap_, r_ap = kb.sb([128, 8, NB], BF16)
    yft, r_yft = kb.sb([128, 6, NB], F32)
    ybt, r_ybt = kb.sb([128, 6, NB], F32)
    sgt, r_sgt = kb.sb([128, 6, NB], BF16)
    t1, r_t1 = kb.sb([128, 6, NB], F32)
    s1, r_s1 = kb.sb([128, 6, NB], BF16)
    s2, r_s2 = kb.sb([128, 6, NB], BF16)
    zs, r_zs = kb.sb([128, NB], F32)
    mg = [kb.sb([128, 3, NB], BF16) for _ in range(2)]
    m1, r_m1 = kb.sb([128, 3, NB], F32)
    mo = [kb.sb([128, NB], BF16) for _ in range(2)]
    pz = [kb.ps() for _ in range(2)]
    pb = [kb.ps() for _ in range(6)]
    blocks = [(s, min(NB, T - s)) for s in range(0, T, NB)]
    k = 0
    for (s0, n) in blocks:
        bs = slice(s0, s0 + n)
        for dst, src, r, eng, kk in ((rt, rT, r_rt, "sp", 8), (at, aT, r_at, "act", 8), (agt, agT, r_agt, "sp", 8), (yft, yf, r_yft, "act", 6), (ybt, yb, r_ybt, "sp", 6), (sgt, sgT, r_sgt, "act", 6)):
            p.add(eng, lambda e, dst=dst, src=src, bs=bs, n=n: e.dma_start(out=dst[:, :, 0:n], in_=src[:, bs].rearrange("(a p) n -> p a n", p=128)), writes=[r], dma=True)
        p.add("dve", lambda e, n=n: e.tensor_tensor(out=ap_[:, :, 0:n], in0=at[:, :, 0:n], in1=agt[:, :, 0:n], op=ALU.mult), reads=[r_at, r_agt], writes=[r_ap])
        p.add("pool", lambda e, n=n: e.tensor_tensor(out=yft[:, :, 0:n], in0=yft[:, :, 0:n], in1=ybt[:, :, 0:n], op=ALU.add), reads=[r_yft, r_ybt], writes=[r_yft])
        p.add("act", lambda e, n=n: e.activation(out=t1[:, :, 0:n], in_=yft[:, :, 0:n], func=AF.Square), reads=[r_yft], writes=[r_t1])
        p.add("dve", lambda e, n=n: e.tensor_scalar(out=t1[:, :, 0:n], in0=t1[:, :, 0:n], scalar1=0.044715, scalar2=1.0, op0=ALU.mult, op1=ALU.add), reads=[r_t1], writes=[r_t1])
        p.add("dve", lambda e, n=n: e.tensor_tensor(out=t1[:, :, 0:n], in0=t1[:, :, 0:n], in1=yft[:, :, 0:n], op=ALU.mult), reads=[r_t1, r_yft], writes=[r_t1])
        p.add("act", lambda e, n=n: e.activation(out=t1[:, :, 0:n], in_=t1[:, :, 0:n], func=AF.Sigmoid, scale=1.5957691216057308), reads=[r_t1], writes=[r_t1])
        p.add("dve", lambda e, n=n: e.tensor_tensor(out=s1[:, :, 0:n], in0=t1[:, :, 0:n], in1=yft[:, :, 0:n], op=ALU.mult), reads=[r_t1, r_yft], writes=[r_s1])
        for j in range(6):
            pzz, r_pz = pz[j % 2]
            for kt in range(6):
                p.add("pe", lambda e, j=j, kt=kt, n=n, pzz=pzz: e.matmul(pzz[:, 0:n], gl[:, kt, j * 128:(j + 1) * 128], s1[:, kt, 0:n], start=(kt == 0), stop=(kt == 5)), reads=[r_gl, r_s1], writes=[r_pz])
            p.add("act", lambda e, j=j, n=n, pzz=pzz: e.activation(out=zs[:, 0:n], in_=pzz[:, 0:n], func=AF.Sigmoid, bias=gb[:, j:j + 1]), reads=[r_pz, r_gb], writes=[r_zs])
            p.add("dve", lambda e, j=j, n=n: e.tensor_tensor(out=zs[:, 0:n], in0=zs[:, 0:n], in1=s1[:, j, 0:n], op=ALU.mult), reads=[r_zs, r_s1], writes=[r_zs])
            p.add("dve", lambda e, j=j, n=n: e.tensor_tensor(out=s2[:, j, 0:n], in0=zs[:, 0:n], in1=sgt[:, j, 0:n], op=ALU.mult), reads=[r_zs, r_sgt], writes=[r_s2])
        for f in range(16):
            mgb, r_mg = mg[k % 2]
            mob, r_mo = mo[k % 2]
            pp = [pb[(3 * k + i) % 6] for i in range(3)]
            k += 1
            fs = slice(f * 128, (f + 1) * 128)
            p.add("sp", lambda e, f=f, bs=bs, n=n, mgb=mgb: e.dma_start(out=mgb[:, :, 0:n], in_=mgT[:, bs].rearrange("(b f p) n -> f p b n", b=3, p=128)[f]), writes=[r_mg], dma=True)
            for (w, r_w, src, r_src, nk, (pt, r_pt)) in ((wr, r_wr, rt, r_rt, 8, pp[0]), (wa, r_wa, ap_, r_ap, 8, pp[1]), (ws, r_ws, s2, r_s2, 6, pp[2])):
                for kt in range(nk):
                    p.add("pe", lambda e, w=w, src=src, kt=kt, nk=nk, fs=fs, n=n, pt=pt: e.matmul(pt[:, 0:n], w[:, kt, fs], src[:, kt, 0:n], start=(kt == 0), stop=(kt == nk - 1)), reads=[r_w, r_src], writes=[r_pt])
            for i in range(3):
                pt, r_pt = pp[i]
                p.add("dve", lambda e, i=i, n=n, pt=pt, mgb=mgb: e.tensor_tensor(out=m1[:, i, 0:n], in0=pt[:, 0:n], in1=mgb[:, i, 0:n], op=ALU.mult), reads=[r_pt, r_mg], writes=[r_m1])
            p.add("pool", lambda e, n=n: e.tensor_tensor(out=m1[:, 0, 0:n], in0=m1[:, 0, 0:n], in1=m1[:, 1, 0:n], op=ALU.add), reads=[r_m1], writes=[r_m1])
            p.add("pool", lambda e, n=n, mob=mob: e.tensor_tensor(out=mob[:, 0:n], in0=m1[:, 0, 0:n], in1=m1[:, 2, 0:n], op=ALU.add), reads=[r_m1], writes=[r_mo])
            p.add("act", lambda e, fs=fs, bs=bs, n=n, mob=mob: e.dma_start(out=mT[fs, bs], in_=mob[:, 0:n]), reads=[r_mo], writes=[r_mT], dma=True)
    return kb.finish([r_mT])


ALPHA = (2.0 * 2) ** 0.25


def build_LC2b(NT=17):
    kb = KB()
    nc, p = kb.nc, kb.p
    T = NT * 128
    mT, r_mT = kb.din("mT", [D, T], BF16)
    wo_, r_wo_ = kb.din("w_out", [D, D])
    x, r_x = kb.din("x", [T, D])
    rows, r_rows = kb.din("rows", [128, 4, D])
    xo, r_xo = kb.dout("xo", [T, D])
    wo, r_wo = kb.sb([128, 16, D], BF16)
    rw, r_rw = kb.sb([128, 4, D], F32)
    p.add("pool", lambda e: e.dma_start(out=wo[:], in_=wo_.rearrange("(a p) n -> p a n", p=128)), writes=[r_wo], dma=True)
    p.add("sp", lambda e: e.dma_start(out=rw[:], in_=rows), writes=[r_rw], dma=True)
    mt = [kb.sb([128, 16, 128], BF16) for _ in range(2)]
    xt = [kb.sb([128, D], F32) for _ in range(2)]
    z = [kb.sb([128, D], F32) for _ in range(2)]
    pb = [kb.ps() for _ in range(8)]
    st, r_st = kb.sb([128, 4, 6], F32)
    mv, r_mv = kb.sb([128, 2], F32)
    rs, r_rs = kb.sb([128, 1], F32)
    for tt in range(NT):
        b = tt % 2
        mtb, r_mt = mt[b]; xtb, r_xt = xt[b]; zb, r_z = z[b]
        tsl = slice(tt * 128, (tt + 1) * 128)
        gi = 0 if tt < NT - 1 else 1
        p.add("sp", lambda e, mtb=mtb, tsl=tsl: e.dma_start(out=mtb[:], in_=mT[:, tsl].rearrange("(a p) n -> p a n", p=128)), writes=[r_mt], dma=True)
        p.add("act", lambda e, xtb=xtb, tsl=tsl: e.dma_start(out=xtb[:], in_=x[tsl, :]), writes=[r_xt], dma=True)
        for cb in range(4):
            pt, r_pt = pb[(tt * 4 + cb) % 8]
            cs = slice(cb * 512, (cb + 1) * 512)
            for kt in range(16):
                p.add("pe", lambda e, kt=kt, cs=cs, pt=pt, mtb=mtb: e.matmul(pt[:, :], mtb[:, kt, :], wo[:, kt, cs], start=(kt == 0), stop=(kt == 15)), reads=[r_mt, r_wo], writes=[r_pt])
            p.add("dve", lambda e, cs=cs, pt=pt, zb=zb, gi=gi: e.tensor_tensor(out=zb[:, cs], in0=pt[:, :], in1=rw[:, gi, cs], op=ALU.mult), reads=[r_pt, r_rw], writes=[r_z])
            p.add("dve", lambda e, cs=cs, zb=zb, xtb=xtb: e.scalar_tensor_tensor(out=zb[:, cs], in0=xtb[:, cs], scalar=ALPHA, in1=zb[:, cs], op0=ALU.mult, op1=ALU.add), reads=[r_xt, r_z], writes=[r_z])
            p.add("dve", lambda e, cb=cb, cs=cs, zb=zb: e.bn_stats(out=st[:, cb, :], in_=zb[:, cs]), reads=[r_z], writes=[r_st])
        p.add("dve", lambda e: e.bn_aggr(out=mv[:, :], in_=st[:, :, :].rearrange("p a b -> p (a b)")), reads=[r_st], writes=[r_mv])
        p.add("act", lambda e: e.activation(out=rs[:, :], in_=mv[:, 1:2], func=AF.Sqrt, bias=EPS), reads=[r_mv], writes=[r_rs])
        p.add("dve", lambda e: e.reciprocal(out=rs[:, :], in_=rs[:, :]), reads=[r_rs], writes=[r_rs])
        p.add("dve", lambda e, zb=zb: e.tensor_scalar(out=zb[:, :], in0=zb[:, :], scalar1=mv[:, 0:1], scalar2=rs[:, 0:1], op0=ALU.subtract, op1=ALU.mult), reads=[r_z, r_mv, r_rs], writes=[r_z])
        p.add("pool", lambda e, zb=zb: e.tensor_tensor(out=zb[:, :], in0=zb[:, :], in1=rw[:, 2, :], op=ALU.mult), reads=[r_z, r_rw], writes=[r_z])
        p.add("pool", lambda e, zb=zb: e.tensor_tensor(out=zb[:, :], in0=zb[:, :], in1=rw[:, 3, :], op=ALU.add), reads=[r_z, r_rw], writes=[r_z])
        p.add("sp", lambda e, zb=zb, tsl=tsl: e.dma_start(out=xo[tsl, :], in_=zb[:, :]), reads=[r_z], writes=[r_xo], dma=True)
    return kb.finish([r_xo])


import ml_dtypes
_BF = ml_dtypes.bfloat16
_PROGS = {}


def _prog(key, fn, *a):
    if key not in _PROGS:
        _PROGS[key] = fn(*a)
    return _PROGS[key]


def _run(nc, maps):
    res = run_bass_kernel_spmd(nc, maps, core_ids=list(range(8)))
    return res.results


def _rope_tables(hd):
    per = hd // 4
    n = np.arange(NLAT)
    row = (n // 64).astype(np.float32)
    col = (n % 64).astype(np.float32)
    inv = (np.float32(10000.0) ** (-np.arange(per, dtype=np.float32) / np.float32(per))).astype(np.float32)
    ang = np.concatenate([row[:, None] * inv, col[:, None] * inv], axis=-1).astype(np.float32)
    return np.cos(ang).astype(np.float32), np.sin(ang).astype(np.float32)


def _ret_cst(lg, diag):
    idx = np.arange(128, dtype=np.float32)
    rel = idx[None, :] - idx[:, None]
    mask = (rel >= 0) if diag else (rel > 0)
    cst = np.zeros((128, 260), np.float32)
    cst[:, 0:128] = np.where(mask, rel, 0)
    cst[:, 128:256] = mask
    cst[:, 256] = lg
    cst[:, 257] = idx + 1
    cst[:, 258] = 127 - idx
    cst[:, 259] = 128
    return cst


def _s5_inputs(u, a_re, a_im, log_dt, b_re, b_im, c_re, c_im, dsk, NB):
    N = u.shape[0]
    prm = np.zeros((128, 3, 6), np.float32)
    Bm = np.zeros((128, 6, 2, 32), np.float32)
    Cm = np.zeros((128, 6, 2, 32), np.float32)
    dk = np.zeros((32, 6), np.float32)
    for mt in range(6):
        for gp in range(2):
            g = 2 * mt + gp
            ps = slice(gp * 64, gp * 64 + 64)
            cs = slice(gp * 16, gp * 16 + 16)
            prm[ps, 0, mt] = a_re[g]
            prm[ps, 1, mt] = a_im[g]
            prm[ps, 2, mt] = log_dt[g]
            Bm[ps, mt, 0, cs] = b_re[g]
            Bm[ps, mt, 1, cs] = b_im[g]
            Cm[ps, mt, 0, cs] = c_re[g].T
            Cm[ps, mt, 1, cs] = c_im[g].T
            dk[cs, mt] = dsk[g]
    uR4 = np.ascontiguousarray(u.reshape(N // 16, 4, 4, 6, 32).transpose(3, 2, 4, 1, 0).reshape(6, 128, 4, N // 16))
    idn4q = np.zeros((128, 4, 32), np.float32)
    for q in range(4):
        idn4q[q * 32:(q + 1) * 32, q, :] = np.eye(32, dtype=np.float32)
    return dict(uR4=uR4, prm=prm, Bm=Bm, Cm=Cm, dsk4=np.ascontiguousarray(np.tile(dk, (4, 1))), idn4q=idn4q, idn=np.eye(128, dtype=np.float32))


def kernel(x, c, ctx, c_ctx, ada_w, ada_b, w_in, ret_log_decay, ret_gn_w, att_q_norm, att_k_norm,
           s5_a_re, s5_a_im, s5_log_dt, s5_b_re, s5_b_im, s5_c_re, s5_c_im, s5_d, s5_glu_w, s5_glu_b,
           w_br_ret, w_br_att, w_br_s5, w_out, ln_w, ln_b):
    A = lambda v: np.ascontiguousarray(np.asarray(v))
    x = A(x); ctx = A(ctx)
    NS = NCTX + NLAT
    C = np.ascontiguousarray
    c3T = C(np.stack([A(c)[0], A(c)[1], A(c_ctx)], axis=1).astype(np.float32))
    maps = []
    for k in range(8):
        cs = slice(k * 768, (k + 1) * 768)
        maps.append(dict(adaw=C(A(ada_w)[:, :, cs]), adab=C(A(ada_b)[:, cs].reshape(2, 6, 128).transpose(2, 0, 1)), c3T=c3T))
    r0 = _run(_prog("L0", build_L0), maps)
    mod = np.concatenate([r["modT"] for r in r0], axis=1)
    cosR, sinR = _rope_tables(256)
    cosA, sinA = _rope_tables(128)
    xl = x.reshape(2 * NLAT, D)
    h = ctx.reshape(2 * NCTX, D)
    for l in range(2):
        shift, scale, gate = mod[l, 0:D], mod[l, D:2 * D], mod[l, 2 * D:3 * D]
        maps = []
        for k in range(8):
            b = k // 4
            ct = h[k * 128:(k + 1) * 128] if k < 4 else np.zeros((128, D), np.float32)
            xT = C(np.concatenate([xl[k * 2048:(k + 1) * 2048], ct], axis=0).T)
            modv = C(np.stack([shift[:, b], scale[:, b], shift[:, 2], scale[:, 2]], axis=1))
            ropeR = np.zeros((128, 17, 2, 128), np.float32)
            ropeA = np.zeros((128, 17, 2, 64), np.float32)
            n0 = (k % 4) * 2048
            ropeR[:, :16, 0] = cosR[n0:n0 + 2048].reshape(16, 128, 128).transpose(1, 0, 2)
            ropeR[:, :16, 1] = sinR[n0:n0 + 2048].reshape(16, 128, 128).transpose(1, 0, 2)
            ropeA[:, :16, 0] = cosA[n0:n0 + 2048].reshape(16, 128, 64).transpose(1, 0, 2)
            ropeA[:, :16, 1] = sinA[n0:n0 + 2048].reshape(16, 128, 64).transpose(1, 0, 2)
            ropeR[:, 16, 0] = 1.0
            ropeA[:, 16, 0] = 1.0
            qkw = C(np.broadcast_to(np.stack([A(att_q_norm)[l], A(att_k_norm)[l]])[None], (128, 2, 128)).astype(np.float32))
            maps.append(dict(xT=xT, modv=modv, w_in=A(w_in)[l], ropeR=ropeR, ropeA=ropeA, qkw=qkw))
        rA = _run(_prog("LA", build_LA), maps)
        O = [r["O"] for r in rA]
        OL = np.concatenate([o[:2048] for o in O], axis=0).reshape(2, NLAT, INW)
        OC = np.concatenate([o[2048:] for o in O[:4]], axis=0).reshape(2, NCTX, INW)
        seq = np.concatenate([OC, OL], axis=1)
        seqb = np.concatenate([OC[:, ::-1], OL[:, ::-1]], axis=1)
        sg = lambda S_, name, w: S_[:, :, SEG[name]:SEG[name] + w]
        ret = []
        for dr, S_ in ((0, seq), (1, seqb)):
            maps = []
            for k in range(8):
                b, j = k // 4, k % 4
                hs = slice(j * 256, (j + 1) * 256)
                q = sg(S_, "ret_q", 1024)[b][:, hs]; kk = sg(S_, "ret_k", 1024)[b][:, hs]; vv = sg(S_, "ret_v", 1024)[b][:, hs]
                maps.append(dict(qT=C(q.T), kT=C(kk.T), k=C(kk), v=C(vv), cst=_ret_cst(np.float32(A(ret_log_decay)[l, dr, j]), dr == 0)))
            rr = _run(_prog("RET", build_RET, NS), maps)
            o = np.stack([r["ro"] for r in rr]).reshape(2, 4, NS, 256).transpose(0, 2, 1, 3).reshape(2, NS, 1024)
            if dr == 1:
                o = np.concatenate([o[:, :NCTX][:, ::-1], o[:, NCTX:][:, ::-1]], axis=1)
            ret.append(o)
        maps = []
        for k in range(8):
            b, j = k // 4, k % 4
            q = sg(OL, "att_q", 1024)[b][:, j * 256:(j + 1) * 256].reshape(NLAT, 2, 128)
            kv = j // 2
            kk = sg(seq, "att_k", 256)[b][:, kv * 128:(kv + 1) * 128]
            vv = sg(seq, "att_v", 256)[b][:, kv * 128:(kv + 1) * 128]
            maps.append(dict(qT=C(q.transpose(1, 2, 0)), kT=C(kk.T), v=C(vv)))
        ra = _run(_prog("ATT", build_ATT, NLAT, NS), maps)
        aTl = np.stack([r["aT"] for r in ra]).reshape(2, 1024, NLAT)
        aTc = np.zeros((2, 1024, NCTX), np.float32)
        if l == 0:
            maps = []
            for k in range(8):
                b, j = k // 4, k % 4
                q = sg(OC, "att_q", 1024)[b][:, j * 256:(j + 1) * 256].reshape(NCTX, 2, 128)
                kv = j // 2
                kk = sg(OC, "att_k", 256)[b][:, kv * 128:(kv + 1) * 128]
                vv = sg(OC, "att_v", 256)[b][:, kv * 128:(kv + 1) * 128]
                maps.append(dict(qT=C(q.transpose(1, 2, 0)), kT=C(kk.T), v=C(vv)))
            rc = _run(_prog("ATTC", build_ATT, NCTX, NCTX, 256), maps)
            aTc = np.stack([r["aT"] for r in rc]).reshape(2, 1024, NCTX)
        ys = []
        for dr, S_ in ((0, seq), (1, seqb)):
            maps = []
            for k in range(8):
                b, j = k // 4, k % 4
                gs = slice(12 * j, 12 * j + 12)
                u = sg(S_, "s5_u", 768)[b][:, j * 192:(j + 1) * 192]
                dk = A(s5_d)[l].reshape(48, 16)[gs] if dr == 0 else np.zeros((12, 16), np.float32)
                maps.append(_s5_inputs(u, A(s5_a_re)[l, dr, gs], A(s5_a_im)[l, dr, gs], A(s5_log_dt)[l, dr, gs], A(s5_b_re)[l, dr, gs], A(s5_b_im)[l, dr, gs],
                                       A(s5_c_re)[l, dr, gs], A(s5_c_im)[l, dr, gs], dk, 256))
            rs_ = _run(_prog("S5v3", build_S5v3, NS // 16), maps)
            y = np.stack([r["yR"].transpose(0, 1, 3, 2).reshape(192, NS) for r in rs_]).reshape(2, 768, NS)
            if dr == 1:
                y = np.concatenate([y[:, :, :NCTX][:, :, ::-1], y[:, :, NCTX:][:, :, ::-1]], axis=2)
            ys.append(y)
        def tok(arr_lat, arr_ctx, k):
            W = arr_lat.shape[-1]
            ct = arr_ctx.reshape(2 * NCTX, W)[k * 128:(k + 1) * 128] if k < 4 else np.zeros((128, W), arr_lat.dtype)
            return np.concatenate([arr_lat.reshape(2 * NLAT, W)[k * 2048:(k + 1) * 2048], ct], axis=0)

        def tokT(arr_lat, arr_ctx, k):
            b = k // 4
            n0 = (k % 4) * 2048
            W = arr_lat.shape[1]
            if k < 4:
                bc, c0 = k // 2, (k % 2) * 128
                ct = arr_ctx[bc][:, c0:c0 + 128]
            else:
                ct = np.zeros((W, 128), arr_lat.dtype)
            return np.concatenate([arr_lat[b][:, n0:n0 + 2048], ct], axis=1)
        gnw = C(np.broadcast_to(A(ret_gn_w)[l][None], (128, 1024)).astype(np.float32))
        maps = [dict(rf=C(tok(ret[0][:, NCTX:], ret[0][:, :NCTX], k)), rb=C(tok(ret[1][:, NCTX:], ret[1][:, :NCTX], k)), g=C(O[k][:, SEG["ret_g"]:SEG["ret_g"] + 1024]), gnw=gnw) for k in range(8)]
        r1 = _run(_prog("LC1", build_LC1), maps)
        maps = []
        for k in range(8):
            Ok = O[k]
            maps.append(dict(rT=C(r1[k]["r"].T), aT=C(tokT(aTl, aTc, k)), agT=C(Ok[:, SEG["att_g"]:SEG["att_g"] + 1024].T),
                             yf=C(tokT(ys[0][:, :, NCTX:], ys[0][:, :, :NCTX], k)), yb=C(tokT(ys[1][:, :, NCTX:], ys[1][:, :, :NCTX], k)),
                             sgT=C(Ok[:, SEG["s5_g"]:SEG["s5_g"] + 768].T), mgT=C(Ok[:, SEG["merge"]:].T),
                             w_br_ret=A(w_br_ret)[l], w_br_att=A(w_br_att)[l], w_br_s5=A(w_br_s5)[l], glu_w=A(s5_glu_w)[l],
                             glu_b=C(A(s5_glu_b)[l].reshape(6, 128).T)))
        r2 = _run(_prog("LC2a", build_LC2a), maps)
        maps = []
        for k in range(8):
            b = k // 4
            ct = h[k * 128:(k + 1) * 128] if k < 4 else np.zeros((128, D), np.float32)
            xk = C(np.concatenate([xl[k * 2048:(k + 1) * 2048], ct], axis=0))
            rows = C(np.broadcast_to(np.stack([gate[:, b], gate[:, 2], A(ln_w)[l], A(ln_b)[l]])[None], (128, 4, D)).astype(np.float32))
            maps.append(dict(mT=r2[k]["mT"], w_out=A(w_out)[l], x=xk, rows=rows))
        r3 = _run(_prog("LC2b", build_LC2b), maps)
        xl = np.concatenate([r["xo"][:2048] for r in r3], axis=0)
        h = np.concatenate([r["xo"][2048:] for r in r3[:4]], axis=0)
    return xl.reshape(2, NLAT, D).astype(np.float32)
```

```python
import numpy as np
import concourse.bass as bass
import concourse.mybir as mybir

F32 = mybir.dt.float32
BF16 = mybir.dt.bfloat16
ALU = mybir.AluOpType
AF = mybir.ActivationFunctionType
AX = mybir.AxisListType

ENGS = ("pe", "act", "dve", "pool", "sp")
N_DMA_SEMS = 40


class Res:
    __slots__ = ("name", "w", "r")

    def __init__(self, name):
        self.name = name
        self.w = None
        self.r = []


class Op:
    __slots__ = ("id", "eng", "fn", "deps", "dma", "sig", "ev")

    def __init__(self, id, eng, fn, deps, dma):
        self.id = id
        self.eng = eng
        self.fn = fn
        self.deps = deps
        self.dma = dma
        self.sig = False
        self.ev = None


class Prog:
    def __init__(self, nc):
        self.nc = nc
        self.ops = []
        self.sems = {}
        self.final_dmas = []

    def res(self, name="r"):
        return Res(name)

    def add(self, eng, fn, reads=(), writes=(), dma=False):
        deps = set()
        for r in reads:
            if r.w is not None:
                deps.add(r.w)
        for w in writes:
            if w.w is not None:
                deps.add(w.w)
            deps.update(w.r)
        op = Op(len(self.ops), eng, fn, deps, dma)
        self.ops.append(op)
        for r in reads:
            r.r.append(op.id)
        for w in writes:
            w.w = op.id
            w.r = []
        return op.id

    def emit(self, sem_ctx):
        nc = self.nc
        ops = self.ops
        for op in ops:
            for d in op.deps:
                dop = ops[d]
                if dop.dma:
                    continue
                if dop.eng == op.eng and not op.dma and op.eng == "pe":
                    continue
                dop.sig = True
        cnt = {e: 0 for e in ENGS}
        dcnt = [0] * N_DMA_SEMS
        dlast = [None] * N_DMA_SEMS
        dnext = 0
        for op in ops:
            if op.dma:
                s = dnext
                dnext = (dnext + 1) % N_DMA_SEMS
                if dlast[s] is not None:
                    op.deps.add(dlast[s])
                dcnt[s] += 16
                op.ev = ("d%d" % s, dcnt[s])
                dlast[s] = op.id
            elif op.sig:
                cnt[op.eng] += 1
                op.ev = (op.eng, cnt[op.eng])
        streams = {e: [] for e in ENGS}
        seen = {e: {} for e in ENGS}
        for op in ops:
            waits = {}
            for d in op.deps:
                dop = ops[d]
                if dop.ev is None:
                    continue
                s, v = dop.ev
                if (not dop.dma) and dop.eng == op.eng and op.eng == "pe" and not op.dma:
                    continue
                if seen[op.eng].get(s, 0) >= v:
                    continue
                if waits.get(s, 0) < v:
                    waits[s] = v
            for s, v in waits.items():
                seen[op.eng][s] = v
            streams[op.eng].append((waits, op))
        return streams

    def run(self, streams, sems, block):
        nc = self.nc

        def mk(engname):
            def body(eng):
                for waits, op in streams[engname]:
                    for s, v in waits.items():
                        eng.wait_ge(sems[s], v)
                    if op.fn is None:
                        continue
                    ins = op.fn(eng)
                    if op.ev is not None:
                        s, v = op.ev
                        ins.then_inc(sems[s], 16 if op.dma else 1)
            return body

        block.tensor(mk("pe"))
        block.scalar(mk("act"))
        block.vector(mk("dve"))
        block.gpsimd(mk("pool"))
        block.sync(mk("sp"))


from contextlib import ExitStack
from concourse.bass_utils import run_bass_kernel_spmd


class KB:
    def __init__(self):
        self.nc = bass.Bass("TRN2", target_bir_lowering=False)
        self.p = Prog(self.nc)
        self.es = ExitStack()
        self.n = 0

    def sb(self, shape, dt, name=None):
        self.n += 1
        t = self.es.enter_context(self.nc.sbuf_tensor(name or ("s%d" % self.n), shape, dt))
        return t, self.p.res(name or "s")

    def ps(self, shape=(128, 512), dt=F32):
        self.n += 1
        t = self.es.enter_context(self.nc.psum_tensor("p%d" % self.n, list(shape), dt))
        return t, self.p.res("p")

    def din(self, name, shape, dt=F32):
        return self.nc.dram_tensor(name, list(shape), dt, kind="ExternalInput").ap(), self.p.res(name)

    def dout(self, name, shape, dt=F32):
        return self.nc.dram_tensor(name, list(shape), dt, kind="ExternalOutput").ap(), self.p.res(name)

    def finish(self, out_res):
        p, nc, es = self.p, self.nc, self.es
        p.add("sp", None, reads=out_res)
        p.add("act", None, reads=out_res)
        sems = {e: es.enter_context(nc.semaphore(e)) for e in ENGS}
        for i in range(N_DMA_SEMS):
            sems["d%d" % i] = es.enter_context(nc.semaphore("d%d" % i))
        streams = p.emit(None)
        with nc.Block() as block:
            p.run(streams, sems, block)
        es.close()
        return nc


D = 2048
NLAT = 8192
NCTX = 256
INW = 14336
EPS = 1e-6
SEG = dict(ret_q=0, ret_k=1024, ret_v=2048, ret_g=3072, att_q=4096, att_k=5120, att_v=5376, att_g=5632,
           s5_u=6656, s5_g=7424, merge=8192)


def rope_ops(kb, dst, dres, src, sres, cos, sin, tres, h, tmp, tmpres, extra_reads=()):
    p = kb.p
    x1, x2 = src[:, 0:h], src[:, h:2 * h]
    rd = [sres, tres] + list(extra_reads)
    p.add("dve", lambda e: e.tensor_tensor(out=tmp[:, 0:h], in0=x1, in1=cos, op=ALU.mult), reads=rd, writes=[tmpres])
    p.add("dve", lambda e: e.tensor_tensor(out=tmp[:, h:2 * h], in0=x2, in1=sin, op=ALU.mult), reads=rd, writes=[tmpres])
    p.add("dve", lambda e: e.tensor_tensor(out=tmp[:, 2 * h:3 * h], in0=x2, in1=cos, op=ALU.mult), reads=rd, writes=[tmpres])
    p.add("dve", lambda e: e.tensor_tensor(out=tmp[:, 3 * h:4 * h], in0=x1, in1=sin, op=ALU.mult), reads=rd, writes=[tmpres])
    p.add("dve", lambda e: e.tensor_tensor(out=dst[:, 0:h], in0=tmp[:, 0:h], in1=tmp[:, h:2 * h], op=ALU.subtract), reads=[tmpres], writes=[dres])
    p.add("dve", lambda e: e.tensor_tensor(out=dst[:, h:2 * h], in0=tmp[:, 2 * h:3 * h], in1=tmp[:, 3 * h:4 * h], op=ALU.add), reads=[tmpres], writes=[dres])


def build_LA(NT=17, CBS=None):
    CBS = list(range(28)) if CBS is None else CBS
    kb = KB()
    nc, p = kb.nc, kb.p
    NTOK = NT * 128
    xT, r_xT = kb.din("xT", [D, NTOK])
    modv, r_modv = kb.din("modv", [D, 4])
    w_in, r_w = kb.din("w_in", [D, INW])
    ropeR, r_ropeR = kb.din("ropeR", [128, NT, 2, 128])
    ropeA, r_ropeA = kb.din("ropeA", [128, NT, 2, 64])
    qkw, r_qkw = kb.din("qkw", [128, 2, 128])
    O, r_O = kb.dout("O", [NTOK, INW], BF16)

    mv, r_mv = kb.sb([128, 16, 4], F32)
    sc1, r_sc1 = kb.sb([128, 16, 2], F32)
    uT, r_uT = kb.sb([128, 16, NTOK], BF16)
    tR, r_tR = kb.sb([128, NT, 2, 128], F32)
    tA, r_tA = kb.sb([128, NT, 2, 64], F32)
    tW, r_tW = kb.sb([128, 2, 128], F32)
    xs = [kb.sb([128, NTOK], F32) for _ in range(2)]
    wb = [kb.sb([128, 16, 512], BF16) for _ in range(2)]
    ob = [kb.sb([128, 512], BF16) for _ in range(3)]
    pm = [kb.ps() for _ in range(4)]
    tmp, r_tmp = kb.sb([128, 512], F32)
    xn, r_xn = kb.sb([128, 512], F32)
    junk, r_junk = kb.sb([128, 128], F32)
    ss, r_ss = kb.sb([128, 4], F32)

    p.add("sp", lambda e: e.dma_start(out=mv[:], in_=modv.rearrange("(a p) c -> p a c", p=128)), writes=[r_mv], dma=True)
    p.add("sp", lambda e: e.dma_start(out=tR[:], in_=ropeR), writes=[r_tR], dma=True)
    p.add("sp", lambda e: e.dma_start(out=tA[:], in_=ropeA), writes=[r_tA], dma=True)
    p.add("sp", lambda e: e.dma_start(out=tW[:], in_=qkw), writes=[r_tW], dma=True)
    p.add("dve", lambda e: e.tensor_scalar(out=sc1[:, :, 0], in0=mv[:, :, 1], scalar1=1.0, scalar2=None, op0=ALU.add), reads=[r_mv], writes=[r_sc1])
    p.add("dve", lambda e: e.tensor_scalar(out=sc1[:, :, 1], in0=mv[:, :, 3], scalar1=1.0, scalar2=None, op0=ALU.add), reads=[r_mv], writes=[r_sc1])
    NL = (NT - 1) * 128
    for kt in range(16):
        b = kt % 2
        xsb, r_xsb = xs[b]
        p.add("sp" if kt % 2 == 0 else "act", lambda e, kt=kt, xsb=xsb: e.dma_start(out=xsb[:], in_=xT[kt * 128:(kt + 1) * 128, :]), writes=[r_xsb], dma=True)
        if NL > 0:
            p.add("act", lambda e, kt=kt, xsb=xsb: e.activation(out=uT[:, kt, 0:NL], in_=xsb[:, 0:NL], func=AF.Identity, scale=sc1[:, kt, 0:1], bias=mv[:, kt, 0:1]),
                  reads=[r_xsb, r_sc1, r_mv], writes=[r_uT])
        p.add("act", lambda e, kt=kt, xsb=xsb: e.activation(out=uT[:, kt, NL:NTOK], in_=xsb[:, NL:NTOK], func=AF.Identity, scale=sc1[:, kt, 1:2], bias=mv[:, kt, 2:3]),
              reads=[r_xsb, r_sc1, r_mv], writes=[r_uT])

    cnt = 0
    for ci, cb in enumerate(CBS):
        wbb, r_wbb = wb[ci % 2]
        p.add("pool", lambda e, cb=cb, wbb=wbb: e.dma_start(out=wbb[:], in_=w_in[:, cb * 512:(cb + 1) * 512].rearrange("(a p) n -> p a n", p=128)), writes=[r_wbb], dma=True)
        c0 = cb * 512
        for tt in range(NT):
            pmm, r_pm = pm[cnt % 4]
            obb, r_ob = ob[cnt % 3]
            cnt += 1
            for kt in range(16):
                p.add("pe", lambda e, kt=kt, tt=tt, pmm=pmm, wbb=wbb: e.matmul(pmm[:, :], uT[:, kt, tt * 128:(tt + 1) * 128], wbb[:, kt, :], start=(kt == 0), stop=(kt == 15)),
                      reads=[r_uT, r_wbb], writes=[r_pm])
            for hf in range(2):
                cc = c0 + hf * 256
                sl = slice(hf * 256, hf * 256 + 256)
                if cc < SEG["ret_v"]:
                    cos, sin = tR[:, tt, 0, :], tR[:, tt, 1, :]
                    if cc >= SEG["ret_k"]:
                        p.add("act", lambda e, pmm=pmm, sl=sl: e.activation(out=xn[:, sl], in_=pmm[:, sl], func=AF.Copy, scale=1.0 / 16.0), reads=[r_pm], writes=[r_xn])
                        rope_ops(kb, obb[:, sl], r_ob, xn[:, sl], r_xn, cos, sin, r_tR, 128, tmp, r_tmp)
                    else:
                        rope_ops(kb, obb[:, sl], r_ob, pmm[:, sl], r_pm, cos, sin, r_tR, 128, tmp, r_tmp)
                elif SEG["att_q"] <= cc < SEG["att_v"]:
                    wi = 0 if cc < SEG["att_k"] else 1
                    p.add("act", lambda e, pmm=pmm, sl=sl: e.activation(out=xn[:, sl], in_=pmm[:, sl], func=AF.Copy), reads=[r_pm], writes=[r_xn])
                    for hh in range(2):
                        s2 = slice(hf * 256 + hh * 128, hf * 256 + hh * 128 + 128)
                        p.add("act", lambda e, s2=s2, hh=hh: e.activation(out=junk[:, :], in_=xn[:, s2], func=AF.Square, accum_out=ss[:, hh:hh + 1]), reads=[r_xn], writes=[r_junk, r_ss])
                    p.add("act", lambda e: e.activation(out=ss[:, 2:4], in_=ss[:, 0:2], func=AF.Sqrt, scale=1.0 / 128.0, bias=EPS), reads=[r_ss], writes=[r_ss])
                    p.add("dve", lambda e: e.reciprocal(out=ss[:, 2:4], in_=ss[:, 2:4]), reads=[r_ss], writes=[r_ss])
                    for hh in range(2):
                        s2 = slice(hf * 256 + hh * 128, hf * 256 + hh * 128 + 128)
                        p.add("dve", lambda e, s2=s2, hh=hh, wi=wi: e.scalar_tensor_tensor(out=xn[:, s2], in0=xn[:, s2], scalar=ss[:, 2 + hh:3 + hh], in1=tW[:, wi, :], op0=ALU.mult, op1=ALU.mult),
                              reads=[r_xn, r_ss, r_tW], writes=[r_xn])
                        rope_ops(kb, obb[:, s2], r_ob, xn[:, s2], r_xn, tA[:, tt, 0, :], tA[:, tt, 1, :], r_tA, 64, tmp, r_tmp)
                else:
                    if cc >= SEG["merge"]:
                        fn = AF.Sigmoid
                    elif (SEG["ret_g"] <= cc < SEG["att_q"]) or (SEG["att_g"] <= cc < SEG["s5_u"]) or (SEG["s5_g"] <= cc < SEG["merge"]):
                        fn = AF.Silu
                    else:
                        fn = AF.Copy
                    p.add("act", lambda e, pmm=pmm, sl=sl, obb=obb, fn=fn: e.activation(out=obb[:, sl], in_=pmm[:, sl], func=fn), reads=[r_pm], writes=[r_ob])
            p.add("sp", lambda e, tt=tt, c0=c0, obb=obb: e.dma_start(out=O[tt * 128:(tt + 1) * 128, c0:c0 + 512], in_=obb[:, :]), reads=[r_ob], writes=[r_O], dma=True)
    return kb.finish([r_O])


def build_ATT(NQ, NK, QB=512):
    kb = KB()
    nc, p = kb.nc, kb.p
    qT, r_q = kb.din("qT", [2, 128, NQ], BF16)
    kT, r_k = kb.din("kT", [128, NK], BF16)
    v, r_v = kb.din("v", [NK, 128], BF16)
    aT, r_a = kb.dout("aT", [2, 128, NQ])
    NKT = NK // 128
    qs, r_qs = kb.sb([128, 2, NQ], BF16)
    ks, r_ks = kb.sb([128, NK], BF16)
    vs, r_vs = kb.sb([128, NKT, 128], BF16)
    ones, r_ones = kb.sb([128, 128], BF16)
    pT = [kb.sb([128, QB], BF16) for _ in range(4)]
    pss = [kb.ps() for _ in range(4)]
    pso = [kb.ps() for _ in range(2)]
    psr = [kb.ps() for _ in range(2)]
    rinv, r_rinv = kb.sb([128, QB], F32)
    ob = [kb.sb([128, QB], F32) for _ in range(2)]
    p.add("sp", lambda e: e.dma_start(out=qs[:], in_=qT.rearrange("h p n -> p h n")), writes=[r_qs], dma=True)
    p.add("act", lambda e: e.dma_start(out=ks[:], in_=kT), writes=[r_ks], dma=True)
    p.add("sp", lambda e: e.dma_start(out=vs[:], in_=v.rearrange("(a p) d -> p a d", p=128)), writes=[r_vs], dma=True)
    p.add("pool", lambda e: e.memset(ones[:], 1.0), writes=[r_ones])
    sc = 128.0 ** -0.5
    LOOK = 3
    its = [(h, qb, kt) for h in range(2) for qb in range(NQ // QB) for kt in range(NKT)]

    def front(i):
        h, qb, kt = its[i]
        pscore, r_ps = pss[i % 4]
        pt_, r_pt = pT[i % 4]
        qsl = slice(qb * QB, (qb + 1) * QB)
        p.add("pe", lambda e: e.matmul(pscore[:, 0:QB], ks[:, kt * 128:(kt + 1) * 128], qs[:, h, qsl], start=True, stop=True), reads=[r_ks, r_qs], writes=[r_ps])
        p.add("act", lambda e: e.activation(out=pt_[:, :], in_=pscore[:, 0:QB], func=AF.Exp, scale=sc), reads=[r_ps], writes=[r_pt])

    for i in range(min(LOOK, len(its))):
        front(i)
    for i, (h, qb, kt) in enumerate(its):
        blk = h * (NQ // QB) + qb
        po, r_po = pso[blk % 2]
        pr, r_pr = psr[blk % 2]
        obb, r_ob = ob[blk % 2]
        pt_, r_pt = pT[i % 4]
        qsl = slice(qb * QB, (qb + 1) * QB)
        p.add("pe", lambda e, kt=kt, po=po, pt_=pt_: e.matmul(po[:, 0:QB], vs[:, kt, :], pt_[:, :], start=(kt == 0), stop=(kt == NKT - 1)), reads=[r_vs, r_pt], writes=[r_po])
        p.add("pe", lambda e, kt=kt, pr=pr, pt_=pt_: e.matmul(pr[:, 0:QB], ones[:, :], pt_[:, :], start=(kt == 0), stop=(kt == NKT - 1)), reads=[r_ones, r_pt], writes=[r_pr])
        if i + LOOK < len(its):
            front(i + LOOK)
        if kt == NKT - 1:
            p.add("dve", lambda e, pr=pr: e.reciprocal(out=rinv[:, :], in_=pr[:, 0:QB]), reads=[r_pr], writes=[r_rinv])
            p.add("dve", lambda e, po=po, obb=obb: e.tensor_tensor(out=obb[:, :], in0=po[:, 0:QB], in1=rinv[:, :], op=ALU.mult), reads=[r_po, r_rinv], writes=[r_ob])
            p.add("sp", lambda e, h=h, qsl=qsl, obb=obb: e.dma_start(out=aT[h, :, qsl], in_=obb[:, :]), reads=[r_ob], writes=[r_a], dma=True)
    return kb.finish([r_a])


def ret_emit(kb, sfx, N, HALVES=2):
    nc, p = kb.nc, kb.p
    NCH = N // 128
    CPH = NCH // HALVES
    NH = CPH * 128
    qT, r_q = kb.din("qT" + sfx, [256, N], BF16)
    kT, r_k = kb.din("kT" + sfx, [256, N], BF16)
    kk, r_kk = kb.din("k" + sfx, [N, 256], BF16)
    vv, r_vv = kb.din("v" + sfx, [N, 256], BF16)
    cst, r_cst = kb.din("cst" + sfx, [128, 260])
    ro, r_ro = kb.dout("ro" + sfx, [N, 256])
    qs, r_qs = kb.sb([128, 2, NH], BF16)
    ks, r_ks = kb.sb([128, 2, NH], BF16)
    kt_, r_kt = kb.sb([128, CPH, 256], BF16)
    vs, r_vs = kb.sb([128, CPH, 256], BF16)
    cs, r_cs = kb.sb([128, 260], F32)
    intra, r_intra = kb.sb([128, 128], F32)
    dec, r_dec = kb.sb([128, 4], F32)
    S, r_S = kb.sb([128, 2, 256], F32)
    Sb, r_Sb = kb.sb([128, 2, 256], BF16)
    pT = [kb.sb([128, 128], BF16) for _ in range(2)]
    kd = [kb.sb([128, 256], BF16) for _ in range(2)]
    isb = [kb.sb([128, 256], F32) for _ in range(2)]
    ob = [kb.sb([128, 256], F32) for _ in range(2)]
    ps_s = [kb.ps() for _ in range(1)]
    ps_i = [kb.ps() for _ in range(1)]
    ps_x = [kb.ps() for _ in range(1)]
    ps_u = [kb.ps() for _ in range(1)]
    p.add("sp", lambda e: e.dma_start(out=cs[:], in_=cst), writes=[r_cs], dma=True)
    lg = cs[:, 256:257]
    p.add("act", lambda e: e.activation(out=intra[:, :], in_=cs[:, 0:128], func=AF.Exp, scale=lg), reads=[r_cs], writes=[r_intra])
    p.add("dve", lambda e: e.tensor_tensor(out=intra[:, :], in0=intra[:, :], in1=cs[:, 128:256], op=ALU.mult), reads=[r_intra, r_cs], writes=[r_intra])
    p.add("act", lambda e: e.activation(out=dec[:, 0:3], in_=cs[:, 257:260], func=AF.Exp, scale=lg), reads=[r_cs], writes=[r_dec])
    p.add("pool", lambda e: e.memset(S[:], 0.0), writes=[r_S])
    p.add("pool", lambda e: e.memset(Sb[:], 0.0), writes=[r_Sb])
    for cg in range(NCH):
        if cg % CPH == 0:
            h0 = cg * 128
            c0 = cg
            p.add("sp", lambda e, h0=h0: e.dma_start(out=qs[:], in_=qT[:, h0:h0 + NH].rearrange("(a p) n -> p a n", p=128)), writes=[r_qs], dma=True)
            p.add("act", lambda e, h0=h0: e.dma_start(out=ks[:], in_=kT[:, h0:h0 + NH].rearrange("(a p) n -> p a n", p=128)), writes=[r_ks], dma=True)
            p.add("sp", lambda e, h0=h0: e.dma_start(out=kt_[:], in_=kk[h0:h0 + NH, :].rearrange("(a p) d -> p a d", p=128)), writes=[r_kt], dma=True)
            p.add("act", lambda e, h0=h0: e.dma_start(out=vs[:], in_=vv[h0:h0 + NH, :].rearrange("(a p) d -> p a d", p=128)), writes=[r_vs], dma=True)
        c = cg % CPH
        b = cg % 2
        csl = slice(c * 128, (c + 1) * 128)
        gsl = slice(cg * 128, (cg + 1) * 128)
        pss, r_pss = ps_s[0]
        psi, r_psi = ps_i[0]
        psx, r_psx = ps_x[0]
        psu, r_psu = ps_u[0]
        ptb, r_ptb = pT[b]
        kdb, r_kdb = kd[b]
        isbb, r_isb = isb[b]
        obb, r_ob = ob[b]
        for dt in range(2):
            p.add("pe", lambda e, dt=dt, csl=csl, pss=pss: e.matmul(pss[:, 0:128], ks[:, dt, csl], qs[:, dt, csl], start=(dt == 0), stop=(dt == 1)), reads=[r_ks, r_qs], writes=[r_pss])
        p.add("dve", lambda e, pss=pss, ptb=ptb: e.tensor_tensor(out=ptb[:, :], in0=pss[:, 0:128], in1=intra[:, :], op=ALU.mult), reads=[r_pss, r_intra], writes=[r_ptb])
        p.add("pe", lambda e, c=c, psi=psi, ptb=ptb: e.matmul(psi[:, 0:256], ptb[:, :], vs[:, c, :], start=True, stop=True), reads=[r_ptb, r_vs], writes=[r_psi])
        for dt in range(2):
            p.add("pe", lambda e, dt=dt, csl=csl, psx=psx: e.matmul(psx[:, 0:256], qs[:, dt, csl], Sb[:, dt, :], start=(dt == 0), stop=(dt == 1)), reads=[r_qs, r_Sb], writes=[r_psx])
        p.add("act", lambda e, psi=psi, isbb=isbb: e.activation(out=isbb[:, :], in_=psi[:, 0:256], func=AF.Copy), reads=[r_psi], writes=[r_isb])
        p.add("dve", lambda e, psx=psx, isbb=isbb, obb=obb: e.scalar_tensor_tensor(out=obb[:, :], in0=psx[:, 0:256], scalar=dec[:, 0:1], in1=isbb[:, :], op0=ALU.mult, op1=ALU.add),
              reads=[r_psx, r_dec, r_isb], writes=[r_ob])
        p.add("sp", lambda e, gsl=gsl, obb=obb: e.dma_start(out=ro[gsl, :], in_=obb[:, :]), reads=[r_ob], writes=[r_ro], dma=True)
        p.add("act", lambda e, c=c, kdb=kdb: e.activation(out=kdb[:, :], in_=kt_[:, c, :], func=AF.Copy, scale=dec[:, 1:2]), reads=[r_kt, r_dec], writes=[r_kdb])
        for dt in range(2):
            p.add("pe", lambda e, dt=dt, c=c, psu=psu, kdb=kdb: e.matmul(psu[:, dt * 256:(dt + 1) * 256], kdb[:, dt * 128:(dt + 1) * 128], vs[:, c, :], start=True, stop=True), reads=[r_kdb, r_vs], writes=[r_psu])
        for dt in range(2):
            p.add("dve", lambda e, dt=dt, psu=psu: e.scalar_tensor_tensor(out=S[:, dt, :], in0=S[:, dt, :], scalar=dec[:, 2:3], in1=psu[:, dt * 256:(dt + 1) * 256], op0=ALU.mult, op1=ALU.add),
                  reads=[r_S, r_dec, r_psu], writes=[r_S])
        p.add("act", lambda e: e.activation(out=Sb[:, :, :], in_=S[:, :, :], func=AF.Copy), reads=[r_S], writes=[r_Sb])
    return r_ro


def _interleave(kb, emitters):
    p = kb.p
    outs, lists = [], []
    for em in emitters:
        lst = []
        orig = p.add
        p.add = lambda *aa, _l=lst, **kk: _l.append((aa, kk))
        try:
            outs.append(em())
        finally:
            p.add = orig
        lists.append(lst)
    tot = max(len(l) for l in lists)
    pos = [0] * len(lists)
    for step in range(1, tot + 1):
        for li, l in enumerate(lists):
            tgt = (step * len(l) + tot - 1) // tot
            while pos[li] < tgt:
                aa, kk = l[pos[li]]
                p.add(*aa, **kk)
                pos[li] += 1
    return outs


def build_RETpair(N):
    kb = KB()
    outs = _interleave(kb, [lambda: ret_emit(kb, "_f", N), lambda: ret_emit(kb, "_b", N)])
    return kb.finish(outs)


PI = float(np.pi)
S5_POOL_ENG = "pool"
S5_FOURMM = True


def range_reduce(kb, dst, r_dst, src, r_src, off, ki, kf, tq, r_sc):
    p = kb.p
    I2P = 1.0 / (2 * PI)
    dve = lambda fn, rd, wr: p.add("dve", fn, reads=rd, writes=wr)
    dve(lambda e: e.tensor_scalar(out=ki, in0=src, scalar1=I2P, scalar2=off * I2P, op0=ALU.mult, op1=ALU.add), [r_src], [r_sc])
    dve(lambda e: e.tensor_copy(out=kf, in_=ki), [r_sc], [r_sc])
    dve(lambda e: e.tensor_scalar(out=tq, in0=src, scalar1=off, scalar2=None, op0=ALU.add), [r_src], [r_sc])
    dve(lambda e: e.scalar_tensor_tensor(out=dst, in0=kf, scalar=-2 * PI, in1=tq, op0=ALU.mult, op1=ALU.add), [r_sc], [r_dst])
    dve(lambda e: e.tensor_scalar(out=tq, in0=dst, scalar1=PI, scalar2=2 * PI, op0=ALU.is_gt, op1=ALU.mult), [r_dst], [r_sc])
    dve(lambda e: e.tensor_tensor(out=dst, in0=dst, in1=tq, op=ALU.subtract), [r_dst, r_sc], [r_dst])
    dve(lambda e: e.tensor_scalar(out=tq, in0=dst, scalar1=-PI, scalar2=2 * PI, op0=ALU.is_lt, op1=ALU.mult), [r_dst], [r_sc])
    dve(lambda e: e.tensor_tensor(out=dst, in0=dst, in1=tq, op=ALU.add), [r_dst, r_sc], [r_dst])


def s5_prep(kb, pr, r_pr, w_, r_w, ki, kf, tq, r_sc):
    p = kb.p
    W = lambda i: w_[:, i, :]
    are, aim, ldt = pr[:, 0, :], pr[:, 1, :], pr[:, 2, :]
    dve = lambda fn, rd, wr: p.add("dve", fn, reads=rd, writes=wr)
    act = lambda fn, rd, wr: p.add("act", fn, reads=rd, writes=wr)
    act(lambda e: e.activation(out=W(0), in_=ldt, func=AF.Exp), [r_pr], [r_w])
    dve(lambda e: e.tensor_tensor(out=W(1), in0=are, in1=W(0), op=ALU.mult), [r_pr, r_w], [r_w])
    act(lambda e: e.activation(out=W(1), in_=W(1), func=AF.Exp), [r_w], [r_w])
    dve(lambda e: e.tensor_tensor(out=W(2), in0=aim, in1=W(0), op=ALU.mult), [r_pr, r_w], [r_w])
    range_reduce(kb, W(3), r_w, W(2), r_w, 0.0, ki[:, 0:6], kf[:, 0:6], tq[:, 0:6], r_sc)
    range_reduce(kb, W(4), r_w, W(2), r_w, 0.5 * PI, ki[:, 0:6], kf[:, 0:6], tq[:, 0:6], r_sc)
    act(lambda e: e.activation(out=W(5), in_=W(3), func=AF.Sin), [r_w], [r_w])
    act(lambda e: e.activation(out=W(6), in_=W(4), func=AF.Sin), [r_w], [r_w])
    dve(lambda e: e.tensor_tensor(out=W(7), in0=W(1), in1=W(6), op=ALU.mult), [r_w], [r_w])
    dve(lambda e: e.tensor_tensor(out=W(8), in0=W(1), in1=W(5), op=ALU.mult), [r_w], [r_w])
    dve(lambda e: e.tensor_tensor(out=W(13), in0=are, in1=are, op=ALU.mult), [r_pr], [r_w])
    dve(lambda e: e.tensor_tensor(out=W(14), in0=aim, in1=aim, op=ALU.mult), [r_pr], [r_w])
    dve(lambda e: e.tensor_tensor(out=W(9), in0=W(13), in1=W(14), op=ALU.add), [r_w], [r_w])
    dve(lambda e: e.reciprocal(out=W(9), in_=W(9)), [r_w], [r_w])
    dve(lambda e: e.tensor_scalar(out=W(12), in0=W(7), scalar1=-1.0, scalar2=None, op0=ALU.add), [r_w], [r_w])
    dve(lambda e: e.tensor_tensor(out=W(13), in0=W(12), in1=are, op=ALU.mult), [r_w, r_pr], [r_w])
    dve(lambda e: e.tensor_tensor(out=W(14), in0=W(8), in1=aim, op=ALU.mult), [r_w, r_pr], [r_w])
    dve(lambda e: e.tensor_tensor(out=W(10), in0=W(13), in1=W(14), op=ALU.add), [r_w], [r_w])
    dve(lambda e: e.tensor_tensor(out=W(10), in0=W(10), in1=W(9), op=ALU.mult), [r_w], [r_w])
    dve(lambda e: e.tensor_tensor(out=W(13), in0=W(8), in1=are, op=ALU.mult), [r_w, r_pr], [r_w])
    dve(lambda e: e.tensor_tensor(out=W(14), in0=W(12), in1=aim, op=ALU.mult), [r_w, r_pr], [r_w])
    dve(lambda e: e.tensor_tensor(out=W(11), in0=W(13), in1=W(14), op=ALU.subtract), [r_w], [r_w])
    dve(lambda e: e.tensor_tensor(out=W(11), in0=W(11), in1=W(9), op=ALU.mult), [r_w], [r_w])


def build_S5v2(NC, L=16):
    kb = KB()
    nc, p = kb.nc, kb.p
    uR, r_u = kb.din("uR", [6, 32, L, NC], BF16)
    prm, r_prm = kb.din("prm", [128, 3, 6])
    Bm, r_Bm = kb.din("Bm", [128, 6, 2, 32])
    Cm, r_Cm = kb.din("Cm", [128, 6, 2, 32])
    dsk, r_dsk = kb.din("dsk", [32, 6])
    idn, r_idn = kb.din("idn", [128, 128])
    yR, r_y = kb.dout("yR", [6, 32, L, NC])
    us, r_us = kb.sb([32, 6, L, NC], BF16)
    pr, r_pr = kb.sb([128, 3, 6], F32)
    Bs, r_Bs = kb.sb([128, 6, 2, 32], F32)
    Cs, r_Cs = kb.sb([128, 6, 3, 32], F32)
    ds, r_ds = kb.sb([32, 6], F32)
    ident, r_id = kb.sb([128, 128], F32)
    w_, r_w = kb.sb([128, 16, 6], F32)
    ki, r_sc = kb.sb([128, 8], mybir.dt.int32)
    kf, _ = kb.sb([128, 8], F32)
    tq, _ = kb.sb([128, 8], F32)
    bb, r_bb = kb.sb([128, 6, 2, 32], F32)
    pw, r_pw = kb.sb([128, L + 2, 2, 6], F32)
    NST = max(1, (NC - 1).bit_length())
    dq, r_dq = kb.sb([128, NST, 3, 6], F32)
    t6, r_t6 = kb.sb([128, 4, 6], F32)
    for dst, src, r, eng in ((us, uR.rearrange("m p r n -> p m r n"), r_us, "sp"), (pr, prm, r_pr, "act"), (Bs, Bm, r_Bs, "sp"),
                             (ds, dsk, r_ds, "sp"), (ident, idn, r_id, "sp")):
        p.add(eng, lambda e, dst=dst, src=src: e.dma_start(out=dst[:], in_=src), writes=[r], dma=True)
    p.add("act", lambda e: e.dma_start(out=Cs[:, :, 0:2, :], in_=Cm), writes=[r_Cs], dma=True)
    dve = lambda fn, rd, wr: p.add("dve", fn, reads=rd, writes=wr)
    act = lambda fn, rd, wr: p.add("act", fn, reads=rd, writes=wr)
    s5_prep(kb, pr, r_pr, w_, r_w, ki[:, 0:6], kf[:, 0:6], tq[:, 0:6], r_sc)
    act(lambda e: e.activation(out=Cs[:, :, 2, :], in_=Cs[:, :, 1, :], func=AF.Copy, scale=-1.0), [r_Cs], [r_Cs])
    abr, abi = w_[:, 7, :], w_[:, 8, :]

    def cmul(o_re, o_im, a_re, a_im, b_re, b_im, rd, wr):
        dve(lambda e: e.tensor_tensor(out=t6[:, 0, :], in0=a_re, in1=b_re, op=ALU.mult), rd, [r_t6])
        dve(lambda e: e.tensor_tensor(out=t6[:, 1, :], in0=a_im, in1=b_im, op=ALU.mult), rd, [r_t6])
        dve(lambda e: e.tensor_tensor(out=t6[:, 2, :], in0=a_re, in1=b_im, op=ALU.mult), rd, [r_t6])
        dve(lambda e: e.tensor_tensor(out=t6[:, 3, :], in0=a_im, in1=b_re, op=ALU.mult), rd, [r_t6])
        dve(lambda e: e.tensor_tensor(out=o_re, in0=t6[:, 0, :], in1=t6[:, 1, :], op=ALU.subtract), [r_t6], wr)
        dve(lambda e: e.tensor_tensor(out=o_im, in0=t6[:, 2, :], in1=t6[:, 3, :], op=ALU.add), [r_t6], wr)

    p.add("pool", lambda e: e.memset(pw[:, 0, 0, :], 1.0), writes=[r_pw])
    p.add("pool", lambda e: e.memset(pw[:, 0, 1, :], 0.0), writes=[r_pw])
    for j in range(1, L + 2):
        cmul(pw[:, j, 0, :], pw[:, j, 1, :], pw[:, j - 1, 0, :], pw[:, j - 1, 1, :], abr, abi, [r_pw, r_w], [r_pw])
    dve(lambda e: e.tensor_copy(out=dq[:, 0, 0:2, :], in_=pw[:, L, :, :]), [r_pw], [r_dq])
    for i in range(1, NST):
        cmul(dq[:, i, 0, :], dq[:, i, 1, :], dq[:, i - 1, 0, :], dq[:, i - 1, 1, :], dq[:, i - 1, 0, :], dq[:, i - 1, 1, :], [r_dq], [r_dq])
    dve(lambda e: e.tensor_scalar(out=dq[:, :, 2, :], in0=dq[:, :, 1, :], scalar1=-1.0, scalar2=None, op0=ALU.mult), [r_dq], [r_dq])
    for mt in range(6):
        fr, fi = w_[:, 10, mt:mt + 1], w_[:, 11, mt:mt + 1]
        dve(lambda e, mt=mt, fi=fi: e.tensor_scalar(out=bb[:, mt, 0, :], in0=Bs[:, mt, 1, :], scalar1=fi, scalar2=None, op0=ALU.mult), [r_Bs, r_w], [r_bb])
        dve(lambda e, mt=mt, fr=fr: e.scalar_tensor_tensor(out=bb[:, mt, 0, :], in0=Bs[:, mt, 0, :], scalar=fr, in1=bb[:, mt, 0, :], op0=ALU.mult, op1=ALU.subtract), [r_Bs, r_w, r_bb], [r_bb])
        dve(lambda e, mt=mt, fi=fi: e.tensor_scalar(out=bb[:, mt, 1, :], in0=Bs[:, mt, 0, :], scalar1=fi, scalar2=None, op0=ALU.mult), [r_Bs, r_w], [r_bb])
        dve(lambda e, mt=mt, fr=fr: e.scalar_tensor_tensor(out=bb[:, mt, 1, :], in0=Bs[:, mt, 1, :], scalar=fr, in1=bb[:, mt, 1, :], op0=ALU.mult, op1=ALU.add), [r_Bs, r_w, r_bb], [r_bb])
    BbS = [kb.sb([128, L, 2, 32], F32) for _ in range(2)]
    Win = [kb.sb([32, L, 2, 128], BF16) for _ in range(2)]
    CRI = [kb.sb([128, L, 2, 32], BF16) for _ in range(3)]
    Kt = [kb.sb([32, L, 32], BF16) for _ in range(3)]
    tsm, r_tsm = kb.sb([128, 2, 32], F32)
    XA = [kb.sb([128, 2, NC], F32) for _ in range(4)]
    Xps = [kb.sb([128, 2, NC], BF16) for _ in range(2)]
    ysb = [kb.sb([32, 512], F32) for _ in range(2)]
    yo = [kb.sb([32, NC], F32) for _ in range(3)]
    ptr = [kb.ps() for _ in range(2)]
    pz = [kb.ps() for _ in range(2)]
    pya = [kb.ps() for _ in range(2)]
    pyb = [kb.ps() for _ in range(2)]
    npc = -(-NC // 512)
    psz = -(-NC // npc)
    pieces = [(s0, min(psz, NC - s0)) for s0 in range(0, NC, psz)]
    cnt_box = [0]

    def setup(mt):
        bbs, r_bbs = BbS[mt % 2]
        win, r_win = Win[mt % 2]
        cri, r_cri = CRI[mt % 3]
        kt, r_kt = Kt[mt % 3]
        for j in range(L):
            pre_, pim_ = pw[:, j, 0, mt:mt + 1], pw[:, j, 1, mt:mt + 1]
            dve(lambda e, pim_=pim_, mt=mt: e.tensor_scalar(out=tsm[:, 0, :], in0=bb[:, mt, 1, :], scalar1=pim_, scalar2=None, op0=ALU.mult), [r_bb, r_pw], [r_tsm])
            dve(lambda e, pre_=pre_, mt=mt, j=j, bbs=bbs: e.scalar_tensor_tensor(out=bbs[:, j, 0, :], in0=bb[:, mt, 0, :], scalar=pre_, in1=tsm[:, 0, :], op0=ALU.mult, op1=ALU.subtract), [r_bb, r_pw, r_tsm], [r_bbs])
            dve(lambda e, pim_=pim_, mt=mt: e.tensor_scalar(out=tsm[:, 1, :], in0=bb[:, mt, 0, :], scalar1=pim_, scalar2=None, op0=ALU.mult), [r_bb, r_pw], [r_tsm])
            dve(lambda e, pre_=pre_, mt=mt, j=j, bbs=bbs: e.scalar_tensor_tensor(out=bbs[:, j, 1, :], in0=bb[:, mt, 1, :], scalar=pre_, in1=tsm[:, 1, :], op0=ALU.mult, op1=ALU.add), [r_bb, r_pw, r_tsm], [r_bbs])
        for g in range(L * 2 // 4):
            pt, r_pt = ptr[g % 2]
            for q in range(4):
                idx = g * 4 + q
                s_, ri = idx // 2, idx % 2
                p.add("pe", lambda e, s_=s_, ri=ri, q=q, pt=pt, bbs=bbs: e.transpose(pt[0:32, q * 128:(q + 1) * 128], bbs[:, L - 1 - s_, ri, :], ident[:, :]), reads=[r_bbs, r_id], writes=[r_pt])
            s0_ = (g * 4) // 2
            act(lambda e, s0_=s0_, pt=pt, win=win: e.activation(out=win[:, s0_:s0_ + 2, :, :], in_=pt[0:32, 0:512].rearrange("p (a b c) -> p a b c", a=2, b=2), func=AF.Copy), [r_pt], [r_win])
        pk, r_pk = ptr[0]
        for j in range(L):
            p.add("pe", lambda e, j=j, mt=mt, pk=pk, bbs=bbs: e.matmul(pk[0:32, j * 32:(j + 1) * 32], bbs[:, j, 0, :], Cs[:, mt, 0, :], start=True, stop=False), reads=[r_bbs, r_Cs], writes=[r_pk])
            p.add("pe", lambda e, j=j, mt=mt, pk=pk, bbs=bbs: e.matmul(pk[0:32, j * 32:(j + 1) * 32], bbs[:, j, 1, :], Cs[:, mt, 2, :], start=False, stop=True), reads=[r_bbs, r_Cs], writes=[r_pk])
        act(lambda e, pk=pk, kt=kt: e.activation(out=kt[:, :, :], in_=pk[0:32, 0:L * 32].rearrange("p (a b) -> p a b", a=L), func=AF.Copy), [r_pk], [r_kt])
        for r in range(L):
            pre_, pim_ = pw[:, r + 1, 0, mt:mt + 1], pw[:, r + 1, 1, mt:mt + 1]
            dve(lambda e, pim_=pim_, mt=mt: e.tensor_scalar(out=tsm[:, 0, :], in0=Cs[:, mt, 2, :], scalar1=pim_, scalar2=None, op0=ALU.mult), [r_Cs, r_pw], [r_tsm])
            dve(lambda e, pre_=pre_, mt=mt, r=r, cri=cri: e.scalar_tensor_tensor(out=cri[:, r, 0, :], in0=Cs[:, mt, 0, :], scalar=pre_, in1=tsm[:, 0, :], op0=ALU.mult, op1=ALU.add), [r_Cs, r_pw, r_tsm], [r_cri])
            dve(lambda e, pim_=pim_, mt=mt: e.tensor_scalar(out=tsm[:, 1, :], in0=Cs[:, mt, 0, :], scalar1=pim_, scalar2=-1.0, op0=ALU.mult, op1=ALU.mult), [r_Cs, r_pw], [r_tsm])
            dve(lambda e, pre_=pre_, mt=mt, r=r, cri=cri: e.scalar_tensor_tensor(out=cri[:, r, 1, :], in0=Cs[:, mt, 2, :], scalar=pre_, in1=tsm[:, 1, :], op0=ALU.mult, op1=ALU.add), [r_Cs, r_pw, r_tsm], [r_cri])

    def mainA(mt):
        bbs, r_bbs = BbS[mt % 2]
        win, r_win = Win[mt % 2]
        cri, r_cri = CRI[mt % 3]
        kt, r_kt = Kt[mt % 3]
        xa, r_xa = XA[2 * (mt % 2)]
        xb_, r_xb = XA[2 * (mt % 2) + 1]
        Xp, r_Xp = Xps[mt % 2]
        for (s0, n) in pieces:
            for ri in range(2):
                pzz, r_pz = pz[ri]
                for s_ in range(L):
                    p.add("pe", lambda e, s_=s_, ri=ri, s0=s0, n=n, pzz=pzz, win=win, mt=mt: e.matmul(pzz[:, 0:n], win[:, s_, ri, :], us[:, mt, s_, s0:s0 + n], start=(s_ == 0), stop=(s_ == L - 1)), reads=[r_win, r_us], writes=[r_pz])
                act(lambda e, ri=ri, s0=s0, n=n, pzz=pzz, xa=xa: e.activation(out=xa[:, ri, s0:s0 + n], in_=pzz[:, 0:n], func=AF.Copy), [r_pz], [r_xa])
        cur, r_cur, nxt, r_nxt = xa, r_xa, xb_, r_xb
        for i in range(NST):
            d = 1 << i
            if d >= NC:
                break
            dre, dim_, ndim = dq[:, i, 0, mt:mt + 1], dq[:, i, 1, mt:mt + 1], dq[:, i, 2, mt:mt + 1]
            act(lambda e, d=d, cur=cur, nxt=nxt: e.activation(out=nxt[:, :, 0:d], in_=cur[:, :, 0:d], func=AF.Copy), [r_cur], [r_nxt])
            dve(lambda e, d=d, cur=cur, nxt=nxt, dre=dre: e.scalar_tensor_tensor(out=nxt[:, 0, d:NC], in0=cur[:, 0, 0:NC - d], scalar=dre, in1=cur[:, 0, d:NC], op0=ALU.mult, op1=ALU.add), [r_cur, r_dq], [r_nxt])
            dve(lambda e, d=d, cur=cur, nxt=nxt, ndim=ndim: e.scalar_tensor_tensor(out=nxt[:, 0, d:NC], in0=cur[:, 1, 0:NC - d], scalar=ndim, in1=nxt[:, 0, d:NC], op0=ALU.mult, op1=ALU.add), [r_cur, r_dq, r_nxt], [r_nxt])
            dve(lambda e, d=d, cur=cur, nxt=nxt, dre=dre: e.scalar_tensor_tensor(out=nxt[:, 1, d:NC], in0=cur[:, 1, 0:NC - d], scalar=dre, in1=cur[:, 1, d:NC], op0=ALU.mult, op1=ALU.add), [r_cur, r_dq], [r_nxt])
            dve(lambda e, d=d, cur=cur, nxt=nxt, dim_=dim_: e.scalar_tensor_tensor(out=nxt[:, 1, d:NC], in0=cur[:, 0, 0:NC - d], scalar=dim_, in1=nxt[:, 1, d:NC], op0=ALU.mult, op1=ALU.add), [r_cur, r_dq, r_nxt], [r_nxt])
            cur, r_cur, nxt, r_nxt = nxt, r_nxt, cur, r_cur
        p.add("pool", lambda e: e.memset(Xp[:, :, 0:1], 0.0), writes=[r_Xp])
        act(lambda e, cur=cur: e.activation(out=Xp[:, :, 1:NC], in_=cur[:, :, 0:NC - 1], func=AF.Copy), [r_cur], [r_Xp])

    def mainB(mt):
        cri, r_cri = CRI[mt % 3]
        kt, r_kt = Kt[mt % 3]
        Xp, r_Xp = Xps[mt % 2]
        for r in range(L):
            yob, r_yo = yo[cnt_box[0] % 3]
            cnt_box[0] += 1
            for pi_, (s0, n) in enumerate(pieces):
                pa, r_pa = pya[(r + pi_) % 2]
                pb_, r_pb = pyb[(r + pi_) % 2]
                ysbb, r_ysb = ysb[(r + pi_) % 2]
                for ri in range(2):
                    p.add("pe", lambda e, r=r, ri=ri, s0=s0, n=n, pa=pa, cri=cri: e.matmul(pa[0:32, 0:n], cri[:, r, ri, :], Xp[:, ri, s0:s0 + n], start=(ri == 0), stop=(ri == 1)), reads=[r_cri, r_Xp], writes=[r_pa])
                for s_ in range(r + 1):
                    p.add("pe", lambda e, r=r, s_=s_, s0=s0, n=n, pb_=pb_, kt=kt, mt=mt: e.matmul(pb_[0:32, 0:n], kt[:, r - s_, :], us[:, mt, s_, s0:s0 + n], start=(s_ == 0), stop=(s_ == r)), reads=[r_kt, r_us], writes=[r_pb])
                act(lambda e, n=n, pa=pa, ysbb=ysbb: e.activation(out=ysbb[:, 0:n], in_=pa[0:32, 0:n], func=AF.Copy), [r_pa], [r_ysb])
                dve(lambda e, s0=s0, n=n, pb_=pb_, ysbb=ysbb, yob=yob: e.tensor_tensor(out=yob[:, s0:s0 + n], in0=pb_[0:32, 0:n], in1=ysbb[:, 0:n], op=ALU.add), [r_pb, r_ysb], [r_yo])
                dve(lambda e, s0=s0, n=n, r=r, mt=mt, yob=yob: e.scalar_tensor_tensor(out=yob[:, s0:s0 + n], in0=us[:, mt, r, s0:s0 + n], scalar=ds[:, mt:mt + 1], in1=yob[:, s0:s0 + n], op0=ALU.mult, op1=ALU.add), [r_us, r_ds, r_yo], [r_yo])
            p.add("sp", lambda e, mt=mt, r=r, yob=yob: e.dma_start(out=yR[mt, :, r, :], in_=yob[:, :]), reads=[r_yo], writes=[r_y], dma=True)

    def record(fn, *a):
        lst = []
        orig = p.add
        p.add = lambda *aa, **kk: lst.append((aa, kk))
        try:
            fn(*a)
        finally:
            p.add = orig
        return lst

    def merged(lists):
        lists = [l for l in lists if l]
        tot = max(len(l) for l in lists) if lists else 0
        pos = [0] * len(lists)
        for step in range(1, tot + 1):
            for li, l in enumerate(lists):
                tgt = (step * len(l) + tot - 1) // tot
                while pos[li] < tgt:
                    aa, kk = l[pos[li]]
                    p.add(*aa, **kk)
                    pos[li] += 1

    setup(0)
    setup(1)
    mainA(0)
    for mt in range(6):
        ls = []
        if mt + 1 < 6:
            ls.append(record(mainA, mt + 1))
        ls.append(record(mainB, mt))
        if mt + 2 < 6:
            ls.append(record(setup, mt + 2))
        merged(ls)
    return kb.finish([r_y])


def s5v3_emit(kb, sfx, NC, L=16):
    nc, p = kb.nc, kb.p
    G4 = L // 4
    uR, r_u = kb.din("uR4" + sfx, [6, 128, G4, NC], BF16)
    prm, r_prm = kb.din("prm" + sfx, [128, 3, 6])
    Bm, r_Bm = kb.din("Bm" + sfx, [128, 6, 2, 32])
    Cm, r_Cm = kb.din("Cm" + sfx, [128, 6, 2, 32])
    dsk, r_dsk = kb.din("dsk4" + sfx, [128, 6])
    idq, r_idq = kb.din("idn4q" + sfx, [128, 4, 32])
    idn, r_idn = kb.din("idn" + sfx, [128, 128])
    yR, r_y = kb.dout("yR" + sfx, [6, 32, L, NC])
    us, r_us = kb.sb([128, 6, G4, NC], BF16)
    pr, r_pr = kb.sb([128, 3, 6], F32)
    Bs, r_Bs = kb.sb([128, 6, 2, 32], F32)
    Cs, r_Cs = kb.sb([128, 6, 3, 32], F32)
    ds, r_ds = kb.sb([128, 6], F32)
    iq, r_iq = kb.sb([128, 4, 32], F32)
    ident, r_id = kb.sb([128, 128], F32)
    w_, r_w = kb.sb([128, 16, 6], F32)
    ki, r_sc = kb.sb([128, 8], mybir.dt.int32)
    kf, _ = kb.sb([128, 8], F32)
    tq, _ = kb.sb([128, 8], F32)
    bb, r_bb = kb.sb([128, 6, 2, 32], F32)
    pw, r_pw = kb.sb([128, L + 2, 2, 6], F32)
    NST = max(1, (NC - 1).bit_length())
    dq, r_dq = kb.sb([128, NST, 3, 6], F32)
    t6, r_t6 = kb.sb([128, 4, 6], F32)
    for dst, src, r, eng in ((us, uR.rearrange("m p g n -> p m g n"), r_us, "sp"), (iq, idq, r_iq, "act"), (pr, prm, r_pr, "act"), (Bs, Bm, r_Bs, "sp"),
                             (ds, dsk, r_ds, "sp"), (ident, idn, r_id, "sp")):
        p.add(eng, lambda e, dst=dst, src=src: e.dma_start(out=dst[:], in_=src), writes=[r], dma=True)
    p.add("act", lambda e: e.dma_start(out=Cs[:, :, 0:2, :], in_=Cm), writes=[r_Cs], dma=True)
    dve = lambda fn, rd, wr: p.add("dve", fn, reads=rd, writes=wr)
    act = lambda fn, rd, wr: p.add("act", fn, reads=rd, writes=wr)
    s5_prep(kb, pr, r_pr, w_, r_w, ki[:, 0:6], kf[:, 0:6], tq[:, 0:6], r_sc)
    act(lambda e: e.activation(out=Cs[:, :, 2, :], in_=Cs[:, :, 1, :], func=AF.Copy, scale=-1.0), [r_Cs], [r_Cs])
    abr, abi = w_[:, 7, :], w_[:, 8, :]

    def cmul(o_re, o_im, a_re, a_im, b_re, b_im, rd, wr):
        dve(lambda e: e.tensor_tensor(out=t6[:, 0, :], in0=a_re, in1=b_re, op=ALU.mult), rd, [r_t6])
        dve(lambda e: e.tensor_tensor(out=t6[:, 1, :], in0=a_im, in1=b_im, op=ALU.mult), rd, [r_t6])
        dve(lambda e: e.tensor_tensor(out=t6[:, 2, :], in0=a_re, in1=b_im, op=ALU.mult), rd, [r_t6])
        dve(lambda e: e.tensor_tensor(out=t6[:, 3, :], in0=a_im, in1=b_re, op=ALU.mult), rd, [r_t6])
        dve(lambda e: e.tensor_tensor(out=o_re, in0=t6[:, 0, :], in1=t6[:, 1, :], op=ALU.subtract), [r_t6], wr)
        dve(lambda e: e.tensor_tensor(out=o_im, in0=t6[:, 2, :], in1=t6[:, 3, :], op=ALU.add), [r_t6], wr)

    p.add("pool", lambda e: e.memset(pw[:, 0, 0, :], 1.0), writes=[r_pw])
    p.add("pool", lambda e: e.memset(pw[:, 0, 1, :], 0.0), writes=[r_pw])
    for j in range(1, L + 2):
        cmul(pw[:, j, 0, :], pw[:, j, 1, :], pw[:, j - 1, 0, :], pw[:, j - 1, 1, :], abr, abi, [r_pw, r_w], [r_pw])
    dve(lambda e: e.tensor_copy(out=dq[:, 0, 0:2, :], in_=pw[:, L, :, :]), [r_pw], [r_dq])
    for i in range(1, NST):
        cmul(dq[:, i, 0, :], dq[:, i, 1, :], dq[:, i - 1, 0, :], dq[:, i - 1, 1, :], dq[:, i - 1, 0, :], dq[:, i - 1, 1, :], [r_dq], [r_dq])
    dve(lambda e: e.tensor_scalar(out=dq[:, :, 2, :], in0=dq[:, :, 1, :], scalar1=-1.0, scalar2=None, op0=ALU.mult), [r_dq], [r_dq])
    for mt in range(6):
        fr, fi = w_[:, 10, mt:mt + 1], w_[:, 11, mt:mt + 1]
        dve(lambda e, mt=mt, fi=fi: e.tensor_scalar(out=bb[:, mt, 0, :], in0=Bs[:, mt, 1, :], scalar1=fi, scalar2=None, op0=ALU.mult), [r_Bs, r_w], [r_bb])
        dve(lambda e, mt=mt, fr=fr: e.scalar_tensor_tensor(out=bb[:, mt, 0, :], in0=Bs[:, mt, 0, :], scalar=fr, in1=bb[:, mt, 0, :], op0=ALU.mult, op1=ALU.subtract), [r_Bs, r_w, r_bb], [r_bb])
        dve(lambda e, mt=mt, fi=fi: e.tensor_scalar(out=bb[:, mt, 1, :], in0=Bs[:, mt, 0, :], scalar1=fi, scalar2=None, op0=ALU.mult), [r_Bs, r_w], [r_bb])
        dve(lambda e, mt=mt, fr=fr: e.scalar_tensor_tensor(out=bb[:, mt, 1, :], in0=Bs[:, mt, 1, :], scalar=fr, in1=bb[:, mt, 1, :], op0=ALU.mult, op1=ALU.add), [r_Bs, r_w, r_bb], [r_bb])
    base = []
    acc_ = 0
    for r in range(L):
        base.append(acc_)
        acc_ += r // 4 + 1
    NK = acc_
    NKB = -(-NK // 16)
    BbS = [kb.sb([128, 2, L + 4, 32], F32) for _ in range(2)]
    Win = [kb.sb([128, 2, G4, 128], BF16) for _ in range(2)]
    CRI = [kb.sb([128, L, 2, 32], BF16) for _ in range(3)]
    Kt = [kb.sb([128, NKB * 16, 32], BF16) for _ in range(3)]
    KSf, r_KSf = kb.sb([128, NKB * 16, 32], F32)
    dqt, r_dqt = kb.sb([128, 4, 32], F32)
    dqh, r_dqh = kb.sb([128, 4, 32], F32)
    Dhl = [kb.sb([128, 2, 4, 32], BF16) for _ in range(3)]
    tsm, r_tsm = kb.sb([128, 2, 32], F32)
    XA = [kb.sb([128, 2, NC], F32) for _ in range(4)]
    Xps = [kb.sb([128, 2, NC], BF16) for _ in range(2)]
    yo = [kb.sb([32, NC], F32) for _ in range(3)]
    ptr = [kb.ps() for _ in range(2)]
    pz = [kb.ps() for _ in range(1)]
    pyb = [kb.ps() for _ in range(1)]
    for bbs_, r_b_ in BbS:
        p.add("pool", lambda e, bbs_=bbs_: e.memset(bbs_[:, :, L:L + 4, :], 0.0), writes=[r_b_])
    npc = -(-NC // 512)
    psz = -(-NC // npc)
    pieces = [(s0, min(psz, NC - s0)) for s0 in range(0, NC, psz)]
    cnt_box = [0]

    def setup(mt):
        bbs, r_bbs = BbS[mt % 2]
        win, r_win = Win[mt % 2]
        cri, r_cri = CRI[mt % 3]
        kt, r_kt = Kt[mt % 3]
        for j in range(L):
            pre_, pim_ = pw[:, j, 0, mt:mt + 1], pw[:, j, 1, mt:mt + 1]
            dve(lambda e, pim_=pim_, mt=mt: e.tensor_scalar(out=tsm[:, 0, :], in0=bb[:, mt, 1, :], scalar1=pim_, scalar2=None, op0=ALU.mult), [r_bb, r_pw], [r_tsm])
            dve(lambda e, pre_=pre_, mt=mt, j=j, bbs=bbs: e.scalar_tensor_tensor(out=bbs[:, 0, L - 1 - j, :], in0=bb[:, mt, 0, :], scalar=pre_, in1=tsm[:, 0, :], op0=ALU.mult, op1=ALU.subtract), [r_bb, r_pw, r_tsm], [r_bbs])
            dve(lambda e, pim_=pim_, mt=mt: e.tensor_scalar(out=tsm[:, 1, :], in0=bb[:, mt, 0, :], scalar1=pim_, scalar2=None, op0=ALU.mult), [r_bb, r_pw], [r_tsm])
            dve(lambda e, pre_=pre_, mt=mt, j=j, bbs=bbs: e.scalar_tensor_tensor(out=bbs[:, 1, L - 1 - j, :], in0=bb[:, mt, 1, :], scalar=pre_, in1=tsm[:, 1, :], op0=ALU.mult, op1=ALU.add), [r_bb, r_pw, r_tsm], [r_bbs])
        for ri in range(2):
            pt, r_pt = ptr[ri]
            for g in range(G4):
                p.add("pe", lambda e, ri=ri, g=g, pt=pt, bbs=bbs: e.transpose(pt[:, g * 128:(g + 1) * 128], bbs[:, ri, 4 * g:4 * g + 4, :].rearrange("p a b -> p (a b)"), ident[:, :]), reads=[r_bbs, r_id], writes=[r_pt])
            act(lambda e, ri=ri, pt=pt, win=win: e.activation(out=win[:, ri, :, :], in_=pt[:, 0:G4 * 128].rearrange("p (a b) -> p a b", a=G4), func=AF.Copy), [r_pt], [r_win])
        dve(lambda e, mt=mt: e.tensor_scalar(out=dqt[:, :, :], in0=iq[:, :, :], scalar1=ds[:, mt:mt + 1], scalar2=None, op0=ALU.mult), [r_iq, r_ds], [r_dqt])
        rg = [(r, g) for r in range(L) for g in range(r // 4 + 1)]
        for bk in range(NKB):
            pk, r_pk = ptr[bk % 2]
            for (r, g) in rg[bk * 16:(bk + 1) * 16]:
                idx = base[r] + g
                sl = slice((idx % 16) * 32, (idx % 16) * 32 + 32)
                jj0 = L - 1 - r + 4 * g
                p.add("pe", lambda e, jj0=jj0, sl=sl, pk=pk, bbs=bbs, mt=mt: e.matmul(pk[:, sl], bbs[:, 0, jj0:jj0 + 4, :].rearrange("p a b -> p (a b)"), Cs[:, mt, 0, :], start=True, stop=False), reads=[r_bbs, r_Cs], writes=[r_pk])
                p.add("pe", lambda e, jj0=jj0, sl=sl, pk=pk, bbs=bbs, mt=mt: e.matmul(pk[:, sl], bbs[:, 1, jj0:jj0 + 4, :].rearrange("p a b -> p (a b)"), Cs[:, mt, 2, :], start=False, stop=True), reads=[r_bbs, r_Cs], writes=[r_pk])
            nv = min(16, NK - bk * 16)
            act(lambda e, bk=bk, pk=pk, nv=nv: e.activation(out=KSf[:, bk * 16:bk * 16 + nv, :], in_=pk[:, 0:nv * 32].rearrange("p (a b) -> p a b", a=nv), func=AF.Copy), [r_pk], [r_KSf])
        dhl, r_dhl = Dhl[mt % 3]
        dve(lambda e, dhl=dhl: e.tensor_copy(out=dhl[:, 0, :, :], in_=dqt[:, :, :]), [r_dqt], [r_dhl])
        dve(lambda e, dhl=dhl: e.tensor_copy(out=dqh[:, :, :], in_=dhl[:, 0, :, :]), [r_dhl], [r_dqh])
        dve(lambda e, dhl=dhl: e.tensor_tensor(out=dhl[:, 1, :, :], in0=dqt[:, :, :], in1=dqh[:, :, :], op=ALU.subtract), [r_dqt, r_dqh], [r_dhl])
        act(lambda e, kt=kt: e.activation(out=kt[:, 0:NK, :], in_=KSf[:, 0:NK, :], func=AF.Copy), [r_KSf], [r_kt])
        for r in range(L):
            pre_, pim_ = pw[:, r + 1, 0, mt:mt + 1], pw[:, r + 1, 1, mt:mt + 1]
            dve(lambda e, pim_=pim_, mt=mt: e.tensor_scalar(out=tsm[:, 0, :], in0=Cs[:, mt, 2, :], scalar1=pim_, scalar2=None, op0=ALU.mult), [r_Cs, r_pw], [r_tsm])
            dve(lambda e, pre_=pre_, mt=mt, r=r, cri=cri: e.scalar_tensor_tensor(out=cri[:, r, 0, :], in0=Cs[:, mt, 0, :], scalar=pre_, in1=tsm[:, 0, :], op0=ALU.mult, op1=ALU.add), [r_Cs, r_pw, r_tsm], [r_cri])
            dve(lambda e, pim_=pim_, mt=mt: e.tensor_scalar(out=tsm[:, 1, :], in0=Cs[:, mt, 0, :], scalar1=pim_, scalar2=-1.0, op0=ALU.mult, op1=ALU.mult), [r_Cs, r_pw], [r_tsm])
            dve(lambda e, pre_=pre_, mt=mt, r=r, cri=cri: e.scalar_tensor_tensor(out=cri[:, r, 1, :], in0=Cs[:, mt, 2, :], scalar=pre_, in1=tsm[:, 1, :], op0=ALU.mult, op1=ALU.add), [r_Cs, r_pw, r_tsm], [r_cri])

    def mainA(mt):
        bbs, r_bbs = BbS[mt % 2]
        win, r_win = Win[mt % 2]
        cri, r_cri = CRI[mt % 3]
        kt, r_kt = Kt[mt % 3]
        xa, r_xa = XA[2 * (mt % 2)]
        xb_, r_xb = XA[2 * (mt % 2) + 1]
        Xp, r_Xp = Xps[mt % 2]
        for (s0, n) in pieces:
            for ri in range(2):
                pzz, r_pz = pz[0]
                for g in range(G4):
                    p.add("pe", lambda e, g=g, ri=ri, s0=s0, n=n, pzz=pzz, win=win, mt=mt: e.matmul(pzz[:, 0:n], win[:, ri, g, :], us[:, mt, g, s0:s0 + n], start=(g == 0), stop=(g == G4 - 1)), reads=[r_win, r_us], writes=[r_pz])
                act(lambda e, ri=ri, s0=s0, n=n, pzz=pzz, xa=xa: e.activation(out=xa[:, ri, s0:s0 + n], in_=pzz[:, 0:n], func=AF.Copy), [r_pz], [r_xa])
        cur, r_cur, nxt, r_nxt = xa, r_xa, xb_, r_xb
        for i in range(NST):
            d = 1 << i
            if d >= NC:
                break
            dre, dim_, ndim = dq[:, i, 0, mt:mt + 1], dq[:, i, 1, mt:mt + 1], dq[:, i, 2, mt:mt + 1]
            act(lambda e, d=d, cur=cur, nxt=nxt: e.activation(out=nxt[:, :, 0:d], in_=cur[:, :, 0:d], func=AF.Copy), [r_cur], [r_nxt])
            dve(lambda e, d=d, cur=cur, nxt=nxt, dre=dre: e.scalar_tensor_tensor(out=nxt[:, 0, d:NC], in0=cur[:, 0, 0:NC - d], scalar=dre, in1=cur[:, 0, d:NC], op0=ALU.mult, op1=ALU.add), [r_cur, r_dq], [r_nxt])
            dve(lambda e, d=d, cur=cur, nxt=nxt, dre=dre: e.scalar_tensor_tensor(out=nxt[:, 1, d:NC], in0=cur[:, 1, 0:NC - d], scalar=dre, in1=cur[:, 1, d:NC], op0=ALU.mult, op1=ALU.add), [r_cur, r_dq], [r_nxt])
            dve(lambda e, d=d, cur=cur, nxt=nxt, ndim=ndim: e.scalar_tensor_tensor(out=nxt[:, 0, d:NC], in0=cur[:, 1, 0:NC - d], scalar=ndim, in1=nxt[:, 0, d:NC], op0=ALU.mult, op1=ALU.add), [r_cur, r_dq, r_nxt], [r_nxt])
            dve(lambda e, d=d, cur=cur, nxt=nxt, dim_=dim_: e.scalar_tensor_tensor(out=nxt[:, 1, d:NC], in0=cur[:, 0, 0:NC - d], scalar=dim_, in1=nxt[:, 1, d:NC], op0=ALU.mult, op1=ALU.add), [r_cur, r_dq, r_nxt], [r_nxt])
            cur, r_cur, nxt, r_nxt = nxt, r_nxt, cur, r_cur
        p.add("pool", lambda e: e.memset(Xp[:, :, 0:1], 0.0), writes=[r_Xp])
        act(lambda e, cur=cur: e.activation(out=Xp[:, :, 1:NC], in_=cur[:, :, 0:NC - 1], func=AF.Copy), [r_cur], [r_Xp])

    def mainB(mt):
        cri, r_cri = CRI[mt % 3]
        kt, r_kt = Kt[mt % 3]
        dhl, r_dhl = Dhl[mt % 3]
        Xp, r_Xp = Xps[mt % 2]
        for r in range(L):
            yob, r_yo = yo[cnt_box[0] % 3]
            for pi_, (s0, n) in enumerate(pieces):
                pb_, r_pb = pyb[0]
                for ri in range(2):
                    p.add("pe", lambda e, r=r, ri=ri, s0=s0, n=n, pb_=pb_, cri=cri: e.matmul(pb_[0:32, 0:n], cri[:, r, ri, :], Xp[:, ri, s0:s0 + n], start=(ri == 0), stop=False), reads=[r_cri, r_Xp], writes=[r_pb])
                ng = r // 4 + 1
                for g in range(ng):
                    p.add("pe", lambda e, r=r, g=g, ng=ng, s0=s0, n=n, pb_=pb_, kt=kt, mt=mt: e.matmul(pb_[0:32, 0:n], kt[:, base[r] + g, :], us[:, mt, g, s0:s0 + n], start=False, stop=False), reads=[r_kt, r_us], writes=[r_pb])
                for hl in range(2):
                    p.add("pe", lambda e, r=r, hl=hl, s0=s0, n=n, pb_=pb_, dhl=dhl, mt=mt: e.matmul(pb_[0:32, 0:n], dhl[:, hl, r % 4, :], us[:, mt, r // 4, s0:s0 + n], start=False, stop=(hl == 1)), reads=[r_dhl, r_us], writes=[r_pb])
                act(lambda e, s0=s0, n=n, pb_=pb_, yob=yob: e.activation(out=yob[:, s0:s0 + n], in_=pb_[0:32, 0:n], func=AF.Copy), [r_pb], [r_yo])
            cnt_box[0] += 1
            p.add("sp", lambda e, mt=mt, r=r, yob=yob: e.dma_start(out=yR[mt, :, r, :], in_=yob[:, :]), reads=[r_yo], writes=[r_y], dma=True)

    def record(fn, *a):
        lst = []
        orig = p.add
        p.add = lambda *aa, **kk: lst.append((aa, kk))
        try:
            fn(*a)
        finally:
            p.add = orig
        return lst

    def merged(lists):
        lists = [l for l in lists if l]
        tot = max(len(l) for l in lists) if lists else 0
        pos = [0] * len(lists)
        for step in range(1, tot + 1):
            for li, l in enumerate(lists):
                tgt = (step * len(l) + tot - 1) // tot
                while pos[li] < tgt:
                    aa, kk = l[pos[li]]
                    p.add(*aa, **kk)
                    pos[li] += 1

    setup(0)
    setup(1)
    mainA(0)
    for mt in range(6):
        ls = []
        if mt + 1 < 6:
            ls.append(record(mainA, mt + 1))
        ls.append(record(mainB, mt))
        if mt + 2 < 6:
            ls.append(record(setup, mt + 2))
        merged(ls)
    return r_y


def build_S5pair(NC, L=16):
    kb = KB()
    p = kb.p
    outs = []
    lists = []
    for sfx in ("_f", "_b"):
        lst = []
        orig = p.add
        p.add = lambda *aa, **kk: lst.append((aa, kk))
        try:
            outs.append(s5v3_emit(kb, sfx, NC, L))
        finally:
            p.add = orig
        lists.append(lst)
    tot = max(len(l) for l in lists)
    pos = [0, 0]
    for step in range(1, tot + 1):
        for li, l in enumerate(lists):
            tgt = (step * len(l) + tot - 1) // tot
            while pos[li] < tgt:
                aa, kk = l[pos[li]]
                p.add(*aa, **kk)
                pos[li] += 1
    return kb.finish(outs)


def build_S5(N, NB=512):
    kb = KB()
    nc, p = kb.nc, kb.p
    NBLK = N // NB
    uT, r_u = kb.din("uT", [6, 32, N], BF16)
    prm, r_prm = kb.din("prm", [128, 3, 6])
    Bm, r_Bm = kb.din("Bm", [128, 6, 2, 32])
    Cm, r_Cm = kb.din("Cm", [128, 6, 2, 32])
    dsk, r_dsk = kb.din("dsk", [32, 6])
    tix, r_tix = kb.din("tix", [128, NB])
    idn, r_idn = kb.din("idn", [128, 128])
    yT, r_y = kb.dout("yT", [6, 32, N])
    us, r_us = kb.sb([32, 6, N], BF16)
    pr, r_pr = kb.sb([128, 3, 6], F32)
    Bs, r_Bs = kb.sb([128, 6, 2, 32], F32)
    Cs, r_Cs = kb.sb([128, 6, 2, 32], F32)
    Cb, r_Cb = kb.sb([128, 6, 3, 32], BF16)
    ds, r_ds = kb.sb([32, 6], F32)
    tx, r_tx = kb.sb([128, NB], F32)
    ident, r_id = kb.sb([128, 128], F32)
    w_, r_w = kb.sb([128, 16, 6], F32)
    bb, r_bb = kb.sb([128, 6, 2, 32], F32)
    BT, r_BT = kb.sb([32, 6, 2, 128], BF16)
    cosT, r_cos = kb.sb([128, 6, NB], F32)
    sinT, r_sin = kb.sb([128, 6, NB], F32)
    rtab, r_rt = kb.sb([128, 6, NB], F32)
    wre, r_wre = kb.sb([128, 6, NB], F32)
    wim, r_wim = kb.sb([128, 6, NB], F32)
    ini, r_ini = kb.sb([128, 2, 6], F32)
    tmpc, r_tmpc = kb.sb([128, 4, 6], F32)
    tt_, r_tt = kb.sb([128, 4, NB], F32)
    yo = [kb.sb([32, 6, NB], F32) for _ in range(2)]
    pbu = [kb.ps() for _ in range(4)]
    py = [kb.ps() for _ in range(2)]
    ptr, r_ptr = kb.ps()
    ki, r_sc = kb.sb([128, NB], mybir.dt.int32)
    kf, _ = kb.sb([128, NB], F32)
    tq, _ = kb.sb([128, NB], F32)
    for dst, src, r, eng in ((us, uT.rearrange("m p n -> p m n"), r_us, "sp"), (pr, prm, r_pr, "act"), (Bs, Bm, r_Bs, "sp"), (Cs, Cm, r_Cs, "act"),
                             (ds, dsk, r_ds, "sp"), (tx, tix, r_tx, "act"), (ident, idn, r_id, "sp")):
        p.add(eng, lambda e, dst=dst, src=src: e.dma_start(out=dst[:], in_=src), writes=[r], dma=True)
    W = lambda i: w_[:, i, :]
    are, aim, ldt = pr[:, 0, :], pr[:, 1, :], pr[:, 2, :]
    dve = lambda fn, rd, wr: p.add("dve", fn, reads=rd, writes=wr)
    act = lambda fn, rd, wr: p.add("act", fn, reads=rd, writes=wr)
    act(lambda e: e.activation(out=W(0), in_=ldt, func=AF.Exp), [r_pr], [r_w])
    dve(lambda e: e.tensor_tensor(out=W(1), in0=are, in1=W(0), op=ALU.mult), [r_pr, r_w], [r_w])
    act(lambda e: e.activation(out=W(1), in_=W(1), func=AF.Exp), [r_w], [r_w])
    dve(lambda e: e.tensor_tensor(out=W(2), in0=aim, in1=W(0), op=ALU.mult), [r_pr, r_w], [r_w])
    range_reduce(kb, W(3), r_w, W(2), r_w, 0.0, ki[:, 0:6], kf[:, 0:6], tq[:, 0:6], r_sc)
    range_reduce(kb, W(4), r_w, W(2), r_w, 0.5 * PI, ki[:, 0:6], kf[:, 0:6], tq[:, 0:6], r_sc)
    act(lambda e: e.activation(out=W(5), in_=W(3), func=AF.Sin), [r_w], [r_w])
    act(lambda e: e.activation(out=W(6), in_=W(4), func=AF.Sin), [r_w], [r_w])
    dve(lambda e: e.tensor_tensor(out=W(7), in0=W(1), in1=W(6), op=ALU.mult), [r_w], [r_w])
    dve(lambda e: e.tensor_tensor(out=W(8), in0=W(1), in1=W(5), op=ALU.mult), [r_w], [r_w])
    dve(lambda e: e.tensor_tensor(out=W(13), in0=are, in1=are, op=ALU.mult), [r_pr], [r_w])
    dve(lambda e: e.tensor_tensor(out=W(14), in0=aim, in1=aim, op=ALU.mult), [r_pr], [r_w])
    dve(lambda e: e.tensor_tensor(out=W(9), in0=W(13), in1=W(14), op=ALU.add), [r_w], [r_w])
    dve(lambda e: e.reciprocal(out=W(9), in_=W(9)), [r_w], [r_w])
    dve(lambda e: e.tensor_scalar(out=W(12), in0=W(7), scalar1=-1.0, scalar2=None, op0=ALU.add), [r_w], [r_w])
    dve(lambda e: e.tensor_tensor(out=W(13), in0=W(12), in1=are, op=ALU.mult), [r_w, r_pr], [r_w])
    dve(lambda e: e.tensor_tensor(out=W(14), in0=W(8), in1=aim, op=ALU.mult), [r_w, r_pr], [r_w])
    dve(lambda e: e.tensor_tensor(out=W(10), in0=W(13), in1=W(14), op=ALU.add), [r_w], [r_w])
    dve(lambda e: e.tensor_tensor(out=W(10), in0=W(10), in1=W(9), op=ALU.mult), [r_w], [r_w])
    dve(lambda e: e.tensor_tensor(out=W(13), in0=W(8), in1=are, op=ALU.mult), [r_w, r_pr], [r_w])
    dve(lambda e: e.tensor_tensor(out=W(14), in0=W(12), in1=aim, op=ALU.mult), [r_w, r_pr], [r_w])
    dve(lambda e: e.tensor_tensor(out=W(11), in0=W(13), in1=W(14), op=ALU.subtract), [r_w], [r_w])
    dve(lambda e: e.tensor_tensor(out=W(11), in0=W(11), in1=W(9), op=ALU.mult), [r_w], [r_w])
    dve(lambda e: e.tensor_scalar(out=W(15), in0=W(2), scalar1=float(NB), scalar2=None, op0=ALU.mult), [r_w], [r_w])
    range_reduce(kb, tmpc[:, 0, :], r_tmpc, W(15), r_w, 0.0, ki[:, 0:6], kf[:, 0:6], tq[:, 0:6], r_sc)
    range_reduce(kb, tmpc[:, 1, :], r_tmpc, W(15), r_w, 0.5 * PI, ki[:, 0:6], kf[:, 0:6], tq[:, 0:6], r_sc)
    act(lambda e: e.activation(out=tmpc[:, 2, :], in_=tmpc[:, 0, :], func=AF.Sin), [r_tmpc], [r_tmpc])
    act(lambda e: e.activation(out=tmpc[:, 3, :], in_=tmpc[:, 1, :], func=AF.Sin), [r_tmpc], [r_tmpc])
    act(lambda e: e.activation(out=Cb[:, :, 0, :], in_=Cs[:, :, 0, :], func=AF.Copy), [r_Cs], [r_Cb])
    act(lambda e: e.activation(out=Cb[:, :, 1, :], in_=Cs[:, :, 1, :], func=AF.Copy, scale=-1.0), [r_Cs], [r_Cb])
    act(lambda e: e.activation(out=Cb[:, :, 2, :], in_=Cs[:, :, 0, :], func=AF.Copy, scale=-1.0), [r_Cs], [r_Cb])
    for mt in range(6):
        fr, fi = w_[:, 10, mt:mt + 1], w_[:, 11, mt:mt + 1]
        dve(lambda e, mt=mt, fi=fi: e.tensor_scalar(out=bb[:, mt, 0, :], in0=Bs[:, mt, 1, :], scalar1=fi, scalar2=None, op0=ALU.mult), [r_Bs, r_w], [r_bb])
        dve(lambda e, mt=mt, fr=fr: e.scalar_tensor_tensor(out=bb[:, mt, 0, :], in0=Bs[:, mt, 0, :], scalar=fr, in1=bb[:, mt, 0, :], op0=ALU.mult, op1=ALU.subtract), [r_Bs, r_w, r_bb], [r_bb])
        dve(lambda e, mt=mt, fi=fi: e.tensor_scalar(out=bb[:, mt, 1, :], in0=Bs[:, mt, 0, :], scalar1=fi, scalar2=None, op0=ALU.mult), [r_Bs, r_w], [r_bb])
        dve(lambda e, mt=mt, fr=fr: e.scalar_tensor_tensor(out=bb[:, mt, 1, :], in0=Bs[:, mt, 1, :], scalar=fr, in1=bb[:, mt, 1, :], op0=ALU.mult, op1=ALU.add), [r_Bs, r_w, r_bb], [r_bb])
        for ri in range(2):
            p.add("pe", lambda e, mt=mt, ri=ri: e.transpose(ptr[0:32, 0:128], bb[:, mt, ri, :], ident[:, :]), reads=[r_bb, r_id], writes=[r_ptr])
            act(lambda e, mt=mt, ri=ri: e.activation(out=BT[:, mt, ri, :], in_=ptr[0:32, 0:128], func=AF.Copy), [r_ptr], [r_BT])
        ang = w_[:, 2, mt:mt + 1]
        dve(lambda e, ang=ang: e.tensor_scalar(out=tt_[:, 2, :], in0=tx[:, :], scalar1=ang, scalar2=None, op0=ALU.mult), [r_tx, r_w], [r_tt])
        range_reduce(kb, tt_[:, 0, :], r_tt, tt_[:, 2, :], r_tt, 0.0, ki[:, :], kf[:, :], tq[:, :], r_sc)
        range_reduce(kb, tt_[:, 1, :], r_tt, tt_[:, 2, :], r_tt, 0.5 * PI, ki[:, :], kf[:, :], tq[:, :], r_sc)
        act(lambda e, mt=mt: e.activation(out=sinT[:, mt, :], in_=tt_[:, 0, :], func=AF.Sin), [r_tt], [r_sin])
        act(lambda e, mt=mt: e.activation(out=cosT[:, mt, :], in_=tt_[:, 1, :], func=AF.Sin), [r_tt], [r_cos])
        act(lambda e, mt=mt: e.activation(out=rtab[:, mt, :], in_=tx[:, :], func=AF.Identity, scale=0.0, bias=w_[:, 1, mt:mt + 1]), [r_tx, r_w], [r_rt])
    p.add("pool", lambda e: e.memset(ini[:], 0.0), writes=[r_ini])
    pool = lambda fn, rd, wr: p.add(S5_POOL_ENG, fn, reads=rd, writes=wr)
    busb = [kb.sb([128, 2, NB], F32) for _ in range(2)]
    tD = [kb.sb([128, 2, NB], F32) for _ in range(2)]
    tP = [kb.sb([128, 2, NB], F32) for _ in range(2)]
    bpr = [kb.sb([128, NB], F32) for _ in range(2)]
    bpi = [kb.sb([128, NB], F32) for _ in range(2)]
    xq = [kb.sb([128, 4, NB], BF16) for _ in range(2)]
    r_wre_m = [p.res("wre%d" % i) for i in range(6)]
    r_wim_m = [p.res("wim%d" % i) for i in range(6)]
    r_yo_m = [[p.res("yo") for _ in range(6)] for _ in range(2)]

    def iter_ops(blk, mt, k):
        ops = []
        rec = lambda eng, fn, rd, wr: ops.append((eng, fn, rd, wr))
        bsl = slice(blk * NB, (blk + 1) * NB)
        yob, _ = yo[blk % 2]
        r_yo = r_yo_m[blk % 2][mt]
        pyb, r_py = py[k % 2]
        pre, r_pre = pbu[(2 * k) % 4]
        pim, r_pim = pbu[(2 * k + 1) % 4]
        bs_, r_bs_ = busb[k % 2]
        td, r_td = tD[k % 2]
        tp, r_tp = tP[k % 2]
        br_, r_br = bpr[k % 2]
        bi_, r_bi = bpi[k % 2]
        xx, r_xx = xq[k % 2]
        r_wre, r_wim = r_wre_m[mt], r_wim_m[mt]
        c_, s_ = cosT[:, mt, :], sinT[:, mt, :]
        rec("pe", lambda e: e.matmul(pre[:, 0:NB], BT[:, mt, 0, :], us[:, mt, bsl], start=True, stop=True), [r_BT, r_us], [r_pre])
        rec("pe", lambda e: e.matmul(pim[:, 0:NB], BT[:, mt, 1, :], us[:, mt, bsl], start=True, stop=True), [r_BT, r_us], [r_pim])
        rec("act", lambda e: e.activation(out=bs_[:, 0, :], in_=pre[:, 0:NB], func=AF.Copy), [r_pre], [r_bs_])
        rec("act", lambda e: e.activation(out=bs_[:, 1, :], in_=pim[:, 0:NB], func=AF.Copy), [r_pim], [r_bs_])
        rec("dve", lambda e: e.tensor_tensor(out=td[:, 0, :], in0=bs_[:, 0, :], in1=c_, op=ALU.mult), [r_bs_, r_cos], [r_td])
        rec("dve", lambda e: e.tensor_tensor(out=td[:, 1, :], in0=bs_[:, 1, :], in1=s_, op=ALU.mult), [r_bs_, r_sin], [r_td])
        rec("dve", lambda e: e.tensor_tensor(out=br_[:, :], in0=td[:, 0, :], in1=td[:, 1, :], op=ALU.add), [r_td], [r_br])
        rec(S5_POOL_ENG, lambda e: e.tensor_tensor(out=tp[:, 0, :], in0=bs_[:, 1, :], in1=c_, op=ALU.mult), [r_bs_, r_cos], [r_tp])
        rec(S5_POOL_ENG, lambda e: e.tensor_tensor(out=tp[:, 1, :], in0=bs_[:, 0, :], in1=s_, op=ALU.mult), [r_bs_, r_sin], [r_tp])
        rec(S5_POOL_ENG, lambda e: e.tensor_tensor(out=bi_[:, :], in0=tp[:, 0, :], in1=tp[:, 1, :], op=ALU.subtract), [r_tp], [r_bi])
        rec("dve", lambda e: e.tensor_tensor_scan(out=wre[:, mt, :], data0=rtab[:, mt, :], data1=br_[:, :], initial=ini[:, 0, mt:mt + 1], op0=ALU.mult, op1=ALU.add), [r_rt, r_br, r_ini], [r_wre])
        rec("dve", lambda e: e.tensor_tensor_scan(out=wim[:, mt, :], data0=rtab[:, mt, :], data1=bi_[:, :], initial=ini[:, 1, mt:mt + 1], op0=ALU.mult, op1=ALU.add), [r_rt, r_bi, r_ini], [r_wim])
        rec("dve", lambda e: e.tensor_tensor(out=xx[:, 0, :], in0=wre[:, mt, :], in1=c_, op=ALU.mult), [r_wre, r_cos], [r_xx])
        rec("dve", lambda e: e.tensor_tensor(out=xx[:, 1, :], in0=wim[:, mt, :], in1=s_, op=ALU.mult), [r_wim, r_sin], [r_xx])
        rec(S5_POOL_ENG, lambda e: e.tensor_tensor(out=xx[:, 2, :], in0=wim[:, mt, :], in1=c_, op=ALU.mult), [r_wim, r_cos], [r_xx])
        rec(S5_POOL_ENG, lambda e: e.tensor_tensor(out=xx[:, 3, :], in0=wre[:, mt, :], in1=s_, op=ALU.mult), [r_wre, r_sin], [r_xx])
        for qi, ci in ((0, 0), (1, 2), (2, 1), (3, 1)):
            rec("pe", lambda e, qi=qi, ci=ci: e.matmul(pyb[0:32, 0:NB], Cb[:, mt, ci, :], xx[:, qi, :], start=(qi == 0), stop=(qi == 3)), [r_Cb, r_xx], [r_py])
        rec("dve", lambda e: e.scalar_tensor_tensor(out=yob[:, mt, :], in0=us[:, mt, bsl], scalar=ds[:, mt:mt + 1], in1=pyb[0:32, 0:NB], op0=ALU.mult, op1=ALU.add), [r_us, r_ds, r_py], [r_yo])
        return ops

    k = 0
    r_wre_all = r_wre_m
    r_wim_all = r_wim_m
    for blk in range(NBLK):
        bsl = slice(blk * NB, (blk + 1) * NB)
        yob, _ = yo[blk % 2]
        for m0 in range(0, 6, 2):
            oa = iter_ops(blk, m0, k)
            ob_ = iter_ops(blk, m0 + 1, k + 1)
            k += 2
            for i in range(max(len(oa), len(ob_))):
                for lst in (oa, ob_):
                    if i < len(lst):
                        eng, fn, rd, wr = lst[i]
                        p.add(eng, fn, reads=rd, writes=wr)
        cN, sN = tmpc[:, 3, :], tmpc[:, 2, :]
        wlr, wli = wre[:, :, NB - 1], wim[:, :, NB - 1]
        dve(lambda e: e.tensor_tensor(out=tmpc[:, 0, :], in0=wlr, in1=cN, op=ALU.mult), r_wre_all + [r_tmpc], [r_tmpc])
        dve(lambda e: e.tensor_tensor(out=tmpc[:, 1, :], in0=wli, in1=sN, op=ALU.mult), r_wim_all + [r_tmpc], [r_tmpc])
        dve(lambda e: e.tensor_tensor(out=ini[:, 0, :], in0=tmpc[:, 0, :], in1=tmpc[:, 1, :], op=ALU.subtract), [r_tmpc], [r_ini])
        dve(lambda e: e.tensor_tensor(out=tmpc[:, 0, :], in0=wli, in1=cN, op=ALU.mult), r_wim_all + [r_tmpc], [r_tmpc])
        dve(lambda e: e.tensor_tensor(out=tmpc[:, 1, :], in0=wlr, in1=sN, op=ALU.mult), r_wre_all + [r_tmpc], [r_tmpc])
        dve(lambda e: e.tensor_tensor(out=ini[:, 1, :], in0=tmpc[:, 0, :], in1=tmpc[:, 1, :], op=ALU.add), [r_tmpc], [r_ini])
        p.add("sp", lambda e, bsl=bsl, yob=yob: e.dma_start(out=yT[:, :, bsl].rearrange("m p n -> p m n"), in_=yob[:, :, :]), reads=r_yo_m[blk % 2], writes=[r_y], dma=True)
    return kb.finish([r_y])


def build_L0():
    kb = KB()
    nc, p = kb.nc, kb.p
    adaw, r_aw = kb.din("adaw", [2, D, 768])
    adab, r_ab = kb.din("adab", [128, 2, 6])
    c3T, r_c = kb.din("c3T", [D, 3])
    modT, r_m = kb.dout("modT", [2, 768, 3])
    aw = [kb.sb([128, 16, 768], F32) for _ in range(2)]
    cs, r_cs = kb.sb([128, 16, 3], F32)
    sc, r_sc = kb.sb([128, 16, 3], F32)
    bs, r_bs = kb.sb([128, 2, 6], F32)
    ob, r_ob = kb.sb([128, 2, 6, 3], F32)
    pp = [kb.ps() for _ in range(2)]
    p.add("sp", lambda e: e.dma_start(out=cs[:], in_=c3T.rearrange("(a p) c -> p a c", p=128)), writes=[r_cs], dma=True)
    p.add("sp", lambda e: e.dma_start(out=bs[:], in_=adab), writes=[r_bs], dma=True)
    p.add("act", lambda e: e.activation(out=sc[:], in_=cs[:], func=AF.Silu), reads=[r_cs], writes=[r_sc])
    k = 0
    for l in range(2):
        awl, r_awl = aw[l]
        p.add("sp" if l == 0 else "act", lambda e, l=l, awl=awl: e.dma_start(out=awl[:], in_=adaw[l].rearrange("(a p) n -> p a n", p=128)), writes=[r_awl], dma=True)
        for j in range(6):
            pj, r_pj = pp[k % 2]
            k += 1
            for kt in range(16):
                p.add("pe", lambda e, kt=kt, j=j, awl=awl, pj=pj: e.matmul(pj[:, 0:3], awl[:, kt, j * 128:(j + 1) * 128], sc[:, kt, :], start=(kt == 0), stop=(kt == 15)), reads=[r_awl, r_sc], writes=[r_pj])
            p.add("act", lambda e, l=l, j=j, pj=pj: e.activation(out=ob[:, l, j, :], in_=pj[:, 0:3], func=AF.Identity, bias=bs[:, l, j:j + 1]), reads=[r_pj, r_bs], writes=[r_ob])
    p.add("sp", lambda e: e.dma_start(out=modT.rearrange("l (j p) c -> p l j c", p=128), in_=ob[:]), reads=[r_ob], writes=[r_m], dma=True)
    return kb.finish([r_m])


def build_LC1(NT=17):
    kb = KB()
    nc, p = kb.nc, kb.p
    T = NT * 128
    rf, r_rf = kb.din("rf", [T, 1024])
    rb, r_rb = kb.din("rb", [T, 1024])
    g, r_g = kb.din("g", [T, 1024], BF16)
    gnw, r_gn = kb.din("gnw", [128, 1024])
    ro, r_ro = kb.dout("r", [T, 1024], BF16)
    gw, r_gw = kb.sb([128, 1024], F32)
    p.add("sp", lambda e: e.dma_start(out=gw[:], in_=gnw), writes=[r_gw], dma=True)
    A = [kb.sb([128, 1024], F32) for _ in range(2)]
    B = [kb.sb([128, 1024], F32) for _ in range(2)]
    Gt = [kb.sb([128, 1024], BF16) for _ in range(2)]
    Ot = [kb.sb([128, 1024], BF16) for _ in range(2)]
    st, r_st = kb.sb([128, 4, 6], F32)
    mv, r_mv = kb.sb([128, 4, 2], F32)
    rs, r_rs = kb.sb([128, 4], F32)
    for tt in range(NT):
        b = tt % 2
        a_, r_a = A[b]; b_, r_b = B[b]; g_, r_g_ = Gt[b]; o_, r_o = Ot[b]
        tsl = slice(tt * 128, (tt + 1) * 128)
        p.add("sp", lambda e, a_=a_, tsl=tsl: e.dma_start(out=a_[:], in_=rf[tsl, :]), writes=[r_a], dma=True)
        p.add("act", lambda e, b_=b_, tsl=tsl: e.dma_start(out=b_[:], in_=rb[tsl, :]), writes=[r_b], dma=True)
        p.add("sp", lambda e, g_=g_, tsl=tsl: e.dma_start(out=g_[:], in_=g[tsl, :]), writes=[r_g_], dma=True)
        p.add("pool", lambda e, a_=a_, b_=b_: e.tensor_tensor(out=a_[:], in0=a_[:], in1=b_[:], op=ALU.add), reads=[r_a, r_b], writes=[r_a])
        for h in range(4):
            p.add("dve", lambda e, h=h, a_=a_: e.bn_stats(out=st[:, h, :], in_=a_[:, h * 256:(h + 1) * 256]), reads=[r_a], writes=[r_st])
            p.add("dve", lambda e, h=h: e.bn_aggr(out=mv[:, h, :], in_=st[:, h, :]), reads=[r_st], writes=[r_mv])
        p.add("act", lambda e: e.activation(out=rs[:, :], in_=mv[:, :, 1], func=AF.Sqrt, bias=EPS), reads=[r_mv], writes=[r_rs])
        p.add("dve", lambda e: e.reciprocal(out=rs[:, :], in_=rs[:, :]), reads=[r_rs], writes=[r_rs])
        for h in range(4):
            hs = slice(h * 256, (h + 1) * 256)
            p.add("dve", lambda e, h=h, hs=hs, a_=a_: e.tensor_scalar(out=a_[:, hs], in0=a_[:, hs], scalar1=mv[:, h, 0:1], scalar2=rs[:, h:h + 1], op0=ALU.subtract, op1=ALU.mult), reads=[r_a, r_mv, r_rs], writes=[r_a])
        p.add("pool", lambda e, a_=a_: e.tensor_tensor(out=a_[:], in0=a_[:], in1=gw[:], op=ALU.mult), reads=[r_a, r_gw], writes=[r_a])
        p.add("dve", lambda e, a_=a_, g_=g_, o_=o_: e.tensor_tensor(out=o_[:], in0=a_[:], in1=g_[:], op=ALU.mult), reads=[r_a, r_g_], writes=[r_o])
        p.add("sp", lambda e, o_=o_, tsl=tsl: e.dma_start(out=ro[tsl, :], in_=o_[:]), reads=[r_o], writes=[r_ro], dma=True)
    return kb.finish([r_ro])


def build_LC2a(NT=17, NB=256):
    kb = KB()
    nc, p = kb.nc, kb.p
    T = NT * 128
    rT, r_rT = kb.din("rT", [1024, T], BF16)
    aT, r_aT = kb.din("aT", [1024, T])
    agT, r_agT = kb.din("agT", [1024, T], BF16)
    yf, r_yf = kb.din("yf", [768, T])
    yb, r_yb = kb.din("yb", [768, T])
    sgT, r_sgT = kb.din("sgT", [768, T], BF16)
    mgT, r_mgT = kb.din("mgT", [6144, T], BF16)
    wr_, r_wr_ = kb.din("w_br_ret", [1024, D])
    wa_, r_wa_ = kb.din("w_br_att", [1024, D])
    ws_, r_ws_ = kb.din("w_br_s5", [768, D])
    gl_, r_gl_ = kb.din("glu_w", [768, 768])
    gb_, r_gb_ = kb.din("glu_b", [128, 6])
    mT, r_mT = kb.dout("mT", [D, T], BF16)
    wr, r_wr = kb.sb([128, 8, D], BF16)
    wa, r_wa = kb.sb([128, 8, D], BF16)
    ws, r_ws = kb.sb([128, 6, D], BF16)
    gl, r_gl = kb.sb([128, 6, 768], BF16)
    gb, r_gb = kb.sb([128, 6], F32)
    p.add("pool", lambda e: e.dma_start(out=wr[:], in_=wr_.rearrange("(a p) n -> p a n", p=128)), writes=[r_wr], dma=True)
    p.add("pool", lambda e: e.dma_start(out=wa[:], in_=wa_.rearrange("(a p) n -> p a n", p=128)), writes=[r_wa], dma=True)
    p.add("pool", lambda e: e.dma_start(out=ws[:], in_=ws_.rearrange("(a p) n -> p a n", p=128)), writes=[r_ws], dma=True)
    p.add("pool", lambda e: e.dma_start(out=gl[:], in_=gl_.rearrange("(a p) n -> p a n", p=128)), writes=[r_gl], dma=True)
    p.add("sp", lambda e: e.dma_start(out=gb[:], in_=gb_), writes=[r_gb], dma=True)
    rt, r_rt = kb.sb([128, 8, NB], BF16)
    at, r_at = kb.sb([128, 8, NB], F32)
    agt, r_agt = kb.sb([128, 8, NB], BF16)
    ap_, r_ap = kb.sb([128, 8, NB], BF16)
    yft, r_yft = kb.sb([128, 6, NB], F32)
    ybt, r_ybt = kb.sb([128, 6, NB], F32)
    sgt, r_sgt = kb.sb([128, 6, NB], BF16)
    t1, r_t1 = kb.sb([128, 6, NB], F32)
    s1, r_s1 = kb.sb([128, 6, NB], BF16)
    s2, r_s2 = kb.sb([128, 6, NB], BF16)
    zs, r_zs = kb.sb([128, NB], F32)
    mg = [kb.sb([128, 3, NB], BF16) for _ in range(2)]
    m1, r_m1 = kb.sb([128, 3, NB], F32)
    mo = [kb.sb([128, NB], BF16) for _ in range(2)]
    pz = [kb.ps() for _ in range(2)]
    pb = [kb.ps() for _ in range(6)]
    blocks = [(s, min(NB, T - s)) for s in range(0, T, NB)]
    k = 0
    for (s0, n) in blocks:
        bs = slice(s0, s0 + n)
        for dst, src, r, eng, kk in ((rt, rT, r_rt, "sp", 8), (at, aT, r_at, "act", 8), (agt, agT, r_agt, "sp", 8), (yft, yf, r_yft, "act", 6), (ybt, yb, r_ybt, "sp", 6), (sgt, sgT, r_sgt, "act", 6)):
            p.add(eng, lambda e, dst=dst, src=src, bs=bs, n=n: e.dma_start(out=dst[:, :, 0:n], in_=src[:, bs].rearrange("(a p) n -> p a n", p=128)), writes=[r], dma=True)
        p.add("dve", lambda e, n=n: e.tensor_tensor(out=ap_[:, :, 0:n], in0=at[:, :, 0:n], in1=agt[:, :, 0:n], op=ALU.mult), reads=[r_at, r_agt], writes=[r_ap])
        p.add("pool", lambda e, n=n: e.tensor_tensor(out=yft[:, :, 0:n], in0=yft[:, :, 0:n], in1=ybt[:, :, 0:n], op=ALU.add), reads=[r_yft, r_ybt], writes=[r_yft])
        p.add("act", lambda e, n=n: e.activation(out=t1[:, :, 0:n], in_=yft[:, :, 0:n], func=AF.Square), reads=[r_yft], writes=[r_t1])
        p.add("dve", lambda e, n=n: e.tensor_scalar(out=t1[:, :, 0:n], in0=t1[:, :, 0:n], scalar1=0.044715, scalar2=1.0, op0=ALU.mult, op1=ALU.add), reads=[r_t1], writes=[r_t1])
        p.add("dve", lambda e, n=n: e.tensor_tensor(out=t1[:, :, 0:n], in0=t1[:, :, 0:n], in1=yft[:, :, 0:n], op=ALU.mult), reads=[r_t1, r_yft], writes=[r_t1])
        p.add("act", lambda e, n=n: e.activation(out=t1[:, :, 0:n], in_=t1[:, :, 0:n], func=AF.Sigmoid, scale=1.5957691216057308), reads=[r_t1], writes=[r_t1])
        p.add("dve", lambda e, n=n: e.tensor_tensor(out=s1[:, :, 0:n], in0=t1[:, :, 0:n], in1=yft[:, :, 0:n], op=ALU.mult), reads=[r_t1, r_yft], writes=[r_s1])
        for j in range(6):
            pzz, r_pz = pz[j % 2]
            for kt in range(6):
                p.add("pe", lambda e, j=j, kt=kt, n=n, pzz=pzz: e.matmul(pzz[:, 0:n], gl[:, kt, j * 128:(j + 1) * 128], s1[:, kt, 0:n], start=(kt == 0), stop=(kt == 5)), reads=[r_gl, r_s1], writes=[r_pz])
            p.add("act", lambda e, j=j, n=n, pzz=pzz: e.activation(out=zs[:, 0:n], in_=pzz[:, 0:n], func=AF.Sigmoid, bias=gb[:, j:j + 1]), reads=[r_pz, r_gb], writes=[r_zs])
            p.add("dve", lambda e, j=j, n=n: e.tensor_tensor(out=zs[:, 0:n], in0=zs[:, 0:n], in1=s1[:, j, 0:n], op=ALU.mult), reads=[r_zs, r_s1], writes=[r_zs])
            p.add("dve", lambda e, j=j, n=n: e.tensor_tensor(out=s2[:, j, 0:n], in0=zs[:, 0:n], in1=sgt[:, j, 0:n], op=ALU.mult), reads=[r_zs, r_sgt], writes=[r_s2])
        for f in range(16):
            mgb, r_mg = mg[k % 2]
            mob, r_mo = mo[k % 2]
            pp = [pb[(3 * k + i) % 6] for i in range(3)]
            k += 1
            fs = slice(f * 128, (f + 1) * 128)
            p.add("sp", lambda e, f=f, bs=bs, n=n, mgb=mgb: e.dma_start(out=mgb[:, :, 0:n], in_=mgT[:, bs].rearrange("(b f p) n -> f p b n", b=3, p=128)[f]), writes=[r_mg], dma=True)
            for (w, r_w, src, r_src, nk, (pt, r_pt)) in ((wr, r_wr, rt, r_rt, 8, pp[0]), (wa, r_wa, ap_, r_ap, 8, pp[1]), (ws, r_ws, s2, r_s2, 6, pp[2])):
                for kt in range(nk):
                    p.add("pe", lambda e, w=w, src=src, kt=kt, nk=nk, fs=fs, n=n, pt=pt: e.matmul(pt[:, 0:n], w[:, kt, fs], src[:, kt, 0:n], start=(kt == 0), stop=(kt == nk - 1)), reads=[r_w, r_src], writes=[r_pt])
            for i in range(3):
                pt, r_pt = pp[i]
                p.add("dve", lambda e, i=i, n=n, pt=pt, mgb=mgb: e.tensor_tensor(out=m1[:, i, 0:n], in0=pt[:, 0:n], in1=mgb[:, i, 0:n], op=ALU.mult), reads=[r_pt, r_mg], writes=[r_m1])
            p.add("pool", lambda e, n=n: e.tensor_tensor(out=m1[:, 0, 0:n], in0=m1[:, 0, 0:n], in1=m1[:, 1, 0:n], op=ALU.add), reads=[r_m1], writes=[r_m1])
            p.add("pool", lambda e, n=n, mob=mob: e.tensor_tensor(out=mob[:, 0:n], in0=m1[:, 0, 0:n], in1=m1[:, 2, 0:n], op=ALU.add), reads=[r_m1], writes=[r_mo])
            p.add("act", lambda e, fs=fs, bs=bs, n=n, mob=mob: e.dma_start(out=mT[fs, bs], in_=mob[:, 0:n]), reads=[r_mo], writes=[r_mT], dma=True)
    return kb.finish([r_mT])


ALPHA = (2.0 * 2) ** 0.25


def build_LC2b(NT=17):
    kb = KB()
    nc, p = kb.nc, kb.p
    T = NT * 128
    mT, r_mT = kb.din("mT", [D, T], BF16)
    wo_, r_wo_ = kb.din("w_out", [D, D])
    x, r_x = kb.din("x", [T, D])
    rows, r_rows = kb.din("rows", [128, 4, D])
    xo, r_xo = kb.dout("xo", [T, D])
    wo, r_wo = kb.sb([128, 16, D], BF16)
    rw, r_rw = kb.sb([128, 4, D], F32)
    p.add("pool", lambda e: e.dma_start(out=wo[:], in_=wo_.rearrange("(a p) n -> p a n", p=128)), writes=[r_wo], dma=True)
    p.add("sp", lambda e: e.dma_start(out=rw[:], in_=rows), writes=[r_rw], dma=True)
    mt = [kb.sb([128, 16, 128], BF16) for _ in range(2)]
    xt = [kb.sb([128, D], F32) for _ in range(2)]
    z = [kb.sb([128, D], F32) for _ in range(2)]
    pb = [kb.ps() for _ in range(8)]
    st, r_st = kb.sb([128, 4, 6], F32)
    mv, r_mv = kb.sb([128, 2], F32)
    rs, r_rs = kb.sb([128, 1], F32)
    for tt in range(NT):
        b = tt % 2
        mtb, r_mt = mt[b]; xtb, r_xt = xt[b]; zb, r_z = z[b]
        tsl = slice(tt * 128, (tt + 1) * 128)
        gi = 0 if tt < NT - 1 else 1
        p.add("sp", lambda e, mtb=mtb, tsl=tsl: e.dma_start(out=mtb[:], in_=mT[:, tsl].rearrange("(a p) n -> p a n", p=128)), writes=[r_mt], dma=True)
        p.add("act", lambda e, xtb=xtb, tsl=tsl: e.dma_start(out=xtb[:], in_=x[tsl, :]), writes=[r_xt], dma=True)
        for cb in range(4):
            pt, r_pt = pb[(tt * 4 + cb) % 8]
            cs = slice(cb * 512, (cb + 1) * 512)
            for kt in range(16):
                p.add("pe", lambda e, kt=kt, cs=cs, pt=pt, mtb=mtb: e.matmul(pt[:, :], mtb[:, kt, :], wo[:, kt, cs], start=(kt == 0), stop=(kt == 15)), reads=[r_mt, r_wo], writes=[r_pt])
            p.add("dve", lambda e, cs=cs, pt=pt, zb=zb, gi=gi: e.tensor_tensor(out=zb[:, cs], in0=pt[:, :], in1=rw[:, gi, cs], op=ALU.mult), reads=[r_pt, r_rw], writes=[r_z])
            p.add("dve", lambda e, cs=cs, zb=zb, xtb=xtb: e.scalar_tensor_tensor(out=zb[:, cs], in0=xtb[:, cs], scalar=ALPHA, in1=zb[:, cs], op0=ALU.mult, op1=ALU.add), reads=[r_xt, r_z], writes=[r_z])
            p.add("dve", lambda e, cb=cb, cs=cs, zb=zb: e.bn_stats(out=st[:, cb, :], in_=zb[:, cs]), reads=[r_z], writes=[r_st])
        p.add("dve", lambda e: e.bn_aggr(out=mv[:, :], in_=st[:, :, :].rearrange("p a b -> p (a b)")), reads=[r_st], writes=[r_mv])
        p.add("act", lambda e: e.activation(out=rs[:, :], in_=mv[:, 1:2], func=AF.Sqrt, bias=EPS), reads=[r_mv], writes=[r_rs])
        p.add("dve", lambda e: e.reciprocal(out=rs[:, :], in_=rs[:, :]), reads=[r_rs], writes=[r_rs])
        p.add("dve", lambda e, zb=zb: e.tensor_scalar(out=zb[:, :], in0=zb[:, :], scalar1=mv[:, 0:1], scalar2=rs[:, 0:1], op0=ALU.subtract, op1=ALU.mult), reads=[r_z, r_mv, r_rs], writes=[r_z])
        p.add("pool", lambda e, zb=zb: e.tensor_tensor(out=zb[:, :], in0=zb[:, :], in1=rw[:, 2, :], op=ALU.mult), reads=[r_z, r_rw], writes=[r_z])
        p.add("pool", lambda e, zb=zb: e.tensor_tensor(out=zb[:, :], in0=zb[:, :], in1=rw[:, 3, :], op=ALU.add), reads=[r_z, r_rw], writes=[r_z])
        p.add("sp", lambda e, zb=zb, tsl=tsl: e.dma_start(out=xo[tsl, :], in_=zb[:, :]), reads=[r_z], writes=[r_xo], dma=True)
    return kb.finish([r_xo])


import ml_dtypes
_BF = ml_dtypes.bfloat16
_PROGS = {}


def _prog(key, fn, *a):
    if key not in _PROGS:
        _PROGS[key] = fn(*a)
    return _PROGS[key]


def _run(nc, maps):
    res = run_bass_kernel_spmd(nc, maps, core_ids=list(range(8)))
    return res.results


def _rope_tables(hd):
    per = hd // 4
    n = np.arange(NLAT)
    row = (n // 64).astype(np.float32)
    col = (n % 64).astype(np.float32)
    inv = (np.float32(10000.0) ** (-np.arange(per, dtype=np.float32) / np.float32(per))).astype(np.float32)
    ang = np.concatenate([row[:, None] * inv, col[:, None] * inv], axis=-1).astype(np.float32)
    return np.cos(ang).astype(np.float32), np.sin(ang).astype(np.float32)


def _ret_cst(lg, diag):
    idx = np.arange(128, dtype=np.float32)
    rel = idx[None, :] - idx[:, None]
    mask = (rel >= 0) if diag else (rel > 0)
    cst = np.zeros((128, 260), np.float32)
    cst[:, 0:128] = np.where(mask, rel, 0)
    cst[:, 128:256] = mask
    cst[:, 256] = lg
    cst[:, 257] = idx + 1
    cst[:, 258] = 127 - idx
    cst[:, 259] = 128
    return cst


def _s5_inputs(u, a_re, a_im, log_dt, b_re, b_im, c_re, c_im, dsk, NB):
    N = u.shape[0]
    prm = np.zeros((128, 3, 6), np.float32)
    Bm = np.zeros((128, 6, 2, 32), np.float32)
    Cm = np.zeros((128, 6, 2, 32), np.float32)
    dk = np.zeros((32, 6), np.float32)
    for mt in range(6):
        for gp in range(2):
            g = 2 * mt + gp
            ps = slice(gp * 64, gp * 64 + 64)
            cs = slice(gp * 16, gp * 16 + 16)
            prm[ps, 0, mt] = a_re[g]
            prm[ps, 1, mt] = a_im[g]
            prm[ps, 2, mt] = log_dt[g]
            Bm[ps, mt, 0, cs] = b_re[g]
            Bm[ps, mt, 1, cs] = b_im[g]
            Cm[ps, mt, 0, cs] = c_re[g].T
            Cm[ps, mt, 1, cs] = c_im[g].T
            dk[cs, mt] = dsk[g]
    uR4 = np.ascontiguousarray(u.reshape(N // 16, 4, 4, 6, 32).transpose(3, 2, 4, 1, 0).reshape(6, 128, 4, N // 16))
    idn4q = np.zeros((128, 4, 32), np.float32)
    for q in range(4):
        idn4q[q * 32:(q + 1) * 32, q, :] = np.eye(32, dtype=np.float32)
    return dict(uR4=uR4, prm=prm, Bm=Bm, Cm=Cm, dsk4=np.ascontiguousarray(np.tile(dk, (4, 1))), idn4q=idn4q, idn=np.eye(128, dtype=np.float32))


def kernel(x, c, ctx, c_ctx, ada_w, ada_b, w_in, ret_log_decay, ret_gn_w, att_q_norm, att_k_norm,
           s5_a_re, s5_a_im, s5_log_dt, s5_b_re, s5_b_im, s5_c_re, s5_c_im, s5_d, s5_glu_w, s5_glu_b,
           w_br_ret, w_br_att, w_br_s5, w_out, ln_w, ln_b):
    A = lambda v: np.ascontiguousarray(np.asarray(v))
    x = A(x); ctx = A(ctx)
    NS = NCTX + NLAT
    C = np.ascontiguousarray
    c3T = C(np.stack([A(c)[0], A(c)[1], A(c_ctx)], axis=1).astype(np.float32))
    maps = []
    for k in range(8):
        cs = slice(k * 768, (k + 1) * 768)
        maps.append(dict(adaw=C(A(ada_w)[:, :, cs]), adab=C(A(ada_b)[:, cs].reshape(2, 6, 128).transpose(2, 0, 1)), c3T=c3T))
    r0 = _run(_prog("L0", build_L0), maps)
    mod = np.concatenate([r["modT"] for r in r0], axis=1)
    cosR, sinR = _rope_tables(256)
    cosA, sinA = _rope_tables(128)
    xl = x.reshape(2 * NLAT, D)
    h = ctx.reshape(2 * NCTX, D)
    for l in range(2):
        shift, scale, gate = mod[l, 0:D], mod[l, D:2 * D], mod[l, 2 * D:3 * D]
        maps = []
        for k in range(8):
            b = k // 4
            ct = h[k * 128:(k + 1) * 128] if k < 4 else np.zeros((128, D), np.float32)
            xT = C(np.concatenate([xl[k * 2048:(k + 1) * 2048], ct], axis=0).T)
            modv = C(np.stack([shift[:, b], scale[:, b], shift[:, 2], scale[:, 2]], axis=1))
            ropeR = np.zeros((128, 17, 2, 128), np.float32)
            ropeA = np.zeros((128, 17, 2, 64), np.float32)
            n0 = (k % 4) * 2048
            ropeR[:, :16, 0] = cosR[n0:n0 + 2048].reshape(16, 128, 128).transpose(1, 0, 2)
            ropeR[:, :16, 1] = sinR[n0:n0 + 2048].reshape(16, 128, 128).transpose(1, 0, 2)
            ropeA[:, :16, 0] = cosA[n0:n0 + 2048].reshape(16, 128, 64).transpose(1, 0, 2)
            ropeA[:, :16, 1] = sinA[n0:n0 + 2048].reshape(16, 128, 64).transpose(1, 0, 2)
            ropeR[:, 16, 0] = 1.0
            ropeA[:, 16, 0] = 1.0
            qkw = C(np.broadcast_to(np.stack([A(att_q_norm)[l], A(att_k_norm)[l]])[None], (128, 2, 128)).astype(np.float32))
            maps.append(dict(xT=xT, modv=modv, w_in=A(w_in)[l], ropeR=ropeR, ropeA=ropeA, qkw=qkw))
        rA = _run(_prog("LA", build_LA), maps)
        O = [r["O"] for r in rA]
        OL = np.concatenate([o[:2048] for o in O], axis=0).reshape(2, NLAT, INW)
        OC = np.concatenate([o[2048:] for o in O[:4]], axis=0).reshape(2, NCTX, INW)
        seq = np.concatenate([OC, OL], axis=1)
        seqb = np.concatenate([OC[:, ::-1], OL[:, ::-1]], axis=1)
        sg = lambda S_, name, w: S_[:, :, SEG[name]:SEG[name] + w]
        maps = [dict() for _ in range(8)]
        for dr, S_, sfx in ((0, seq, "_f"), (1, seqb, "_b")):
            for k in range(8):
                b, j = k // 4, k % 4
                hs = slice(j * 256, (j + 1) * 256)
                q = sg(S_, "ret_q", 1024)[b][:, hs]; kk = sg(S_, "ret_k", 1024)[b][:, hs]; vv = sg(S_, "ret_v", 1024)[b][:, hs]
                for kk_, vv_ in dict(qT=C(q.T), kT=C(kk.T), k=C(kk), v=C(vv), cst=_ret_cst(np.float32(A(ret_log_decay)[l, dr, j]), dr == 0)).items():
                    maps[k][kk_ + sfx] = vv_
        rr = _run(_prog("RETpair", build_RETpair, NS), maps)
        ret = []
        for dr, sfx in ((0, "_f"), (1, "_b")):
            o = np.stack([r["ro" + sfx] for r in rr]).reshape(2, 4, NS, 256).transpose(0, 2, 1, 3).reshape(2, NS, 1024)
            if dr == 1:
                o = np.concatenate([o[:, :NCTX][:, ::-1], o[:, NCTX:][:, ::-1]], axis=1)
            ret.append(o)
        maps = []
        for k in range(8):
            b, j = k // 4, k % 4
            q = sg(OL, "att_q", 1024)[b][:, j * 256:(j + 1) * 256].reshape(NLAT, 2, 128)
            kv = j // 2
            kk = sg(seq, "att_k", 256)[b][:, kv * 128:(kv + 1) * 128]
            vv = sg(seq, "att_v", 256)[b][:, kv * 128:(kv + 1) * 128]
            maps.append(dict(qT=C(q.transpose(1, 2, 0)), kT=C(kk.T), v=C(vv)))
        ra = _run(_prog("ATT", build_ATT, NLAT, NS), maps)
        aTl = np.stack([r["aT"] for r in ra]).reshape(2, 1024, NLAT)
        aTc = np.zeros((2, 1024, NCTX), np.float32)
        if l == 0:
            maps = []
            for k in range(8):
                b, j = k // 4, k % 4
                q = sg(OC, "att_q", 1024)[b][:, j * 256:(j + 1) * 256].reshape(NCTX, 2, 128)
                kv = j // 2
                kk = sg(OC, "att_k", 256)[b][:, kv * 128:(kv + 1) * 128]
                vv = sg(OC, "att_v", 256)[b][:, kv * 128:(kv + 1) * 128]
                maps.append(dict(qT=C(q.transpose(1, 2, 0)), kT=C(kk.T), v=C(vv)))
            rc = _run(_prog("ATTC", build_ATT, NCTX, NCTX, 256), maps)
            aTc = np.stack([r["aT"] for r in rc]).reshape(2, 1024, NCTX)
        maps = [dict() for _ in range(8)]
        for dr, S_, sfx in ((0, seq, "_f"), (1, seqb, "_b")):
            for k in range(8):
                b, j = k // 4, k % 4
                gs = slice(12 * j, 12 * j + 12)
                u = sg(S_, "s5_u", 768)[b][:, j * 192:(j + 1) * 192]
                dk = A(s5_d)[l].reshape(48, 16)[gs] if dr == 0 else np.zeros((12, 16), np.float32)
                im = _s5_inputs(u, A(s5_a_re)[l, dr, gs], A(s5_a_im)[l, dr, gs], A(s5_log_dt)[l, dr, gs], A(s5_b_re)[l, dr, gs], A(s5_b_im)[l, dr, gs],
                                A(s5_c_re)[l, dr, gs], A(s5_c_im)[l, dr, gs], dk, 256)
                for kk_, vv_ in im.items():
                    maps[k][kk_ + sfx] = vv_
        rs_ = _run(_prog("S5pair", build_S5pair, NS // 16), maps)
        ys = []
        for dr, sfx in ((0, "_f"), (1, "_b")):
            y = np.stack([r["yR" + sfx].transpose(0, 1, 3, 2).reshape(192, NS) for r in rs_]).reshape(2, 768, NS)
            if dr == 1:
                y = np.concatenate([y[:, :, :NCTX][:, :, ::-1], y[:, :, NCTX:][:, :, ::-1]], axis=2)
            ys.append(y)
        def tok(arr_lat, arr_ctx, k):
            W = arr_lat.shape[-1]
            ct = arr_ctx.reshape(2 * NCTX, W)[k * 128:(k + 1) * 128] if k < 4 else np.zeros((128, W), arr_lat.dtype)
            return np.concatenate([arr_lat.reshape(2 * NLAT, W)[k * 2048:(k + 1) * 2048], ct], axis=0)

        def tokT(arr_lat, arr_ctx, k):
            b = k // 4
            n0 = (k % 4) * 2048
            W = arr_lat.shape[1]
            if k < 4:
                bc, c0 = k // 2, (k % 2) * 128
                ct = arr_ctx[bc][:, c0:c0 + 128]
            else:
                ct = np.zeros((W, 128), arr_lat.dtype)
            return np.concatenate([arr_lat[b][:, n0:n0 + 2048], ct], axis=1)
        gnw = C(np.broadcast_to(A(ret_gn_w)[l][None], (128, 1024)).astype(np.float32))
        maps = [dict(rf=C(tok(ret[0][:, NCTX:], ret[0][:, :NCTX], k)), rb=C(tok(ret[1][:, NCTX:], ret[1][:, :NCTX], k)), g=C(O[k][:, SEG["ret_g"]:SEG["ret_g"] + 1024]), gnw=gnw) for k in range(8)]
        r1 = _run(_prog("LC1", build_LC1), maps)
        maps = []
        for k in range(8):
            Ok = O[k]
            maps.append(dict(rT=C(r1[k]["r"].T), aT=C(tokT(aTl, aTc, k)), agT=C(Ok[:, SEG["att_g"]:SEG["att_g"] + 1024].T),
                             yf=C(tokT(ys[0][:, :, NCTX:], ys[0][:, :, :NCTX], k)), yb=C(tokT(ys[1][:, :, NCTX:], ys[1][:, :, :NCTX], k)),
                             sgT=C(Ok[:, SEG["s5_g"]:SEG["s5_g"] + 768].T), mgT=C(Ok[:, SEG["merge"]:].T),
                             w_br_ret=A(w_br_ret)[l], w_br_att=A(w_br_att)[l], w_br_s5=A(w_br_s5)[l], glu_w=A(s5_glu_w)[l],
                             glu_b=C(A(s5_glu_b)[l].reshape(6, 128).T)))
        r2 = _run(_prog("LC2a", build_LC2a), maps)
        maps = []
        for k in range(8):
            b = k // 4
            ct = h[k * 128:(k + 1) * 128] if k < 4 else np.zeros((128, D), np.float32)
            xk = C(np.concatenate([xl[k * 2048:(k + 1) * 2048], ct], axis=0))
            rows = C(np.broadcast_to(np.stack([gate[:, b], gate[:, 2], A(ln_w)[l], A(ln_b)[l]])[None], (128, 4, D)).astype(np.float32))
            maps.append(dict(mT=r2[k]["mT"], w_out=A(w_out)[l], x=xk, rows=rows))
        r3 = _run(_prog("LC2b", build_LC2b), maps)
        xl = np.concatenate([r["xo"][:2048] for r in r3], axis=0)
        h = np.concatenate([r["xo"][2048:] for r in r3[:4]], axis=0)
    return xl.reshape(2, NLAT, D).astype(np.float32)
```
